# Optimizing a Trainium2 kernel written in Bass

```python
import jax, jax.numpy as jnp
from jax import lax
import numpy as np

D_MODEL = 1024
BATCH = 8
SEQ = 2048
DEPTH = 2

HEAD_DIM = 64
NSA_HEADS = 8
NSA_KV_GROUPS = 2
NSA_WIDTH = NSA_HEADS * HEAD_DIM
CMP_STRIDE = 16
CMP_BLOCK = 2 * CMP_STRIDE
CMP_HIDDEN = 128
SLC_BLOCK = 64
SLC_TOPN = 16
NSA_WINDOW = 512
SLC_QUERY_BLOCK = 64
SWA_HEADS = 4
SWA_KV_HEADS = 2
SWA_WIDTH = SWA_HEADS * HEAD_DIM
SWA_WINDOW = 128
RWKV_HEADS = 4
RWKV_WIDTH = RWKV_HEADS * HEAD_DIM
DECAY_LORA = 64
ICLR_LORA = 64
VRES_LORA = 32
RWKV_SHIFT_WIDTH = 3 * RWKV_WIDTH + DECAY_LORA + ICLR_LORA

N_BRANCHES = 3
QUERY_BLOCK = 128
NORM_EPS = 1e-6
GN_EPS = 64e-5
NEG_INF = -1e30
FORCE = 1e9
F32 = jnp.float32

IN_SEGMENTS = (
    ("a_q", NSA_WIDTH),
    ("a_kv_cmp", 2 * NSA_KV_GROUPS * HEAD_DIM),
    ("a_kv_slc", 2 * NSA_KV_GROUPS * HEAD_DIM),
    ("a_kv_win", 2 * NSA_KV_GROUPS * HEAD_DIM),
    ("a_gate", 3 * NSA_HEADS),
    ("a_z", NSA_WIDTH),
    ("b_q", SWA_WIDTH),
    ("b_kv", 2 * SWA_KV_HEADS * HEAD_DIM),
    ("b_z", SWA_WIDTH),
    ("c_shift", RWKV_SHIFT_WIDTH),
    ("c_z", RWKV_WIDTH),
    ("merge", N_BRANCHES * D_MODEL),
)
N_IN = sum(size for _, size in IN_SEGMENTS)

kernel_name = "hybrid_nsa_swa_rwkv7_gated_parallel"


def rms_norm(x, g):
    xf = x.astype(F32)
    y = xf * lax.rsqrt(jnp.mean(xf * xf, axis=-1, keepdims=True) + NORM_EPS)
    return (y * g.astype(F32)).astype(x.dtype)


def split_columns(h):
    out = {}
    off = 0
    for name, size in IN_SEGMENTS:
        out[name] = h[..., off:off + size]
        off += size
    return out


def masked_softmax(s, valid, sink=None):
    s = jnp.where(valid, s, NEG_INF)
    m = jnp.max(s, axis=-1, keepdims=True)
    if sink is not None:
        m = jnp.maximum(m, sink)
    p = jnp.where(valid, jnp.exp(s - m), 0.0)
    denom = jnp.sum(p, axis=-1, keepdims=True)
    if sink is not None:
        denom = denom + jnp.exp(sink - m)
    denom = jnp.where(denom > 0, denom, 1.0)
    return p / denom


def banded_attention(q, k, v, window, sink=None):
    b, t, g, hg, dh = q.shape
    nb = t // QUERY_BLOCK
    span = window + QUERY_BLOCK
    pad = ((0, 0), (window, 0), (0, 0), (0, 0))
    kp = jnp.pad(k, pad)
    vp = jnp.pad(v, pad)
    scale = dh ** -0.5
    sink_b = None if sink is None else sink.astype(F32)[None, :, :, None, None]

    def block(i):
        start = i * QUERY_BLOCK
        qb = lax.dynamic_slice_in_dim(q, start, QUERY_BLOCK, axis=1)
        kb = lax.dynamic_slice_in_dim(kp, start, span, axis=1)
        vb = lax.dynamic_slice_in_dim(vp, start, span, axis=1)
        s = jnp.einsum('bqghd,bkgd->bghqk', qb, kb, preferred_element_type=F32) * scale
        tpos = start + jnp.arange(QUERY_BLOCK)
        kpos = start - window + jnp.arange(span)
        rel = tpos[:, None] - kpos[None, :]
        valid = (rel >= 0) & (rel < window) & (kpos[None, :] >= 0)
        p = masked_softmax(s, valid, sink_b)
        return jnp.einsum('bghqk,bkgd->bqghd', p, vb.astype(F32)).astype(q.dtype)

    out = lax.map(block, jnp.arange(nb))
    return jnp.moveaxis(out, 0, 1).reshape(b, t, g, hg, dh)


def nsa_attention(q, kv_cmp, kv_slc, kv_win, gate_logits, pe_k, w1_k, w2_k, pe_v, w1_v, w2_v):
    b, t, _ = q.shape
    g, hg, dh = NSA_KV_GROUPS, NSA_HEADS // NSA_KV_GROUPS, HEAD_DIM
    q = q.reshape(b, t, g, hg, dh)
    scale = dh ** -0.5

    def kv_split(kv):
        kv = kv.reshape(b, t, 2, g, dh)
        return kv[:, :, 0], kv[:, :, 1]

    k_c, v_c = kv_split(kv_cmp)
    k_s, v_s = kv_split(kv_slc)
    k_w, v_w = kv_split(kv_win)

    n_cmp = t // CMP_STRIDE - 1

    def compress(z, pe, w1, w2):
        c = z.reshape(b, t // CMP_STRIDE, CMP_STRIDE, g, dh)
        blocks = jnp.concatenate([c[:, :-1], c[:, 1:]], axis=2) + pe[None, None, :, None, :]
        flat = jnp.moveaxis(blocks, 3, 2).reshape(b, n_cmp, g, CMP_BLOCK * dh)
        return jax.nn.silu(flat @ w1) @ w2

    kc = compress(k_c, pe_k, w1_k, w2_k)
    vc = compress(v_c, pe_v, w1_v, w2_v)
    s = jnp.einsum('btghd,bngd->bghtn', q, kc, preferred_element_type=F32) * scale
    tpos = jnp.arange(t)
    cmp_end = jnp.arange(n_cmp) * CMP_STRIDE + CMP_BLOCK - 1
    valid_c = cmp_end[None, :] <= tpos[:, None]
    p_cmp = masked_softmax(s, valid_c)
    o_cmp = jnp.einsum('bghtn,bngd->btghd', p_cmp, vc.astype(F32)).astype(q.dtype)

    n_slc = t // SLC_BLOCK
    n_sel = min(SLC_TOPN, n_slc)
    ci = jnp.arange(n_cmp)[:, None] * CMP_STRIDE
    sj = jnp.arange(n_slc)[None, :] * SLC_BLOCK
    overlap = ((ci < sj + SLC_BLOCK) & (ci + CMP_BLOCK > sj)).astype(F32)
    imp = jnp.einsum('bghtn,nj->bgtj', p_cmp, overlap)
    blk = jnp.arange(n_slc)[None, :]
    cur = (tpos // SLC_BLOCK)[:, None]
    forced = (blk == 0) | (blk == cur) | (blk == cur - 1)
    score = jnp.where(forced, FORCE, jnp.where(blk <= cur, imp, -FORCE))
    _, idx = lax.top_k(score, n_sel)

    ks = jnp.moveaxis(k_s.reshape(b, n_slc, SLC_BLOCK, g, dh), 3, 1)
    vs = jnp.moveaxis(v_s.reshape(b, n_slc, SLC_BLOCK, g, dh), 3, 1)
    nqb = t // SLC_QUERY_BLOCK
    q_blocks = jnp.moveaxis(q.reshape(b, nqb, SLC_QUERY_BLOCK, g, hg, dh), 1, 0)
    idx_blocks = jnp.moveaxis(idx.reshape(b, g, nqb, SLC_QUERY_BLOCK, n_sel), 2, 0)
    starts = jnp.arange(nqb) * SLC_QUERY_BLOCK
    bi = jnp.arange(b)[:, None, None, None]
    gi = jnp.arange(g)[None, :, None, None]

    def sel_block(args):
        qb, ib, start = args
        kg = ks[bi, gi, ib]
        vg = vs[bi, gi, ib]
        sc = jnp.einsum('bqghd,bgqnkd->bghqnk', qb, kg, preferred_element_type=F32) * scale
        kpos = ib[..., None] * SLC_BLOCK + jnp.arange(SLC_BLOCK)
        qpos = start + jnp.arange(SLC_QUERY_BLOCK)
        valid = (kpos <= qpos[None, None, :, None, None])[:, :, None]
        sc = sc.reshape(b, g, hg, SLC_QUERY_BLOCK, n_sel * SLC_BLOCK)
        valid = valid.reshape(b, g, 1, SLC_QUERY_BLOCK, n_sel * SLC_BLOCK)
        p = masked_softmax(sc, valid).reshape(b, g, hg, SLC_QUERY_BLOCK, n_sel, SLC_BLOCK)
        return jnp.einsum('bghqnk,bgqnkd->bqghd', p, vg.astype(F32)).astype(qb.dtype)

    o_slc = lax.map(sel_block, (q_blocks, idx_blocks, starts))
    o_slc = jnp.moveaxis(o_slc, 0, 1).reshape(b, t, g, hg, dh)

    o_win = banded_attention(q, k_w, v_w, NSA_WINDOW)

    gates = jax.nn.sigmoid(gate_logits.reshape(b, t, 3, g, hg))[..., None]
    o = gates[:, :, 0] * o_cmp + gates[:, :, 1] * o_slc + gates[:, :, 2] * o_win
    return o.reshape(b, t, NSA_WIDTH)


def swa_sink_attention(q, kv, sinks):
    b, t, _ = q.shape
    g, hg = SWA_KV_HEADS, SWA_HEADS // SWA_KV_HEADS
    q = q.reshape(b, t, g, hg, HEAD_DIM)
    kv = kv.reshape(b, t, 2, g, HEAD_DIM)
    o = banded_attention(q, kv[:, :, 0], kv[:, :, 1], SWA_WINDOW, sinks.reshape(g, hg))
    return o.reshape(b, t, SWA_WIDTH)


def token_shift(z, mu):
    prev = jnp.pad(z, ((0, 0), (1, 0), (0, 0)))[:, :-1]
    return z + (prev - z) * mu


def rwkv7_step(state, inp):
    r, w, k, v, a, bb = inp
    sa = jnp.einsum('bhvk,bhk->bhv', state, a)
    state = state * w[:, :, None, :] + sa[..., None] * bb[:, :, None, :] + v[..., None] * k[:, :, None, :]
    return state, jnp.einsum('bhvk,bhk->bhv', state, r)


def rwkv7_time_mix(feat, v_first, mu, w0, w2, a0, a2, k_k, k_a, r_k, ln_w, ln_b, v_res):
    b, t, _ = feat.shape
    h, n, c = RWKV_HEADS, HEAD_DIM, RWKV_WIDTH
    xs = token_shift(feat, mu)
    r = xs[..., :c]
    k = xs[..., c:2 * c]
    v = xs[..., 2 * c:3 * c]
    wd = xs[..., 3 * c:3 * c + DECAY_LORA]
    ad = xs[..., 3 * c + DECAY_LORA:]
    w = -jax.nn.softplus(-(w0 + jnp.tanh(wd) @ w2)) - 0.5
    decay = jnp.exp(-jnp.exp(w.astype(F32)))
    if v_res is None:
        v_first = v
    else:
        v0, v1, v2 = v_res
        v = v + (v_first - v) * jax.nn.sigmoid(v0 + (v @ v1) @ v2)
    a = jax.nn.sigmoid(a0 + ad @ a2)
    kk = (k * k_k).reshape(b, t, h, n).astype(F32)
    kk = kk / jnp.maximum(jnp.sqrt(jnp.sum(kk * kk, axis=-1, keepdims=True)), 1e-12)
    k = k * (1.0 + (a - 1.0) * k_a)

    def heads(z):
        return z.reshape(b, t, h, n).astype(F32)

    r_h, k_h, v_h, a_h, w_h = heads(r), heads(k), heads(v), heads(a), heads(decay)
    seq_in = tuple(jnp.moveaxis(z, 1, 0) for z in (r_h, w_h, k_h, v_h, -kk, kk * a_h))
    state0 = jnp.zeros((b, h, n, n), F32)
    _, ys = lax.scan(rwkv7_step, state0, seq_in)
    y = jnp.moveaxis(ys, 0, 1)
    mean = jnp.mean(y, axis=-1, keepdims=True)
    var = jnp.mean(jnp.square(y - mean), axis=-1, keepdims=True)
    y = (y - mean) * lax.rsqrt(var + GN_EPS) * ln_w.astype(F32).reshape(h, n) + ln_b.astype(F32).reshape(h, n)
    y = y + jnp.sum(r_h * k_h * r_k.astype(F32), axis=-1, keepdims=True) * v_h
    return y.reshape(b, t, c).astype(feat.dtype), v_first


def setup_inputs(seed: int = 0) -> dict:
    key = jax.random.key(seed)
    keys = iter(jax.random.split(key, 40))
    L = DEPTH
    C = RWKV_WIDTH

    def nrm(shape, scale):
        return jax.random.normal(next(keys), shape, F32) * scale

    return {
        "x": nrm((BATCH, SEQ, D_MODEL), 1.0),
        "norm_g": 1.0 + nrm((L, D_MODEL), 0.02),
        "w_in": nrm((L, D_MODEL, N_IN), D_MODEL ** -0.5),
        "b_merge": nrm((L, N_BRANCHES, D_MODEL), 0.02),
        "cmp_pe_k": nrm((L, CMP_BLOCK, HEAD_DIM), 0.1),
        "cmp_w1_k": nrm((L, CMP_BLOCK * HEAD_DIM, CMP_HIDDEN), (CMP_BLOCK * HEAD_DIM) ** -0.5),
        "cmp_w2_k": nrm((L, CMP_HIDDEN, HEAD_DIM), CMP_HIDDEN ** -0.5),
        "cmp_pe_v": nrm((L, CMP_BLOCK, HEAD_DIM), 0.1),
        "cmp_w1_v": nrm((L, CMP_BLOCK * HEAD_DIM, CMP_HIDDEN), (CMP_BLOCK * HEAD_DIM) ** -0.5),
        "cmp_w2_v": nrm((L, CMP_HIDDEN, HEAD_DIM), CMP_HIDDEN ** -0.5),
        "swa_sinks": nrm((L, SWA_HEADS), 0.5),
        "rwkv_mu": jax.random.uniform(next(keys), (L, RWKV_SHIFT_WIDTH), F32),
        "rwkv_w0": jax.random.uniform(next(keys), (L, C), F32, -6.0, -1.0),
        "rwkv_w2": nrm((L, DECAY_LORA, C), 0.5 * DECAY_LORA ** -0.5),
        "rwkv_a0": nrm((L, C), 0.5),
        "rwkv_a2": nrm((L, ICLR_LORA, C), 0.5 * ICLR_LORA ** -0.5),
        "rwkv_k_k": 0.85 + nrm((L, C), 0.02),
        "rwkv_k_a": 1.0 + nrm((L, C), 0.02),
        "rwkv_r_k": nrm((L, RWKV_HEADS, HEAD_DIM), 0.1),
        "rwkv_ln_w": 1.0 + nrm((L, C), 0.02),
        "rwkv_ln_b": nrm((L, C), 0.02),
        "rwkv_v0": nrm((L - 1, C), 0.5),
        "rwkv_v1": nrm((L - 1, C, VRES_LORA), C ** -0.5),
        "rwkv_v2": nrm((L - 1, VRES_LORA, C), 0.5 * VRES_LORA ** -0.5),
        "proj_a": nrm((L, NSA_WIDTH, D_MODEL), NSA_WIDTH ** -0.5),
        "proj_b": nrm((L, SWA_WIDTH, D_MODEL), SWA_WIDTH ** -0.5),
        "proj_c": nrm((L, RWKV_WIDTH, D_MODEL), RWKV_WIDTH ** -0.5),
        "w_out": nrm((L, D_MODEL, D_MODEL), D_MODEL ** -0.5),
        "final_g": 1.0 + nrm((D_MODEL,), 0.02),
    }


def reference(x, norm_g, w_in, b_merge, cmp_pe_k, cmp_w1_k, cmp_w2_k, cmp_pe_v, cmp_w1_v, cmp_w2_v,
              swa_sinks, rwkv_mu, rwkv_w0, rwkv_w2, rwkv_a0, rwkv_a2, rwkv_k_k, rwkv_k_a, rwkv_r_k,
              rwkv_ln_w, rwkv_ln_b, rwkv_v0, rwkv_v1, rwkv_v2, proj_a, proj_b, proj_c, w_out, final_g):
    b, t, d = x.shape
    v_first = None
    for l in range(DEPTH):
        xn = rms_norm(x, norm_g[l])
        cols = split_columns(xn @ w_in[l])
        y_a = nsa_attention(cols["a_q"], cols["a_kv_cmp"], cols["a_kv_slc"], cols["a_kv_win"], cols["a_gate"],
                            cmp_pe_k[l], cmp_w1_k[l], cmp_w2_k[l], cmp_pe_v[l], cmp_w1_v[l], cmp_w2_v[l])
        y_a = y_a * jax.nn.silu(cols["a_z"])
        y_b = swa_sink_attention(cols["b_q"], cols["b_kv"], swa_sinks[l]) * jax.nn.silu(cols["b_z"])
        v_res = None if l == 0 else (rwkv_v0[l - 1], rwkv_v1[l - 1], rwkv_v2[l - 1])
        y_c, v_first = rwkv7_time_mix(cols["c_shift"], v_first, rwkv_mu[l], rwkv_w0[l], rwkv_w2[l],
                                      rwkv_a0[l], rwkv_a2[l], rwkv_k_k[l], rwkv_k_a[l], rwkv_r_k[l],
                                      rwkv_ln_w[l], rwkv_ln_b[l], v_res)
        y_c = y_c * jax.nn.silu(cols["c_z"])
        gates = jax.nn.sigmoid(cols["merge"].reshape(b, t, N_BRANCHES, d) + b_merge[l])
        mixed = (gates[:, :, 0] * (y_a @ proj_a[l])
                 + gates[:, :, 1] * (y_b @ proj_b[l])
                 + gates[:, :, 2] * (y_c @ proj_c[l]))
        x = x + mixed @ w_out[l]
    return rms_norm(x, final_g)
```

```python
from contextlib import ExitStack
import numpy as np
import ml_dtypes
import concourse.bass as bass
import concourse.mybir as mybir
from concourse.bass_utils import run_bass_kernel_spmd

F32 = mybir.dt.float32
BF16 = mybir.dt.bfloat16
AF = mybir.ActivationFunctionType
ALU = mybir.AluOpType
AX = mybir.AxisListType

ENGINES = ("sp", "act", "dve", "pool", "pe")
INORDER_ENGINES = ("pe", "sp")
NDMASEM = 12


class _Op:
    __slots__ = ("eng", "fn", "deps", "dma", "sig", "sem", "val", "prev", "idx")


class Prog:
    def __init__(self, nc):
        self.nc = nc
        self.ops = []
        self.lastw = {}
        self.readers = {}

    def add(self, eng, fn, r=(), w=(), dma=False):
        o = _Op()
        o.eng, o.fn, o.dma, o.sig = eng, fn, dma, False
        o.idx = len(self.ops)
        deps = set()
        for k in r:
            if k in self.lastw:
                deps.add(self.lastw[k])
        for k in w:
            if k in self.lastw:
                deps.add(self.lastw[k])
            deps.update(self.readers.get(k, ()))
        o.deps = deps
        for k in r:
            lst = self.readers.setdefault(k, [])
            if not dma:
                lst[:] = [i for i in lst if self.ops[i].dma or self.ops[i].eng != eng]
            lst.append(o.idx)
        for k in w:
            self.lastw[k] = o.idx
            self.readers[k] = []
        self.ops.append(o)
        return o

    def barrier_keys(self):
        return list(self.lastw.keys())

    def _skip(self, d, o):
        if d.dma:
            return False
        if d.eng == "sp":
            return True
        if d.eng == o.eng and o.eng in INORDER_ENGINES:
            return True
        return False

    def emit(self, stack):
        nc = self.nc
        ops = self.ops
        for o in ops:
            for di in o.deps:
                d = ops[di]
                if not self._skip(d, o):
                    d.sig = True
        engsem = {e: stack.enter_context(nc.semaphore("s_" + e)) for e in ENGINES if e != "sp"}
        dmasem = {e: [stack.enter_context(nc.semaphore("d_%s%d" % (e, i))) for i in range(NDMASEM)]
                  for e in ("sp", "pool", "act")}
        cnt = {e: 0 for e in ENGINES}
        dcnt = {e: 0 for e in ENGINES}
        per = {e: [] for e in ENGINES}
        for o in ops:
            per[o.eng].append(o)
            if o.dma:
                n = dcnt[o.eng]
                dcnt[o.eng] += 1
                o.sem = dmasem[o.eng][n % NDMASEM]
                o.val = 16 * (n // NDMASEM + 1)
                o.prev = 16 * (n // NDMASEM)
            elif o.sig:
                cnt[o.eng] += 1
                o.sem = engsem[o.eng]
                o.val = cnt[o.eng]
        self.stats = dict(cnt=cnt, dcnt=dcnt, n={e: len(per[e]) for e in ENGINES})

        def run(e, eng):
            waited = {}
            nw = 0
            for o in per[eng]:
                for di in sorted(o.deps):
                    d = ops[di]
                    if self._skip(d, o):
                        continue
                    key = id(d.sem)
                    if waited.get(key, 0) >= d.val:
                        continue
                    e.wait_ge(d.sem, d.val)
                    nw += 1
                    waited[key] = d.val
                if o.dma and o.prev > 0 and waited.get(id(o.sem), 0) < o.prev:
                    e.wait_ge(o.sem, o.prev)
                    waited[id(o.sem)] = o.prev
                ins = o.fn(e)
                if o.dma:
                    ins.then_inc(o.sem, 16)
                elif o.sig:
                    ins.then_inc(o.sem, 1)
            self.stats.setdefault("waits", {})[eng] = nw

        with nc.Block() as block:
            @block.sync
            def _(e):
                run(e, "sp")

            @block.scalar
            def _(e):
                run(e, "act")

            @block.vector
            def _(e):
                run(e, "dve")

            @block.gpsimd
            def _(e):
                run(e, "pool")

            @block.tensor
            def _(e):
                run(e, "pe")

    def fence(self):
        start = getattr(self, "_fpos", 0)
        deps = set(o.idx for o in self.ops[start:] if o.dma)
        last = {}
        for o in self.ops:
            last[o.eng] = o.idx
        deps.update(last.values())
        for e in ENGINES:
            o = self.add(e, lambda eng: eng.nop())
            o.deps = set(deps)
        self._fpos = len(self.ops)


T = 2048
D = 1024
NT = 16
N_IN = 6808
DEPTH = 2
BIG = 30000.0
NCMP = 127
SEG = dict(a_q=(0, 512), a_kv_cmp=(512, 256), a_kv_slc=(768, 256), a_kv_win=(1024, 256), a_gate=(1280, 24),
           a_z=(1304, 512), b_q=(1816, 256), b_kv=(2072, 256), b_z=(2328, 256), c_shift=(2584, 896),
           c_z=(3480, 256), merge=(3736, 3072))
NFEAT = 18 * 128
TOK0 = NFEAT


def _perm():
    def seg(name, a, b):
        o = SEG[name][0]
        return list(range(o + a, o + b))
    p = []
    for hh in range(4):
        p += seg("a_q", hh * 64, hh * 64 + 64) + seg("a_q", (4 + hh) * 64, (4 + hh) * 64 + 64)
    p += seg("a_kv_cmp", 0, 128)
    p += seg("a_kv_cmp", 128, 256)
    p += seg("a_kv_slc", 0, 128)
    p += seg("a_kv_win", 0, 128)
    p += seg("b_q", 0, 64) + seg("b_q", 128, 192)
    p += seg("b_q", 64, 128) + seg("b_q", 192, 256)
    p += seg("b_kv", 0, 128)
    p += seg("c_shift", 0, 896)
    assert len(p) == NFEAT
    p += seg("a_kv_slc", 128, 256) + seg("a_kv_win", 128, 256) + seg("b_kv", 128, 256) + seg("a_gate", 0, 24)
    p += seg("a_z", 0, 512) + seg("b_z", 0, 256) + seg("c_z", 0, 256)
    p += seg("merge", 0, 3072)
    assert len(p) == N_IN and len(set(p)) == N_IN
    return np.array(p)


def _consts():
    c = {}
    c["identf"] = np.eye(128, dtype=np.float32)
    s = np.arange(128)[:, None]
    t = np.arange(128)[None, :]
    c["triA"] = np.where(s <= t, 0.0, -BIG).astype(np.float32)
    c["triB"] = np.where(s > t, 0.0, -BIG).astype(np.float32)
    E = np.zeros((32, 16, 128), np.float32)
    for j in range(16):
        for p in range(128):
            E[2 * j + p // 64, j, p] = BIG
    E2 = np.zeros((128, 2048), np.float32)
    E2[0:32] = E.reshape(32, 2048)
    E2[64:96] = E.reshape(32, 2048)
    c["Eall"] = E2
    n = np.arange(128)[:, None]
    tt = np.arange(T)[None, :]
    c["penc"] = np.where(16 * n + 31 <= tt, 0.0, -BIG).astype(np.float32)
    Mi = np.zeros((128, 16, 32), np.float32)
    Bc = np.zeros((128, 16, 32), np.float32)
    for i in range(16):
        for p in range(128):
            cur = 2 * i + p // 64
            for j in range(32):
                if j == 0:
                    Bc[p, i, j] = 10.0
                elif j == cur:
                    Bc[p, i, j] = 20.0
                elif j == cur - 1:
                    Bc[p, i, j] = 30.0
                elif j > cur:
                    Bc[p, i, j] = -1.0 - j
                else:
                    Mi[p, i, j] = 1.0
            if cur == 0:
                Bc[p, i, 0] = 20.0
            if cur == 1:
                Bc[p, i, 0] = 30.0
    c["Mi"] = Mi.reshape(128, 512)
    c["Bc"] = Bc.reshape(128, 512)
    ci = np.arange(128)[:, None] * 16
    sj = np.arange(32)[None, :] * 64
    c["ovl"] = ((ci < sj + 64) & (ci + 32 > sj)).astype(np.float32)
    r = np.arange(128)[:, None]
    q = np.arange(128)[None, :]
    c["mSL"] = (q < r).astype(np.float32)
    c["mSU"] = (r < q).astype(np.float32)
    c["mIU"] = (r <= q).astype(np.float32)
    c["bones"] = ((r // 64) == (q // 64)).astype(np.float32)
    c["hsel"] = ((np.arange(128)[:, None] // 64) == np.arange(2)[None, :]).astype(np.float32)
    return c


CONST_SHAPES = dict(identf=[128, 128], triA=[128, 128], triB=[128, 128], Eall=[128, 2048], penc=[128, 2048],
                    Mi=[128, 512], Bc=[128, 512], ovl=[128, 32], mSL=[128, 128], mSU=[128, 128], mIU=[128, 128],
                    bones=[128, 128], hsel=[128, 2])


def layer_input_shapes(l):
    s = dict(win=[D, N_IN], ng=[128, 8], ngb=[128, D], bmerge=[128, 3072], w1k=[128, 32, 128], w1v=[128, 32, 128],
             pek=[128, 32], pev=[128, 32], w2k=[128, 64], w2v=[128, 64], sinks=[128, 4], rv=[128, 20],
             wa=[128, 256], lnw=[128, 256], lnb=[128, 256], pw=[128, 16, 1024])
    if l > 0:
        s["v1"] = [128, 2, 32]
        s["v2"] = [32, 256]
    return s


def prep_layer(inp, l, perm):
    f = lambda a: np.ascontiguousarray(a, dtype=np.float32)
    o = {}
    o["win"] = f(inp["w_in"][l][:, perm])
    o["ng"] = f(inp["norm_g"][l].reshape(8, 128).T)
    o["ngb"] = f(np.broadcast_to(inp["norm_g"][l].reshape(1, D), (128, D)))
    o["bmerge"] = f(np.broadcast_to(inp["b_merge"][l].reshape(1, 3072), (128, 3072)))
    for nm, src in (("w1k", "cmp_w1_k"), ("w1v", "cmp_w1_v")):
        w = inp[src][l].reshape(32, 64, 128).transpose(1, 0, 2)
        o[nm] = f(np.concatenate([w, w], 0))
    for nm, src in (("pek", "cmp_pe_k"), ("pev", "cmp_pe_v")):
        p = inp[src][l].T
        o[nm] = f(np.concatenate([p, p], 0))
    o["w2k"] = f(inp["cmp_w2_k"][l])
    o["w2v"] = f(inp["cmp_w2_v"][l])
    o["sinks"] = f(np.broadcast_to(inp["swa_sinks"][l].reshape(1, 4), (128, 4)))
    rv = np.zeros((128, 20), np.float32)
    rv[:, 0:7] = inp["rwkv_mu"][l].reshape(7, 128).T
    for k, nm in enumerate(("rwkv_w0", "rwkv_a0", "rwkv_k_k", "rwkv_k_a")):
        rv[:, 7 + 2 * k:9 + 2 * k] = inp[nm][l].reshape(2, 128).T
    rv[:, 15:17] = inp["rwkv_r_k"][l].reshape(2, 128).T
    if l > 0:
        rv[:, 17:19] = inp["rwkv_v0"][l - 1].reshape(2, 128).T
    o["rv"] = rv
    o["wa"] = f(np.concatenate([inp["rwkv_w2"][l], inp["rwkv_a2"][l]], 0))
    o["lnw"] = f(np.broadcast_to(inp["rwkv_ln_w"][l].reshape(1, 256), (128, 256)))
    o["lnb"] = f(np.broadcast_to(inp["rwkv_ln_b"][l].reshape(1, 256), (128, 256)))
    pw = np.concatenate([inp["proj_a"][l].reshape(4, 128, 1024), inp["proj_b"][l].reshape(2, 128, 1024),
                         inp["proj_c"][l].reshape(2, 128, 1024), inp["w_out"][l].reshape(8, 128, 1024)], 0)
    o["pw"] = f(pw.transpose(1, 0, 2))
    if l > 0:
        o["v1"] = f(inp["rwkv_v1"][l - 1].reshape(2, 128, 32).transpose(1, 0, 2))
        o["v2"] = f(inp["rwkv_v2"][l - 1])
    return o


class KB:
    def __init__(self, layers, debug=False, upto="out"):
        self.layers = list(layers)
        self.debug = debug
        self.upto = upto
        nc = bass.Bass("TRN2", target_bir_lowering=False)
        nc.allow_low_precision("bf16 matmul operands, fp32 accumulation")
        self.nc = nc
        self.P = Prog(nc)
        self.rotc = {}

    def dma(self, eng, out, in_, r=(), w=()):
        return self.P.add(eng, lambda e: e.dma_start(out=out, in_=in_), r=r, w=w, dma=True)

    def mm(self, out, lhsT, rhs, start, stop, r=(), w=()):
        return self.P.add("pe", lambda e: e.matmul(out, lhsT=lhsT, rhs=rhs, start=start, stop=stop,
                                                   skip_group_check=True), r=r, w=w)

    def tr(self, out, in_, ident, r=(), w=()):
        return self.P.add("pe", lambda e: e.transpose(out=out, in_=in_, identity=ident), r=r, w=w)

    def act(self, out, in_, func, r=(), w=(), bias=None, scale=None, accum=None):
        kw = {}
        if bias is not None:
            kw["bias"] = bias
        if scale is not None:
            kw["scale"] = scale
        if accum is not None:
            kw["accum_out"] = accum
        return self.P.add("act", lambda e: e.activation(out=out, in_=in_, func=func, **kw), r=r, w=w)

    def tt(self, eng, out, in0, in1, op, r=(), w=()):
        return self.P.add(eng, lambda e: e.tensor_tensor(out=out, in0=in0, in1=in1, op=op), r=r, w=w)

    def ts(self, eng, out, in0, s1, s2, op0, op1=None, r=(), w=()):
        if op1 is None:
            return self.P.add(eng, lambda e: e.tensor_scalar(out=out, in0=in0, scalar1=s1, scalar2=None, op0=op0), r=r, w=w)
        return self.P.add(eng, lambda e: e.tensor_scalar(out=out, in0=in0, scalar1=s1, scalar2=s2, op0=op0, op1=op1), r=r, w=w)

    def stt(self, eng, out, in0, scalar, in1, op0, op1, r=(), w=()):
        return self.P.add(eng, lambda e: e.scalar_tensor_tensor(out=out, in0=in0, scalar=scalar, in1=in1, op0=op0, op1=op1), r=r, w=w)

    def cp(self, eng, out, in_, r=(), w=()):
        if eng == "act":
            return self.P.add("act", lambda e: e.copy(out=out, in_=in_), r=r, w=w)
        return self.P.add(eng, lambda e: e.tensor_copy(out=out, in_=in_), r=r, w=w)

    def memset(self, eng, ap, val, w=()):
        return self.P.add(eng, lambda e: e.memset(ap, val), w=w)

    def recip(self, out, in_, r=(), w=()):
        return self.P.add("dve", lambda e: e.reciprocal(out=out, in_=in_), r=r, w=w)

    def red(self, out, in_, op, r=(), w=()):
        return self.P.add("dve", lambda e: e.tensor_reduce(out=out, in_=in_, axis=AX.X, op=op), r=r, w=w)

    def rot(self, lo, hi):
        k = (lo, hi)
        c = self.rotc.get(k, 0)
        self.rotc[k] = c + 1
        b = lo + c % (hi - lo)
        return self.ps[b], "ps%d" % b

    def sbt(self, st, name, shape, dt=F32):
        self._nm = getattr(self, "_nm", 0) + 1
        return st.enter_context(self.nc.sbuf_tensor("%s_%d" % (name, self._nm), shape, dt))

    def build(self):
        nc = self.nc
        layers = self.layers
        dk = "ExternalOutput" if self.debug else "Internal"
        self.d = {}
        self.d["x"] = nc.dram_tensor("x", [T, D], F32, kind="ExternalInput").ap()
        for k, shp in CONST_SHAPES.items():
            self.d[k] = nc.dram_tensor(k, shp, F32, kind="ExternalInput").ap()
        self.d["fg"] = nc.dram_tensor("fg", [128, D], F32, kind="ExternalInput").ap()
        for l in layers:
            for k, shp in layer_input_shapes(l).items():
                self.d["%s%d" % (k, l)] = nc.dram_tensor("%s%d" % (k, l), shp, F32, kind="ExternalInput").ap()
        self.d["Z"] = nc.dram_tensor("scrZ", [T, 1024], F32, kind=dk).ap()
        self.d["G"] = nc.dram_tensor("scrG", [T, 3072], F32, kind=dk).ap()
        self.d["Y"] = nc.dram_tensor("scrY", [T, 1024], F32, kind=dk).ap()
        self.d["F"] = nc.dram_tensor("scrF", [7, 128, T], F32, kind=dk).ap()
        if 0 in layers and 1 in layers:
            self.d["vf"] = nc.dram_tensor("vfirst", [2, 128, T], F32, kind=dk).ap()
        elif 0 in layers:
            self.d["vf"] = nc.dram_tensor("vfirst", [2, 128, T], F32, kind="ExternalOutput").ap()
        else:
            self.d["vf"] = nc.dram_tensor("vfirst", [2, 128, T], F32, kind="ExternalInput").ap()
        self.d["xout"] = nc.dram_tensor("xout", [T, D], F32, kind="ExternalOutput").ap()
        if len(layers) > 1:
            self.d["xmid"] = nc.dram_tensor("xmid", [T, D], F32, kind=dk).ap()

        with ExitStack() as st:
            self.ps = [st.enter_context(nc.psum_tensor("psb%d" % b, [128, 512], F32)) for b in range(8)]
            self.load_consts(st)
            xin = self.d["x"]
            xkey = "x"
            for li, l in enumerate(layers):
                last = (li == len(layers) - 1)
                xo = self.d["xout"] if last else self.d["xmid"]
                xokey = "xout" if last else "xmid"
                self.layer(l, xin, xkey, xo, xokey)
                xin, xkey = xo, xokey
            self.P.fence()
            self.P.emit(st)
        return nc

    def load_consts(self, st):
        d = self.d
        c = self.c = {}
        f32c = ["identf", "Mi", "Bc", "mSL", "mSU", "mIU", "bones", "hsel"]
        for k in f32c:
            c[k] = self.sbt(st, k, CONST_SHAPES[k])
            self.dma("sp", c[k][:], d[k], w=[k])
        c["fg"] = self.sbt(st, "fg", [128, D])
        self.dma("sp", c["fg"][:], d["fg"], w=["fg"])
        bl = ["identf", "triA", "triB", "Eall", "penc", "ovl"]
        for k in bl:
            nm = "identb" if k == "identf" else k
            c[nm] = self.sbt(st, nm, CONST_SHAPES[k], BF16)
        with ExitStack() as s2:
            for k in bl:
                nm = "identb" if k == "identf" else k
                tmp = self.sbt(s2, k + "_f", CONST_SHAPES[k])
                self.dma("sp", tmp[:], d[k], w=[k + "_f"])
                self.cp("pool", c[nm][:], tmp[:], r=[k + "_f"], w=[nm])
            self.P.fence()

    def layer(self, l, xin, xkey, xo, xokey):
        with ExitStack() as sA:
            A = self.alloc_attn(sA)
            with ExitStack() as s1:
                self.phase01(s1, l, xin, xkey, A)
                self.P.fence()
            if self.upto in ("p0", "p1"):
                return
            with ExitStack() as s2:
                self.phase_attn(s2, l, A)
                self.P.fence()
        if self.upto in ("cmp", "attn"):
            return
        with ExitStack() as s3:
            self.phase_rwkv(s3, l)
            self.P.fence()
        if self.upto == "rwkv":
            return
        with ExitStack() as s4:
            self.phase_out(s4, l, xin, xkey, xo, xokey)
            self.P.fence()

    def alloc_attn(self, st):
        A = {}
        A["qTA"] = self.sbt(st, "qTA", [128, 4, T], BF16)
        A["qTB"] = self.sbt(st, "qTB", [128, 2, T], BF16)
        for k in ("kTc", "vTc", "kTs", "kTw", "kTb"):
            A[k] = self.sbt(st, k, [128, T], BF16)
        for k in ("Vs", "Vw", "Vb"):
            A[k] = self.sbt(st, k, [128, 16, 2, 68], BF16)
        A["gate"] = self.sbt(st, "gate", [128, 16, 24])
        return A

    def phase01(self, st, l, xin, xkey, A):
        c, d = self.c, self.d
        L = lambda k: d["%s%d" % (k, l)]
        xnT = self.sbt(st, "xnT", [128, 8, T], BF16)
        ng = self.sbt(st, "ng", [128, 8])
        self.dma("sp", ng[:], L("ng"), w=["ng"])
        wst = [self.sbt(st, "wst", [128, 8, 512]) for _ in range(2)]
        wbf = [self.sbt(st, "wbf", [128, 8, 512], BF16) for _ in range(2)]
        bm = self.sbt(st, "bm", [128, 3072])
        win = L("win").rearrange("(k p) n -> p k n", p=128)
        self._blk = 0

        def load_block(c0, ncol):
            b = self._blk % 2
            self._blk += 1
            for kc in range(8):
                self.dma("sp", wst[b][:, kc, 0:ncol], win[:, kc, c0:c0 + ncol], w=["wst%d" % b])
            self.cp("dve", wbf[b][:, 0:4, 0:ncol], wst[b][:, 0:4, 0:ncol], r=["wst%d" % b], w=["wbf%d" % b])
            self.cp("act", wbf[b][:, 4:6, 0:ncol], wst[b][:, 4:6, 0:ncol], r=["wst%d" % b], w=["wbf%d" % b])
            self.cp("pool", wbf[b][:, 6:8, 0:ncol], wst[b][:, 6:8, 0:ncol], r=["wst%d" % b], w=["wbf%d" % b])
            return wbf[b], "wbf%d" % b

        pre = {}
        s0 = ExitStack()
        xt = [self.sbt(s0, "xt", [128, D]) for _ in range(2)]
        xs = [self.sbt(s0, "xs", [128, D], BF16) for _ in range(2)]
        ngb = self.sbt(s0, "ngb", [128, D])
        self.dma("sp", ngb[:], L("ngb"), w=["ngb"])
        junk = self.sbt(s0, "junk", [128, D], BF16)
        ss = [self.sbt(s0, "ss", [128, 1]) for _ in range(2)]
        for k in ("Vs", "Vw", "Vb"):
            self.memset("pool", A[k][:], 0.0, w=[k])
            self.memset("pool", A[k][:, :, :, 64:65], 1.0, w=[k])
        for i in range(NT):
            b = i % 2
            kx, ks, kss = "xt%d" % b, "xs%d" % b, "ss%d" % b
            self.dma("sp", xt[b][:], xin[i * 128:(i + 1) * 128, :], r=[xkey], w=[kx])
            self.memset("pool", ss[b][:], 0.0, w=[kss])
            self.act(junk[:], xt[b][:], AF.Square, r=[kx], w=["junk", kss], accum=ss[b][:])
            self.ts("dve", ss[b][:], ss[b][:], 1.0 / D, 1e-6, ALU.mult, ALU.add, w=[kss])
            self.act(ss[b][:], ss[b][:], AF.Sqrt, w=[kss])
            self.recip(ss[b][:], ss[b][:], w=[kss])
            self.stt("dve", xs[b][:], xt[b][:], ss[b][:, 0:1], ngb[:], ALU.mult, ALU.mult, r=[kx, kss, "ngb"], w=[ks])
            pb, pk = self.rot(0, 8)
            pbb = pb[:].bitcast(BF16)
            for ch in range(8):
                self.tr(pbb[:, ch * 128:(ch + 1) * 128], xs[b][:, ch * 128:(ch + 1) * 128], c["identb"][:],
                        r=[ks, "identb"], w=[pk])
            self.cp("act" if i % 2 else "dve", xnT[:, :, i * 128:(i + 1) * 128], pbb.rearrange("p (c t) -> p c t", c=8),
                    w=[pk, "xnT"])
            if i == 5:
                pre[0] = load_block(0, 512)
                self.dma("sp", bm[:], L("bmerge"), w=["bm"])
            if i == 11:
                pre[1] = load_block(512, 512)
        self.P.fence()
        s0.close()
        if self.upto == "p0":
            return
        fst = [self.sbt(st, "fst", [128, T]) for _ in range(1)]
        zst = [self.sbt(st, "zst", [128, 512]) for _ in range(3)]

        fdest = []
        for hh in range(4):
            fdest.append((A["qTA"], hh, "qTA"))
        for k in ("kTc", "vTc", "kTs", "kTw"):
            fdest.append((A[k], None, k))
        for hh in range(2):
            fdest.append((A["qTB"], hh, "qTB"))
        fdest.append((A["kTb"], None, "kTb"))
        for cc in range(7):
            fdest.append((None, cc, "F"))
        self._ev = 0
        self._zi = 0

        def comp_feat(s0, nsub):
            def f(wb, wk):
                for s in range(s0, s0 + nsub):
                    dst, idx, dkey = fdest[s]
                    fb = 0
                    for tc in range(4):
                        pb, pk = self.rot(0, 8)
                        for kc in range(8):
                            self.mm(pb[:, :], wb[:, kc, (s - s0) * 128:(s - s0 + 1) * 128], xnT[:, kc, tc * 512:(tc + 1) * 512],
                                    kc == 0, kc == 7, r=[wk, "xnT"], w=[pk])
                        if dst is None:
                            o_ap, okey = fst[fb][:, tc * 512:(tc + 1) * 512], "fst%d" % fb
                        elif idx is None:
                            o_ap, okey = dst[:, tc * 512:(tc + 1) * 512], dkey
                        else:
                            o_ap, okey = dst[:, idx, tc * 512:(tc + 1) * 512], dkey
                        self.cp("act" if self._ev % 2 == 0 else "dve", o_ap, pb[:, :], w=[pk, okey])
                        self._ev += 1
                    if dst is None:
                        self.dma("pool", d["F"][idx], fst[fb][:], r=["fst%d" % fb], w=[("F", idx)])
            return f

        def comp_tok0(wb, wk):
            for i in range(NT):
                pb, pk = self.rot(0, 8)
                for kc in range(8):
                    self.mm(pb[:, 0:408], xnT[:, kc, i * 128:(i + 1) * 128], wb[:, kc, 0:408], kc == 0, kc == 7,
                            r=[wk, "xnT"], w=[pk])
                for vi, k in enumerate(("Vs", "Vw", "Vb")):
                    self.cp("dve" if vi == 1 else "act", A[k][:, i, :, 0:64],
                            pb[:, vi * 128:(vi + 1) * 128].rearrange("p (g e) -> p g e", g=2), w=[pk, k])
                self.act(A["gate"][:, i, :], pb[:, 384:408], AF.Sigmoid, w=[pk, "gate"])

        def comp_zm(blk):
            def f(wb, wk):
                for i in range(NT):
                    pb, pk = self.rot(0, 8)
                    for kc in range(8):
                        self.mm(pb[:, :], xnT[:, kc, i * 128:(i + 1) * 128], wb[:, kc, :], kc == 0, kc == 7,
                                r=[wk, "xnT"], w=[pk])
                    zb = self._zi % 3
                    self._zi += 1
                    zk = "zst%d" % zb
                    if blk < 2:
                        self.act(zst[zb][:], pb[:, :], AF.Silu, w=[pk, zk])
                        self.dma("pool", d["Z"][i * 128:(i + 1) * 128, blk * 512:(blk + 1) * 512], zst[zb][:], r=[zk], w=[("Z", i, blk)])
                    else:
                        mb = blk - 2
                        self.tt("dve", zst[zb][:], pb[:, :], bm[:, mb * 512:(mb + 1) * 512], ALU.add, r=["bm"], w=[pk, zk])
                        self.act(zst[zb][:], zst[zb][:], AF.Sigmoid, w=[zk])
                        self.dma("pool", d["G"][i * 128:(i + 1) * 128, mb * 512:(mb + 1) * 512], zst[zb][:], r=[zk], w=[("G", i, mb)])
            return f

        blocks = []
        for s0 in range(0, 18, 4):
            nsub = min(4, 18 - s0)
            blocks.append((s0 * 128, nsub * 128, comp_feat(s0, nsub)))
        blocks.append((TOK0, 408, comp_tok0))
        for blk in range(8):
            blocks.append((TOK0 + 408 + blk * 512, 512, comp_zm(blk)))
        assert blocks[0][:2] == (0, 512) and blocks[1][:2] == (512, 512)
        cur = pre[0]
        for bi, (c0, ncol, fn) in enumerate(blocks):
            if bi == 0:
                nxt = pre[1]
            else:
                nxt = load_block(blocks[bi + 1][0], blocks[bi + 1][1]) if bi + 1 < len(blocks) else None
            fn(cur[0], cur[1])
            cur = nxt

    def phase_attn(self, st, l, A):
        c, d = self.c, self.d
        L = lambda k: d["%s%d" % (k, l)]
        ident, identb = c["identf"], c["identb"]
        kcT = self.sbt(st, "kcT", [128, 128], BF16)
        vcaug = self.sbt(st, "vcaug", [128, 2, 100], BF16)
        self.memset("pool", kcT[:], 0.0, w=["kcT"])
        self.memset("pool", vcaug[:], 0.0, w=["vcaug"])
        self.memset("pool", vcaug[:, :, 64:65], 1.0, w=["vcaug"])
        for g in range(2):
            self.cp("pool", vcaug[:, g, 65:97], c["ovl"][:], r=["ovl"], w=["vcaug"])
        with ExitStack() as sc:
            w1f = [self.sbt(sc, "w1f", [128, 32, 128]) for _ in range(2)]
            w1b = [self.sbt(sc, "w1b", [128, 32, 128], BF16) for _ in range(2)]
            pef = self.sbt(sc, "pef", [128, 2, 32])
            peb = self.sbt(sc, "peb", [128, 2, 32], BF16)
            w2f = self.sbt(sc, "w2f", [128, 2, 64])
            w2kd = self.sbt(sc, "w2kd", [128, 128], BF16)
            w2vb = self.sbt(sc, "w2vb", [128, 64], BF16)
            cb = self.sbt(sc, "cb", [128, 4])
            hsb = [self.sbt(sc, "hsb", [128, 128], BF16) for _ in range(2)]
            for wi, nm in enumerate(("w1k", "w1v")):
                self.dma("sp", w1f[wi][:], L(nm), w=["w1f%d" % wi])
                self.cp("dve", w1b[wi][:, 0:16, :], w1f[wi][:, 0:16, :], r=["w1f%d" % wi], w=["w1b%d" % wi])
                self.cp("act" if wi else "pool", w1b[wi][:, 16:32, :], w1f[wi][:, 16:32, :], r=["w1f%d" % wi], w=["w1b%d" % wi])
            self.dma("sp", pef[:, 0, :], L("pek"), w=["pef"])
            self.dma("sp", pef[:, 1, :], L("pev"), w=["pef"])
            self.cp("dve", peb[:], pef[:], r=["pef"], w=["peb"])
            self.dma("sp", w2f[:, 0, :], L("w2k"), w=["w2f"])
            self.dma("sp", w2f[:, 1, :], L("w2v"), w=["w2f"])
            self.cp("dve", w2kd[:, 0:64], w2f[:, 0, :], r=["w2f"], w=["w2kd"])
            self.cp("dve", w2kd[:, 64:128], w2f[:, 0, :], r=["w2f"], w=["w2kd"])
            self.cp("dve", w2vb[:], w2f[:, 1, :], r=["w2f"], w=["w2vb"])
            cnt = 0
            for wi, (src, skey) in enumerate(((A["kTc"], "kTc"), (A["vTc"], "vTc"))):
                for g in range(2):
                    rows = slice(g * 64, (g + 1) * 64)
                    sv = src[rows, :].rearrange("p (n s) -> p n s", s=16)
                    pb, pk = self.rot(0, 8)
                    for pos in range(32):
                        self.mm(pb[:, 0:1], w1b[wi][rows, pos, :], peb[rows, wi, pos:pos + 1], pos == 0, pos == 31,
                                r=["w1b%d" % wi, "peb"], w=[pk])
                    self.cp("dve", cb[:, cnt:cnt + 1], pb[:, 0:1], w=[pk, "cb"])
                    pb2, pk2 = self.rot(0, 8)
                    for pos in range(32):
                        rhs = sv[:, 0:127, pos] if pos < 16 else sv[:, 1:128, pos - 16]
                        self.mm(pb2[:, 0:127], w1b[wi][rows, pos, :], rhs, pos == 0, pos == 31,
                                r=["w1b%d" % wi, skey], w=[pk2])
                    hb = hsb[cnt % 2]
                    hk = "hsb%d" % (cnt % 2)
                    self.act(hb[:, 0:127], pb2[:, 0:127], AF.Silu, bias=cb[:, cnt:cnt + 1], r=["cb"], w=[pk2, hk])
                    pb3, pk3 = self.rot(0, 8)
                    if wi == 0:
                        self.mm(pb3[:, 0:127], w2kd[:, :], hb[:, 0:127], True, True, r=["w2kd", hk], w=[pk3])
                        self.cp("dve", kcT[rows, 0:127], pb3[rows, 0:127], w=[pk3, "kcT"])
                    else:
                        self.mm(pb3[0:127, 0:64], hb[:, 0:127], w2vb[:, :], True, True, r=["w2vb", hk], w=[pk3])
                        self.cp("dve", vcaug[0:127, g, 0:64], pb3[0:127, 0:64], w=[pk3, "vcaug"])
                    cnt += 1
            self.P.fence()
        if self.upto == "cmp":
            return
        NPT = 4
        PT = [self.sbt(st, "PT", [128, 17, 512], BF16) for _ in range(NPT)]
        PTc = [self.sbt(st, "PTc", [128, 512], BF16) for _ in range(2)]
        ya = [self.sbt(st, "ya", [128, 512]) for _ in range(2)]
        yb = [self.sbt(st, "yb", [128, 256]) for _ in range(2)]
        selT2 = self.sbt(st, "selT2", [128, T], BF16)
        self.memset("pool", selT2[:], 0.0, w=["selT2"])
        esink = self.sbt(st, "esink", [128, 4])
        self.dma("sp", esink[:], L("sinks"), w=["esink"])
        self.act(esink[:], esink[:], AF.Exp, w=["esink"])
        NR = 4
        rd = [self.sbt(st, "rd", [128, 4]) for _ in range(NR)]
        coef = [self.sbt(st, "coef", [128, 4]) for _ in range(NR)]
        tmpo = [self.sbt(st, "tmpo", [128, 4, 64]) for _ in range(NR)]
        tmpi = [self.sbt(st, "tmpi", [128, 4, 32]) for _ in range(2)]
        imp = [self.sbt(st, "imp", [128, 32]) for _ in range(2)]
        score = [self.sbt(st, "score", [128, 32]) for _ in range(2)]
        scw = [self.sbt(st, "scw", [128, 32]) for _ in range(2)]
        m8 = [self.sbt(st, "m8", [128, 16]) for _ in range(2)]
        selm = [self.sbt(st, "selm", [128, 96]) for _ in range(2)]
        for t_ in range(2):
            self.memset("pool", selm[t_][:], 0.0, w=["selm%d" % t_])
        self._ri = 0
        self._pt = 0
        self._ptA = 0
        gate = A["gate"]

        def epilogue(ob, ok, nh, wdt, i, g, br, dst, dkey, first):
            k = self._ri % NR
            self._ri += 1
            rk_, ck_, tk_ = "rd%d" % k, "coef%d" % k, "tmpo%d" % k
            O3 = ob[:, 0:nh * wdt].rearrange("p (h c) -> p h c", c=wdt)
            if br == "swa":
                self.tt("dve", rd[k][:, 0:nh], O3[:, :, 64], esink[:, g * 2:(g + 1) * 2], ALU.add, r=["esink"], w=[ok, rk_])
            else:
                self.ts("dve", rd[k][:, 0:nh], O3[:, :, 64], 1e-30, None, ALU.max, w=[ok, rk_])
            self.recip(rd[k][:, 0:nh], rd[k][:, 0:nh], w=[rk_])
            if br == "swa":
                cf = rd[k]
                cfk = rk_
            else:
                gc = {"cmp": 0, "slc": 8, "win": 16}[br] + g * 4
                self.tt("dve", coef[k][:, 0:nh], rd[k][:, 0:nh], gate[:, i, gc:gc + 4], ALU.mult, r=[rk_, "gate"], w=[ck_])
                cf = coef[k]
                cfk = ck_
            dst3 = dst.rearrange("p (h c) -> p h c", c=64)
            if first:
                self.tt("dve", dst3, O3[:, :, 0:64], cf[:, 0:nh].unsqueeze(2).to_broadcast([128, nh, 64]), ALU.mult,
                        r=[cfk], w=[ok, dkey])
            else:
                self.tt("dve", tmpo[k][:, 0:nh, :], O3[:, :, 0:64], cf[:, 0:nh].unsqueeze(2).to_broadcast([128, nh, 64]),
                        ALU.mult, r=[cfk], w=[ok, tk_])
                self.tt("pool", dst3, dst3, tmpo[k][:, 0:nh, :], ALU.add, r=[tk_], w=[dkey])
            return k

        def cmp_tile(i, g, yat, yak):
            M = 128
            rows = slice(g * 64, (g + 1) * 64)
            b = self._pt % 2
            self._pt += 1
            sb_, sk = self.rot(0, 5)
            self.mm(sb_[0:M, :], kcT[rows, 0:M], A["qTA"][rows, :, i * 128:(i + 1) * 128], True, False, r=["kcT", "qTA"], w=[sk])
            self.mm(sb_[0:M, :], identb[0:M, 0:M], c["penc"][0:M, i * 128:(i + 1) * 128].unsqueeze(1).to_broadcast([M, 4, 128]),
                    False, True, r=["identb", "penc"], w=[sk])
            pk_ = "PTc%d" % b
            self.act(PTc[b][0:M, :], sb_[0:M, :], AF.Exp, scale=0.125, w=[sk, pk_])
            cst = getattr(self, "cstage", 9)
            if cst < 2:
                return
            ob, ok = self.rot(5, 7)
            for hh in range(4):
                self.mm(ob[:, hh * 100:hh * 100 + 100], PTc[b][0:M, hh * 128:(hh + 1) * 128], vcaug[0:M, g, 0:100], True, True,
                        r=[pk_, "vcaug"], w=[ok])
            if cst < 3:
                return
            k = epilogue(ob, ok, 4, 100, i, g, "cmp", yat[:, g * 256:(g + 1) * 256], yak, True)
            if cst < 4:
                return
            O3 = ob[:, 0:400].rearrange("p (h c) -> p h c", c=100)
            tb = g
            self.tt("dve", tmpi[tb][:], O3[:, :, 65:97], rd[k][:, 0:4].unsqueeze(2).to_broadcast([128, 4, 32]), ALU.mult,
                    r=["rd%d" % k], w=[ok, "tmpi%d" % tb])
            self.red(imp[tb][:], tmpi[tb][:].rearrange("p h j -> p j h"), ALU.add, r=["tmpi%d" % tb], w=["imp%d" % tb])
            if cst < 5:
                return
            sc_, sk_ = score[tb], "score%d" % tb
            self.tt("dve", sc_[:], imp[tb][:], c["Mi"][:, i * 32:(i + 1) * 32], ALU.mult, r=["imp%d" % tb, "Mi"], w=[sk_])
            self.tt("dve", sc_[:], sc_[:], c["Bc"][:, i * 32:(i + 1) * 32], ALU.add, r=["Bc"], w=[sk_])
            mk, wk_, lk = "m8%d" % tb, "scw%d" % tb, "selm%d" % (i % 2)
            sm_ = selm[i % 2]
            self.P.add("dve", lambda e: e.max(out=m8[tb][:, 0:8], in_=sc_[:]), r=[sk_], w=[mk])
            self.P.add("dve", lambda e: e.match_replace(out=scw[tb][:], in_to_replace=m8[tb][:, 0:8], in_values=sc_[:],
                                                        imm_value=-1e30), r=[sk_, mk], w=[wk_])
            self.P.add("dve", lambda e: e.max(out=m8[tb][:, 8:16], in_=scw[tb][:]), r=[wk_], w=[mk])
            self.ts("dve", sm_[:, g * 64:g * 64 + 32], sc_[:], m8[tb][:, 15:16], -1.0, ALU.is_ge, ALU.add, r=[sk_, mk], w=[lk])
            if cst < 6 or g == 0:
                return
            xb, xk = self.rot(7, 8)
            self.tr(xb[0:96, 0:128], sm_[:, :], ident[:, :], r=[lk, "identf"], w=[xk])
            self.cp("act", selT2[0:96, i * 128:(i + 1) * 128], xb[0:96, 0:128], w=[xk, "selT2"])

        def attn_A2(i, br, dsts):
            if br == "slc":
                kT, kk_, V, vk, q, qk, nh, js = A["kTs"], "kTs", A["Vs"], "Vs", A["qTA"], "qTA", 4, list(range(0, i + 1))
            elif br == "win":
                kT, kk_, V, vk, q, qk, nh, js = A["kTw"], "kTw", A["Vw"], "Vw", A["qTA"], "qTA", 4, list(range(max(0, i - 4), i + 1))
            else:
                kT, kk_, V, vk, q, qk, nh, js = A["kTb"], "kTb", A["Vb"], "Vb", A["qTB"], "qTB", 2, list(range(max(0, i - 1), i + 1))
            N = nh * 128
            qs = slice(i * 128, (i + 1) * 128)
            pts = []
            for g in range(2):
                b = self._ptA % NPT
                self._ptA += 1
                pts.append((PT[b], "PT%d" % b))
            for idx, j in enumerate(js):
                banks = [self.rot(0, 5) for _ in range(2)]
                pens = [[], []]
                for g in range(2):
                    if br == "slc" and j < i and i >= 8:
                        er = slice(g * 64, g * 64 + 32)
                        pens[g].append((c["Eall"][er, j * 128:(j + 1) * 128],
                                        selT2[er, qs].unsqueeze(1).to_broadcast([32, nh, 128]), ["Eall", "selT2"]))
                    if j == i:
                        pens[g].append((identb[:, :], c["triA"][:, :].unsqueeze(1).to_broadcast([128, nh, 128]), ["identb", "triA"]))
                    if (br == "win" and j == i - 4) or (br == "swa" and j == i - 1):
                        pens[g].append((identb[:, :], c["triB"][:, :].unsqueeze(1).to_broadcast([128, nh, 128]), ["identb", "triB"]))
                for g in range(2):
                    rows = slice(g * 64, (g + 1) * 64)
                    sb_, sk = banks[g]
                    self.mm(sb_[:, 0:N], kT[rows, j * 128:(j + 1) * 128], q[rows, 0:nh, qs], True, len(pens[g]) == 0,
                            r=[kk_, qk], w=[sk])
                for g in range(2):
                    sb_, sk = banks[g]
                    for pi, (l_, r_, keys) in enumerate(pens[g]):
                        self.mm(sb_[:, 0:N], l_, r_, False, pi == len(pens[g]) - 1, r=keys, w=[sk])
                for g in range(2):
                    sb_, sk = banks[g]
                    self.act(pts[g][0][:, idx, 0:N], sb_[:, 0:N], AF.Exp, scale=0.125, w=[sk, pts[g][1]])
            return [dict(i=i, g=g, br=br, dst=dsts[g][0], dkey=dsts[g][1], pt=pts[g][0], ptk=pts[g][1], V=V, vk=vk, nh=nh, js=js, after=None)
                    for g in range(2)]

        def attn_B(S_):
            i, g, br, nh, js, pt, ptk, V, vk = S_["i"], S_["g"], S_["br"], S_["nh"], S_["js"], S_["pt"], S_["ptk"], S_["V"], S_["vk"]
            ob, ok = self.rot(5, 7)
            for hh in range(nh):
                for idx, j in enumerate(js):
                    self.mm(ob[:, hh * 68:hh * 68 + 68], pt[:, idx, hh * 128:(hh + 1) * 128], V[:, j, g, 0:68],
                            idx == 0, idx == len(js) - 1, r=[ptk, vk], w=[ok])
            epilogue(ob, ok, nh, 68, i, g, br, S_["dst"], S_["dkey"], br == "swa")
            if S_["after"] is not None:
                S_["after"]()

        def mk_after(i, b, yak, ybk):
            def f():
                self.dma("pool", d["Y"][i * 128:(i + 1) * 128, 0:512], ya[b][:], r=[yak], w=[("Y", i, 0)])
                self.dma("pool", d["Y"][i * 128:(i + 1) * 128, 512:768], yb[b][:], r=[ybk], w=[("Y", i, 1)])
            return f

        pending = []
        for i in range(NT):
            b = i % 2
            yak, ybk = "ya%d" % b, "yb%d" % b
            for g in range(2):
                cmp_tile(i, g, ya[b], yak)
            for br in ("slc", "win", "swa"):
                if br == "swa":
                    dsts = [(yb[b][:, g * 128:(g + 1) * 128], ybk) for g in range(2)]
                else:
                    dsts = [(ya[b][:, g * 256:(g + 1) * 256], yak) for g in range(2)]
                sts = attn_A2(i, br, dsts)
                if br == "swa":
                    sts[1]["after"] = mk_after(i, b, yak, ybk)
                while pending:
                    attn_B(pending.pop(0))
                pending.extend(sts)
        while pending:
            attn_B(pending.pop(0))

    def phase_rwkv(self, st, l):
        c, d = self.c, self.d
        L = lambda k: d["%s%d" % (k, l)]
        F = d["F"]
        identf, bones, hsel = c["identf"], c["bones"], c["hsel"]
        rv = self.sbt(st, "rv", [128, 20])
        self.dma("sp", rv[:], L("rv"), w=["rv"])
        omka = self.sbt(st, "omka", [128, 2])
        self.ts("dve", omka[:], rv[:, 13:15], -1.0, 1.0, ALU.mult, ALU.add, r=["rv"], w=["omka"])
        lnw = self.sbt(st, "lnw", [128, 256])
        lnb = self.sbt(st, "lnb", [128, 256])
        self.dma("sp", lnw[:], L("lnw"), w=["lnw"])
        self.dma("sp", lnb[:], L("lnb"), w=["lnb"])
        wab = self.sbt(st, "wab", [128, 256], BF16)
        thad = self.sbt(st, "thad", [128, T], BF16)
        if l > 0:
            v1b = self.sbt(st, "v1b", [128, 2, 32], BF16)
            v2b = self.sbt(st, "v2b", [32, 256], BF16)
            t1 = self.sbt(st, "t1", [32, T], BF16)

        def load_shift(dst, dkey, cidx, f, fp, fk, fpk):
            self.dma("sp", f[:, :], F[cidx], r=[("F", cidx)], w=[fk])
            self.memset("pool", fp[:, 0:1], 0.0, w=[fpk])
            self.dma("sp", fp[:, 1:T], F[cidx][:, 0:T - 1], r=[("F", cidx)], w=[fpk])
            self.tt("pool", fp[:, :], fp[:, :], f[:, :], ALU.subtract, r=[fk], w=[fpk])
            self.stt("dve", dst, fp[:, :], rv[:, cidx:cidx + 1], f[:, :], ALU.mult, ALU.add, r=[fpk, fk, "rv"], w=[dkey])

        with ExitStack() as sp:
            f = self.sbt(sp, "f", [128, T])
            fp = self.sbt(sp, "fp", [128, T])
            wdx = self.sbt(sp, "wdx", [128, T])
            wf = self.sbt(sp, "wf", [128, 256])
            self.dma("sp", wf[:], L("wa"), w=["wf"])
            self.cp("dve", wab[:], wf[:], r=["wf"], w=["wab"])
            load_shift(wdx[:, :], "wdx", 6, f, fp, "f", "fp")
            self.act(thad[0:64, :], wdx[0:64, :], AF.Tanh, r=["wdx"], w=["thad"])
            self.cp("dve", thad[64:128, :], wdx[64:128, :], r=["wdx"], w=["thad"])
            if l > 0:
                v1f = self.sbt(sp, "v1f", [128, 2, 32])
                v2f = self.sbt(sp, "v2f", [32, 256])
                vxb = self.sbt(sp, "vxb", [128, T], BF16)
                self.dma("sp", v1f[:], L("v1"), w=["v1f"])
                self.dma("sp", v2f[:], L("v2"), w=["v2f"])
                self.cp("dve", v1b[:], v1f[:], r=["v1f"], w=["v1b"])
                self.cp("dve", v2b[:], v2f[:], r=["v2f"], w=["v2b"])
                banks = [self.rot(0, 8) for _ in range(4)]
                for p in range(2):
                    load_shift(wdx[:, :], "wdx", 4 + p, f, fp, "f", "fp")
                    self.cp("dve", vxb[:], wdx[:], r=["wdx"], w=["vxb"])
                    for tc in range(4):
                        pb, pk = banks[tc]
                        self.mm(pb[0:32, :], v1b[:, p, :], vxb[:, tc * 512:(tc + 1) * 512], p == 0, p == 1, r=["v1b", "vxb"], w=[pk])
                for tc in range(4):
                    pb, pk = banks[tc]
                    self.cp("act", t1[0:32, tc * 512:(tc + 1) * 512], pb[0:32, :], w=[pk, "t1"])
            self.P.fence()

        v2d = lambda t: t[:].rearrange("p n t -> p (n t)")
        for p in range(2):
            ps_ = slice(p * 128, (p + 1) * 128)
            with ExitStack() as sP:
                Rs = self.sbt(sP, "Rs", [128, T])
                AhT = self.sbt(sP, "AhT", [128, T])
                rkb = self.sbt(sP, "rkb", [128, T])
                ArbT = [self.sbt(sP, "ArbT", [128, 16, 128]) for _ in range(2)]
                UV = self.sbt(sP, "UV", [128, 16, 128])
                YV = self.sbt(sP, "YV", [128, 16, 128])
                KVbd = self.sbt(sP, "KVbd", [128, 16, 128])
                tokB0 = self.sbt(sP, "tokB0", [128, 16, 128])
                tokB1 = self.sbt(sP, "tokB1", [128, 16, 128])
                tokV = self.sbt(sP, "tokV", [128, 16, 128])
                ycst = self.sbt(sP, "ycst", [128, 16, 128])
                Gbd = self.sbt(sP, "Gbd", [128, 128])
                gam = self.sbt(sP, "gam", [128, 16])
                sM = ExitStack()
                As = self.sbt(sM, "As", [128, T])
                Ks = self.sbt(sM, "Ks", [128, T])
                Bs = self.sbt(sM, "Bs", [128, T])
                tokA = self.sbt(sM, "tokA", [128, 16, 128])
                tokK = self.sbt(sM, "tokK", [128, 16, 128])
                pkk = lambda n: ("Pk", n // 4)
                Vt, kVt = AhT[:, :], "AhT"
                lw, klw = v2d(ArbT[0]), "ArbT0"
                aT, kaT = v2d(ArbT[1]), "ArbT1"
                kk, kkk = v2d(UV), "UV"
                e1, ke1 = v2d(YV), "YV"
                e2, ke2 = v2d(KVbd), "KVbd"
                e3, ke3 = v2d(ycst), "ycst"
                cl = [(v2d(tokB0), "tokB0"), (v2d(tokV), "tokV")]
                HF = 1024
                H = lambda ap, hf: ap[:, hf * HF:(hf + 1) * HF]
                K2 = lambda k, hf: (k, hf)
                KB2 = lambda k: [(k, 0), (k, 1)]
                cl0, kcl0 = cl[0]
                cl1, kcl1 = cl[1]

                def load_shift2(dst, dkey, cidx, f, fk, fp, fpk):
                    self.dma("sp", f[:, 0:HF], F[cidx][:, 0:HF], r=[("F", cidx)], w=[K2(fk, 0)])
                    self.dma("sp", f[:, HF:T], F[cidx][:, HF:T], r=[("F", cidx)], w=[K2(fk, 1)])
                    self.ts("pool", fp[:, 0:1], f[:, 0:1], -1.0, None, ALU.mult, r=[K2(fk, 0)], w=[K2(fpk, 0)])
                    self.tt("pool", fp[:, 1:HF], f[:, 0:HF - 1], f[:, 1:HF], ALU.subtract, r=[K2(fk, 0)], w=[K2(fpk, 0)])
                    self.tt("pool", fp[:, HF:T], f[:, HF - 1:T - 1], f[:, HF:T], ALU.subtract, r=KB2(fk), w=[K2(fpk, 1)])
                    for hf in range(2):
                        self.stt("dve", H(dst, hf), H(fp, hf), rv[:, cidx:cidx + 1], H(f, hf), ALU.mult, ALU.add,
                                 r=[K2(fpk, hf), K2(fk, hf), "rv"], w=[K2(dkey, hf)])

                load_shift2(Rs[:, :], "Rs", p, e2, ke2, e3, ke3)
                load_shift2(Ks[:, :], "Ks", 2 + p, kk, kkk, e1, ke1)
                load_shift2(Vt, kVt, 4 + p, cl0, kcl0, cl1, kcl1)
                Rsa, Ksa, Bsa, Asa, rkba = Rs[:, :], Ks[:, :], Bs[:, :], As[:, :], rkb[:, :]
                steps = []

                def st_sig(hf):
                    for tc in (2 * hf, 2 * hf + 1):
                        ts_ = slice(tc * 512, (tc + 1) * 512)
                        pb, pk = self.rot(0, 8)
                        self.mm(pb[:, :], wab[0:64, ps_], thad[0:64, ts_], True, True, r=["wab", "thad"], w=[pk])
                        self.act(lw[:, ts_], pb[:, :], AF.Sigmoid, bias=rv[:, 7 + p:8 + p], r=["rv"], w=[pk, K2(klw, hf)])
                        pb, pk = self.rot(0, 8)
                        self.mm(pb[:, :], wab[64:128, ps_], thad[64:128, ts_], True, True, r=["wab", "thad"], w=[pk])
                        self.act(aT[:, ts_], pb[:, :], AF.Sigmoid, bias=rv[:, 9 + p:10 + p], r=["rv"], w=[pk, K2(kaT, hf)])
                        if l > 0:
                            pb, pk = self.rot(0, 8)
                            self.mm(pb[:, :], v2b[0:32, ps_], t1[0:32, ts_], True, True, r=["v2b", "t1"], w=[pk])
                            self.act(e1[:, ts_], pb[:, :], AF.Sigmoid, bias=rv[:, 17 + p:18 + p], r=["rv"], w=[pk, K2(ke1, hf)])
                steps.append(st_sig)
                if l > 0:
                    self.dma("sp", e2, d["vf"][p], r=[("vf", p)], w=KB2(ke2))
                    steps.append(lambda hf: self.tt("pool", H(e2, hf), H(e2, hf), H(Vt, hf), ALU.subtract, r=[K2(kVt, hf)], w=[K2(ke2, hf)]))
                    steps.append(lambda hf: self.tt("dve", H(e2, hf), H(e2, hf), H(e1, hf), ALU.mult, r=[K2(ke1, hf)], w=[K2(ke2, hf)]))
                    steps.append(lambda hf: self.tt("pool", H(Vt, hf), H(Vt, hf), H(e2, hf), ALU.add, r=[K2(ke2, hf)], w=[K2(kVt, hf)]))
                else:
                    self.dma("pool", d["vf"][p], Vt, r=KB2(kVt), w=[("vf", p)])
                steps.append(lambda hf: self.act(H(kk, hf), H(Ksa, hf), AF.Copy, scale=rv[:, 11 + p:12 + p], r=[K2("Ks", hf), "rv"], w=[K2(kkk, hf)]))
                steps.append(lambda hf: self.act(H(e1, hf), H(kk, hf), AF.Square, r=[K2(kkk, hf)], w=[K2(ke1, hf)]))

                def st_norm(hf):
                    for tc in (2 * hf, 2 * hf + 1):
                        ts_ = slice(tc * 512, (tc + 1) * 512)
                        pb, pk = self.rot(0, 8)
                        self.mm(pb[:, :], bones[:, :], e1[:, ts_], True, True, r=["bones", K2(ke1, hf)], w=[pk])
                        self.act(e2[:, ts_], pb[:, :], AF.Sqrt, w=[pk, K2(ke2, hf)])
                steps.append(st_norm)
                steps.append(lambda hf: self.ts("dve", H(e2, hf), H(e2, hf), 1e-12, None, ALU.max, w=[K2(ke2, hf)]))
                steps.append(lambda hf: self.recip(H(e2, hf), H(e2, hf), w=[K2(ke2, hf)]))
                steps.append(lambda hf: self.tt("dve", H(kk, hf), H(kk, hf), H(e2, hf), ALU.mult, r=[K2(ke2, hf)], w=[K2(kkk, hf)]))
                steps.append(lambda hf: self.act(H(e1, hf), H(aT, hf), AF.Identity, bias=omka[:, p:p + 1], scale=rv[:, 13 + p:14 + p],
                                                 r=[K2(kaT, hf), "rv", "omka"], w=[K2(ke1, hf)]))
                steps.append(lambda hf: self.tt("pool", H(Ksa, hf), H(Ksa, hf), H(e1, hf), ALU.mult, r=[K2(ke1, hf)], w=[K2("Ks", hf)]))
                steps.append(lambda hf: self.stt("dve", H(rkba, hf), H(Rsa, hf), rv[:, 15 + p:16 + p], H(Ksa, hf), ALU.mult, ALU.mult,
                                                 r=[K2("Rs", hf), K2("Ks", hf), "rv"], w=[K2("rkb", hf)]))
                steps.append(lambda hf: self.tt("pool", H(Bsa, hf), H(kk, hf), H(aT, hf), ALU.mult, r=[K2(kkk, hf), K2(kaT, hf)], w=[K2("Bs", hf)]))
                chain_src = [(lw, klw)]
                for si, sh in enumerate((1, 2, 4, 8, 16, 32, 64)):
                    dst, dkey = cl[si % 2]
                    src, skey = chain_src[-1]

                    def st_scan(hf, src=src, skey=skey, dst=dst, dkey=dkey, sh=sh):
                        s3 = H(src, hf).rearrange("p (n t) -> p n t", t=128)
                        d3 = H(dst, hf).rearrange("p (n t) -> p n t", t=128)
                        self.cp("act", d3[:, :, 0:sh], s3[:, :, 0:sh], r=[K2(skey, hf)], w=[K2(dkey, hf)])
                        self.tt("dve", d3[:, :, sh:128], s3[:, :, sh:128], s3[:, :, 0:128 - sh], ALU.add, r=[K2(skey, hf)], w=[K2(dkey, hf)])
                    steps.append(st_scan)
                    chain_src.append((dst, dkey))
                csrc, cskey = chain_src[-1]
                CW = -float(np.exp(-0.5))
                steps.append(lambda hf: self.act(H(e1, hf), H(csrc, hf), AF.Exp, scale=CW, r=[K2(cskey, hf)], w=[K2(ke1, hf)]))
                steps.append(lambda hf: self.act(H(e2, hf), H(csrc, hf), AF.Exp, scale=-CW, r=[K2(cskey, hf)], w=[K2(ke2, hf)]))
                steps.append(lambda hf: self.tt("pool", H(e3, hf), H(csrc, hf), H(lw, hf), ALU.subtract, r=[K2(cskey, hf), K2(klw, hf)], w=[K2(ke3, hf)]))
                steps.append(lambda hf: self.act(H(e3, hf), H(e3, hf), AF.Exp, scale=CW, w=[K2(ke3, hf)]))
                steps.append(lambda hf: self.cp("act", gam[:, hf * 8:(hf + 1) * 8], H(e1, hf).rearrange("p (n t) -> p n t", t=128)[:, :, 127],
                                                r=[K2(ke1, hf)], w=[K2("gam", hf)]))
                steps.append(lambda hf: self.tt("dve", H(Rsa, hf), H(Rsa, hf), H(e1, hf), ALU.mult, r=[K2(ke1, hf)], w=[K2("Rs", hf)]))
                steps.append(lambda hf: self.tt("pool", H(Ksa, hf), H(Ksa, hf), H(e2, hf), ALU.mult, r=[K2(ke2, hf)], w=[K2("Ks", hf)]))
                steps.append(lambda hf: self.tt("pool", H(Bsa, hf), H(Bsa, hf), H(e2, hf), ALU.mult, r=[K2(ke2, hf)], w=[K2("Bs", hf)]))
                steps.append(lambda hf: self.stt("dve", H(Asa, hf), H(kk, hf), -1.0, H(e3, hf), ALU.mult, ALU.mult, r=[K2(kkk, hf), K2(ke3, hf)], w=[K2("As", hf)]))
                for stp in steps:
                    for hf in range(2):
                        stp(hf)
                self.memset("pool", tokB0[:], 0.0, w=KB2("tokB0"))
                self.memset("pool", tokB1[:], 0.0, w=["tokB1"])
                for n in range(16):
                    ch = slice(n * 128, (n + 1) * 128)
                    hf = n // 8
                    pb, pk = self.rot(0, 8)
                    self.tr(pb[:, 0:128], Ks[:, ch], identf[:, :], r=[K2("Ks", hf), "identf"], w=[pk])
                    self.tr(pb[:, 128:256], As[:, ch], identf[:, :], r=[K2("As", hf), "identf"], w=[pk])
                    self.tr(pb[:, 256:384], Bs[:, ch], identf[:, :], r=[K2("Bs", hf), "identf"], w=[pk])
                    self.tr(pb[:, 384:512], Vt[:, ch], identf[:, :], r=[K2(kVt, hf), "identf"], w=[pk])
                    self.cp("act", tokK[:, n, :], pb[:, 0:128], w=[pk, "tokK"])
                    self.cp("act", tokA[:, n, :], pb[:, 128:256], w=[pk, "tokA"])
                    self.cp("dve", tokB0[:, n, 0:64], pb[:, 256:320], w=[pk, K2("tokB0", hf)])
                    self.cp("dve", tokB1[:, n, 64:128], pb[:, 320:384], w=[pk, "tokB1"])
                    self.cp("dve", tokV[:, n, :], pb[:, 384:512], w=[pk, K2("tokV", hf)])
                self.P.fence()
                rstage = getattr(self, "rstage", 9)
                if rstage < 3:
                    sM.close()
                    continue
                self.memset("pool", KVbd[:], 0.0, w=["KVbd"])
                self.memset("pool", Gbd[:], 0.0, w=["Gbd"])
                for n0 in range(0, 16, 4):
                    pb, pk = self.rot(0, 8)
                    for q in range(4):
                        self.mm(pb[:, q * 128:(q + 1) * 128], tokK[:, n0 + q, :], tokV[:, n0 + q, :], True, True, r=["tokK", "tokV"], w=[pk])
                    p3 = pb[:].rearrange("p (q t) -> p q t", q=4)
                    for hd in range(2):
                        hs = slice(hd * 64, (hd + 1) * 64)
                        self.cp("act" if hd else "dve", KVbd[hs, n0:n0 + 4, hs], p3[hs, :, hs], w=[pk, "KVbd"])
                sC = ExitStack()
                XT = [ycst, tokK]
                Pk = [self.sbt(sC, "Pk", [128, 16, 128], BF16) for _ in range(2)]
                Nk = [self.sbt(sC, "Nk", [128, 16, 128], BF16) for _ in range(2)]
                XTb = [self.sbt(sC, "XTb", [128, 16, 128], BF16) for _ in range(2)]
                Lk = [self.sbt(sC, "Lk", [128, 4, 2, 128]) for _ in range(2)]
                WV = [self.sbt(sC, "WV", [128, 4, 64]) for _ in range(2)]
                pkk = lambda hd, n: ("Pk", hd, n // 4)
                nkk = lambda hd, n: ("Nk", hd, n // 4)
                xtk = lambda hd, n: ("XT", hd, n // 4)
                xbk = lambda hd, n: ("XTb", hd, n // 4)
                bc4 = lambda m: c[m][:, :].unsqueeze(1).to_broadcast([128, 4, 128])
                q4 = lambda pb: pb[:].rearrange("p (q t) -> p q t", q=4)
                HS = [slice(0, 64), slice(64, 128)]
                ev = 0
                for n0 in range(0, 16, 4):
                    bk = [[self.rot(0, 8) for _ in range(2)] for _ in range(3)]
                    for q in range(4):
                        ch = slice((n0 + q) * 128, (n0 + q + 1) * 128)
                        qs = slice(q * 128, (q + 1) * 128)
                        for which, (lh, rh, lkey, rkey) in enumerate(((As, Bs, "As", "Bs"), (Bs, As, "Bs", "As"), (Bs, Rs, "Bs", "Rs"))):
                            for hd in range(2):
                                hs = HS[hd]
                                self.mm(bk[which][hd][0][:, qs], lh[hs, ch], rh[hs, ch], True, True, r=[lkey, rkey], w=[bk[which][hd][1]])
                    for hd in range(2):
                        xk = [xtk(hd, n0)] + (["tokK"] if hd == 1 else [])
                        self.tt("dve", Pk[hd][:, n0:n0 + 4, :], q4(bk[0][hd][0]), bc4("mSL"), ALU.mult, r=["mSL"], w=[bk[0][hd][1], pkk(hd, n0)])
                        self.tt("dve", XT[hd][:, n0:n0 + 4, :], q4(bk[1][hd][0]), bc4("mSU"), ALU.mult, r=["mSU"], w=[bk[1][hd][1]] + xk)
                        self.tt("dve", ArbT[hd][:, n0:n0 + 4, :], q4(bk[2][hd][0]), bc4("mIU"), ALU.mult, r=["mIU"], w=[bk[2][hd][1], "ArbT%d" % hd])
                        self.cp("act", Nk[hd][:, n0:n0 + 4, :], XT[hd][:, n0:n0 + 4, :], r=[xtk(hd, n0)], w=[nkk(hd, n0)])
                        self.tt("pool", XT[hd][:, n0:n0 + 4, :], XT[hd][:, n0:n0 + 4, :], identf[:, :].unsqueeze(1).to_broadcast([128, 4, 128]),
                                ALU.add, r=["identf"], w=[xtk(hd, n0)])
                        self.cp("act", XTb[hd][:, n0:n0 + 4, :], XT[hd][:, n0:n0 + 4, :], r=[xtk(hd, n0)], w=[xbk(hd, n0)])
                G4 = list(range(0, 16, 4))
                for hd in range(2):
                    for k in range(1, 7):
                        pbanks, nbanks, xbanks = {}, {}, {}
                        for n0 in G4:
                            pbanks[n0] = self.rot(0, 8)
                            for q in range(4):
                                n = n0 + q
                                self.mm(pbanks[n0][0][:, q * 128:(q + 1) * 128], Nk[hd][:, n, :], Pk[hd][:, n, :], True, True,
                                        r=[nkk(hd, n0), pkk(hd, n0)], w=[pbanks[n0][1]])
                        if k <= 5:
                            for n0 in G4:
                                nbanks[n0] = self.rot(0, 8)
                                for q in range(4):
                                    n = n0 + q
                                    self.mm(nbanks[n0][0][:, q * 128:(q + 1) * 128], Pk[hd][:, n, :], Nk[hd][:, n, :], True, True,
                                            r=[nkk(hd, n0), pkk(hd, n0)], w=[nbanks[n0][1]])
                        for n0 in G4:
                            self.cp("act", Pk[hd][:, n0:n0 + 4, :], q4(pbanks[n0][0]), w=[pbanks[n0][1], pkk(hd, n0)])
                        if k <= 5:
                            for n0 in G4:
                                self.cp("act" if (n0 // 4) % 2 else "dve", Nk[hd][:, n0:n0 + 4, :], q4(nbanks[n0][0]), w=[nbanks[n0][1], nkk(hd, n0)])
                        for n0 in G4:
                            xbanks[n0] = self.rot(0, 8)
                            for q in range(4):
                                n = n0 + q
                                self.mm(xbanks[n0][0][:, q * 128:(q + 1) * 128], Pk[hd][:, n, :], XTb[hd][:, n, :], True, True,
                                        r=[pkk(hd, n0), xbk(hd, n0)], w=[xbanks[n0][1]])
                        for n0 in G4:
                            self.tt("dve", XT[hd][:, n0:n0 + 4, :], q4(xbanks[n0][0]), XT[hd][:, n0:n0 + 4, :], ALU.add, w=[xbanks[n0][1], xtk(hd, n0)])
                        if k < 6:
                            for n0 in G4:
                                self.cp("act" if (n0 // 4) % 2 else "pool", XTb[hd][:, n0:n0 + 4, :], XT[hd][:, n0:n0 + 4, :], r=[xtk(hd, n0)], w=[xbk(hd, n0)])
                for n0 in range(0, 16, 4):
                    bL = [self.rot(0, 8) for _ in range(2)]
                    bA = [self.rot(0, 8) for _ in range(2)]
                    for q in range(4):
                        ch = slice((n0 + q) * 128, (n0 + q + 1) * 128)
                        qs = slice(q * 128, (q + 1) * 128)
                        for hd in range(2):
                            self.mm(bL[hd][0][:, qs], Ks[HS[hd], ch], As[HS[hd], ch], True, True, r=["Ks", "As"], w=[bL[hd][1]])
                        for hd in range(2):
                            self.mm(bA[hd][0][:, qs], Ks[HS[hd], ch], Rs[HS[hd], ch], True, True, r=["Ks", "Rs"], w=[bA[hd][1]])
                    for hd in range(2):
                        self.tt("dve", Lk[hd][:, :, 0, :], q4(bL[hd][0]), bc4("mSU"), ALU.mult, r=["mSU"], w=[bL[hd][1], "Lk%d" % hd])
                        self.tt("dve", Lk[hd][:, :, 1, :], q4(bA[hd][0]), bc4("mIU"), ALU.mult, r=["mIU"], w=[bA[hd][1], "Lk%d" % hd])
                    for hd in range(2):
                        hs = HS[hd]
                        lk_, wk_ = "Lk%d" % hd, "WV%d" % hd
                        bw, kw = self.rot(0, 8)
                        bh, kh = self.rot(0, 8)
                        for q in range(4):
                            n = n0 + q
                            self.mm(bw[:, q * 64:(q + 1) * 64], Lk[hd][:, q, 0, :], tokV[:, n, hs], True, True, r=[lk_, "tokV"], w=[kw])
                            self.mm(bw[:, 256 + q * 64:256 + (q + 1) * 64], Lk[hd][:, q, 1, :], tokV[:, n, hs], True, True, r=[lk_, "tokV"], w=[kw])
                            self.mm(bh[:, q * 128:(q + 1) * 128], tokA[:, n, :], XT[hd][:, n, :], True, True, r=["tokA", xtk(hd, n0)], w=[kh])
                        self.cp("act", WV[hd][:, :, :], bw[:, 0:256].rearrange("p (q v) -> p q v", q=4), w=[kw, wk_])
                        self.cp("act", YV[:, n0:n0 + 4, hs], bw[:, 256:512].rearrange("p (q v) -> p q v", q=4), w=[kw, "YV"])
                        self.cp("dve", AhT[hs, n0 * 128:(n0 + 4) * 128], bh[hs, :], w=[kh, "AhT"])
                        bu, ku = self.rot(0, 8)
                        for q in range(4):
                            n = n0 + q
                            self.mm(bu[:, q * 64:(q + 1) * 64], XT[hd][:, n, :], WV[hd][:, q, :], True, True, r=[xtk(hd, n0), wk_], w=[ku])
                        self.cp("act", UV[:, n0:n0 + 4, hs], bu[:, 0:256].rearrange("p (q v) -> p q v", q=4), w=[ku, "UV"])
                self.tt("pool", KVbd[:], KVbd[:], gam[:, :].unsqueeze(2).to_broadcast([128, 16, 128]), ALU.mult, r=["gam"], w=["KVbd"])
                self.P.fence()
                sC.close()
                sM.close()
                cS = ExitStack()
                Usb = [self.sbt(cS, "Usb", [128, 128]) for _ in range(2)]
                Tg = [self.sbt(cS, "Tg", [128, 128]) for _ in range(2)]
                ysqA = self.sbt(cS, "ysqA", [128, 16, 128])
                smA = self.sbt(cS, "smA", [128, 64])
                bon = self.sbt(cS, "bon", [128, 16, 2])
                for n in range(16 if rstage >= 4 else 0):
                    ch = slice(n * 128, (n + 1) * 128)
                    q = n % 2
                    uk = "Usb%d" % q
                    self.stt("dve", Tg[q][:, :], Gbd[:, :], gam[:, n:n + 1], KVbd[:, n, :], ALU.mult, ALU.add, r=["Gbd", "KVbd", "gam"], w=["Tg%d" % q])
                    pbu, pku = self.rot(0, 8)
                    self.mm(pbu[:, 0:128], AhT[:, ch], Gbd[:, :], True, True, r=["AhT", "Gbd"], w=[pku])
                    self.tt("dve", Usb[q][:, :], pbu[:, 0:128], UV[:, n, :], ALU.add, r=["UV"], w=[pku, uk])
                    pby, pky = self.rot(0, 8)
                    self.mm(pby[:, 0:128], Rs[:, ch], Gbd[:, :], True, False, r=["Rs", "Gbd"], w=[pky])
                    self.mm(pby[:, 0:64], ArbT[0][:, n, :], Usb[q][:, 0:64], False, False, r=["ArbT0", uk], w=[pky])
                    self.mm(pby[:, 64:128], ArbT[1][:, n, :], Usb[q][:, 64:128], False, True, r=["ArbT1", uk], w=[pky])
                    self.mm(pby[:, 128:130], rkb[:, ch], hsel[:, 0:2], True, True, r=["rkb", "hsel"], w=[pky])
                    pbg, pkg = self.rot(0, 8)
                    self.mm(pbg[:, 0:64], tokB0[:, n, :], Usb[q][:, 0:64], True, True, r=["tokB0", uk], w=[pkg])
                    self.mm(pbg[:, 64:128], tokB1[:, n, :], Usb[q][:, 64:128], True, True, r=["tokB1", uk], w=[pkg])
                    self.stt("dve", Gbd[:, :], pbg[:, 0:128], gam[:, n:n + 1], Tg[q][:, :], ALU.mult, ALU.add, r=["gam", "Tg%d" % q], w=[pkg, "Gbd"])
                    self.tt("dve", ycst[:, n, :], pby[:, 0:128], YV[:, n, :], ALU.add, r=["YV"], w=[pky, ("ycst", n)])
                    self.cp("act", bon[:, n, :], pby[:, 128:130], w=[pky, ("bon", n)])
                if rstage >= 4:
                    yk_all = [("ycst", n) for n in range(16)]
                    y4 = ycst[:].rearrange("p n (h c) -> p (n h) c", c=64)
                    yf = ycst[:].rearrange("p n c -> p (n c)")
                    sq4 = ysqA[:].rearrange("p n (h c) -> p (n h) c", c=64)
                    bc32 = lambda t: t.unsqueeze(2).to_broadcast([128, 32, 64])
                    self.red(smA[:, 0:32], y4, ALU.add, r=yk_all, w=["smA"])
                    self.ts("dve", smA[:, 0:32], smA[:, 0:32], -1.0 / 64, None, ALU.mult, w=["smA"])
                    self.tt("dve", y4, y4, bc32(smA[:, 0:32]), ALU.add, r=["smA"], w=yk_all)
                    self.tt("pool", ysqA[:], ycst[:], ycst[:], ALU.mult, r=yk_all, w=["ysqA"])
                    self.red(smA[:, 32:64], sq4, ALU.add, r=["ysqA"], w=["smA"])
                    self.ts("dve", smA[:, 32:64], smA[:, 32:64], 1.0 / 64, 64e-5, ALU.mult, ALU.add, w=["smA"])
                    self.act(smA[:, 32:64], smA[:, 32:64], AF.Sqrt, w=["smA"])
                    self.recip(smA[:, 32:64], smA[:, 32:64], w=["smA"])
                    self.tt("dve", y4, y4, bc32(smA[:, 32:64]), ALU.mult, r=["smA"], w=yk_all)
                    self.tt("pool", ycst[:], ycst[:], lnw[:, ps_].unsqueeze(1).to_broadcast([128, 16, 128]), ALU.mult, r=["lnw"], w=yk_all)
                    self.tt("pool", ycst[:], ycst[:], lnb[:, ps_].unsqueeze(1).to_broadcast([128, 16, 128]), ALU.add, r=["lnb"], w=yk_all)
                    self.tt("dve", sq4, tokV[:].rearrange("p n (h c) -> p (n h) c", c=64),
                            bc32(bon[:].rearrange("p n h -> p (n h)")), ALU.mult, r=["tokV"] + [("bon", n) for n in range(16)], w=["ysqA"])
                    self.tt("pool", ycst[:], ycst[:], ysqA[:], ALU.add, r=["ysqA"], w=yk_all)
                    for n in range(16):
                        self.dma("pool", d["Y"][n * 128:(n + 1) * 128, 768 + p * 128:768 + (p + 1) * 128], ycst[:, n, :],
                                 r=[("ycst", n)], w=[("Y", "c", p, n)])
                self.P.fence()
                cS.close()

    def phase_out(self, st, l, xin, xkey, xo, xokey):
        c, d = self.c, self.d
        L = lambda k: d["%s%d" % (k, l)]
        identf = c["identf"]
        final = (l == DEPTH - 1)
        pwb = self.sbt(st, "pwb", [128, 16, 1024], BF16)
        pst = [self.sbt(st, "pst", [128, 2, 1024]) for _ in range(2)]
        Yt = [self.sbt(st, "Yt", [128, 1024]) for _ in range(2)]
        Zt = [self.sbt(st, "Zt", [128, 1024]) for _ in range(2)]
        Gt = [self.sbt(st, "Gt", [128, 3072]) for _ in range(2)]
        xt = [self.sbt(st, "xt", [128, 1024]) for _ in range(2)]
        yzT = [self.sbt(st, "yzT", [128, 8, 128], BF16) for _ in range(2)]
        mixed = [self.sbt(st, "mixed", [128, 1024]) for _ in range(2)]
        mixb = [self.sbt(st, "mixb", [128, 1024], BF16) for _ in range(2)]
        yzb = [self.sbt(st, "yzb", [128, 1024], BF16) for _ in range(2)]
        mxT = [self.sbt(st, "mxT", [128, 8, 128], BF16) for _ in range(2)]
        tmpa = [self.sbt(st, "tmpa", [128, 512]) for _ in range(2)]
        xn = [self.sbt(st, "xn", [128, 1024]) for _ in range(2)]
        jk = self.sbt(st, "jkb", [128, 1024], BF16)
        ss = [self.sbt(st, "ss", [128, 1]) for _ in range(2)]
        self._ta = 0

        def stage_A(i):
            b = i % 2
            rs = slice(i * 128, (i + 1) * 128)
            ky, kz = "Yt%d" % b, "Zt%d" % b
            self.dma("sp", Yt[b][:], d["Y"][rs, :], r=[("Y", i, 0), ("Y", i, 1)] + [("Y", "c", pp, i) for pp in range(2)], w=[ky])
            self.dma("sp", Zt[b][:], d["Z"][rs, :], r=[("Z", i, 0), ("Z", i, 1)], w=[kz])
            kyb = "yzb%d" % b
            self.tt("pool", yzb[b][:], Yt[b][:], Zt[b][:], ALU.mult, r=[kz, ky], w=[kyb])
            pb, pk = self.rot(0, 8)
            pbb = pb[:].bitcast(BF16)
            for ch in range(8):
                self.tr(pbb[:, ch * 128:(ch + 1) * 128], yzb[b][:, ch * 128:(ch + 1) * 128], c["identb"][:], r=[kyb, "identb"], w=[pk])
            self.cp("act", yzT[b][:, :, :], pbb.rearrange("p (c t) -> p c t", c=8), w=[pk, "yzT%d" % b])

        def stage_B(i):
            b = i % 2
            rs = slice(i * 128, (i + 1) * 128)
            kg = "Gt%d" % b
            self.dma("sp", Gt[b][:], d["G"][rs, :], r=[("G", i, m) for m in range(6)], w=[kg])
            for half in range(2):
                cs = slice(half * 512, (half + 1) * 512)
                for bi, (k0, k1) in enumerate(((0, 4), (4, 6), (6, 8))):
                    pb, pk = self.rot(0, 8)
                    for kc in range(k0, k1):
                        self.mm(pb[:, :], yzT[b][:, kc, :], pwb[:, kc, cs], kc == k0, kc == k1 - 1, r=["yzT%d" % b, "pwb"], w=[pk])
                    gsl = Gt[b][:, bi * 1024 + half * 512:bi * 1024 + (half + 1) * 512]
                    if bi == 0:
                        self.tt("dve", mixed[b][:, cs], pb[:, :], gsl, ALU.mult, r=[kg], w=[pk, "mixed%d" % b])
                    else:
                        t_ = self._ta % 2
                        self._ta += 1
                        self.tt("dve", tmpa[t_][:], pb[:, :], gsl, ALU.mult, r=[kg], w=[pk, "tmpa%d" % t_])
                        if bi == 1:
                            self.tt("pool", mixed[b][:, cs], mixed[b][:, cs], tmpa[t_][:], ALU.add, r=["tmpa%d" % t_], w=["mixed%d" % b])
                        else:
                            self.tt("pool", mixb[b][:, cs], mixed[b][:, cs], tmpa[t_][:], ALU.add, r=["tmpa%d" % t_, "mixed%d" % b], w=["mixb%d" % b])

        def stage_CD(i):
            b = i % 2
            rs = slice(i * 128, (i + 1) * 128)
            kx = "xt%d" % b
            self.dma("sp", xt[b][:], xin[rs, :], r=[(xkey, i)], w=[kx])
            pb, pk = self.rot(0, 8)
            pbb = pb[:].bitcast(BF16)
            for ch in range(8):
                self.tr(pbb[:, ch * 128:(ch + 1) * 128], mixb[b][:, ch * 128:(ch + 1) * 128], c["identb"][:], r=["mixb%d" % b, "identb"], w=[pk])
            self.cp("act", mxT[b][:, :, :], pbb.rearrange("p (c t) -> p c t", c=8), w=[pk, "mxT%d" % b])
            for half in range(2):
                cs = slice(half * 512, (half + 1) * 512)
                pb, pk = self.rot(0, 8)
                for kc in range(8):
                    self.mm(pb[:, :], mxT[b][:, kc, :], pwb[:, 8 + kc, cs], kc == 0, kc == 7, r=["mxT%d" % b, "pwb"], w=[pk])
                self.tt("dve", xn[b][:, cs], pb[:, :], xt[b][:, cs], ALU.add, r=[kx], w=[pk, "xn%d" % b])
            if final:
                kss = "ss%d" % b
                self.memset("pool", ss[b][:], 0.0, w=[kss])
                self.act(jk[:], xn[b][:], AF.Square, accum=ss[b][:], r=["xn%d" % b], w=["jkb", kss])
                self.ts("dve", ss[b][:], ss[b][:], 1.0 / D, 1e-6, ALU.mult, ALU.add, w=[kss])
                self.act(ss[b][:], ss[b][:], AF.Sqrt, w=[kss])
                self.recip(ss[b][:], ss[b][:], w=[kss])
                self.stt("dve", xn[b][:], xn[b][:], ss[b][:, 0:1], c["fg"][:], ALU.mult, ALU.mult, r=[kss, "fg"], w=["xn%d" % b])
            self.dma("pool", xo[rs, :], xn[b][:], r=["xn%d" % b], w=[(xokey, i)])

        stage_A(0)
        stage_A(1)
        for q in range(8):
            b = q % 2
            self.dma("sp", pst[b][:], L("pw")[:, 2 * q:2 * q + 2, :], w=["pst%d" % b])
            self.cp("dve", pwb[:, 2 * q, :], pst[b][:, 0, :], r=["pst%d" % b], w=["pwb"])
            self.cp("act" if q % 2 else "pool", pwb[:, 2 * q + 1, :], pst[b][:, 1, :], r=["pst%d" % b], w=["pwb"])
        stage_B(0)
        for i in range(NT):
            if i + 2 < NT:
                stage_A(i + 2)
            if i + 1 < NT:
                stage_B(i + 1)
            stage_CD(i)


FUSED = True
_CACHE = {}


def _get_nc(layers, debug=False):
    key = (tuple(layers), debug)
    if key not in _CACHE:
        kb = KB(layers, debug)
        _CACHE[key] = kb.build()
    return _CACHE[key]


def _run(layers, inp, xs, vfs, debug=False):
    nc = _get_nc(layers, debug)
    perm = _perm()
    consts = _consts()
    base = dict(consts)
    base["fg"] = np.ascontiguousarray(np.broadcast_to(np.asarray(inp["final_g"], np.float32).reshape(1, D), (128, D)))
    for l in layers:
        for k, v in prep_layer(inp, l, perm).items():
            base["%s%d" % (k, l)] = v
    maps = []
    for b in range(8):
        m = dict(base)
        m["x"] = np.ascontiguousarray(xs[b], dtype=np.float32)
        if vfs is not None:
            m["vfirst"] = np.ascontiguousarray(vfs[b], dtype=np.float32)
        maps.append(m)
    res = run_bass_kernel_spmd(nc, maps, core_ids=list(range(8)))
    return res.results


def kernel(**inputs):
    inp = {k: np.asarray(v) for k, v in inputs.items()}
    x = np.asarray(inp["x"], np.float32)
    if FUSED:
        r = _run([0, 1], inp, [x[b] for b in range(8)], None)
        return np.stack([r[b]["xout"] for b in range(8)], 0).astype(np.float32)
    r0 = _run([0], inp, [x[b] for b in range(8)], None)
    r1 = _run([1], inp, [r0[b]["xout"] for b in range(8)], [r0[b]["vfirst"] for b in range(8)])
    return np.stack([r1[b]["xout"] for b in range(8)], 0).astype(np.float32)
```

```python
from contextlib import ExitStack
import numpy as np
import ml_dtypes
import concourse.bass as bass
import concourse.mybir as mybir
from concourse.bass_utils import run_bass_kernel_spmd

F32 = mybir.dt.float32
BF16 = mybir.dt.bfloat16
AF = mybir.ActivationFunctionType
ALU = mybir.AluOpType
AX = mybir.AxisListType

ENGINES = ("sp", "act", "dve", "pool", "pe")
INORDER_ENGINES = ("pe", "sp")
NDMASEM = 12


class _Op:
    __slots__ = ("eng", "fn", "deps", "dma", "sig", "sem", "val", "prev", "idx")


class Prog:
    def __init__(self, nc):
        self.nc = nc
        self.ops = []
        self.lastw = {}
        self.readers = {}

    def add(self, eng, fn, r=(), w=(), dma=False):
        o = _Op()
        o.eng, o.fn, o.dma, o.sig = eng, fn, dma, False
        o.idx = len(self.ops)
        deps = set()
        for k in r:
            if k in self.lastw:
                deps.add(self.lastw[k])
        for k in w:
            if k in self.lastw:
                deps.add(self.lastw[k])
            deps.update(self.readers.get(k, ()))
        o.deps = deps
        for k in r:
            lst = self.readers.setdefault(k, [])
            if not dma:
                lst[:] = [i for i in lst if self.ops[i].dma or self.ops[i].eng != eng]
            lst.append(o.idx)
        for k in w:
            self.lastw[k] = o.idx
            self.readers[k] = []
        self.ops.append(o)
        return o

    def barrier_keys(self):
        return list(self.lastw.keys())

    def _skip(self, d, o):
        if d.dma:
            return False
        if d.eng == "sp":
            return True
        if d.eng == o.eng and o.eng in INORDER_ENGINES:
            return True
        return False

    def emit(self, stack):
        nc = self.nc
        ops = self.ops
        for o in ops:
            for di in o.deps:
                d = ops[di]
                if not self._skip(d, o):
                    d.sig = True
        engsem = {e: stack.enter_context(nc.semaphore("s_" + e)) for e in ENGINES if e != "sp"}
        dmasem = {e: [stack.enter_context(nc.semaphore("d_%s%d" % (e, i))) for i in range(NDMASEM)]
                  for e in ("sp", "pool", "act")}
        cnt = {e: 0 for e in ENGINES}
        dcnt = {e: 0 for e in ENGINES}
        per = {e: [] for e in ENGINES}
        for o in ops:
            per[o.eng].append(o)
            if o.dma:
                n = dcnt[o.eng]
                dcnt[o.eng] += 1
                o.sem = dmasem[o.eng][n % NDMASEM]
                o.val = 16 * (n // NDMASEM + 1)
                o.prev = 16 * (n // NDMASEM)
            elif o.sig:
                cnt[o.eng] += 1
                o.sem = engsem[o.eng]
                o.val = cnt[o.eng]
        self.stats = dict(cnt=cnt, dcnt=dcnt, n={e: len(per[e]) for e in ENGINES})

        def run(e, eng):
            waited = {}
            nw = 0
            for o in per[eng]:
                for di in sorted(o.deps):
                    d = ops[di]
                    if self._skip(d, o):
                        continue
                    key = id(d.sem)
                    if waited.get(key, 0) >= d.val:
                        continue
                    e.wait_ge(d.sem, d.val)
                    nw += 1
                    waited[key] = d.val
                if o.dma and o.prev > 0 and waited.get(id(o.sem), 0) < o.prev:
                    e.wait_ge(o.sem, o.prev)
                    waited[id(o.sem)] = o.prev
                ins = o.fn(e)
                if o.dma:
                    ins.then_inc(o.sem, 16)
                elif o.sig:
                    ins.then_inc(o.sem, 1)
            self.stats.setdefault("waits", {})[eng] = nw

        with nc.Block() as block:
            @block.sync
            def _(e):
                run(e, "sp")

            @block.scalar
            def _(e):
                run(e, "act")

            @block.vector
            def _(e):
                run(e, "dve")

            @block.gpsimd
            def _(e):
                run(e, "pool")

            @block.tensor
            def _(e):
                run(e, "pe")

    def fence(self):
        start = getattr(self, "_fpos", 0)
        deps = set(o.idx for o in self.ops[start:] if o.dma)
        last = {}
        for o in self.ops:
            last[o.eng] = o.idx
        deps.update(last.values())
        for e in ENGINES:
            o = self.add(e, lambda eng: eng.nop())
            o.deps = set(deps)
        self._fpos = len(self.ops)


T = 2048
D = 1024
NT = 16
N_IN = 6808
DEPTH = 2
BIG = 30000.0
NCMP = 127
SEG = dict(a_q=(0, 512), a_kv_cmp=(512, 256), a_kv_slc=(768, 256), a_kv_win=(1024, 256), a_gate=(1280, 24),
           a_z=(1304, 512), b_q=(1816, 256), b_kv=(2072, 256), b_z=(2328, 256), c_shift=(2584, 896),
           c_z=(3480, 256), merge=(3736, 3072))
NFEAT = 18 * 128
TOK0 = NFEAT


def _perm():
    def seg(name, a, b):
        o = SEG[name][0]
        return list(range(o + a, o + b))
    p = []
    for hh in range(4):
        p += seg("a_q", hh * 64, hh * 64 + 64) + seg("a_q", (4 + hh) * 64, (4 + hh) * 64 + 64)
    p += seg("a_kv_cmp", 0, 128)
    p += seg("a_kv_cmp", 128, 256)
    p += seg("a_kv_slc", 0, 128)
    p += seg("a_kv_win", 0, 128)
    p += seg("b_q", 0, 64) + seg("b_q", 128, 192)
    p += seg("b_q", 64, 128) + seg("b_q", 192, 256)
    p += seg("b_kv", 0, 128)
    p += seg("c_shift", 0, 896)
    assert len(p) == NFEAT
    p += seg("a_kv_slc", 128, 256) + seg("a_kv_win", 128, 256) + seg("b_kv", 128, 256) + seg("a_gate", 0, 24)
    p += seg("a_z", 0, 512) + seg("b_z", 0, 256) + seg("c_z", 0, 256)
    p += seg("merge", 0, 3072)
    assert len(p) == N_IN and len(set(p)) == N_IN
    return np.array(p)


def _consts():
    c = {}
    c["identf"] = np.eye(128, dtype=np.float32)
    s = np.arange(128)[:, None]
    t = np.arange(128)[None, :]
    c["triA"] = np.where(s <= t, 0.0, -BIG).astype(np.float32)
    c["triB"] = np.where(s > t, 0.0, -BIG).astype(np.float32)
    E = np.zeros((32, 16, 128), np.float32)
    for j in range(16):
        for p in range(128):
            E[2 * j + p // 64, j, p] = BIG
    E2 = np.zeros((128, 2048), np.float32)
    E2[0:32] = E.reshape(32, 2048)
    E2[64:96] = E.reshape(32, 2048)
    c["Eall"] = E2
    n = np.arange(128)[:, None]
    tt = np.arange(T)[None, :]
    c["penc"] = np.where(16 * n + 31 <= tt, 0.0, -BIG).astype(np.float32)
    Mi = np.zeros((128, 16, 32), np.float32)
    Bc = np.zeros((128, 16, 32), np.float32)
    for i in range(16):
        for p in range(128):
            cur = 2 * i + p // 64
            for j in range(32):
                if j == 0:
                    Bc[p, i, j] = 10.0
                elif j == cur:
                    Bc[p, i, j] = 20.0
                elif j == cur - 1:
                    Bc[p, i, j] = 30.0
                elif j > cur:
                    Bc[p, i, j] = -1.0 - j
                else:
                    Mi[p, i, j] = 1.0
            if cur == 0:
                Bc[p, i, 0] = 20.0
            if cur == 1:
                Bc[p, i, 0] = 30.0
    c["Mi"] = Mi.reshape(128, 512)
    c["Bc"] = Bc.reshape(128, 512)
    ci = np.arange(128)[:, None] * 16
    sj = np.arange(32)[None, :] * 64
    c["ovl"] = ((ci < sj + 64) & (ci + 32 > sj)).astype(np.float32)
    r = np.arange(128)[:, None]
    q = np.arange(128)[None, :]
    c["mSL"] = (q < r).astype(np.float32)
    c["mSU"] = (r < q).astype(np.float32)
    c["mIU"] = (r <= q).astype(np.float32)
    c["bones"] = ((r // 64) == (q // 64)).astype(np.float32)
    c["hsel"] = ((np.arange(128)[:, None] // 64) == np.arange(2)[None, :]).astype(np.float32)
    return c


CONST_SHAPES = dict(identf=[128, 128], triA=[128, 128], triB=[128, 128], Eall=[128, 2048], penc=[128, 2048],
                    Mi=[128, 512], Bc=[128, 512], ovl=[128, 32], mSL=[128, 128], mSU=[128, 128], mIU=[128, 128],
                    bones=[128, 128], hsel=[128, 2])


def layer_input_shapes(l):
    s = dict(win=[D, N_IN], ng=[128, 8], ngb=[128, D], bmerge=[128, 3072], w1k=[128, 32, 128], w1v=[128, 32, 128],
             pek=[128, 32], pev=[128, 32], w2k=[128, 64], w2v=[128, 64], sinks=[128, 4], rv=[128, 20],
             wa=[128, 256], lnw=[128, 256], lnb=[128, 256], pw=[128, 16, 1024])
    if l > 0:
        s["v1"] = [128, 2, 32]
        s["v2"] = [32, 256]
    return s


def prep_layer(inp, l, perm):
    f = lambda a: np.ascontiguousarray(a, dtype=np.float32)
    o = {}
    o["win"] = f(inp["w_in"][l][:, perm])
    o["ng"] = f(inp["norm_g"][l].reshape(8, 128).T)
    o["ngb"] = f(np.broadcast_to(inp["norm_g"][l].reshape(1, D), (128, D)))
    o["bmerge"] = f(np.broadcast_to(inp["b_merge"][l].reshape(1, 3072), (128, 3072)))
    for nm, src in (("w1k", "cmp_w1_k"), ("w1v", "cmp_w1_v")):
        w = inp[src][l].reshape(32, 64, 128).transpose(1, 0, 2)
        o[nm] = f(np.concatenate([w, w], 0))
    for nm, src in (("pek", "cmp_pe_k"), ("pev", "cmp_pe_v")):
        p = inp[src][l].T
        o[nm] = f(np.concatenate([p, p], 0))
    o["w2k"] = f(inp["cmp_w2_k"][l])
    o["w2v"] = f(inp["cmp_w2_v"][l])
    o["sinks"] = f(np.broadcast_to(inp["swa_sinks"][l].reshape(1, 4), (128, 4)))
    rv = np.zeros((128, 20), np.float32)
    rv[:, 0:7] = inp["rwkv_mu"][l].reshape(7, 128).T
    for k, nm in enumerate(("rwkv_w0", "rwkv_a0", "rwkv_k_k", "rwkv_k_a")):
        rv[:, 7 + 2 * k:9 + 2 * k] = inp[nm][l].reshape(2, 128).T
    rv[:, 15:17] = inp["rwkv_r_k"][l].reshape(2, 128).T
    if l > 0:
        rv[:, 17:19] = inp["rwkv_v0"][l - 1].reshape(2, 128).T
    o["rv"] = rv
    o["wa"] = f(np.concatenate([inp["rwkv_w2"][l], inp["rwkv_a2"][l]], 0))
    o["lnw"] = f(np.broadcast_to(inp["rwkv_ln_w"][l].reshape(1, 256), (128, 256)))
    o["lnb"] = f(np.broadcast_to(inp["rwkv_ln_b"][l].reshape(1, 256), (128, 256)))
    pw = np.concatenate([inp["proj_a"][l].reshape(4, 128, 1024), inp["proj_b"][l].reshape(2, 128, 1024),
                         inp["proj_c"][l].reshape(2, 128, 1024), inp["w_out"][l].reshape(8, 128, 1024)], 0)
    o["pw"] = f(pw.transpose(1, 0, 2))
    if l > 0:
        o["v1"] = f(inp["rwkv_v1"][l - 1].reshape(2, 128, 32).transpose(1, 0, 2))
        o["v2"] = f(inp["rwkv_v2"][l - 1])
    return o


class KB:
    def __init__(self, layers, debug=False, upto="out"):
        self.layers = list(layers)
        self.debug = debug
        self.upto = upto
        nc = bass.Bass("TRN2", target_bir_lowering=False)
        nc.allow_low_precision("bf16 matmul operands, fp32 accumulation")
        self.nc = nc
        self.P = Prog(nc)
        self.rotc = {}

    def dma(self, eng, out, in_, r=(), w=()):
        return self.P.add(eng, lambda e: e.dma_start(out=out, in_=in_), r=r, w=w, dma=True)

    def mm(self, out, lhsT, rhs, start, stop, r=(), w=()):
        return self.P.add("pe", lambda e: e.matmul(out, lhsT=lhsT, rhs=rhs, start=start, stop=stop,
                                                   skip_group_check=True), r=r, w=w)

    def tr(self, out, in_, ident, r=(), w=()):
        return self.P.add("pe", lambda e: e.transpose(out=out, in_=in_, identity=ident), r=r, w=w)

    def act(self, out, in_, func, r=(), w=(), bias=None, scale=None, accum=None):
        kw = {}
        if bias is not None:
            kw["bias"] = bias
        if scale is not None:
            kw["scale"] = scale
        if accum is not None:
            kw["accum_out"] = accum
        return self.P.add("act", lambda e: e.activation(out=out, in_=in_, func=func, **kw), r=r, w=w)

    def tt(self, eng, out, in0, in1, op, r=(), w=()):
        return self.P.add(eng, lambda e: e.tensor_tensor(out=out, in0=in0, in1=in1, op=op), r=r, w=w)

    def ts(self, eng, out, in0, s1, s2, op0, op1=None, r=(), w=()):
        if op1 is None:
            return self.P.add(eng, lambda e: e.tensor_scalar(out=out, in0=in0, scalar1=s1, scalar2=None, op0=op0), r=r, w=w)
        return self.P.add(eng, lambda e: e.tensor_scalar(out=out, in0=in0, scalar1=s1, scalar2=s2, op0=op0, op1=op1), r=r, w=w)

    def stt(self, eng, out, in0, scalar, in1, op0, op1, r=(), w=()):
        return self.P.add(eng, lambda e: e.scalar_tensor_tensor(out=out, in0=in0, scalar=scalar, in1=in1, op0=op0, op1=op1), r=r, w=w)

    def cp(self, eng, out, in_, r=(), w=()):
        if eng == "act":
            return self.P.add("act", lambda e: e.copy(out=out, in_=in_), r=r, w=w)
        return self.P.add(eng, lambda e: e.tensor_copy(out=out, in_=in_), r=r, w=w)

    def memset(self, eng, ap, val, w=()):
        return self.P.add(eng, lambda e: e.memset(ap, val), w=w)

    def recip(self, out, in_, r=(), w=()):
        return self.P.add("dve", lambda e: e.reciprocal(out=out, in_=in_), r=r, w=w)

    def red(self, out, in_, op, r=(), w=()):
        return self.P.add("dve", lambda e: e.tensor_reduce(out=out, in_=in_, axis=AX.X, op=op), r=r, w=w)

    def rot(self, lo, hi):
        k = (lo, hi)
        c = self.rotc.get(k, 0)
        self.rotc[k] = c + 1
        b = lo + c % (hi - lo)
        return self.ps[b], "ps%d" % b

    def sbt(self, st, name, shape, dt=F32):
        self._nm = getattr(self, "_nm", 0) + 1
        return st.enter_context(self.nc.sbuf_tensor("%s_%d" % (name, self._nm), shape, dt))

    def build(self):
        nc = self.nc
        layers = self.layers
        dk = "ExternalOutput" if self.debug else "Internal"
        self.d = {}
        self.d["x"] = nc.dram_tensor("x", [T, D], F32, kind="ExternalInput").ap()
        for k, shp in CONST_SHAPES.items():
            self.d[k] = nc.dram_tensor(k, shp, F32, kind="ExternalInput").ap()
        self.d["fg"] = nc.dram_tensor("fg", [128, D], F32, kind="ExternalInput").ap()
        for l in layers:
            for k, shp in layer_input_shapes(l).items():
                self.d["%s%d" % (k, l)] = nc.dram_tensor("%s%d" % (k, l), shp, F32, kind="ExternalInput").ap()
        self.d["Z"] = nc.dram_tensor("scrZ", [T, 1024], F32, kind=dk).ap()
        self.d["G"] = nc.dram_tensor("scrG", [T, 3072], F32, kind=dk).ap()
        self.d["Y"] = nc.dram_tensor("scrY", [T, 1024], F32, kind=dk).ap()
        self.d["F"] = nc.dram_tensor("scrF", [7, 128, T], F32, kind=dk).ap()
        if 0 in layers and 1 in layers:
            self.d["vf"] = nc.dram_tensor("vfirst", [2, 128, T], F32, kind=dk).ap()
        elif 0 in layers:
            self.d["vf"] = nc.dram_tensor("vfirst", [2, 128, T], F32, kind="ExternalOutput").ap()
        else:
            self.d["vf"] = nc.dram_tensor("vfirst", [2, 128, T], F32, kind="ExternalInput").ap()
        self.d["xout"] = nc.dram_tensor("xout", [T, D], F32, kind="ExternalOutput").ap()
        if len(layers) > 1:
            self.d["xmid"] = nc.dram_tensor("xmid", [T, D], F32, kind=dk).ap()

        with ExitStack() as st:
            self.ps = [st.enter_context(nc.psum_tensor("psb%d" % b, [128, 512], F32)) for b in range(8)]
            self.load_consts(st)
            xin = self.d["x"]
            xkey = "x"
            for li, l in enumerate(layers):
                last = (li == len(layers) - 1)
                xo = self.d["xout"] if last else self.d["xmid"]
                xokey = "xout" if last else "xmid"
                self.layer(l, xin, xkey, xo, xokey)
                xin, xkey = xo, xokey
            self.P.fence()
            self.P.emit(st)
        return nc

    def load_consts(self, st):
        d = self.d
        c = self.c = {}
        f32c = ["identf", "Mi", "Bc", "mSL", "mSU", "mIU", "bones", "hsel"]
        for k in f32c:
            c[k] = self.sbt(st, k, CONST_SHAPES[k])
            self.dma("sp", c[k][:], d[k], w=[k])
        c["fg"] = self.sbt(st, "fg", [128, D])
        self.dma("sp", c["fg"][:], d["fg"], w=["fg"])
        bl = ["identf", "triA", "triB", "Eall", "penc", "ovl"]
        for k in bl:
            nm = "identb" if k == "identf" else k
            c[nm] = self.sbt(st, nm, CONST_SHAPES[k], BF16)
        with ExitStack() as s2:
            for k in bl:
                nm = "identb" if k == "identf" else k
                tmp = self.sbt(s2, k + "_f", CONST_SHAPES[k])
                self.dma("sp", tmp[:], d[k], w=[k + "_f"])
                self.cp("pool", c[nm][:], tmp[:], r=[k + "_f"], w=[nm])
            self.P.fence()

    def layer(self, l, xin, xkey, xo, xokey):
        with ExitStack() as sA:
            A = self.alloc_attn(sA)
            with ExitStack() as s1:
                self.phase01(s1, l, xin, xkey, A)
                self.P.fence()
            if self.upto in ("p0", "p1"):
                return
            with ExitStack() as s2:
                self.phase_attn(s2, l, A)
                self.P.fence()
        if self.upto in ("cmp", "attn"):
            return
        with ExitStack() as s3:
            self.phase_rwkv(s3, l)
            self.P.fence()
        if self.upto == "rwkv":
            return
        with ExitStack() as s4:
            self.phase_out(s4, l, xin, xkey, xo, xokey)
            self.P.fence()

    def alloc_attn(self, st):
        A = {}
        A["qTA"] = self.sbt(st, "qTA", [128, 4, T], BF16)
        A["qTB"] = self.sbt(st, "qTB", [128, 2, T], BF16)
        for k in ("kTc", "vTc", "kTs", "kTw", "kTb"):
            A[k] = self.sbt(st, k, [128, T], BF16)
        for k in ("Vs", "Vw", "Vb"):
            A[k] = self.sbt(st, k, [128, 16, 2, 68], BF16)
        A["gate"] = self.sbt(st, "gate", [128, 16, 24])
        return A

    def phase01(self, st, l, xin, xkey, A):
        c, d = self.c, self.d
        L = lambda k: d["%s%d" % (k, l)]
        xnT = self.sbt(st, "xnT", [128, 8, T], BF16)
        ng = self.sbt(st, "ng", [128, 8])
        self.dma("sp", ng[:], L("ng"), w=["ng"])
        wst = [self.sbt(st, "wst", [128, 8, 512]) for _ in range(2)]
        wbf = [self.sbt(st, "wbf", [128, 8, 512], BF16) for _ in range(2)]
        bm = self.sbt(st, "bm", [128, 3072])
        win = L("win").rearrange("(k p) n -> p k n", p=128)
        self._blk = 0

        def load_block(c0, ncol):
            b = self._blk % 2
            self._blk += 1
            for kc in range(8):
                self.dma("sp", wst[b][:, kc, 0:ncol], win[:, kc, c0:c0 + ncol], w=["wst%d" % b])
            self.cp("dve", wbf[b][:, 0:4, 0:ncol], wst[b][:, 0:4, 0:ncol], r=["wst%d" % b], w=["wbf%d" % b])
            self.cp("act", wbf[b][:, 4:6, 0:ncol], wst[b][:, 4:6, 0:ncol], r=["wst%d" % b], w=["wbf%d" % b])
            self.cp("pool", wbf[b][:, 6:8, 0:ncol], wst[b][:, 6:8, 0:ncol], r=["wst%d" % b], w=["wbf%d" % b])
            return wbf[b], "wbf%d" % b

        pre = {}
        s0 = ExitStack()
        xt = [self.sbt(s0, "xt", [128, D]) for _ in range(2)]
        xs = [self.sbt(s0, "xs", [128, D], BF16) for _ in range(2)]
        ngb = self.sbt(s0, "ngb", [128, D])
        self.dma("sp", ngb[:], L("ngb"), w=["ngb"])
        junk = self.sbt(s0, "junk", [128, D], BF16)
        ss = [self.sbt(s0, "ss", [128, 1]) for _ in range(2)]
        for k in ("Vs", "Vw", "Vb"):
            self.memset("pool", A[k][:], 0.0, w=[k])
            self.memset("pool", A[k][:, :, :, 64:65], 1.0, w=[k])
        for i in range(NT):
            b = i % 2
            kx, ks, kss = "xt%d" % b, "xs%d" % b, "ss%d" % b
            self.dma("sp", xt[b][:], xin[i * 128:(i + 1) * 128, :], r=[xkey], w=[kx])
            self.memset("pool", ss[b][:], 0.0, w=[kss])
            self.act(junk[:], xt[b][:], AF.Square, r=[kx], w=["junk", kss], accum=ss[b][:])
            self.ts("dve", ss[b][:], ss[b][:], 1.0 / D, 1e-6, ALU.mult, ALU.add, w=[kss])
            self.act(ss[b][:], ss[b][:], AF.Sqrt, w=[kss])
            self.recip(ss[b][:], ss[b][:], w=[kss])
            self.stt("dve", xs[b][:], xt[b][:], ss[b][:, 0:1], ngb[:], ALU.mult, ALU.mult, r=[kx, kss, "ngb"], w=[ks])
            pb, pk = self.rot(0, 8)
            pbb = pb[:].bitcast(BF16)
            for ch in range(8):
                self.tr(pbb[:, ch * 128:(ch + 1) * 128], xs[b][:, ch * 128:(ch + 1) * 128], c["identb"][:],
                        r=[ks, "identb"], w=[pk])
            self.cp("act" if i % 2 else "dve", xnT[:, :, i * 128:(i + 1) * 128], pbb.rearrange("p (c t) -> p c t", c=8),
                    w=[pk, "xnT"])
            if i == 5:
                pre[0] = load_block(0, 512)
                self.dma("sp", bm[:], L("bmerge"), w=["bm"])
            if i == 11:
                pre[1] = load_block(512, 512)
        self.P.fence()
        s0.close()
        if self.upto == "p0":
            return
        fst = [self.sbt(st, "fst", [128, T]) for _ in range(1)]
        zst = [self.sbt(st, "zst", [128, 512]) for _ in range(3)]

        fdest = []
        for hh in range(4):
            fdest.append((A["qTA"], hh, "qTA"))
        for k in ("kTc", "vTc", "kTs", "kTw"):
            fdest.append((A[k], None, k))
        for hh in range(2):
            fdest.append((A["qTB"], hh, "qTB"))
        fdest.append((A["kTb"], None, "kTb"))
        for cc in range(7):
            fdest.append((None, cc, "F"))
        self._ev = 0
        self._zi = 0

        def comp_feat(s0, nsub):
            def f(wb, wk):
                for s in range(s0, s0 + nsub):
                    dst, idx, dkey = fdest[s]
                    fb = 0
                    for tc in range(4):
                        pb, pk = self.rot(0, 8)
                        for kc in range(8):
                            self.mm(pb[:, :], wb[:, kc, (s - s0) * 128:(s - s0 + 1) * 128], xnT[:, kc, tc * 512:(tc + 1) * 512],
                                    kc == 0, kc == 7, r=[wk, "xnT"], w=[pk])
                        if dst is None:
                            o_ap, okey = fst[fb][:, tc * 512:(tc + 1) * 512], "fst%d" % fb
                        elif idx is None:
                            o_ap, okey = dst[:, tc * 512:(tc + 1) * 512], dkey
                        else:
                            o_ap, okey = dst[:, idx, tc * 512:(tc + 1) * 512], dkey
                        self.cp("act" if self._ev % 2 == 0 else "dve", o_ap, pb[:, :], w=[pk, okey])
                        self._ev += 1
                    if dst is None:
                        self.dma("pool", d["F"][idx], fst[fb][:], r=["fst%d" % fb], w=[("F", idx)])
            return f

        def comp_tok0(wb, wk):
            for i in range(NT):
                pb, pk = self.rot(0, 8)
                for kc in range(8):
                    self.mm(pb[:, 0:408], xnT[:, kc, i * 128:(i + 1) * 128], wb[:, kc, 0:408], kc == 0, kc == 7,
                            r=[wk, "xnT"], w=[pk])
                for vi, k in enumerate(("Vs", "Vw", "Vb")):
                    self.cp("dve" if vi == 1 else "act", A[k][:, i, :, 0:64],
                            pb[:, vi * 128:(vi + 1) * 128].rearrange("p (g e) -> p g e", g=2), w=[pk, k])
                self.act(A["gate"][:, i, :], pb[:, 384:408], AF.Sigmoid, w=[pk, "gate"])

        def comp_zm(blk):
            def f(wb, wk):
                for i in range(NT):
                    pb, pk = self.rot(0, 8)
                    for kc in range(8):
                        self.mm(pb[:, :], xnT[:, kc, i * 128:(i + 1) * 128], wb[:, kc, :], kc == 0, kc == 7,
                                r=[wk, "xnT"], w=[pk])
                    zb = self._zi % 3
                    self._zi += 1
                    zk = "zst%d" % zb
                    if blk < 2:
                        self.act(zst[zb][:], pb[:, :], AF.Silu, w=[pk, zk])
                        self.dma("pool", d["Z"][i * 128:(i + 1) * 128, blk * 512:(blk + 1) * 512], zst[zb][:], r=[zk], w=[("Z", i, blk)])
                    else:
                        mb = blk - 2
                        self.tt("dve", zst[zb][:], pb[:, :], bm[:, mb * 512:(mb + 1) * 512], ALU.add, r=["bm"], w=[pk, zk])
                        self.act(zst[zb][:], zst[zb][:], AF.Sigmoid, w=[zk])
                        self.dma("pool", d["G"][i * 128:(i + 1) * 128, mb * 512:(mb + 1) * 512], zst[zb][:], r=[zk], w=[("G", i, mb)])
            return f

        blocks = []
        for s0 in range(0, 18, 4):
            nsub = min(4, 18 - s0)
            blocks.append((s0 * 128, nsub * 128, comp_feat(s0, nsub)))
        blocks.append((TOK0, 408, comp_tok0))
        for blk in range(8):
            blocks.append((TOK0 + 408 + blk * 512, 512, comp_zm(blk)))
        assert blocks[0][:2] == (0, 512) and blocks[1][:2] == (512, 512)
        cur = pre[0]
        for bi, (c0, ncol, fn) in enumerate(blocks):
            if bi == 0:
                nxt = pre[1]
            else:
                nxt = load_block(blocks[bi + 1][0], blocks[bi + 1][1]) if bi + 1 < len(blocks) else None
            fn(cur[0], cur[1])
            cur = nxt

    def phase_attn(self, st, l, A):
        c, d = self.c, self.d
        L = lambda k: d["%s%d" % (k, l)]
        ident, identb = c["identf"], c["identb"]
        kcT = self.sbt(st, "kcT", [128, 128], BF16)
        vcaug = self.sbt(st, "vcaug", [128, 2, 100], BF16)
        self.memset("pool", kcT[:], 0.0, w=["kcT"])
        self.memset("pool", vcaug[:], 0.0, w=["vcaug"])
        self.memset("pool", vcaug[:, :, 64:65], 1.0, w=["vcaug"])
        for g in range(2):
            self.cp("pool", vcaug[:, g, 65:97], c["ovl"][:], r=["ovl"], w=["vcaug"])
        with ExitStack() as sc:
            w1f = [self.sbt(sc, "w1f", [128, 32, 128]) for _ in range(2)]
            w1b = [self.sbt(sc, "w1b", [128, 32, 128], BF16) for _ in range(2)]
            pef = self.sbt(sc, "pef", [128, 2, 32])
            peb = self.sbt(sc, "peb", [128, 2, 32], BF16)
            w2f = self.sbt(sc, "w2f", [128, 2, 64])
            w2kd = self.sbt(sc, "w2kd", [128, 128], BF16)
            w2vb = self.sbt(sc, "w2vb", [128, 64], BF16)
            cb = self.sbt(sc, "cb", [128, 4])
            hsb = [self.sbt(sc, "hsb", [128, 128], BF16) for _ in range(2)]
            for wi, nm in enumerate(("w1k", "w1v")):
                self.dma("sp", w1f[wi][:], L(nm), w=["w1f%d" % wi])
                self.cp("dve", w1b[wi][:, 0:16, :], w1f[wi][:, 0:16, :], r=["w1f%d" % wi], w=["w1b%d" % wi])
                self.cp("act" if wi else "pool", w1b[wi][:, 16:32, :], w1f[wi][:, 16:32, :], r=["w1f%d" % wi], w=["w1b%d" % wi])
            self.dma("sp", pef[:, 0, :], L("pek"), w=["pef"])
            self.dma("sp", pef[:, 1, :], L("pev"), w=["pef"])
            self.cp("dve", peb[:], pef[:], r=["pef"], w=["peb"])
            self.dma("sp", w2f[:, 0, :], L("w2k"), w=["w2f"])
            self.dma("sp", w2f[:, 1, :], L("w2v"), w=["w2f"])
            self.cp("dve", w2kd[:, 0:64], w2f[:, 0, :], r=["w2f"], w=["w2kd"])
            self.cp("dve", w2kd[:, 64:128], w2f[:, 0, :], r=["w2f"], w=["w2kd"])
            self.cp("dve", w2vb[:], w2f[:, 1, :], r=["w2f"], w=["w2vb"])
            cnt = 0
            for wi, (src, skey) in enumerate(((A["kTc"], "kTc"), (A["vTc"], "vTc"))):
                for g in range(2):
                    rows = slice(g * 64, (g + 1) * 64)
                    sv = src[rows, :].rearrange("p (n s) -> p n s", s=16)
                    pb, pk = self.rot(0, 8)
                    for pos in range(32):
                        self.mm(pb[:, 0:1], w1b[wi][rows, pos, :], peb[rows, wi, pos:pos + 1], pos == 0, pos == 31,
                                r=["w1b%d" % wi, "peb"], w=[pk])
                    self.cp("dve", cb[:, cnt:cnt + 1], pb[:, 0:1], w=[pk, "cb"])
                    pb2, pk2 = self.rot(0, 8)
                    for pos in range(32):
                        rhs = sv[:, 0:127, pos] if pos < 16 else sv[:, 1:128, pos - 16]
                        self.mm(pb2[:, 0:127], w1b[wi][rows, pos, :], rhs, pos == 0, pos == 31,
                                r=["w1b%d" % wi, skey], w=[pk2])
                    hb = hsb[cnt % 2]
                    hk = "hsb%d" % (cnt % 2)
                    self.act(hb[:, 0:127], pb2[:, 0:127], AF.Silu, bias=cb[:, cnt:cnt + 1], r=["cb"], w=[pk2, hk])
                    pb3, pk3 = self.rot(0, 8)
                    if wi == 0:
                        self.mm(pb3[:, 0:127], w2kd[:, :], hb[:, 0:127], True, True, r=["w2kd", hk], w=[pk3])
                        self.cp("dve", kcT[rows, 0:127], pb3[rows, 0:127], w=[pk3, "kcT"])
                    else:
                        self.mm(pb3[0:127, 0:64], hb[:, 0:127], w2vb[:, :], True, True, r=["w2vb", hk], w=[pk3])
                        self.cp("dve", vcaug[0:127, g, 0:64], pb3[0:127, 0:64], w=[pk3, "vcaug"])
                    cnt += 1
            self.P.fence()
        if self.upto == "cmp":
            return
        NPT = 4
        PT = [self.sbt(st, "PT", [128, 17, 512], BF16) for _ in range(NPT)]
        PTc = [self.sbt(st, "PTc", [128, 512], BF16) for _ in range(2)]
        ya = [self.sbt(st, "ya", [128, 512]) for _ in range(2)]
        yb = [self.sbt(st, "yb", [128, 256]) for _ in range(2)]
        selT2 = self.sbt(st, "selT2", [128, T], BF16)
        self.memset("pool", selT2[:], 0.0, w=["selT2"])
        esink = self.sbt(st, "esink", [128, 4])
        self.dma("sp", esink[:], L("sinks"), w=["esink"])
        self.act(esink[:], esink[:], AF.Exp, w=["esink"])
        NR = 4
        rd = [self.sbt(st, "rd", [128, 4]) for _ in range(NR)]
        coef = [self.sbt(st, "coef", [128, 4]) for _ in range(NR)]
        tmpo = [self.sbt(st, "tmpo", [128, 4, 64]) for _ in range(NR)]
        tmpi = [self.sbt(st, "tmpi", [128, 4, 32]) for _ in range(2)]
        imp = [self.sbt(st, "imp", [128, 32]) for _ in range(2)]
        score = [self.sbt(st, "score", [128, 32]) for _ in range(2)]
        scw = [self.sbt(st, "scw", [128, 32]) for _ in range(2)]
        m8 = [self.sbt(st, "m8", [128, 16]) for _ in range(2)]
        selm = [self.sbt(st, "selm", [128, 96]) for _ in range(2)]
        for t_ in range(2):
            self.memset("pool", selm[t_][:], 0.0, w=["selm%d" % t_])
        self._ri = 0
        self._pt = 0
        self._ptA = 0
        gate = A["gate"]

        def epilogue(ob, ok, nh, wdt, i, g, br, dst, dkey, first):
            k = self._ri % NR
            self._ri += 1
            rk_, ck_, tk_ = "rd%d" % k, "coef%d" % k, "tmpo%d" % k
            O3 = ob[:, 0:nh * wdt].rearrange("p (h c) -> p h c", c=wdt)
            if br == "swa":
                self.tt("dve", rd[k][:, 0:nh], O3[:, :, 64], esink[:, g * 2:(g + 1) * 2], ALU.add, r=["esink"], w=[ok, rk_])
            else:
                self.ts("dve", rd[k][:, 0:nh], O3[:, :, 64], 1e-30, None, ALU.max, w=[ok, rk_])
            self.recip(rd[k][:, 0:nh], rd[k][:, 0:nh], w=[rk_])
            if br == "swa":
                cf = rd[k]
                cfk = rk_
            else:
                gc = {"cmp": 0, "slc": 8, "win": 16}[br] + g * 4
                self.tt("dve", coef[k][:, 0:nh], rd[k][:, 0:nh], gate[:, i, gc:gc + 4], ALU.mult, r=[rk_, "gate"], w=[ck_])
                cf = coef[k]
                cfk = ck_
            dst3 = dst.rearrange("p (h c) -> p h c", c=64)
            if first:
                self.tt("dve", dst3, O3[:, :, 0:64], cf[:, 0:nh].unsqueeze(2).to_broadcast([128, nh, 64]), ALU.mult,
                        r=[cfk], w=[ok, dkey])
            else:
                self.tt("dve", tmpo[k][:, 0:nh, :], O3[:, :, 0:64], cf[:, 0:nh].unsqueeze(2).to_broadcast([128, nh, 64]),
                        ALU.mult, r=[cfk], w=[ok, tk_])
                self.tt("pool", dst3, dst3, tmpo[k][:, 0:nh, :], ALU.add, r=[tk_], w=[dkey])
            return k

        def cmp_tile(i, g, yat, yak):
            M = 128
            rows = slice(g * 64, (g + 1) * 64)
            b = self._pt % 2
            self._pt += 1
            sb_, sk = self.rot(0, 6)
            self.mm(sb_[0:M, :], kcT[rows, 0:M], A["qTA"][rows, :, i * 128:(i + 1) * 128], True, False, r=["kcT", "qTA"], w=[sk])
            self.mm(sb_[0:M, :], identb[0:M, 0:M], c["penc"][0:M, i * 128:(i + 1) * 128].unsqueeze(1).to_broadcast([M, 4, 128]),
                    False, True, r=["identb", "penc"], w=[sk])
            pk_ = "PTc%d" % b
            self.act(PTc[b][0:M, :], sb_[0:M, :], AF.Exp, scale=0.125, w=[sk, pk_])
            cst = getattr(self, "cstage", 9)
            if cst < 2:
                return
            ob, ok = self.rot(6, 8)
            for hh in range(4):
                self.mm(ob[:, hh * 100:hh * 100 + 100], PTc[b][0:M, hh * 128:(hh + 1) * 128], vcaug[0:M, g, 0:100], True, True,
                        r=[pk_, "vcaug"], w=[ok])
            if cst < 3:
                return
            k = epilogue(ob, ok, 4, 100, i, g, "cmp", yat[:, g * 256:(g + 1) * 256], yak, True)
            if cst < 4:
                return
            O3 = ob[:, 0:400].rearrange("p (h c) -> p h c", c=100)
            tb = g
            self.tt("dve", tmpi[tb][:], O3[:, :, 65:97], rd[k][:, 0:4].unsqueeze(2).to_broadcast([128, 4, 32]), ALU.mult,
                    r=["rd%d" % k], w=[ok, "tmpi%d" % tb])
            self.red(imp[tb][:], tmpi[tb][:].rearrange("p h j -> p j h"), ALU.add, r=["tmpi%d" % tb], w=["imp%d" % tb])
            if cst < 5:
                return
            sc_, sk_ = score[tb], "score%d" % tb
            self.tt("dve", sc_[:], imp[tb][:], c["Mi"][:, i * 32:(i + 1) * 32], ALU.mult, r=["imp%d" % tb, "Mi"], w=[sk_])
            self.tt("dve", sc_[:], sc_[:], c["Bc"][:, i * 32:(i + 1) * 32], ALU.add, r=["Bc"], w=[sk_])
            mk, wk_, lk = "m8%d" % tb, "scw%d" % tb, "selm%d" % (i % 2)
            sm_ = selm[i % 2]
            self.P.add("dve", lambda e: e.max(out=m8[tb][:, 0:8], in_=sc_[:]), r=[sk_], w=[mk])
            self.P.add("dve", lambda e: e.match_replace(out=scw[tb][:], in_to_replace=m8[tb][:, 0:8], in_values=sc_[:],
                                                        imm_value=-1e30), r=[sk_, mk], w=[wk_])
            self.P.add("dve", lambda e: e.max(out=m8[tb][:, 8:16], in_=scw[tb][:]), r=[wk_], w=[mk])
            self.ts("dve", sm_[:, g * 64:g * 64 + 32], sc_[:], m8[tb][:, 15:16], -1.0, ALU.is_ge, ALU.add, r=[sk_, mk], w=[lk])
            if cst < 6 or g == 0:
                return
            xb, xk = self.rot(7, 8)
            self.tr(xb[0:96, 0:128], sm_[:, :], ident[:, :], r=[lk, "identf"], w=[xk])
            self.cp("act", selT2[0:96, i * 128:(i + 1) * 128], xb[0:96, 0:128], w=[xk, "selT2"])

        def attn_A2(i, br, dsts):
            if br == "slc":
                kT, kk_, V, vk, q, qk, nh, js = A["kTs"], "kTs", A["Vs"], "Vs", A["qTA"], "qTA", 4, list(range(0, i + 1))
            elif br == "win":
                kT, kk_, V, vk, q, qk, nh, js = A["kTw"], "kTw", A["Vw"], "Vw", A["qTA"], "qTA", 4, list(range(max(0, i - 4), i + 1))
            else:
                kT, kk_, V, vk, q, qk, nh, js = A["kTb"], "kTb", A["Vb"], "Vb", A["qTB"], "qTB", 2, list(range(max(0, i - 1), i + 1))
            N = nh * 128
            qs = slice(i * 128, (i + 1) * 128)
            pts = []
            for g in range(2):
                b = self._ptA % NPT
                self._ptA += 1
                pts.append((PT[b], "PT%d" % b))
            for idx, j in enumerate(js):
                banks = [self.rot(0, 6) for _ in range(2)]
                pens = [[], []]
                for g in range(2):
                    if br == "slc" and j < i and i >= 8:
                        er = slice(g * 64, g * 64 + 32)
                        pens[g].append((c["Eall"][er, j * 128:(j + 1) * 128],
                                        selT2[er, qs].unsqueeze(1).to_broadcast([32, nh, 128]), ["Eall", "selT2"]))
                    if j == i:
                        pens[g].append((identb[:, :], c["triA"][:, :].unsqueeze(1).to_broadcast([128, nh, 128]), ["identb", "triA"]))
                    if (br == "win" and j == i - 4) or (br == "swa" and j == i - 1):
                        pens[g].append((identb[:, :], c["triB"][:, :].unsqueeze(1).to_broadcast([128, nh, 128]), ["identb", "triB"]))
                for g in range(2):
                    rows = slice(g * 64, (g + 1) * 64)
                    sb_, sk = banks[g]
                    self.mm(sb_[:, 0:N], kT[rows, j * 128:(j + 1) * 128], q[rows, 0:nh, qs], True, len(pens[g]) == 0,
                            r=[kk_, qk], w=[sk])
                for g in range(2):
                    sb_, sk = banks[g]
                    for pi, (l_, r_, keys) in enumerate(pens[g]):
                        self.mm(sb_[:, 0:N], l_, r_, False, pi == len(pens[g]) - 1, r=keys, w=[sk])
                for g in range(2):
                    sb_, sk = banks[g]
                    self.act(pts[g][0][:, idx, 0:N], sb_[:, 0:N], AF.Exp, scale=0.125, w=[sk, pts[g][1]])
            return [dict(i=i, g=g, br=br, dst=dsts[g][0], dkey=dsts[g][1], pt=pts[g][0], ptk=pts[g][1], V=V, vk=vk, nh=nh, js=js, after=None)
                    for g in range(2)]

        def attn_B(S_):
            i, g, br, nh, js, pt, ptk, V, vk = S_["i"], S_["g"], S_["br"], S_["nh"], S_["js"], S_["pt"], S_["ptk"], S_["V"], S_["vk"]
            ob, ok = self.rot(6, 8)
            for hh in range(nh):
                for idx, j in enumerate(js):
                    self.mm(ob[:, hh * 68:hh * 68 + 68], pt[:, idx, hh * 128:(hh + 1) * 128], V[:, j, g, 0:68],
                            idx == 0, idx == len(js) - 1, r=[ptk, vk], w=[ok])
            epilogue(ob, ok, nh, 68, i, g, br, S_["dst"], S_["dkey"], br == "swa")
            if S_["after"] is not None:
                S_["after"]()

        def mk_after(i, b, yak, ybk):
            def f():
                self.dma("pool", d["Y"][i * 128:(i + 1) * 128, 0:512], ya[b][:], r=[yak], w=[("Y", i, 0)])
                self.dma("pool", d["Y"][i * 128:(i + 1) * 128, 512:768], yb[b][:], r=[ybk], w=[("Y", i, 1)])
            return f

        pending = []
        for i in range(NT):
            b = i % 2
            yak, ybk = "ya%d" % b, "yb%d" % b
            for g in range(2):
                cmp_tile(i, g, ya[b], yak)
            for br in ("slc", "win", "swa"):
                if br == "swa":
                    dsts = [(yb[b][:, g * 128:(g + 1) * 128], ybk) for g in range(2)]
                else:
                    dsts = [(ya[b][:, g * 256:(g + 1) * 256], yak) for g in range(2)]
                sts = attn_A2(i, br, dsts)
                if br == "swa":
                    sts[1]["after"] = mk_after(i, b, yak, ybk)
                while pending:
                    attn_B(pending.pop(0))
                pending.extend(sts)
        while pending:
            attn_B(pending.pop(0))

    def phase_rwkv(self, st, l):
        c, d = self.c, self.d
        L = lambda k: d["%s%d" % (k, l)]
        F = d["F"]
        identf, bones, hsel = c["identf"], c["bones"], c["hsel"]
        rv = self.sbt(st, "rv", [128, 20])
        self.dma("sp", rv[:], L("rv"), w=["rv"])
        omka = self.sbt(st, "omka", [128, 2])
        self.ts("dve", omka[:], rv[:, 13:15], -1.0, 1.0, ALU.mult, ALU.add, r=["rv"], w=["omka"])
        lnw = self.sbt(st, "lnw", [128, 256])
        lnb = self.sbt(st, "lnb", [128, 256])
        self.dma("sp", lnw[:], L("lnw"), w=["lnw"])
        self.dma("sp", lnb[:], L("lnb"), w=["lnb"])
        wab = self.sbt(st, "wab", [128, 256], BF16)
        thad = self.sbt(st, "thad", [128, T], BF16)
        if l > 0:
            v1b = self.sbt(st, "v1b", [128, 2, 32], BF16)
            v2b = self.sbt(st, "v2b", [32, 256], BF16)
            t1 = self.sbt(st, "t1", [32, T], BF16)

        def load_shift(dst, dkey, cidx, f, fp, fk, fpk):
            self.dma("sp", f[:, :], F[cidx], r=[("F", cidx)], w=[fk])
            self.memset("pool", fp[:, 0:1], 0.0, w=[fpk])
            self.dma("sp", fp[:, 1:T], F[cidx][:, 0:T - 1], r=[("F", cidx)], w=[fpk])
            self.tt("pool", fp[:, :], fp[:, :], f[:, :], ALU.subtract, r=[fk], w=[fpk])
            self.stt("dve", dst, fp[:, :], rv[:, cidx:cidx + 1], f[:, :], ALU.mult, ALU.add, r=[fpk, fk, "rv"], w=[dkey])

        with ExitStack() as sp:
            f = self.sbt(sp, "f", [128, T])
            fp = self.sbt(sp, "fp", [128, T])
            wdx = self.sbt(sp, "wdx", [128, T])
            wf = self.sbt(sp, "wf", [128, 256])
            self.dma("sp", wf[:], L("wa"), w=["wf"])
            self.cp("dve", wab[:], wf[:], r=["wf"], w=["wab"])
            load_shift(wdx[:, :], "wdx", 6, f, fp, "f", "fp")
            self.act(thad[0:64, :], wdx[0:64, :], AF.Tanh, r=["wdx"], w=["thad"])
            self.cp("dve", thad[64:128, :], wdx[64:128, :], r=["wdx"], w=["thad"])
            if l > 0:
                v1f = self.sbt(sp, "v1f", [128, 2, 32])
                v2f = self.sbt(sp, "v2f", [32, 256])
                vxb = self.sbt(sp, "vxb", [128, T], BF16)
                self.dma("sp", v1f[:], L("v1"), w=["v1f"])
                self.dma("sp", v2f[:], L("v2"), w=["v2f"])
                self.cp("dve", v1b[:], v1f[:], r=["v1f"], w=["v1b"])
                self.cp("dve", v2b[:], v2f[:], r=["v2f"], w=["v2b"])
                banks = [self.rot(0, 8) for _ in range(4)]
                for p in range(2):
                    load_shift(wdx[:, :], "wdx", 4 + p, f, fp, "f", "fp")
                    self.cp("dve", vxb[:], wdx[:], r=["wdx"], w=["vxb"])
                    for tc in range(4):
                        pb, pk = banks[tc]
                        self.mm(pb[0:32, :], v1b[:, p, :], vxb[:, tc * 512:(tc + 1) * 512], p == 0, p == 1, r=["v1b", "vxb"], w=[pk])
                for tc in range(4):
                    pb, pk = banks[tc]
                    self.cp("act", t1[0:32, tc * 512:(tc + 1) * 512], pb[0:32, :], w=[pk, "t1"])
            self.P.fence()

        v2d = lambda t: t[:].rearrange("p n t -> p (n t)")
        for p in range(2):
            ps_ = slice(p * 128, (p + 1) * 128)
            with ExitStack() as sP:
                Rs = self.sbt(sP, "Rs", [128, T])
                AhT = self.sbt(sP, "AhT", [128, T])
                rkb = self.sbt(sP, "rkb", [128, T])
                ArbT = [self.sbt(sP, "ArbT", [128, 16, 128]) for _ in range(2)]
                UV = self.sbt(sP, "UV", [128, 16, 128])
                YV = self.sbt(sP, "YV", [128, 16, 128])
                KVbd = self.sbt(sP, "KVbd", [128, 16, 128])
                tokB0 = self.sbt(sP, "tokB0", [128, 16, 128])
                tokB1 = self.sbt(sP, "tokB1", [128, 16, 128])
                tokV = self.sbt(sP, "tokV", [128, 16, 128])
                ycst = self.sbt(sP, "ycst", [128, 16, 128])
                Gbd = self.sbt(sP, "Gbd", [128, 128])
                gam = self.sbt(sP, "gam", [128, 16])
                sM = ExitStack()
                As = self.sbt(sM, "As", [128, T])
                Ks = self.sbt(sM, "Ks", [128, T])
                Bs = self.sbt(sM, "Bs", [128, T])
                tokA = self.sbt(sM, "tokA", [128, 16, 128])
                tokK = self.sbt(sM, "tokK", [128, 16, 128])
                pkk = lambda n: ("Pk", n // 4)
                Vt, kVt = AhT[:, :], "AhT"
                lw, klw = v2d(ArbT[0]), "ArbT0"
                aT, kaT = v2d(ArbT[1]), "ArbT1"
                kk, kkk = v2d(UV), "UV"
                e1, ke1 = v2d(YV), "YV"
                e2, ke2 = v2d(KVbd), "KVbd"
                e3, ke3 = v2d(ycst), "ycst"
                cl = [(v2d(tokB0), "tokB0"), (v2d(tokV), "tokV")]
                HF = 1024
                H = lambda ap, hf: ap[:, hf * HF:(hf + 1) * HF]
                K2 = lambda k, hf: (k, hf)
                KB2 = lambda k: [(k, 0), (k, 1)]
                cl0, kcl0 = cl[0]
                cl1, kcl1 = cl[1]

                def load_shift2(dst, dkey, cidx, f, fk, fp, fpk):
                    self.dma("sp", f[:, 0:HF], F[cidx][:, 0:HF], r=[("F", cidx)], w=[K2(fk, 0)])
                    self.dma("sp", f[:, HF:T], F[cidx][:, HF:T], r=[("F", cidx)], w=[K2(fk, 1)])
                    self.ts("pool", fp[:, 0:1], f[:, 0:1], -1.0, None, ALU.mult, r=[K2(fk, 0)], w=[K2(fpk, 0)])
                    self.tt("pool", fp[:, 1:HF], f[:, 0:HF - 1], f[:, 1:HF], ALU.subtract, r=[K2(fk, 0)], w=[K2(fpk, 0)])
                    self.tt("pool", fp[:, HF:T], f[:, HF - 1:T - 1], f[:, HF:T], ALU.subtract, r=KB2(fk), w=[K2(fpk, 1)])
                    for hf in range(2):
                        self.stt("dve", H(dst, hf), H(fp, hf), rv[:, cidx:cidx + 1], H(f, hf), ALU.mult, ALU.add,
                                 r=[K2(fpk, hf), K2(fk, hf), "rv"], w=[K2(dkey, hf)])

                load_shift2(Rs[:, :], "Rs", p, e2, ke2, e3, ke3)
                load_shift2(Ks[:, :], "Ks", 2 + p, kk, kkk, e1, ke1)
                load_shift2(Vt, kVt, 4 + p, cl0, kcl0, cl1, kcl1)
                Rsa, Ksa, Bsa, Asa, rkba = Rs[:, :], Ks[:, :], Bs[:, :], As[:, :], rkb[:, :]
                steps = []

                def st_sig(hf):
                    for tc in (2 * hf, 2 * hf + 1):
                        ts_ = slice(tc * 512, (tc + 1) * 512)
                        pb, pk = self.rot(0, 8)
                        self.mm(pb[:, :], wab[0:64, ps_], thad[0:64, ts_], True, True, r=["wab", "thad"], w=[pk])
                        self.act(lw[:, ts_], pb[:, :], AF.Sigmoid, bias=rv[:, 7 + p:8 + p], r=["rv"], w=[pk, K2(klw, hf)])
                        pb, pk = self.rot(0, 8)
                        self.mm(pb[:, :], wab[64:128, ps_], thad[64:128, ts_], True, True, r=["wab", "thad"], w=[pk])
                        self.act(aT[:, ts_], pb[:, :], AF.Sigmoid, bias=rv[:, 9 + p:10 + p], r=["rv"], w=[pk, K2(kaT, hf)])
                        if l > 0:
                            pb, pk = self.rot(0, 8)
                            self.mm(pb[:, :], v2b[0:32, ps_], t1[0:32, ts_], True, True, r=["v2b", "t1"], w=[pk])
                            self.act(e1[:, ts_], pb[:, :], AF.Sigmoid, bias=rv[:, 17 + p:18 + p], r=["rv"], w=[pk, K2(ke1, hf)])
                steps.append(st_sig)
                if l > 0:
                    self.dma("sp", e2, d["vf"][p], r=[("vf", p)], w=KB2(ke2))
                    steps.append(lambda hf: self.tt("pool", H(e2, hf), H(e2, hf), H(Vt, hf), ALU.subtract, r=[K2(kVt, hf)], w=[K2(ke2, hf)]))
                    steps.append(lambda hf: self.tt("dve", H(e2, hf), H(e2, hf), H(e1, hf), ALU.mult, r=[K2(ke1, hf)], w=[K2(ke2, hf)]))
                    steps.append(lambda hf: self.tt("pool", H(Vt, hf), H(Vt, hf), H(e2, hf), ALU.add, r=[K2(ke2, hf)], w=[K2(kVt, hf)]))
                else:
                    self.dma("pool", d["vf"][p], Vt, r=KB2(kVt), w=[("vf", p)])
                steps.append(lambda hf: self.act(H(kk, hf), H(Ksa, hf), AF.Copy, scale=rv[:, 11 + p:12 + p], r=[K2("Ks", hf), "rv"], w=[K2(kkk, hf)]))
                steps.append(lambda hf: self.act(H(e1, hf), H(kk, hf), AF.Square, r=[K2(kkk, hf)], w=[K2(ke1, hf)]))

                def st_norm(hf):
                    for tc in (2 * hf, 2 * hf + 1):
                        ts_ = slice(tc * 512, (tc + 1) * 512)
                        pb, pk = self.rot(0, 8)
                        self.mm(pb[:, :], bones[:, :], e1[:, ts_], True, True, r=["bones", K2(ke1, hf)], w=[pk])
                        self.act(e2[:, ts_], pb[:, :], AF.Sqrt, w=[pk, K2(ke2, hf)])
                steps.append(st_norm)
                steps.append(lambda hf: self.ts("dve", H(e2, hf), H(e2, hf), 1e-12, None, ALU.max, w=[K2(ke2, hf)]))
                steps.append(lambda hf: self.recip(H(e2, hf), H(e2, hf), w=[K2(ke2, hf)]))
                steps.append(lambda hf: self.tt("dve", H(kk, hf), H(kk, hf), H(e2, hf), ALU.mult, r=[K2(ke2, hf)], w=[K2(kkk, hf)]))
                steps.append(lambda hf: self.act(H(e1, hf), H(aT, hf), AF.Identity, bias=omka[:, p:p + 1], scale=rv[:, 13 + p:14 + p],
                                                 r=[K2(kaT, hf), "rv", "omka"], w=[K2(ke1, hf)]))
                steps.append(lambda hf: self.tt("pool", H(Ksa, hf), H(Ksa, hf), H(e1, hf), ALU.mult, r=[K2(ke1, hf)], w=[K2("Ks", hf)]))
                steps.append(lambda hf: self.stt("dve", H(rkba, hf), H(Rsa, hf), rv[:, 15 + p:16 + p], H(Ksa, hf), ALU.mult, ALU.mult,
                                                 r=[K2("Rs", hf), K2("Ks", hf), "rv"], w=[K2("rkb", hf)]))
                steps.append(lambda hf: self.tt("pool", H(Bsa, hf), H(kk, hf), H(aT, hf), ALU.mult, r=[K2(kkk, hf), K2(kaT, hf)], w=[K2("Bs", hf)]))
                chain_src = [(lw, klw)]
                for si, sh in enumerate((1, 2, 4, 8, 16, 32, 64)):
                    dst, dkey = cl[si % 2]
                    src, skey = chain_src[-1]

                    def st_scan(hf, src=src, skey=skey, dst=dst, dkey=dkey, sh=sh):
                        s3 = H(src, hf).rearrange("p (n t) -> p n t", t=128)
                        d3 = H(dst, hf).rearrange("p (n t) -> p n t", t=128)
                        self.cp("act", d3[:, :, 0:sh], s3[:, :, 0:sh], r=[K2(skey, hf)], w=[K2(dkey, hf)])
                        self.tt("dve", d3[:, :, sh:128], s3[:, :, sh:128], s3[:, :, 0:128 - sh], ALU.add, r=[K2(skey, hf)], w=[K2(dkey, hf)])
                    steps.append(st_scan)
                    chain_src.append((dst, dkey))
                csrc, cskey = chain_src[-1]
                CW = -float(np.exp(-0.5))
                steps.append(lambda hf: self.act(H(e1, hf), H(csrc, hf), AF.Exp, scale=CW, r=[K2(cskey, hf)], w=[K2(ke1, hf)]))
                steps.append(lambda hf: self.act(H(e2, hf), H(csrc, hf), AF.Exp, scale=-CW, r=[K2(cskey, hf)], w=[K2(ke2, hf)]))
                steps.append(lambda hf: self.tt("pool", H(e3, hf), H(csrc, hf), H(lw, hf), ALU.subtract, r=[K2(cskey, hf), K2(klw, hf)], w=[K2(ke3, hf)]))
                steps.append(lambda hf: self.act(H(e3, hf), H(e3, hf), AF.Exp, scale=CW, w=[K2(ke3, hf)]))
                steps.append(lambda hf: self.cp("act", gam[:, hf * 8:(hf + 1) * 8], H(e1, hf).rearrange("p (n t) -> p n t", t=128)[:, :, 127],
                                                r=[K2(ke1, hf)], w=[K2("gam", hf)]))
                steps.append(lambda hf: self.tt("dve", H(Rsa, hf), H(Rsa, hf), H(e1, hf), ALU.mult, r=[K2(ke1, hf)], w=[K2("Rs", hf)]))
                steps.append(lambda hf: self.tt("pool", H(Ksa, hf), H(Ksa, hf), H(e2, hf), ALU.mult, r=[K2(ke2, hf)], w=[K2("Ks", hf)]))
                steps.append(lambda hf: self.tt("pool", H(Bsa, hf), H(Bsa, hf), H(e2, hf), ALU.mult, r=[K2(ke2, hf)], w=[K2("Bs", hf)]))
                steps.append(lambda hf: self.stt("dve", H(Asa, hf), H(kk, hf), -1.0, H(e3, hf), ALU.mult, ALU.mult, r=[K2(kkk, hf), K2(ke3, hf)], w=[K2("As", hf)]))
                for stp in steps:
                    for hf in range(2):
                        stp(hf)
                self.memset("pool", tokB0[:], 0.0, w=KB2("tokB0"))
                self.memset("pool", tokB1[:], 0.0, w=["tokB1"])
                for n in range(16):
                    ch = slice(n * 128, (n + 1) * 128)
                    hf = n // 8
                    pb, pk = self.rot(0, 8)
                    self.tr(pb[:, 0:128], Ks[:, ch], identf[:, :], r=[K2("Ks", hf), "identf"], w=[pk])
                    self.tr(pb[:, 128:256], As[:, ch], identf[:, :], r=[K2("As", hf), "identf"], w=[pk])
                    self.tr(pb[:, 256:384], Bs[:, ch], identf[:, :], r=[K2("Bs", hf), "identf"], w=[pk])
                    self.tr(pb[:, 384:512], Vt[:, ch], identf[:, :], r=[K2(kVt, hf), "identf"], w=[pk])
                    self.cp("act", tokK[:, n, :], pb[:, 0:128], w=[pk, "tokK"])
                    self.cp("act", tokA[:, n, :], pb[:, 128:256], w=[pk, "tokA"])
                    self.cp("dve", tokB0[:, n, 0:64], pb[:, 256:320], w=[pk, K2("tokB0", hf)])
                    self.cp("dve", tokB1[:, n, 64:128], pb[:, 320:384], w=[pk, "tokB1"])
                    self.cp("dve", tokV[:, n, :], pb[:, 384:512], w=[pk, K2("tokV", hf)])
                self.P.fence()
                rstage = getattr(self, "rstage", 9)
                if rstage < 3:
                    sM.close()
                    continue
                self.memset("pool", KVbd[:], 0.0, w=["KVbd"])
                self.memset("pool", Gbd[:], 0.0, w=["Gbd"])
                for n0 in range(0, 16, 4):
                    pb, pk = self.rot(0, 8)
                    for q in range(4):
                        self.mm(pb[:, q * 128:(q + 1) * 128], tokK[:, n0 + q, :], tokV[:, n0 + q, :], True, True, r=["tokK", "tokV"], w=[pk])
                    p3 = pb[:].rearrange("p (q t) -> p q t", q=4)
                    for hd in range(2):
                        hs = slice(hd * 64, (hd + 1) * 64)
                        self.cp("act" if hd else "dve", KVbd[hs, n0:n0 + 4, hs], p3[hs, :, hs], w=[pk, "KVbd"])
                sC = ExitStack()
                XT = [ycst, tokK]
                Pk = [self.sbt(sC, "Pk", [128, 16, 128], BF16) for _ in range(2)]
                Nk = [self.sbt(sC, "Nk", [128, 16, 128], BF16) for _ in range(2)]
                XTb = [self.sbt(sC, "XTb", [128, 16, 128], BF16) for _ in range(2)]
                Lk = [self.sbt(sC, "Lk", [128, 4, 2, 128]) for _ in range(2)]
                WV = [self.sbt(sC, "WV", [128, 4, 64]) for _ in range(2)]
                pkk = lambda hd, n: ("Pk", hd, n // 4)
                nkk = lambda hd, n: ("Nk", hd, n // 4)
                xtk = lambda hd, n: ("XT", hd, n // 4)
                xbk = lambda hd, n: ("XTb", hd, n // 4)
                bc4 = lambda m: c[m][:, :].unsqueeze(1).to_broadcast([128, 4, 128])
                q4 = lambda pb: pb[:].rearrange("p (q t) -> p q t", q=4)
                HS = [slice(0, 64), slice(64, 128)]
                ev = 0
                for n0 in range(0, 16, 4):
                    bk = [[self.rot(0, 8) for _ in range(2)] for _ in range(3)]
                    for q in range(4):
                        ch = slice((n0 + q) * 128, (n0 + q + 1) * 128)
                        qs = slice(q * 128, (q + 1) * 128)
                        for which, (lh, rh, lkey, rkey) in enumerate(((As, Bs, "As", "Bs"), (Bs, As, "Bs", "As"), (Bs, Rs, "Bs", "Rs"))):
                            for hd in range(2):
                                hs = HS[hd]
                                self.mm(bk[which][hd][0][:, qs], lh[hs, ch], rh[hs, ch], True, True, r=[lkey, rkey], w=[bk[which][hd][1]])
                    for hd in range(2):
                        xk = [xtk(hd, n0)] + (["tokK"] if hd == 1 else [])
                        self.tt("dve", Pk[hd][:, n0:n0 + 4, :], q4(bk[0][hd][0]), bc4("mSL"), ALU.mult, r=["mSL"], w=[bk[0][hd][1], pkk(hd, n0)])
                        self.tt("dve", XT[hd][:, n0:n0 + 4, :], q4(bk[1][hd][0]), bc4("mSU"), ALU.mult, r=["mSU"], w=[bk[1][hd][1]] + xk)
                        self.tt("dve", ArbT[hd][:, n0:n0 + 4, :], q4(bk[2][hd][0]), bc4("mIU"), ALU.mult, r=["mIU"], w=[bk[2][hd][1], "ArbT%d" % hd])
                        self.cp("act", Nk[hd][:, n0:n0 + 4, :], XT[hd][:, n0:n0 + 4, :], r=[xtk(hd, n0)], w=[nkk(hd, n0)])
                        self.tt("pool", XT[hd][:, n0:n0 + 4, :], XT[hd][:, n0:n0 + 4, :], identf[:, :].unsqueeze(1).to_broadcast([128, 4, 128]),
                                ALU.add, r=["identf"], w=[xtk(hd, n0)])
                        self.cp("act", XTb[hd][:, n0:n0 + 4, :], XT[hd][:, n0:n0 + 4, :], r=[xtk(hd, n0)], w=[xbk(hd, n0)])
                G4 = list(range(0, 16, 4))
                for hd in range(2):
                    for k in range(1, 7):
                        pbanks, nbanks, xbanks = {}, {}, {}
                        for n0 in G4:
                            pbanks[n0] = self.rot(0, 8)
                            for q in range(4):
                                n = n0 + q
                                self.mm(pbanks[n0][0][:, q * 128:(q + 1) * 128], Nk[hd][:, n, :], Pk[hd][:, n, :], True, True,
                                        r=[nkk(hd, n0), pkk(hd, n0)], w=[pbanks[n0][1]])
                        if k <= 5:
                            for n0 in G4:
                                nbanks[n0] = self.rot(0, 8)
                                for q in range(4):
                                    n = n0 + q
                                    self.mm(nbanks[n0][0][:, q * 128:(q + 1) * 128], Pk[hd][:, n, :], Nk[hd][:, n, :], True, True,
                                            r=[nkk(hd, n0), pkk(hd, n0)], w=[nbanks[n0][1]])
                        for n0 in G4:
                            self.cp("act", Pk[hd][:, n0:n0 + 4, :], q4(pbanks[n0][0]), w=[pbanks[n0][1], pkk(hd, n0)])
                        if k <= 5:
                            for n0 in G4:
                                self.cp("act" if (n0 // 4) % 2 else "dve", Nk[hd][:, n0:n0 + 4, :], q4(nbanks[n0][0]), w=[nbanks[n0][1], nkk(hd, n0)])
                        for n0 in G4:
                            xbanks[n0] = self.rot(0, 8)
                            for q in range(4):
                                n = n0 + q
                                self.mm(xbanks[n0][0][:, q * 128:(q + 1) * 128], Pk[hd][:, n, :], XTb[hd][:, n, :], True, True,
                                        r=[pkk(hd, n0), xbk(hd, n0)], w=[xbanks[n0][1]])
                        for n0 in G4:
                            self.tt("dve", XT[hd][:, n0:n0 + 4, :], q4(xbanks[n0][0]), XT[hd][:, n0:n0 + 4, :], ALU.add, w=[xbanks[n0][1], xtk(hd, n0)])
                        if k < 6:
                            for n0 in G4:
                                self.cp("act" if (n0 // 4) % 2 else "pool", XTb[hd][:, n0:n0 + 4, :], XT[hd][:, n0:n0 + 4, :], r=[xtk(hd, n0)], w=[xbk(hd, n0)])
                for n0 in range(0, 16, 4):
                    bL = [self.rot(0, 8) for _ in range(2)]
                    bA = [self.rot(0, 8) for _ in range(2)]
                    for q in range(4):
                        ch = slice((n0 + q) * 128, (n0 + q + 1) * 128)
                        qs = slice(q * 128, (q + 1) * 128)
                        for hd in range(2):
                            self.mm(bL[hd][0][:, qs], Ks[HS[hd], ch], As[HS[hd], ch], True, True, r=["Ks", "As"], w=[bL[hd][1]])
                        for hd in range(2):
                            self.mm(bA[hd][0][:, qs], Ks[HS[hd], ch], Rs[HS[hd], ch], True, True, r=["Ks", "Rs"], w=[bA[hd][1]])
                    for hd in range(2):
                        self.tt("dve", Lk[hd][:, :, 0, :], q4(bL[hd][0]), bc4("mSU"), ALU.mult, r=["mSU"], w=[bL[hd][1], "Lk%d" % hd])
                        self.tt("dve", Lk[hd][:, :, 1, :], q4(bA[hd][0]), bc4("mIU"), ALU.mult, r=["mIU"], w=[bA[hd][1], "Lk%d" % hd])
                    for hd in range(2):
                        hs = HS[hd]
                        lk_, wk_ = "Lk%d" % hd, "WV%d" % hd
                        bw, kw = self.rot(0, 8)
                        bh, kh = self.rot(0, 8)
                        for q in range(4):
                            n = n0 + q
                            self.mm(bw[:, q * 64:(q + 1) * 64], Lk[hd][:, q, 0, :], tokV[:, n, hs], True, True, r=[lk_, "tokV"], w=[kw])
                            self.mm(bw[:, 256 + q * 64:256 + (q + 1) * 64], Lk[hd][:, q, 1, :], tokV[:, n, hs], True, True, r=[lk_, "tokV"], w=[kw])
                            self.mm(bh[:, q * 128:(q + 1) * 128], tokA[:, n, :], XT[hd][:, n, :], True, True, r=["tokA", xtk(hd, n0)], w=[kh])
                        self.cp("act", WV[hd][:, :, :], bw[:, 0:256].rearrange("p (q v) -> p q v", q=4), w=[kw, wk_])
                        self.cp("act", YV[:, n0:n0 + 4, hs], bw[:, 256:512].rearrange("p (q v) -> p q v", q=4), w=[kw, "YV"])
                        self.cp("dve", AhT[hs, n0 * 128:(n0 + 4) * 128], bh[hs, :], w=[kh, "AhT"])
                        bu, ku = self.rot(0, 8)
                        for q in range(4):
                            n = n0 + q
                            self.mm(bu[:, q * 64:(q + 1) * 64], XT[hd][:, n, :], WV[hd][:, q, :], True, True, r=[xtk(hd, n0), wk_], w=[ku])
                        self.cp("act", UV[:, n0:n0 + 4, hs], bu[:, 0:256].rearrange("p (q v) -> p q v", q=4), w=[ku, "UV"])
                self.tt("pool", KVbd[:], KVbd[:], gam[:, :].unsqueeze(2).to_broadcast([128, 16, 128]), ALU.mult, r=["gam"], w=["KVbd"])
                self.P.fence()
                sC.close()
                sM.close()
                cS = ExitStack()
                Usb = [self.sbt(cS, "Usb", [128, 128]) for _ in range(2)]
                Tg = [self.sbt(cS, "Tg", [128, 128]) for _ in range(2)]
                ysqA = self.sbt(cS, "ysqA", [128, 16, 128])
                smA = self.sbt(cS, "smA", [128, 64])
                bon = self.sbt(cS, "bon", [128, 16, 2])
                for n in range(16 if rstage >= 4 else 0):
                    ch = slice(n * 128, (n + 1) * 128)
                    q = n % 2
                    uk = "Usb%d" % q
                    self.stt("dve", Tg[q][:, :], Gbd[:, :], gam[:, n:n + 1], KVbd[:, n, :], ALU.mult, ALU.add, r=["Gbd", "KVbd", "gam"], w=["Tg%d" % q])
                    pbu, pku = self.rot(0, 8)
                    self.mm(pbu[:, 0:128], AhT[:, ch], Gbd[:, :], True, True, r=["AhT", "Gbd"], w=[pku])
                    self.tt("dve", Usb[q][:, :], pbu[:, 0:128], UV[:, n, :], ALU.add, r=["UV"], w=[pku, uk])
                    pby, pky = self.rot(0, 8)
                    self.mm(pby[:, 0:128], Rs[:, ch], Gbd[:, :], True, False, r=["Rs", "Gbd"], w=[pky])
                    self.mm(pby[:, 0:64], ArbT[0][:, n, :], Usb[q][:, 0:64], False, False, r=["ArbT0", uk], w=[pky])
                    self.mm(pby[:, 64:128], ArbT[1][:, n, :], Usb[q][:, 64:128], False, True, r=["ArbT1", uk], w=[pky])
                    self.mm(pby[:, 128:130], rkb[:, ch], hsel[:, 0:2], True, True, r=["rkb", "hsel"], w=[pky])
                    pbg, pkg = self.rot(0, 8)
                    self.mm(pbg[:, 0:64], tokB0[:, n, :], Usb[q][:, 0:64], True, True, r=["tokB0", uk], w=[pkg])
                    self.mm(pbg[:, 64:128], tokB1[:, n, :], Usb[q][:, 64:128], True, True, r=["tokB1", uk], w=[pkg])
                    self.stt("dve", Gbd[:, :], pbg[:, 0:128], gam[:, n:n + 1], Tg[q][:, :], ALU.mult, ALU.add, r=["gam", "Tg%d" % q], w=[pkg, "Gbd"])
                    self.tt("dve", ycst[:, n, :], pby[:, 0:128], YV[:, n, :], ALU.add, r=["YV"], w=[pky, ("ycst", n)])
                    self.cp("act", bon[:, n, :], pby[:, 128:130], w=[pky, ("bon", n)])
                if rstage >= 4:
                    yk_all = [("ycst", n) for n in range(16)]
                    y4 = ycst[:].rearrange("p n (h c) -> p (n h) c", c=64)
                    yf = ycst[:].rearrange("p n c -> p (n c)")
                    sq4 = ysqA[:].rearrange("p n (h c) -> p (n h) c", c=64)
                    bc32 = lambda t: t.unsqueeze(2).to_broadcast([128, 32, 64])
                    self.red(smA[:, 0:32], y4, ALU.add, r=yk_all, w=["smA"])
                    self.ts("dve", smA[:, 0:32], smA[:, 0:32], -1.0 / 64, None, ALU.mult, w=["smA"])
                    self.tt("dve", y4, y4, bc32(smA[:, 0:32]), ALU.add, r=["smA"], w=yk_all)
                    self.tt("pool", ysqA[:], ycst[:], ycst[:], ALU.mult, r=yk_all, w=["ysqA"])
                    self.red(smA[:, 32:64], sq4, ALU.add, r=["ysqA"], w=["smA"])
                    self.ts("dve", smA[:, 32:64], smA[:, 32:64], 1.0 / 64, 64e-5, ALU.mult, ALU.add, w=["smA"])
                    self.act(smA[:, 32:64], smA[:, 32:64], AF.Sqrt, w=["smA"])
                    self.recip(smA[:, 32:64], smA[:, 32:64], w=["smA"])
                    self.tt("dve", y4, y4, bc32(smA[:, 32:64]), ALU.mult, r=["smA"], w=yk_all)
                    self.tt("pool", ycst[:], ycst[:], lnw[:, ps_].unsqueeze(1).to_broadcast([128, 16, 128]), ALU.mult, r=["lnw"], w=yk_all)
                    self.tt("pool", ycst[:], ycst[:], lnb[:, ps_].unsqueeze(1).to_broadcast([128, 16, 128]), ALU.add, r=["lnb"], w=yk_all)
                    self.tt("dve", sq4, tokV[:].rearrange("p n (h c) -> p (n h) c", c=64),
                            bc32(bon[:].rearrange("p n h -> p (n h)")), ALU.mult, r=["tokV"] + [("bon", n) for n in range(16)], w=["ysqA"])
                    self.tt("pool", ycst[:], ycst[:], ysqA[:], ALU.add, r=["ysqA"], w=yk_all)
                    for n in range(16):
                        self.dma("pool", d["Y"][n * 128:(n + 1) * 128, 768 + p * 128:768 + (p + 1) * 128], ycst[:, n, :],
                                 r=[("ycst", n)], w=[("Y", "c", p, n)])
                self.P.fence()
                cS.close()

    def phase_out(self, st, l, xin, xkey, xo, xokey):
        c, d = self.c, self.d
        L = lambda k: d["%s%d" % (k, l)]
        identf = c["identf"]
        final = (l == DEPTH - 1)
        pwb = self.sbt(st, "pwb", [128, 16, 1024], BF16)
        pst = [self.sbt(st, "pst", [128, 2, 1024]) for _ in range(2)]
        Yt = [self.sbt(st, "Yt", [128, 1024]) for _ in range(2)]
        Zt = [self.sbt(st, "Zt", [128, 1024]) for _ in range(2)]
        Gt = [self.sbt(st, "Gt", [128, 3072]) for _ in range(2)]
        xt = [self.sbt(st, "xt", [128, 1024]) for _ in range(2)]
        yzT = [self.sbt(st, "yzT", [128, 8, 128], BF16) for _ in range(2)]
        mixed = [self.sbt(st, "mixed", [128, 1024]) for _ in range(2)]
        mixb = [self.sbt(st, "mixb", [128, 1024], BF16) for _ in range(2)]
        yzb = [self.sbt(st, "yzb", [128, 1024], BF16) for _ in range(2)]
        mxT = [self.sbt(st, "mxT", [128, 8, 128], BF16) for _ in range(2)]
        tmpa = [self.sbt(st, "tmpa", [128, 512]) for _ in range(2)]
        xn = [self.sbt(st, "xn", [128, 1024]) for _ in range(2)]
        jk = self.sbt(st, "jkb", [128, 1024], BF16)
        ss = [self.sbt(st, "ss", [128, 1]) for _ in range(2)]
        self._ta = 0

        def stage_A(i):
            b = i % 2
            rs = slice(i * 128, (i + 1) * 128)
            ky, kz = "Yt%d" % b, "Zt%d" % b
            self.dma("sp", Yt[b][:], d["Y"][rs, :], r=[("Y", i, 0), ("Y", i, 1)] + [("Y", "c", pp, i) for pp in range(2)], w=[ky])
            self.dma("sp", Zt[b][:], d["Z"][rs, :], r=[("Z", i, 0), ("Z", i, 1)], w=[kz])
            kyb = "yzb%d" % b
            self.tt("pool", yzb[b][:], Yt[b][:], Zt[b][:], ALU.mult, r=[kz, ky], w=[kyb])
            pb, pk = self.rot(0, 8)
            pbb = pb[:].bitcast(BF16)
            for ch in range(8):
                self.tr(pbb[:, ch * 128:(ch + 1) * 128], yzb[b][:, ch * 128:(ch + 1) * 128], c["identb"][:], r=[kyb, "identb"], w=[pk])
            self.cp("act", yzT[b][:, :, :], pbb.rearrange("p (c t) -> p c t", c=8), w=[pk, "yzT%d" % b])

        def stage_B(i):
            b = i % 2
            rs = slice(i * 128, (i + 1) * 128)
            kg = "Gt%d" % b
            self.dma("sp", Gt[b][:], d["G"][rs, :], r=[("G", i, m) for m in range(6)], w=[kg])
            for half in range(2):
                cs = slice(half * 512, (half + 1) * 512)
                for bi, (k0, k1) in enumerate(((0, 4), (4, 6), (6, 8))):
                    pb, pk = self.rot(0, 8)
                    for kc in range(k0, k1):
                        self.mm(pb[:, :], yzT[b][:, kc, :], pwb[:, kc, cs], kc == k0, kc == k1 - 1, r=["yzT%d" % b, "pwb"], w=[pk])
                    gsl = Gt[b][:, bi * 1024 + half * 512:bi * 1024 + (half + 1) * 512]
                    if bi == 0:
                        self.tt("dve", mixed[b][:, cs], pb[:, :], gsl, ALU.mult, r=[kg], w=[pk, "mixed%d" % b])
                    else:
                        t_ = self._ta % 2
                        self._ta += 1
                        self.tt("dve", tmpa[t_][:], pb[:, :], gsl, ALU.mult, r=[kg], w=[pk, "tmpa%d" % t_])
                        if bi == 1:
                            self.tt("pool", mixed[b][:, cs], mixed[b][:, cs], tmpa[t_][:], ALU.add, r=["tmpa%d" % t_], w=["mixed%d" % b])
                        else:
                            self.tt("pool", mixb[b][:, cs], mixed[b][:, cs], tmpa[t_][:], ALU.add, r=["tmpa%d" % t_, "mixed%d" % b], w=["mixb%d" % b])

        def stage_CD(i):
            b = i % 2
            rs = slice(i * 128, (i + 1) * 128)
            kx = "xt%d" % b
            self.dma("sp", xt[b][:], xin[rs, :], r=[(xkey, i)], w=[kx])
            pb, pk = self.rot(0, 8)
            pbb = pb[:].bitcast(BF16)
            for ch in range(8):
                self.tr(pbb[:, ch * 128:(ch + 1) * 128], mixb[b][:, ch * 128:(ch + 1) * 128], c["identb"][:], r=["mixb%d" % b, "identb"], w=[pk])
            self.cp("act", mxT[b][:, :, :], pbb.rearrange("p (c t) -> p c t", c=8), w=[pk, "mxT%d" % b])
            for half in range(2):
                cs = slice(half * 512, (half + 1) * 512)
                pb, pk = self.rot(0, 8)
                for kc in range(8):
                    self.mm(pb[:, :], mxT[b][:, kc, :], pwb[:, 8 + kc, cs], kc == 0, kc == 7, r=["mxT%d" % b, "pwb"], w=[pk])
                self.tt("dve", xn[b][:, cs], pb[:, :], xt[b][:, cs], ALU.add, r=[kx], w=[pk, "xn%d" % b])
            if final:
                kss = "ss%d" % b
                self.memset("pool", ss[b][:], 0.0, w=[kss])
                self.act(jk[:], xn[b][:], AF.Square, accum=ss[b][:], r=["xn%d" % b], w=["jkb", kss])
                self.ts("dve", ss[b][:], ss[b][:], 1.0 / D, 1e-6, ALU.mult, ALU.add, w=[kss])
                self.act(ss[b][:], ss[b][:], AF.Sqrt, w=[kss])
                self.recip(ss[b][:], ss[b][:], w=[kss])
                self.stt("dve", xn[b][:], xn[b][:], ss[b][:, 0:1], c["fg"][:], ALU.mult, ALU.mult, r=[kss, "fg"], w=["xn%d" % b])
            self.dma("pool", xo[rs, :], xn[b][:], r=["xn%d" % b], w=[(xokey, i)])

        stage_A(0)
        stage_A(1)
        for q in range(8):
            b = q % 2
            self.dma("sp", pst[b][:], L("pw")[:, 2 * q:2 * q + 2, :], w=["pst%d" % b])
            self.cp("dve", pwb[:, 2 * q, :], pst[b][:, 0, :], r=["pst%d" % b], w=["pwb"])
            self.cp("act" if q % 2 else "pool", pwb[:, 2 * q + 1, :], pst[b][:, 1, :], r=["pst%d" % b], w=["pwb"])
        stage_B(0)
        for i in range(NT):
            if i + 2 < NT:
                stage_A(i + 2)
            if i + 1 < NT:
                stage_B(i + 1)
            stage_CD(i)


FUSED = True
_CACHE = {}


def _get_nc(layers, debug=False):
    key = (tuple(layers), debug)
    if key not in _CACHE:
        kb = KB(layers, debug)
        _CACHE[key] = kb.build()
    return _CACHE[key]


def _run(layers, inp, xs, vfs, debug=False):
    nc = _get_nc(layers, debug)
    perm = _perm()
    consts = _consts()
    base = dict(consts)
    base["fg"] = np.ascontiguousarray(np.broadcast_to(np.asarray(inp["final_g"], np.float32).reshape(1, D), (128, D)))
    for l in layers:
        for k, v in prep_layer(inp, l, perm).items():
            base["%s%d" % (k, l)] = v
    maps = []
    for b in range(8):
        m = dict(base)
        m["x"] = np.ascontiguousarray(xs[b], dtype=np.float32)
        if vfs is not None:
            m["vfirst"] = np.ascontiguousarray(vfs[b], dtype=np.float32)
        maps.append(m)
    res = run_bass_kernel_spmd(nc, maps, core_ids=list(range(8)))
    return res.results


def kernel(**inputs):
    inp = {k: np.asarray(v) for k, v in inputs.items()}
    x = np.asarray(inp["x"], np.float32)
    if FUSED:
        r = _run([0, 1], inp, [x[b] for b in range(8)], None)
        return np.stack([r[b]["xout"] for b in range(8)], 0).astype(np.float32)
    r0 = _run([0], inp, [x[b] for b in range(8)], None)
    r1 = _run([1], inp, [r0[b]["xout"] for b in range(8)], [r0[b]["vfirst"] for b in range(8)])
    return np.stack([r1[b]["xout"] for b in range(8)], 0).astype(np.float32)
```

```python
from contextlib import ExitStack
import numpy as np
import ml_dtypes
import concourse.bass as bass
import concourse.mybir as mybir
from concourse.bass_utils import run_bass_kernel_spmd

F32 = mybir.dt.float32
BF16 = mybir.dt.bfloat16
AF = mybir.ActivationFunctionType
ALU = mybir.AluOpType
AX = mybir.AxisListType

ENGINES = ("sp", "act", "dve", "pool", "pe")
INORDER_ENGINES = ("pe", "sp")
NDMASEM = 12


class _Op:
    __slots__ = ("eng", "fn", "deps", "dma", "sig", "sem", "val", "prev", "idx")


class Prog:
    def __init__(self, nc):
        self.nc = nc
        self.ops = []
        self.lastw = {}
        self.readers = {}

    def add(self, eng, fn, r=(), w=(), dma=False):
        o = _Op()
        o.eng, o.fn, o.dma, o.sig = eng, fn, dma, False
        o.idx = len(self.ops)
        deps = set()
        for k in r:
            if k in self.lastw:
                deps.add(self.lastw[k])
        for k in w:
            if k in self.lastw:
                deps.add(self.lastw[k])
            deps.update(self.readers.get(k, ()))
        o.deps = deps
        for k in r:
            lst = self.readers.setdefault(k, [])
            if not dma:
                lst[:] = [i for i in lst if self.ops[i].dma or self.ops[i].eng != eng]
            lst.append(o.idx)
        for k in w:
            self.lastw[k] = o.idx
            self.readers[k] = []
        self.ops.append(o)
        return o

    def barrier_keys(self):
        return list(self.lastw.keys())

    def _skip(self, d, o):
        if d.dma:
            return False
        if d.eng == "sp":
            return True
        if d.eng == o.eng and o.eng in INORDER_ENGINES:
            return True
        return False

    def emit(self, stack):
        nc = self.nc
        ops = self.ops
        for o in ops:
            for di in o.deps:
                d = ops[di]
                if not self._skip(d, o):
                    d.sig = True
        engsem = {e: stack.enter_context(nc.semaphore("s_" + e)) for e in ENGINES if e != "sp"}
        dmasem = {e: [stack.enter_context(nc.semaphore("d_%s%d" % (e, i))) for i in range(NDMASEM)]
                  for e in ("sp", "pool", "act")}
        cnt = {e: 0 for e in ENGINES}
        dcnt = {e: 0 for e in ENGINES}
        per = {e: [] for e in ENGINES}
        for o in ops:
            per[o.eng].append(o)
            if o.dma:
                n = dcnt[o.eng]
                dcnt[o.eng] += 1
                o.sem = dmasem[o.eng][n % NDMASEM]
                o.val = 16 * (n // NDMASEM + 1)
                o.prev = 16 * (n // NDMASEM)
            elif o.sig:
                cnt[o.eng] += 1
                o.sem = engsem[o.eng]
                o.val = cnt[o.eng]
        self.stats = dict(cnt=cnt, dcnt=dcnt, n={e: len(per[e]) for e in ENGINES})

        def run(e, eng):
            waited = {}
            nw = 0
            for o in per[eng]:
                for di in sorted(o.deps):
                    d = ops[di]
                    if self._skip(d, o):
                        continue
                    key = id(d.sem)
                    if waited.get(key, 0) >= d.val:
                        continue
                    e.wait_ge(d.sem, d.val)
                    nw += 1
                    waited[key] = d.val
                if o.dma and o.prev > 0 and waited.get(id(o.sem), 0) < o.prev:
                    e.wait_ge(o.sem, o.prev)
                    waited[id(o.sem)] = o.prev
                ins = o.fn(e)
                if o.dma:
                    ins.then_inc(o.sem, 16)
                elif o.sig:
                    ins.then_inc(o.sem, 1)
            self.stats.setdefault("waits", {})[eng] = nw

        with nc.Block() as block:
            @block.sync
            def _(e):
                run(e, "sp")

            @block.scalar
            def _(e):
                run(e, "act")

            @block.vector
            def _(e):
                run(e, "dve")

            @block.gpsimd
            def _(e):
                run(e, "pool")

            @block.tensor
            def _(e):
                run(e, "pe")

    def fence(self):
        start = getattr(self, "_fpos", 0)
        deps = set(o.idx for o in self.ops[start:] if o.dma)
        last = {}
        for o in self.ops:
            last[o.eng] = o.idx
        deps.update(last.values())
        for e in ENGINES:
            o = self.add(e, lambda eng: eng.nop())
            o.deps = set(deps)
        self._fpos = len(self.ops)


T = 2048
D = 1024
NT = 16
N_IN = 6808
DEPTH = 2
BIG = 30000.0
NCMP = 127
SEG = dict(a_q=(0, 512), a_kv_cmp=(512, 256), a_kv_slc=(768, 256), a_kv_win=(1024, 256), a_gate=(1280, 24),
           a_z=(1304, 512), b_q=(1816, 256), b_kv=(2072, 256), b_z=(2328, 256), c_shift=(2584, 896),
           c_z=(3480, 256), merge=(3736, 3072))
NFEAT = 18 * 128
TOK0 = NFEAT


def _perm():
    def seg(name, a, b):
        o = SEG[name][0]
        return list(range(o + a, o + b))
    p = []
    for hh in range(4):
        p += seg("a_q", hh * 64, hh * 64 + 64) + seg("a_q", (4 + hh) * 64, (4 + hh) * 64 + 64)
    p += seg("a_kv_cmp", 0, 128)
    p += seg("a_kv_cmp", 128, 256)
    p += seg("a_kv_slc", 0, 128)
    p += seg("a_kv_win", 0, 128)
    p += seg("b_q", 0, 64) + seg("b_q", 128, 192)
    p += seg("b_q", 64, 128) + seg("b_q", 192, 256)
    p += seg("b_kv", 0, 128)
    p += seg("c_shift", 0, 896)
    assert len(p) == NFEAT
    p += seg("a_kv_slc", 128, 256) + seg("a_kv_win", 128, 256) + seg("b_kv", 128, 256) + seg("a_gate", 0, 24)
    p += seg("a_z", 0, 512) + seg("b_z", 0, 256) + seg("c_z", 0, 256)
    p += seg("merge", 0, 3072)
    assert len(p) == N_IN and len(set(p)) == N_IN
    return np.array(p)


def _consts():
    c = {}
    c["identf"] = np.eye(128, dtype=np.float32)
    s = np.arange(128)[:, None]
    t = np.arange(128)[None, :]
    c["triA"] = np.where(s <= t, 0.0, -BIG).astype(np.float32)
    c["triB"] = np.where(s > t, 0.0, -BIG).astype(np.float32)
    E = np.zeros((32, 16, 128), np.float32)
    for j in range(16):
        for p in range(128):
            E[2 * j + p // 64, j, p] = BIG
    E2 = np.zeros((128, 2048), np.float32)
    E2[0:32] = E.reshape(32, 2048)
    E2[64:96] = E.reshape(32, 2048)
    c["Eall"] = E2
    n = np.arange(128)[:, None]
    tt = np.arange(T)[None, :]
    c["penc"] = np.where(16 * n + 31 <= tt, 0.0, -BIG).astype(np.float32)
    Mi = np.zeros((128, 16, 32), np.float32)
    Bc = np.zeros((128, 16, 32), np.float32)
    for i in range(16):
        for p in range(128):
            cur = 2 * i + p // 64
            for j in range(32):
                if j == 0:
                    Bc[p, i, j] = 10.0
                elif j == cur:
                    Bc[p, i, j] = 20.0
                elif j == cur - 1:
                    Bc[p, i, j] = 30.0
                elif j > cur:
                    Bc[p, i, j] = -1.0 - j
                else:
                    Mi[p, i, j] = 1.0
            if cur == 0:
                Bc[p, i, 0] = 20.0
            if cur == 1:
                Bc[p, i, 0] = 30.0
    c["Mi"] = Mi.reshape(128, 512)
    c["Bc"] = Bc.reshape(128, 512)
    ci = np.arange(128)[:, None] * 16
    sj = np.arange(32)[None, :] * 64
    c["ovl"] = ((ci < sj + 64) & (ci + 32 > sj)).astype(np.float32)
    r = np.arange(128)[:, None]
    q = np.arange(128)[None, :]
    c["mSL"] = (q < r).astype(np.float32)
    c["mSU"] = (r < q).astype(np.float32)
    c["mIU"] = (r <= q).astype(np.float32)
    c["bones"] = ((r // 64) == (q // 64)).astype(np.float32)
    c["hsel"] = ((np.arange(128)[:, None] // 64) == np.arange(2)[None, :]).astype(np.float32)
    return c


CONST_SHAPES = dict(identf=[128, 128], triA=[128, 128], triB=[128, 128], Eall=[128, 2048], penc=[128, 2048],
                    Mi=[128, 512], Bc=[128, 512], ovl=[128, 32], mSL=[128, 128], mSU=[128, 128], mIU=[128, 128],
                    bones=[128, 128], hsel=[128, 2])


def layer_input_shapes(l):
    s = dict(win=[D, N_IN], ng=[128, 8], ngb=[128, D], bmerge=[128, 3072], w1k=[128, 32, 128], w1v=[128, 32, 128],
             pek=[128, 32], pev=[128, 32], w2k=[128, 64], w2v=[128, 64], sinks=[128, 4], rv=[128, 20],
             wa=[128, 256], lnw=[128, 256], lnb=[128, 256], pw=[128, 16, 1024])
    if l > 0:
        s["v1"] = [128, 2, 32]
        s["v2"] = [32, 256]
    return s


def prep_layer(inp, l, perm):
    f = lambda a: np.ascontiguousarray(a, dtype=np.float32)
    o = {}
    o["win"] = f(inp["w_in"][l][:, perm])
    o["ng"] = f(inp["norm_g"][l].reshape(8, 128).T)
    o["ngb"] = f(np.broadcast_to(inp["norm_g"][l].reshape(1, D), (128, D)))
    o["bmerge"] = f(np.broadcast_to(inp["b_merge"][l].reshape(1, 3072), (128, 3072)))
    for nm, src in (("w1k", "cmp_w1_k"), ("w1v", "cmp_w1_v")):
        w = inp[src][l].reshape(32, 64, 128).transpose(1, 0, 2)
        o[nm] = f(np.concatenate([w, w], 0))
    for nm, src in (("pek", "cmp_pe_k"), ("pev", "cmp_pe_v")):
        p = inp[src][l].T
        o[nm] = f(np.concatenate([p, p], 0))
    o["w2k"] = f(inp["cmp_w2_k"][l])
    o["w2v"] = f(inp["cmp_w2_v"][l])
    o["sinks"] = f(np.broadcast_to(inp["swa_sinks"][l].reshape(1, 4), (128, 4)))
    rv = np.zeros((128, 20), np.float32)
    rv[:, 0:7] = inp["rwkv_mu"][l].reshape(7, 128).T
    for k, nm in enumerate(("rwkv_w0", "rwkv_a0", "rwkv_k_k", "rwkv_k_a")):
        rv[:, 7 + 2 * k:9 + 2 * k] = inp[nm][l].reshape(2, 128).T
    rv[:, 15:17] = inp["rwkv_r_k"][l].reshape(2, 128).T
    if l > 0:
        rv[:, 17:19] = inp["rwkv_v0"][l - 1].reshape(2, 128).T
    o["rv"] = rv
    o["wa"] = f(np.concatenate([inp["rwkv_w2"][l], inp["rwkv_a2"][l]], 0))
    o["lnw"] = f(np.broadcast_to(inp["rwkv_ln_w"][l].reshape(1, 256), (128, 256)))
    o["lnb"] = f(np.broadcast_to(inp["rwkv_ln_b"][l].reshape(1, 256), (128, 256)))
    pw = np.concatenate([inp["proj_a"][l].reshape(4, 128, 1024), inp["proj_b"][l].reshape(2, 128, 1024),
                         inp["proj_c"][l].reshape(2, 128, 1024), inp["w_out"][l].reshape(8, 128, 1024)], 0)
    o["pw"] = f(pw.transpose(1, 0, 2))
    if l > 0:
        o["v1"] = f(inp["rwkv_v1"][l - 1].reshape(2, 128, 32).transpose(1, 0, 2))
        o["v2"] = f(inp["rwkv_v2"][l - 1])
    return o


class KB:
    def __init__(self, layers, debug=False, upto="out"):
        self.layers = list(layers)
        self.debug = debug
        self.upto = upto
        nc = bass.Bass("TRN2", target_bir_lowering=False)
        nc.allow_low_precision("bf16 matmul operands, fp32 accumulation")
        self.nc = nc
        self.P = Prog(nc)
        self.rotc = {}

    def dma(self, eng, out, in_, r=(), w=()):
        return self.P.add(eng, lambda e: e.dma_start(out=out, in_=in_), r=r, w=w, dma=True)

    def mm(self, out, lhsT, rhs, start, stop, r=(), w=()):
        return self.P.add("pe", lambda e: e.matmul(out, lhsT=lhsT, rhs=rhs, start=start, stop=stop,
                                                   skip_group_check=True), r=r, w=w)

    def tr(self, out, in_, ident, r=(), w=()):
        return self.P.add("pe", lambda e: e.transpose(out=out, in_=in_, identity=ident), r=r, w=w)

    def act(self, out, in_, func, r=(), w=(), bias=None, scale=None, accum=None):
        kw = {}
        if bias is not None:
            kw["bias"] = bias
        if scale is not None:
            kw["scale"] = scale
        if accum is not None:
            kw["accum_out"] = accum
        return self.P.add("act", lambda e: e.activation(out=out, in_=in_, func=func, **kw), r=r, w=w)

    def tt(self, eng, out, in0, in1, op, r=(), w=()):
        return self.P.add(eng, lambda e: e.tensor_tensor(out=out, in0=in0, in1=in1, op=op), r=r, w=w)

    def ts(self, eng, out, in0, s1, s2, op0, op1=None, r=(), w=()):
        if op1 is None:
            return self.P.add(eng, lambda e: e.tensor_scalar(out=out, in0=in0, scalar1=s1, scalar2=None, op0=op0), r=r, w=w)
        return self.P.add(eng, lambda e: e.tensor_scalar(out=out, in0=in0, scalar1=s1, scalar2=s2, op0=op0, op1=op1), r=r, w=w)

    def stt(self, eng, out, in0, scalar, in1, op0, op1, r=(), w=()):
        return self.P.add(eng, lambda e: e.scalar_tensor_tensor(out=out, in0=in0, scalar=scalar, in1=in1, op0=op0, op1=op1), r=r, w=w)

    def cp(self, eng, out, in_, r=(), w=()):
        if eng == "act":
            return self.P.add("act", lambda e: e.copy(out=out, in_=in_), r=r, w=w)
        return self.P.add(eng, lambda e: e.tensor_copy(out=out, in_=in_), r=r, w=w)

    def memset(self, eng, ap, val, w=()):
        return self.P.add(eng, lambda e: e.memset(ap, val), w=w)

    def recip(self, out, in_, r=(), w=()):
        return self.P.add("dve", lambda e: e.reciprocal(out=out, in_=in_), r=r, w=w)

    def red(self, out, in_, op, r=(), w=()):
        return self.P.add("dve", lambda e: e.tensor_reduce(out=out, in_=in_, axis=AX.X, op=op), r=r, w=w)

    def rot(self, lo, hi):
        k = (lo, hi)
        c = self.rotc.get(k, 0)
        self.rotc[k] = c + 1
        b = lo + c % (hi - lo)
        return self.ps[b], "ps%d" % b

    def sbt(self, st, name, shape, dt=F32):
        self._nm = getattr(self, "_nm", 0) + 1
        return st.enter_context(self.nc.sbuf_tensor("%s_%d" % (name, self._nm), shape, dt))

    def build(self):
        nc = self.nc
        layers = self.layers
        dk = "ExternalOutput" if self.debug else "Internal"
        self.d = {}
        self.d["x"] = nc.dram_tensor("x", [T, D], F32, kind="ExternalInput").ap()
        for k, shp in CONST_SHAPES.items():
            self.d[k] = nc.dram_tensor(k, shp, F32, kind="ExternalInput").ap()
        self.d["fg"] = nc.dram_tensor("fg", [128, D], F32, kind="ExternalInput").ap()
        for l in layers:
            for k, shp in layer_input_shapes(l).items():
                self.d["%s%d" % (k, l)] = nc.dram_tensor("%s%d" % (k, l), shp, F32, kind="ExternalInput").ap()
        self.d["Z"] = nc.dram_tensor("scrZ", [T, 1024], F32, kind=dk).ap()
        self.d["G"] = nc.dram_tensor("scrG", [T, 3072], F32, kind=dk).ap()
        self.d["Y"] = nc.dram_tensor("scrY", [T, 1024], F32, kind=dk).ap()
        self.d["F"] = nc.dram_tensor("scrF", [7, 128, T], F32, kind=dk).ap()
        if 0 in layers and 1 in layers:
            self.d["vf"] = nc.dram_tensor("vfirst", [2, 128, T], F32, kind=dk).ap()
        elif 0 in layers:
            self.d["vf"] = nc.dram_tensor("vfirst", [2, 128, T], F32, kind="ExternalOutput").ap()
        else:
            self.d["vf"] = nc.dram_tensor("vfirst", [2, 128, T], F32, kind="ExternalInput").ap()
        self.d["xout"] = nc.dram_tensor("xout", [T, D], F32, kind="ExternalOutput").ap()
        if len(layers) > 1:
            self.d["xmid"] = nc.dram_tensor("xmid", [T, D], F32, kind=dk).ap()

        with ExitStack() as st:
            self.ps = [st.enter_context(nc.psum_tensor("psb%d" % b, [128, 512], F32)) for b in range(8)]
            self.load_consts(st)
            xin = self.d["x"]
            xkey = "x"
            for li, l in enumerate(layers):
                last = (li == len(layers) - 1)
                xo = self.d["xout"] if last else self.d["xmid"]
                xokey = "xout" if last else "xmid"
                self.layer(l, xin, xkey, xo, xokey)
                xin, xkey = xo, xokey
            self.P.fence()
            self.P.emit(st)
        return nc

    def load_consts(self, st):
        d = self.d
        c = self.c = {}
        f32c = ["identf", "Mi", "Bc", "mSL", "mSU", "mIU", "bones", "hsel"]
        for k in f32c:
            c[k] = self.sbt(st, k, CONST_SHAPES[k])
            self.dma("sp", c[k][:], d[k], w=[k])
        c["fg"] = self.sbt(st, "fg", [128, D])
        self.dma("sp", c["fg"][:], d["fg"], w=["fg"])
        bl = ["identf", "triA", "triB", "Eall", "penc", "ovl"]
        for k in bl:
            nm = "identb" if k == "identf" else k
            c[nm] = self.sbt(st, nm, CONST_SHAPES[k], BF16)
        with ExitStack() as s2:
            for k in bl:
                nm = "identb" if k == "identf" else k
                tmp = self.sbt(s2, k + "_f", CONST_SHAPES[k])
                self.dma("sp", tmp[:], d[k], w=[k + "_f"])
                self.cp("pool", c[nm][:], tmp[:], r=[k + "_f"], w=[nm])
            self.P.fence()

    def layer(self, l, xin, xkey, xo, xokey):
        with ExitStack() as sA:
            A = self.alloc_attn(sA)
            with ExitStack() as s1:
                self.phase01(s1, l, xin, xkey, A)
                self.P.fence()
            if self.upto in ("p0", "p1"):
                return
            with ExitStack() as s2:
                self.phase_attn(s2, l, A)
                self.P.fence()
        if self.upto in ("cmp", "attn"):
            return
        with ExitStack() as s3:
            self.phase_rwkv(s3, l)
            self.P.fence()
        if self.upto == "rwkv":
            return
        with ExitStack() as s4:
            self.phase_out(s4, l, xin, xkey, xo, xokey)
            self.P.fence()

    def alloc_attn(self, st):
        A = {}
        A["qTA"] = self.sbt(st, "qTA", [128, 4, T], BF16)
        A["qTB"] = self.sbt(st, "qTB", [128, 2, T], BF16)
        for k in ("kTc", "vTc", "kTs", "kTw", "kTb"):
            A[k] = self.sbt(st, k, [128, T], BF16)
        for k in ("Vs", "Vw", "Vb"):
            A[k] = self.sbt(st, k, [128, 16, 2, 68], BF16)
        A["gate"] = self.sbt(st, "gate", [128, 16, 24])
        return A

    def phase01(self, st, l, xin, xkey, A):
        c, d = self.c, self.d
        L = lambda k: d["%s%d" % (k, l)]
        xnT = self.sbt(st, "xnT", [128, 8, T], BF16)
        ng = self.sbt(st, "ng", [128, 8])
        self.dma("sp", ng[:], L("ng"), w=["ng"])
        wst = [self.sbt(st, "wst", [128, 8, 512]) for _ in range(2)]
        wbf = [self.sbt(st, "wbf", [128, 8, 512], BF16) for _ in range(2)]
        bm = self.sbt(st, "bm", [128, 3072])
        win = L("win").rearrange("(k p) n -> p k n", p=128)
        self._blk = 0

        def load_block(c0, ncol):
            b = self._blk % 2
            self._blk += 1
            for kc in range(8):
                self.dma("sp", wst[b][:, kc, 0:ncol], win[:, kc, c0:c0 + ncol], w=["wst%d" % b])
            self.cp("dve", wbf[b][:, 0:4, 0:ncol], wst[b][:, 0:4, 0:ncol], r=["wst%d" % b], w=["wbf%d" % b])
            self.cp("act", wbf[b][:, 4:6, 0:ncol], wst[b][:, 4:6, 0:ncol], r=["wst%d" % b], w=["wbf%d" % b])
            self.cp("pool", wbf[b][:, 6:8, 0:ncol], wst[b][:, 6:8, 0:ncol], r=["wst%d" % b], w=["wbf%d" % b])
            return wbf[b], "wbf%d" % b

        pre = {}
        s0 = ExitStack()
        xt = [self.sbt(s0, "xt", [128, D]) for _ in range(2)]
        xs = [self.sbt(s0, "xs", [128, D], BF16) for _ in range(2)]
        ngb = self.sbt(s0, "ngb", [128, D])
        mhalf = self.sbt(s0, "mhalf", [128, 1])
        self.memset("pool", mhalf[:], -0.5, w=["mhalf"])
        self.dma("sp", ngb[:], L("ngb"), w=["ngb"])
        junk = self.sbt(s0, "junk", [128, D], BF16)
        ss = [self.sbt(s0, "ss", [128, 1]) for _ in range(2)]
        for k in ("Vs", "Vw", "Vb"):
            self.memset("pool", A[k][:], 0.0, w=[k])
            self.memset("pool", A[k][:, :, :, 64:65], 1.0, w=[k])
        for i in range(NT):
            b = i % 2
            kx, ks, kss = "xt%d" % b, "xs%d" % b, "ss%d" % b
            self.dma("sp", xt[b][:], xin[i * 128:(i + 1) * 128, :], r=[xkey], w=[kx])
            self.memset("pool", ss[b][:], 0.0, w=[kss])
            self.act(junk[:], xt[b][:], AF.Square, r=[kx], w=["junk", kss], accum=ss[b][:])
            self.ts("dve", ss[b][:], ss[b][:], 1.0 / D, 1e-6, ALU.mult, ALU.add, w=[kss])
            self.tt("pool", ss[b][:], ss[b][:], mhalf[:], ALU.pow, r=["mhalf"], w=[kss])
            self.stt("dve", xs[b][:], xt[b][:], ss[b][:, 0:1], ngb[:], ALU.mult, ALU.mult, r=[kx, kss, "ngb"], w=[ks])
            pb, pk = self.rot(0, 8)
            pbb = pb[:].bitcast(BF16)
            for ch in range(8):
                self.tr(pbb[:, ch * 128:(ch + 1) * 128], xs[b][:, ch * 128:(ch + 1) * 128], c["identb"][:],
                        r=[ks, "identb"], w=[pk])
            self.cp("act" if i % 2 else "dve", xnT[:, :, i * 128:(i + 1) * 128], pbb.rearrange("p (c t) -> p c t", c=8),
                    w=[pk, "xnT"])
            if i == 5:
                pre[0] = load_block(0, 512)
                self.dma("sp", bm[:], L("bmerge"), w=["bm"])
            if i == 11:
                pre[1] = load_block(512, 512)
        self.P.fence()
        s0.close()
        if self.upto == "p0":
            return
        fst = [self.sbt(st, "fst", [128, T]) for _ in range(1)]
        zst = [self.sbt(st, "zst", [128, 512]) for _ in range(3)]

        fdest = []
        for hh in range(4):
            fdest.append((A["qTA"], hh, "qTA"))
        for k in ("kTc", "vTc", "kTs", "kTw"):
            fdest.append((A[k], None, k))
        for hh in range(2):
            fdest.append((A["qTB"], hh, "qTB"))
        fdest.append((A["kTb"], None, "kTb"))
        for cc in range(7):
            fdest.append((None, cc, "F"))
        self._ev = 0
        self._zi = 0

        def comp_feat(s0, nsub):
            def f(wb, wk):
                for s in range(s0, s0 + nsub):
                    dst, idx, dkey = fdest[s]
                    fb = 0
                    for tc in range(4):
                        pb, pk = self.rot(0, 8)
                        for kc in range(8):
                            self.mm(pb[:, :], wb[:, kc, (s - s0) * 128:(s - s0 + 1) * 128], xnT[:, kc, tc * 512:(tc + 1) * 512],
                                    kc == 0, kc == 7, r=[wk, "xnT"], w=[pk])
                        if dst is None:
                            o_ap, okey = fst[fb][:, tc * 512:(tc + 1) * 512], "fst%d" % fb
                        elif idx is None:
                            o_ap, okey = dst[:, tc * 512:(tc + 1) * 512], dkey
                        else:
                            o_ap, okey = dst[:, idx, tc * 512:(tc + 1) * 512], dkey
                        self.cp("act" if self._ev % 2 == 0 else "dve", o_ap, pb[:, :], w=[pk, okey])
                        self._ev += 1
                    if dst is None:
                        self.dma("pool", d["F"][idx], fst[fb][:], r=["fst%d" % fb], w=[("F", idx)])
            return f

        def comp_tok0(wb, wk):
            for i in range(NT):
                pb, pk = self.rot(0, 8)
                for kc in range(8):
                    self.mm(pb[:, 0:408], xnT[:, kc, i * 128:(i + 1) * 128], wb[:, kc, 0:408], kc == 0, kc == 7,
                            r=[wk, "xnT"], w=[pk])
                for vi, k in enumerate(("Vs", "Vw", "Vb")):
                    self.cp("dve" if vi == 1 else "act", A[k][:, i, :, 0:64],
                            pb[:, vi * 128:(vi + 1) * 128].rearrange("p (g e) -> p g e", g=2), w=[pk, k])
                self.act(A["gate"][:, i, :], pb[:, 384:408], AF.Sigmoid, w=[pk, "gate"])

        def comp_zm(blk):
            def f(wb, wk):
                for i in range(NT):
                    pb, pk = self.rot(0, 8)
                    for kc in range(8):
                        self.mm(pb[:, :], xnT[:, kc, i * 128:(i + 1) * 128], wb[:, kc, :], kc == 0, kc == 7,
                                r=[wk, "xnT"], w=[pk])
                    zb = self._zi % 3
                    self._zi += 1
                    zk = "zst%d" % zb
                    if blk < 2:
                        self.act(zst[zb][:], pb[:, :], AF.Silu, w=[pk, zk])
                        self.dma("pool", d["Z"][i * 128:(i + 1) * 128, blk * 512:(blk + 1) * 512], zst[zb][:], r=[zk], w=[("Z", i, blk)])
                    else:
                        mb = blk - 2
                        self.tt("dve", zst[zb][:], pb[:, :], bm[:, mb * 512:(mb + 1) * 512], ALU.add, r=["bm"], w=[pk, zk])
                        self.act(zst[zb][:], zst[zb][:], AF.Sigmoid, w=[zk])
                        self.dma("pool", d["G"][i * 128:(i + 1) * 128, mb * 512:(mb + 1) * 512], zst[zb][:], r=[zk], w=[("G", i, mb)])
            return f

        blocks = []
        for s0 in range(0, 18, 4):
            nsub = min(4, 18 - s0)
            blocks.append((s0 * 128, nsub * 128, comp_feat(s0, nsub)))
        blocks.append((TOK0, 408, comp_tok0))
        for blk in range(8):
            blocks.append((TOK0 + 408 + blk * 512, 512, comp_zm(blk)))
        assert blocks[0][:2] == (0, 512) and blocks[1][:2] == (512, 512)
        cur = pre[0]
        for bi, (c0, ncol, fn) in enumerate(blocks):
            if bi == 0:
                nxt = pre[1]
            else:
                nxt = load_block(blocks[bi + 1][0], blocks[bi + 1][1]) if bi + 1 < len(blocks) else None
            fn(cur[0], cur[1])
            cur = nxt

    def phase_attn(self, st, l, A):
        c, d = self.c, self.d
        L = lambda k: d["%s%d" % (k, l)]
        ident, identb = c["identf"], c["identb"]
        kcT = self.sbt(st, "kcT", [128, 128], BF16)
        vcaug = self.sbt(st, "vcaug", [128, 2, 100], BF16)
        self.memset("pool", kcT[:], 0.0, w=["kcT"])
        self.memset("pool", vcaug[:], 0.0, w=["vcaug"])
        self.memset("pool", vcaug[:, :, 64:65], 1.0, w=["vcaug"])
        for g in range(2):
            self.cp("pool", vcaug[:, g, 65:97], c["ovl"][:], r=["ovl"], w=["vcaug"])
        with ExitStack() as sc:
            w1f = [self.sbt(sc, "w1f", [128, 32, 128]) for _ in range(2)]
            w1b = [self.sbt(sc, "w1b", [128, 32, 128], BF16) for _ in range(2)]
            pef = self.sbt(sc, "pef", [128, 2, 32])
            peb = self.sbt(sc, "peb", [128, 2, 32], BF16)
            w2f = self.sbt(sc, "w2f", [128, 2, 64])
            w2kd = self.sbt(sc, "w2kd", [128, 128], BF16)
            w2vb = self.sbt(sc, "w2vb", [128, 64], BF16)
            cb = self.sbt(sc, "cb", [128, 4])
            hsb = [self.sbt(sc, "hsb", [128, 128], BF16) for _ in range(2)]
            for wi, nm in enumerate(("w1k", "w1v")):
                self.dma("sp", w1f[wi][:], L(nm), w=["w1f%d" % wi])
                self.cp("dve", w1b[wi][:, 0:16, :], w1f[wi][:, 0:16, :], r=["w1f%d" % wi], w=["w1b%d" % wi])
                self.cp("act" if wi else "pool", w1b[wi][:, 16:32, :], w1f[wi][:, 16:32, :], r=["w1f%d" % wi], w=["w1b%d" % wi])
            self.dma("sp", pef[:, 0, :], L("pek"), w=["pef"])
            self.dma("sp", pef[:, 1, :], L("pev"), w=["pef"])
            self.cp("dve", peb[:], pef[:], r=["pef"], w=["peb"])
            self.dma("sp", w2f[:, 0, :], L("w2k"), w=["w2f"])
            self.dma("sp", w2f[:, 1, :], L("w2v"), w=["w2f"])
            self.cp("dve", w2kd[:, 0:64], w2f[:, 0, :], r=["w2f"], w=["w2kd"])
            self.cp("dve", w2kd[:, 64:128], w2f[:, 0, :], r=["w2f"], w=["w2kd"])
            self.cp("dve", w2vb[:], w2f[:, 1, :], r=["w2f"], w=["w2vb"])
            cnt = 0
            for wi, (src, skey) in enumerate(((A["kTc"], "kTc"), (A["vTc"], "vTc"))):
                for g in range(2):
                    rows = slice(g * 64, (g + 1) * 64)
                    sv = src[rows, :].rearrange("p (n s) -> p n s", s=16)
                    pb, pk = self.rot(0, 8)
                    for pos in range(32):
                        self.mm(pb[:, 0:1], w1b[wi][rows, pos, :], peb[rows, wi, pos:pos + 1], pos == 0, pos == 31,
                                r=["w1b%d" % wi, "peb"], w=[pk])
                    self.cp("dve", cb[:, cnt:cnt + 1], pb[:, 0:1], w=[pk, "cb"])
                    pb2, pk2 = self.rot(0, 8)
                    for pos in range(32):
                        rhs = sv[:, 0:127, pos] if pos < 16 else sv[:, 1:128, pos - 16]
                        self.mm(pb2[:, 0:127], w1b[wi][rows, pos, :], rhs, pos == 0, pos == 31,
                                r=["w1b%d" % wi, skey], w=[pk2])
                    hb = hsb[cnt % 2]
                    hk = "hsb%d" % (cnt % 2)
                    self.act(hb[:, 0:127], pb2[:, 0:127], AF.Silu, bias=cb[:, cnt:cnt + 1], r=["cb"], w=[pk2, hk])
                    pb3, pk3 = self.rot(0, 8)
                    if wi == 0:
                        self.mm(pb3[:, 0:127], w2kd[:, :], hb[:, 0:127], True, True, r=["w2kd", hk], w=[pk3])
                        self.cp("dve", kcT[rows, 0:127], pb3[rows, 0:127], w=[pk3, "kcT"])
                    else:
                        self.mm(pb3[0:127, 0:64], hb[:, 0:127], w2vb[:, :], True, True, r=["w2vb", hk], w=[pk3])
                        self.cp("dve", vcaug[0:127, g, 0:64], pb3[0:127, 0:64], w=[pk3, "vcaug"])
                    cnt += 1
            self.P.fence()
        if self.upto == "cmp":
            return
        NPT = 4
        PT = [self.sbt(st, "PT", [128, 17, 512], BF16) for _ in range(NPT)]
        PTc = [self.sbt(st, "PTc", [128, 512], BF16) for _ in range(2)]
        ya = [self.sbt(st, "ya", [128, 512]) for _ in range(2)]
        yb = [self.sbt(st, "yb", [128, 256]) for _ in range(2)]
        selT2 = self.sbt(st, "selT2", [128, T], BF16)
        self.memset("pool", selT2[:], 0.0, w=["selT2"])
        esink = self.sbt(st, "esink", [128, 4])
        self.dma("sp", esink[:], L("sinks"), w=["esink"])
        self.act(esink[:], esink[:], AF.Exp, w=["esink"])
        NR = 4
        rd = [self.sbt(st, "rd", [128, 4]) for _ in range(NR)]
        coef = [self.sbt(st, "coef", [128, 4]) for _ in range(NR)]
        tmpo = [self.sbt(st, "tmpo", [128, 4, 64]) for _ in range(NR)]
        tmpi = [self.sbt(st, "tmpi", [128, 4, 32]) for _ in range(2)]
        imp = [self.sbt(st, "imp", [128, 32]) for _ in range(2)]
        score = [self.sbt(st, "score", [128, 32]) for _ in range(2)]
        scw = [self.sbt(st, "scw", [128, 32]) for _ in range(2)]
        m8 = [self.sbt(st, "m8", [128, 16]) for _ in range(2)]
        selm = [self.sbt(st, "selm", [128, 96]) for _ in range(2)]
        for t_ in range(2):
            self.memset("pool", selm[t_][:], 0.0, w=["selm%d" % t_])
        self._ri = 0
        self._pt = 0
        self._ptA = 0
        gate = A["gate"]

        def epilogue(ob, ok, nh, wdt, i, g, br, dst, dkey, first):
            k = self._ri % NR
            self._ri += 1
            rk_, ck_, tk_ = "rd%d" % k, "coef%d" % k, "tmpo%d" % k
            O3 = ob[:, 0:nh * wdt].rearrange("p (h c) -> p h c", c=wdt)
            if br == "swa":
                self.tt("dve", rd[k][:, 0:nh], O3[:, :, 64], esink[:, g * 2:(g + 1) * 2], ALU.add, r=["esink"], w=[ok, rk_])
            else:
                self.ts("dve", rd[k][:, 0:nh], O3[:, :, 64], 1e-30, None, ALU.max, w=[ok, rk_])
            self.recip(rd[k][:, 0:nh], rd[k][:, 0:nh], w=[rk_])
            if br == "swa":
                cf = rd[k]
                cfk = rk_
            else:
                gc = {"cmp": 0, "slc": 8, "win": 16}[br] + g * 4
                self.tt("dve", coef[k][:, 0:nh], rd[k][:, 0:nh], gate[:, i, gc:gc + 4], ALU.mult, r=[rk_, "gate"], w=[ck_])
                cf = coef[k]
                cfk = ck_
            dst3 = dst.rearrange("p (h c) -> p h c", c=64)
            if first:
                self.tt("dve", dst3, O3[:, :, 0:64], cf[:, 0:nh].unsqueeze(2).to_broadcast([128, nh, 64]), ALU.mult,
                        r=[cfk], w=[ok, dkey])
            else:
                self.tt("dve", tmpo[k][:, 0:nh, :], O3[:, :, 0:64], cf[:, 0:nh].unsqueeze(2).to_broadcast([128, nh, 64]),
                        ALU.mult, r=[cfk], w=[ok, tk_])
                self.tt("pool", dst3, dst3, tmpo[k][:, 0:nh, :], ALU.add, r=[tk_], w=[dkey])
            return k

        def cmp_tile(i, g, yat, yak):
            M = 128
            rows = slice(g * 64, (g + 1) * 64)
            b = self._pt % 2
            self._pt += 1
            sb_, sk = self.rot(0, 6)
            self.mm(sb_[0:M, :], kcT[rows, 0:M], A["qTA"][rows, :, i * 128:(i + 1) * 128], True, False, r=["kcT", "qTA"], w=[sk])
            self.mm(sb_[0:M, :], identb[0:M, 0:M], c["penc"][0:M, i * 128:(i + 1) * 128].unsqueeze(1).to_broadcast([M, 4, 128]),
                    False, True, r=["identb", "penc"], w=[sk])
            pk_ = "PTc%d" % b
            self.act(PTc[b][0:M, :], sb_[0:M, :], AF.Exp, scale=0.125, w=[sk, pk_])
            cst = getattr(self, "cstage", 9)
            if cst < 2:
                return
            ob, ok = self.rot(6, 8)
            for hh in range(4):
                self.mm(ob[:, hh * 100:hh * 100 + 100], PTc[b][0:M, hh * 128:(hh + 1) * 128], vcaug[0:M, g, 0:100], True, True,
                        r=[pk_, "vcaug"], w=[ok])
            if cst < 3:
                return
            k = epilogue(ob, ok, 4, 100, i, g, "cmp", yat[:, g * 256:(g + 1) * 256], yak, True)
            if cst < 4:
                return
            O3 = ob[:, 0:400].rearrange("p (h c) -> p h c", c=100)
            tb = g
            self.tt("dve", tmpi[tb][:], O3[:, :, 65:97], rd[k][:, 0:4].unsqueeze(2).to_broadcast([128, 4, 32]), ALU.mult,
                    r=["rd%d" % k], w=[ok, "tmpi%d" % tb])
            self.red(imp[tb][:], tmpi[tb][:].rearrange("p h j -> p j h"), ALU.add, r=["tmpi%d" % tb], w=["imp%d" % tb])
            if cst < 5:
                return
            sc_, sk_ = score[tb], "score%d" % tb
            self.tt("dve", sc_[:], imp[tb][:], c["Mi"][:, i * 32:(i + 1) * 32], ALU.mult, r=["imp%d" % tb, "Mi"], w=[sk_])
            self.tt("dve", sc_[:], sc_[:], c["Bc"][:, i * 32:(i + 1) * 32], ALU.add, r=["Bc"], w=[sk_])
            mk, wk_, lk = "m8%d" % tb, "scw%d" % tb, "selm%d" % (i % 2)
            sm_ = selm[i % 2]
            self.P.add("dve", lambda e: e.max(out=m8[tb][:, 0:8], in_=sc_[:]), r=[sk_], w=[mk])
            self.P.add("dve", lambda e: e.match_replace(out=scw[tb][:], in_to_replace=m8[tb][:, 0:8], in_values=sc_[:],
                                                        imm_value=-1e30), r=[sk_, mk], w=[wk_])
            self.P.add("dve", lambda e: e.max(out=m8[tb][:, 8:16], in_=scw[tb][:]), r=[wk_], w=[mk])
            self.ts("dve", sm_[:, g * 64:g * 64 + 32], sc_[:], m8[tb][:, 15:16], -1.0, ALU.is_ge, ALU.add, r=[sk_, mk], w=[lk])
            if cst < 6 or g == 0:
                return
            xb, xk = self.rot(7, 8)
            self.tr(xb[0:96, 0:128], sm_[:, :], ident[:, :], r=[lk, "identf"], w=[xk])
            self.cp("act", selT2[0:96, i * 128:(i + 1) * 128], xb[0:96, 0:128], w=[xk, "selT2"])

        def attn_A2(i, br, dsts):
            if br == "slc":
                kT, kk_, V, vk, q, qk, nh, js = A["kTs"], "kTs", A["Vs"], "Vs", A["qTA"], "qTA", 4, list(range(0, i + 1))
            elif br == "win":
                kT, kk_, V, vk, q, qk, nh, js = A["kTw"], "kTw", A["Vw"], "Vw", A["qTA"], "qTA", 4, list(range(max(0, i - 4), i + 1))
            else:
                kT, kk_, V, vk, q, qk, nh, js = A["kTb"], "kTb", A["Vb"], "Vb", A["qTB"], "qTB", 2, list(range(max(0, i - 1), i + 1))
            N = nh * 128
            qs = slice(i * 128, (i + 1) * 128)
            pts = []
            for g in range(2):
                b = self._ptA % NPT
                self._ptA += 1
                pts.append((PT[b], "PT%d" % b))
            for idx, j in enumerate(js):
                banks = [self.rot(0, 6) for _ in range(2)]
                pens = [[], []]
                for g in range(2):
                    if br == "slc" and j < i and i >= 8:
                        er = slice(g * 64, g * 64 + 32)
                        pens[g].append((c["Eall"][er, j * 128:(j + 1) * 128],
                                        selT2[er, qs].unsqueeze(1).to_broadcast([32, nh, 128]), ["Eall", "selT2"]))
                    if j == i:
                        pens[g].append((identb[:, :], c["triA"][:, :].unsqueeze(1).to_broadcast([128, nh, 128]), ["identb", "triA"]))
                    if (br == "win" and j == i - 4) or (br == "swa" and j == i - 1):
                        pens[g].append((identb[:, :], c["triB"][:, :].unsqueeze(1).to_broadcast([128, nh, 128]), ["identb", "triB"]))
                for g in range(2):
                    rows = slice(g * 64, (g + 1) * 64)
                    sb_, sk = banks[g]
                    self.mm(sb_[:, 0:N], kT[rows, j * 128:(j + 1) * 128], q[rows, 0:nh, qs], True, len(pens[g]) == 0,
                            r=[kk_, qk], w=[sk])
                for g in range(2):
                    sb_, sk = banks[g]
                    for pi, (l_, r_, keys) in enumerate(pens[g]):
                        self.mm(sb_[:, 0:N], l_, r_, False, pi == len(pens[g]) - 1, r=keys, w=[sk])
                for g in range(2):
                    sb_, sk = banks[g]
                    self.act(pts[g][0][:, idx, 0:N], sb_[:, 0:N], AF.Exp, scale=0.125, w=[sk, pts[g][1]])
            return [dict(i=i, g=g, br=br, dst=dsts[g][0], dkey=dsts[g][1], pt=pts[g][0], ptk=pts[g][1], V=V, vk=vk, nh=nh, js=js, after=None)
                    for g in range(2)]

        def attn_B(S_):
            i, g, br, nh, js, pt, ptk, V, vk = S_["i"], S_["g"], S_["br"], S_["nh"], S_["js"], S_["pt"], S_["ptk"], S_["V"], S_["vk"]
            ob, ok = self.rot(6, 8)
            for hh in range(nh):
                for idx, j in enumerate(js):
                    self.mm(ob[:, hh * 68:hh * 68 + 68], pt[:, idx, hh * 128:(hh + 1) * 128], V[:, j, g, 0:68],
                            idx == 0, idx == len(js) - 1, r=[ptk, vk], w=[ok])
            epilogue(ob, ok, nh, 68, i, g, br, S_["dst"], S_["dkey"], br == "swa")
            if S_["after"] is not None:
                S_["after"]()

        def mk_after(i, b, yak, ybk):
            def f():
                self.dma("pool", d["Y"][i * 128:(i + 1) * 128, 0:512], ya[b][:], r=[yak], w=[("Y", i, 0)])
                self.dma("pool", d["Y"][i * 128:(i + 1) * 128, 512:768], yb[b][:], r=[ybk], w=[("Y", i, 1)])
            return f

        pending = []
        for i in range(NT):
            b = i % 2
            yak, ybk = "ya%d" % b, "yb%d" % b
            for g in range(2):
                cmp_tile(i, g, ya[b], yak)
            for br in ("slc", "win", "swa"):
                if br == "swa":
                    dsts = [(yb[b][:, g * 128:(g + 1) * 128], ybk) for g in range(2)]
                else:
                    dsts = [(ya[b][:, g * 256:(g + 1) * 256], yak) for g in range(2)]
                sts = attn_A2(i, br, dsts)
                if br == "swa":
                    sts[1]["after"] = mk_after(i, b, yak, ybk)
                while pending:
                    attn_B(pending.pop(0))
                pending.extend(sts)
        while pending:
            attn_B(pending.pop(0))

    def phase_rwkv(self, st, l):
        c, d = self.c, self.d
        L = lambda k: d["%s%d" % (k, l)]
        F = d["F"]
        identf, bones, hsel = c["identf"], c["bones"], c["hsel"]
        rv = self.sbt(st, "rv", [128, 20])
        self.dma("sp", rv[:], L("rv"), w=["rv"])
        omka = self.sbt(st, "omka", [128, 2])
        self.ts("dve", omka[:], rv[:, 13:15], -1.0, 1.0, ALU.mult, ALU.add, r=["rv"], w=["omka"])
        lnw = self.sbt(st, "lnw", [128, 256])
        lnb = self.sbt(st, "lnb", [128, 256])
        self.dma("sp", lnw[:], L("lnw"), w=["lnw"])
        self.dma("sp", lnb[:], L("lnb"), w=["lnb"])
        wab = self.sbt(st, "wab", [128, 256], BF16)
        thad = self.sbt(st, "thad", [128, T], BF16)
        if l > 0:
            v1b = self.sbt(st, "v1b", [128, 2, 32], BF16)
            v2b = self.sbt(st, "v2b", [32, 256], BF16)
            t1 = self.sbt(st, "t1", [32, T], BF16)

        def load_shift(dst, dkey, cidx, f, fp, fk, fpk):
            self.dma("sp", f[:, :], F[cidx], r=[("F", cidx)], w=[fk])
            self.memset("pool", fp[:, 0:1], 0.0, w=[fpk])
            self.dma("sp", fp[:, 1:T], F[cidx][:, 0:T - 1], r=[("F", cidx)], w=[fpk])
            self.tt("pool", fp[:, :], fp[:, :], f[:, :], ALU.subtract, r=[fk], w=[fpk])
            self.stt("dve", dst, fp[:, :], rv[:, cidx:cidx + 1], f[:, :], ALU.mult, ALU.add, r=[fpk, fk, "rv"], w=[dkey])

        with ExitStack() as sp:
            f = self.sbt(sp, "f", [128, T])
            fp = self.sbt(sp, "fp", [128, T])
            wdx = self.sbt(sp, "wdx", [128, T])
            wf = self.sbt(sp, "wf", [128, 256])
            self.dma("sp", wf[:], L("wa"), w=["wf"])
            self.cp("dve", wab[:], wf[:], r=["wf"], w=["wab"])
            load_shift(wdx[:, :], "wdx", 6, f, fp, "f", "fp")
            self.act(thad[0:64, :], wdx[0:64, :], AF.Tanh, r=["wdx"], w=["thad"])
            self.cp("dve", thad[64:128, :], wdx[64:128, :], r=["wdx"], w=["thad"])
            if l > 0:
                v1f = self.sbt(sp, "v1f", [128, 2, 32])
                v2f = self.sbt(sp, "v2f", [32, 256])
                vxb = self.sbt(sp, "vxb", [128, T], BF16)
                self.dma("sp", v1f[:], L("v1"), w=["v1f"])
                self.dma("sp", v2f[:], L("v2"), w=["v2f"])
                self.cp("dve", v1b[:], v1f[:], r=["v1f"], w=["v1b"])
                self.cp("dve", v2b[:], v2f[:], r=["v2f"], w=["v2b"])
                banks = [self.rot(0, 8) for _ in range(4)]
                for p in range(2):
                    load_shift(wdx[:, :], "wdx", 4 + p, f, fp, "f", "fp")
                    self.cp("dve", vxb[:], wdx[:], r=["wdx"], w=["vxb"])
                    for tc in range(4):
                        pb, pk = banks[tc]
                        self.mm(pb[0:32, :], v1b[:, p, :], vxb[:, tc * 512:(tc + 1) * 512], p == 0, p == 1, r=["v1b", "vxb"], w=[pk])
                for tc in range(4):
                    pb, pk = banks[tc]
                    self.cp("act", t1[0:32, tc * 512:(tc + 1) * 512], pb[0:32, :], w=[pk, "t1"])
            self.P.fence()

        v2d = lambda t: t[:].rearrange("p n t -> p (n t)")
        for p in range(2):
            ps_ = slice(p * 128, (p + 1) * 128)
            with ExitStack() as sP:
                Rs = self.sbt(sP, "Rs", [128, T])
                AhT = self.sbt(sP, "AhT", [128, T])
                rkb = self.sbt(sP, "rkb", [128, T])
                ArbT = [self.sbt(sP, "ArbT", [128, 16, 128]) for _ in range(2)]
                UV = self.sbt(sP, "UV", [128, 16, 128])
                YV = self.sbt(sP, "YV", [128, 16, 128])
                KVbd = self.sbt(sP, "KVbd", [128, 16, 128])
                tokB0 = self.sbt(sP, "tokB0", [128, 16, 128])
                tokB1 = self.sbt(sP, "tokB1", [128, 16, 128])
                tokV = self.sbt(sP, "tokV", [128, 16, 128])
                ycst = self.sbt(sP, "ycst", [128, 16, 128])
                Gbd = self.sbt(sP, "Gbd", [128, 128])
                gam = self.sbt(sP, "gam", [128, 16])
                sM = ExitStack()
                As = self.sbt(sM, "As", [128, T])
                Ks = self.sbt(sM, "Ks", [128, T])
                Bs = self.sbt(sM, "Bs", [128, T])
                tokA = self.sbt(sM, "tokA", [128, 16, 128])
                tokK = self.sbt(sM, "tokK", [128, 16, 128])
                pkk = lambda n: ("Pk", n // 4)
                Vt, kVt = AhT[:, :], "AhT"
                lw, klw = v2d(ArbT[0]), "ArbT0"
                aT, kaT = v2d(ArbT[1]), "ArbT1"
                kk, kkk = v2d(UV), "UV"
                e1, ke1 = v2d(YV), "YV"
                e2, ke2 = v2d(KVbd), "KVbd"
                e3, ke3 = v2d(ycst), "ycst"
                cl = [(v2d(tokB0), "tokB0"), (v2d(tokV), "tokV")]
                HF = 1024
                H = lambda ap, hf: ap[:, hf * HF:(hf + 1) * HF]
                K2 = lambda k, hf: (k, hf)
                KB2 = lambda k: [(k, 0), (k, 1)]
                cl0, kcl0 = cl[0]
                cl1, kcl1 = cl[1]

                def load_shift2(dst, dkey, cidx, f, fk, fp, fpk):
                    self.dma("sp", f[:, 0:HF], F[cidx][:, 0:HF], r=[("F", cidx)], w=[K2(fk, 0)])
                    self.dma("sp", f[:, HF:T], F[cidx][:, HF:T], r=[("F", cidx)], w=[K2(fk, 1)])
                    self.ts("pool", fp[:, 0:1], f[:, 0:1], -1.0, None, ALU.mult, r=[K2(fk, 0)], w=[K2(fpk, 0)])
                    self.tt("pool", fp[:, 1:HF], f[:, 0:HF - 1], f[:, 1:HF], ALU.subtract, r=[K2(fk, 0)], w=[K2(fpk, 0)])
                    self.tt("pool", fp[:, HF:T], f[:, HF - 1:T - 1], f[:, HF:T], ALU.subtract, r=KB2(fk), w=[K2(fpk, 1)])
                    for hf in range(2):
                        self.stt("dve", H(dst, hf), H(fp, hf), rv[:, cidx:cidx + 1], H(f, hf), ALU.mult, ALU.add,
                                 r=[K2(fpk, hf), K2(fk, hf), "rv"], w=[K2(dkey, hf)])

                load_shift2(Rs[:, :], "Rs", p, e2, ke2, e3, ke3)
                load_shift2(Ks[:, :], "Ks", 2 + p, kk, kkk, e1, ke1)
                load_shift2(Vt, kVt, 4 + p, cl0, kcl0, cl1, kcl1)
                Rsa, Ksa, Bsa, Asa, rkba = Rs[:, :], Ks[:, :], Bs[:, :], As[:, :], rkb[:, :]
                steps = []

                def st_sig(hf):
                    for tc in (2 * hf, 2 * hf + 1):
                        ts_ = slice(tc * 512, (tc + 1) * 512)
                        pb, pk = self.rot(0, 8)
                        self.mm(pb[:, :], wab[0:64, ps_], thad[0:64, ts_], True, True, r=["wab", "thad"], w=[pk])
                        self.act(lw[:, ts_], pb[:, :], AF.Sigmoid, bias=rv[:, 7 + p:8 + p], r=["rv"], w=[pk, K2(klw, hf)])
                        pb, pk = self.rot(0, 8)
                        self.mm(pb[:, :], wab[64:128, ps_], thad[64:128, ts_], True, True, r=["wab", "thad"], w=[pk])
                        self.act(aT[:, ts_], pb[:, :], AF.Sigmoid, bias=rv[:, 9 + p:10 + p], r=["rv"], w=[pk, K2(kaT, hf)])
                        if l > 0:
                            pb, pk = self.rot(0, 8)
                            self.mm(pb[:, :], v2b[0:32, ps_], t1[0:32, ts_], True, True, r=["v2b", "t1"], w=[pk])
                            self.act(e1[:, ts_], pb[:, :], AF.Sigmoid, bias=rv[:, 17 + p:18 + p], r=["rv"], w=[pk, K2(ke1, hf)])
                steps.append(st_sig)
                if l > 0:
                    self.dma("sp", e2, d["vf"][p], r=[("vf", p)], w=KB2(ke2))
                    steps.append(lambda hf: self.tt("pool", H(e2, hf), H(e2, hf), H(Vt, hf), ALU.subtract, r=[K2(kVt, hf)], w=[K2(ke2, hf)]))
                    steps.append(lambda hf: self.tt("dve", H(e2, hf), H(e2, hf), H(e1, hf), ALU.mult, r=[K2(ke1, hf)], w=[K2(ke2, hf)]))
                    steps.append(lambda hf: self.tt("pool", H(Vt, hf), H(Vt, hf), H(e2, hf), ALU.add, r=[K2(ke2, hf)], w=[K2(kVt, hf)]))
                else:
                    self.dma("pool", d["vf"][p], Vt, r=KB2(kVt), w=[("vf", p)])
                steps.append(lambda hf: self.act(H(kk, hf), H(Ksa, hf), AF.Copy, scale=rv[:, 11 + p:12 + p], r=[K2("Ks", hf), "rv"], w=[K2(kkk, hf)]))
                steps.append(lambda hf: self.act(H(e1, hf), H(kk, hf), AF.Square, r=[K2(kkk, hf)], w=[K2(ke1, hf)]))

                def st_norm(hf):
                    for tc in (2 * hf, 2 * hf + 1):
                        ts_ = slice(tc * 512, (tc + 1) * 512)
                        pb, pk = self.rot(0, 8)
                        self.mm(pb[:, :], bones[:, :], e1[:, ts_], True, True, r=["bones", K2(ke1, hf)], w=[pk])
                        self.act(e2[:, ts_], pb[:, :], AF.Sqrt, w=[pk, K2(ke2, hf)])
                steps.append(st_norm)
                steps.append(lambda hf: self.ts("dve", H(e2, hf), H(e2, hf), 1e-12, None, ALU.max, w=[K2(ke2, hf)]))
                steps.append(lambda hf: self.recip(H(e2, hf), H(e2, hf), w=[K2(ke2, hf)]))
                steps.append(lambda hf: self.tt("dve", H(kk, hf), H(kk, hf), H(e2, hf), ALU.mult, r=[K2(ke2, hf)], w=[K2(kkk, hf)]))
                steps.append(lambda hf: self.act(H(e1, hf), H(aT, hf), AF.Identity, bias=omka[:, p:p + 1], scale=rv[:, 13 + p:14 + p],
                                                 r=[K2(kaT, hf), "rv", "omka"], w=[K2(ke1, hf)]))
                steps.append(lambda hf: self.tt("pool", H(Ksa, hf), H(Ksa, hf), H(e1, hf), ALU.mult, r=[K2(ke1, hf)], w=[K2("Ks", hf)]))
                steps.append(lambda hf: self.stt("dve", H(rkba, hf), H(Rsa, hf), rv[:, 15 + p:16 + p], H(Ksa, hf), ALU.mult, ALU.mult,
                                                 r=[K2("Rs", hf), K2("Ks", hf), "rv"], w=[K2("rkb", hf)]))
                steps.append(lambda hf: self.tt("pool", H(Bsa, hf), H(kk, hf), H(aT, hf), ALU.mult, r=[K2(kkk, hf), K2(kaT, hf)], w=[K2("Bs", hf)]))
                chain_src = [(lw, klw)]
                for si, sh in enumerate((1, 2, 4, 8, 16, 32, 64)):
                    dst, dkey = cl[si % 2]
                    src, skey = chain_src[-1]

                    def st_scan(hf, src=src, skey=skey, dst=dst, dkey=dkey, sh=sh):
                        s3 = H(src, hf).rearrange("p (n t) -> p n t", t=128)
                        d3 = H(dst, hf).rearrange("p (n t) -> p n t", t=128)
                        self.cp("act", d3[:, :, 0:sh], s3[:, :, 0:sh], r=[K2(skey, hf)], w=[K2(dkey, hf)])
                        self.tt("dve", d3[:, :, sh:128], s3[:, :, sh:128], s3[:, :, 0:128 - sh], ALU.add, r=[K2(skey, hf)], w=[K2(dkey, hf)])
                    steps.append(st_scan)
                    chain_src.append((dst, dkey))
                csrc, cskey = chain_src[-1]
                CW = -float(np.exp(-0.5))
                steps.append(lambda hf: self.act(H(e1, hf), H(csrc, hf), AF.Exp, scale=CW, r=[K2(cskey, hf)], w=[K2(ke1, hf)]))
                steps.append(lambda hf: self.act(H(e2, hf), H(csrc, hf), AF.Exp, scale=-CW, r=[K2(cskey, hf)], w=[K2(ke2, hf)]))
                steps.append(lambda hf: self.tt("pool", H(e3, hf), H(csrc, hf), H(lw, hf), ALU.subtract, r=[K2(cskey, hf), K2(klw, hf)], w=[K2(ke3, hf)]))
                steps.append(lambda hf: self.act(H(e3, hf), H(e3, hf), AF.Exp, scale=CW, w=[K2(ke3, hf)]))
                steps.append(lambda hf: self.cp("act", gam[:, hf * 8:(hf + 1) * 8], H(e1, hf).rearrange("p (n t) -> p n t", t=128)[:, :, 127],
                                                r=[K2(ke1, hf)], w=[K2("gam", hf)]))
                steps.append(lambda hf: self.tt("dve", H(Rsa, hf), H(Rsa, hf), H(e1, hf), ALU.mult, r=[K2(ke1, hf)], w=[K2("Rs", hf)]))
                steps.append(lambda hf: self.tt("pool", H(Ksa, hf), H(Ksa, hf), H(e2, hf), ALU.mult, r=[K2(ke2, hf)], w=[K2("Ks", hf)]))
                steps.append(lambda hf: self.tt("pool", H(Bsa, hf), H(Bsa, hf), H(e2, hf), ALU.mult, r=[K2(ke2, hf)], w=[K2("Bs", hf)]))
                steps.append(lambda hf: self.stt("dve", H(Asa, hf), H(kk, hf), -1.0, H(e3, hf), ALU.mult, ALU.mult, r=[K2(kkk, hf), K2(ke3, hf)], w=[K2("As", hf)]))
                for stp in steps:
                    for hf in range(2):
                        stp(hf)
                self.memset("pool", tokB0[:], 0.0, w=KB2("tokB0"))
                self.memset("pool", tokB1[:], 0.0, w=["tokB1"])
                for n in range(16):
                    ch = slice(n * 128, (n + 1) * 128)
                    hf = n // 8
                    pb, pk = self.rot(0, 8)
                    self.tr(pb[:, 0:128], Ks[:, ch], identf[:, :], r=[K2("Ks", hf), "identf"], w=[pk])
                    self.tr(pb[:, 128:256], As[:, ch], identf[:, :], r=[K2("As", hf), "identf"], w=[pk])
                    self.tr(pb[:, 256:384], Bs[:, ch], identf[:, :], r=[K2("Bs", hf), "identf"], w=[pk])
                    self.tr(pb[:, 384:512], Vt[:, ch], identf[:, :], r=[K2(kVt, hf), "identf"], w=[pk])
                    self.cp("act", tokK[:, n, :], pb[:, 0:128], w=[pk, "tokK"])
                    self.cp("act", tokA[:, n, :], pb[:, 128:256], w=[pk, "tokA"])
                    self.cp("dve", tokB0[:, n, 0:64], pb[:, 256:320], w=[pk, K2("tokB0", hf)])
                    self.cp("dve", tokB1[:, n, 64:128], pb[:, 320:384], w=[pk, "tokB1"])
                    self.cp("dve", tokV[:, n, :], pb[:, 384:512], w=[pk, K2("tokV", hf)])
                self.P.fence()
                rstage = getattr(self, "rstage", 9)
                if rstage < 3:
                    sM.close()
                    continue
                self.memset("pool", KVbd[:], 0.0, w=["KVbd"])
                self.memset("pool", Gbd[:], 0.0, w=["Gbd"])
                for n0 in range(0, 16, 4):
                    pb, pk = self.rot(0, 8)
                    for q in range(4):
                        self.mm(pb[:, q * 128:(q + 1) * 128], tokK[:, n0 + q, :], tokV[:, n0 + q, :], True, True, r=["tokK", "tokV"], w=[pk])
                    p3 = pb[:].rearrange("p (q t) -> p q t", q=4)
                    for hd in range(2):
                        hs = slice(hd * 64, (hd + 1) * 64)
                        self.cp("act" if hd else "dve", KVbd[hs, n0:n0 + 4, hs], p3[hs, :, hs], w=[pk, "KVbd"])
                sC = ExitStack()
                XT = [ycst, tokK]
                Pk = [self.sbt(sC, "Pk", [128, 16, 128], BF16) for _ in range(2)]
                Nk = [self.sbt(sC, "Nk", [128, 16, 128], BF16) for _ in range(2)]
                XTb = [self.sbt(sC, "XTb", [128, 16, 128], BF16) for _ in range(2)]
                Lk = [self.sbt(sC, "Lk", [128, 4, 2, 128]) for _ in range(2)]
                WV = [self.sbt(sC, "WV", [128, 4, 64]) for _ in range(2)]
                pkk = lambda hd, n: ("Pk", hd, n // 4)
                nkk = lambda hd, n: ("Nk", hd, n // 4)
                xtk = lambda hd, n: ("XT", hd, n // 4)
                xbk = lambda hd, n: ("XTb", hd, n // 4)
                bc4 = lambda m: c[m][:, :].unsqueeze(1).to_broadcast([128, 4, 128])
                q4 = lambda pb: pb[:].rearrange("p (q t) -> p q t", q=4)
                HS = [slice(0, 64), slice(64, 128)]
                ev = 0
                for n0 in range(0, 16, 4):
                    bk = [[self.rot(0, 8) for _ in range(2)] for _ in range(3)]
                    for q in range(4):
                        ch = slice((n0 + q) * 128, (n0 + q + 1) * 128)
                        qs = slice(q * 128, (q + 1) * 128)
                        for which, (lh, rh, lkey, rkey) in enumerate(((As, Bs, "As", "Bs"), (Bs, As, "Bs", "As"), (Bs, Rs, "Bs", "Rs"))):
                            for hd in range(2):
                                hs = HS[hd]
                                self.mm(bk[which][hd][0][:, qs], lh[hs, ch], rh[hs, ch], True, True, r=[lkey, rkey], w=[bk[which][hd][1]])
                    for hd in range(2):
                        xk = [xtk(hd, n0)] + (["tokK"] if hd == 1 else [])
                        self.tt("dve", Pk[hd][:, n0:n0 + 4, :], q4(bk[0][hd][0]), bc4("mSL"), ALU.mult, r=["mSL"], w=[bk[0][hd][1], pkk(hd, n0)])
                        self.tt("dve", XT[hd][:, n0:n0 + 4, :], q4(bk[1][hd][0]), bc4("mSU"), ALU.mult, r=["mSU"], w=[bk[1][hd][1]] + xk)
                        self.tt("dve", ArbT[hd][:, n0:n0 + 4, :], q4(bk[2][hd][0]), bc4("mIU"), ALU.mult, r=["mIU"], w=[bk[2][hd][1], "ArbT%d" % hd])
                        self.cp("act", Nk[hd][:, n0:n0 + 4, :], XT[hd][:, n0:n0 + 4, :], r=[xtk(hd, n0)], w=[nkk(hd, n0)])
                        self.tt("pool", XT[hd][:, n0:n0 + 4, :], XT[hd][:, n0:n0 + 4, :], identf[:, :].unsqueeze(1).to_broadcast([128, 4, 128]),
                                ALU.add, r=["identf"], w=[xtk(hd, n0)])
                        self.cp("act", XTb[hd][:, n0:n0 + 4, :], XT[hd][:, n0:n0 + 4, :], r=[xtk(hd, n0)], w=[xbk(hd, n0)])
                G4 = list(range(0, 16, 4))
                for hd in range(2):
                    for k in range(1, 7):
                        pbanks, nbanks, xbanks = {}, {}, {}
                        for n0 in G4:
                            pbanks[n0] = self.rot(0, 8)
                            for q in range(4):
                                n = n0 + q
                                self.mm(pbanks[n0][0][:, q * 128:(q + 1) * 128], Nk[hd][:, n, :], Pk[hd][:, n, :], True, True,
                                        r=[nkk(hd, n0), pkk(hd, n0)], w=[pbanks[n0][1]])
                        if k <= 5:
                            for n0 in G4:
                                nbanks[n0] = self.rot(0, 8)
                                for q in range(4):
                                    n = n0 + q
                                    self.mm(nbanks[n0][0][:, q * 128:(q + 1) * 128], Pk[hd][:, n, :], Nk[hd][:, n, :], True, True,
                                            r=[nkk(hd, n0), pkk(hd, n0)], w=[nbanks[n0][1]])
                        for n0 in G4:
                            self.cp("act", Pk[hd][:, n0:n0 + 4, :], q4(pbanks[n0][0]), w=[pbanks[n0][1], pkk(hd, n0)])
                        if k <= 5:
                            for n0 in G4:
                                self.cp("act" if (n0 // 4) % 2 else "dve", Nk[hd][:, n0:n0 + 4, :], q4(nbanks[n0][0]), w=[nbanks[n0][1], nkk(hd, n0)])
                        for n0 in G4:
                            xbanks[n0] = self.rot(0, 8)
                            for q in range(4):
                                n = n0 + q
                                self.mm(xbanks[n0][0][:, q * 128:(q + 1) * 128], Pk[hd][:, n, :], XTb[hd][:, n, :], True, True,
                                        r=[pkk(hd, n0), xbk(hd, n0)], w=[xbanks[n0][1]])
                        for n0 in G4:
                            self.tt("dve", XT[hd][:, n0:n0 + 4, :], q4(xbanks[n0][0]), XT[hd][:, n0:n0 + 4, :], ALU.add, w=[xbanks[n0][1], xtk(hd, n0)])
                        if k < 6:
                            for n0 in G4:
                                self.cp("act" if (n0 // 4) % 2 else "pool", XTb[hd][:, n0:n0 + 4, :], XT[hd][:, n0:n0 + 4, :], r=[xtk(hd, n0)], w=[xbk(hd, n0)])
                for n0 in range(0, 16, 4):
                    bL = [self.rot(0, 8) for _ in range(2)]
                    bA = [self.rot(0, 8) for _ in range(2)]
                    for q in range(4):
                        ch = slice((n0 + q) * 128, (n0 + q + 1) * 128)
                        qs = slice(q * 128, (q + 1) * 128)
                        for hd in range(2):
                            self.mm(bL[hd][0][:, qs], Ks[HS[hd], ch], As[HS[hd], ch], True, True, r=["Ks", "As"], w=[bL[hd][1]])
                        for hd in range(2):
                            self.mm(bA[hd][0][:, qs], Ks[HS[hd], ch], Rs[HS[hd], ch], True, True, r=["Ks", "Rs"], w=[bA[hd][1]])
                    for hd in range(2):
                        self.tt("dve", Lk[hd][:, :, 0, :], q4(bL[hd][0]), bc4("mSU"), ALU.mult, r=["mSU"], w=[bL[hd][1], "Lk%d" % hd])
                        self.tt("dve", Lk[hd][:, :, 1, :], q4(bA[hd][0]), bc4("mIU"), ALU.mult, r=["mIU"], w=[bA[hd][1], "Lk%d" % hd])
                    for hd in range(2):
                        hs = HS[hd]
                        lk_, wk_ = "Lk%d" % hd, "WV%d" % hd
                        bw, kw = self.rot(0, 8)
                        bh, kh = self.rot(0, 8)
                        for q in range(4):
                            n = n0 + q
                            self.mm(bw[:, q * 64:(q + 1) * 64], Lk[hd][:, q, 0, :], tokV[:, n, hs], True, True, r=[lk_, "tokV"], w=[kw])
                            self.mm(bw[:, 256 + q * 64:256 + (q + 1) * 64], Lk[hd][:, q, 1, :], tokV[:, n, hs], True, True, r=[lk_, "tokV"], w=[kw])
                            self.mm(bh[:, q * 128:(q + 1) * 128], tokA[:, n, :], XT[hd][:, n, :], True, True, r=["tokA", xtk(hd, n0)], w=[kh])
                        self.cp("act", WV[hd][:, :, :], bw[:, 0:256].rearrange("p (q v) -> p q v", q=4), w=[kw, wk_])
                        self.cp("act", YV[:, n0:n0 + 4, hs], bw[:, 256:512].rearrange("p (q v) -> p q v", q=4), w=[kw, "YV"])
                        self.cp("dve", AhT[hs, n0 * 128:(n0 + 4) * 128], bh[hs, :], w=[kh, "AhT"])
                        bu, ku = self.rot(0, 8)
                        for q in range(4):
                            n = n0 + q
                            self.mm(bu[:, q * 64:(q + 1) * 64], XT[hd][:, n, :], WV[hd][:, q, :], True, True, r=[xtk(hd, n0), wk_], w=[ku])
                        self.cp("act", UV[:, n0:n0 + 4, hs], bu[:, 0:256].rearrange("p (q v) -> p q v", q=4), w=[ku, "UV"])
                self.tt("pool", KVbd[:], KVbd[:], gam[:, :].unsqueeze(2).to_broadcast([128, 16, 128]), ALU.mult, r=["gam"], w=["KVbd"])
                self.P.fence()
                sC.close()
                sM.close()
                cS = ExitStack()
                Usb = [self.sbt(cS, "Usb", [128, 128]) for _ in range(2)]
                Tg = [self.sbt(cS, "Tg", [128, 128]) for _ in range(2)]
                ysqA = self.sbt(cS, "ysqA", [128, 16, 128])
                smA = self.sbt(cS, "smA", [128, 64])
                bon = self.sbt(cS, "bon", [128, 16, 2])
                for n in range(16 if rstage >= 4 else 0):
                    ch = slice(n * 128, (n + 1) * 128)
                    q = n % 2
                    uk = "Usb%d" % q
                    self.stt("dve", Tg[q][:, :], Gbd[:, :], gam[:, n:n + 1], KVbd[:, n, :], ALU.mult, ALU.add, r=["Gbd", "KVbd", "gam"], w=["Tg%d" % q])
                    pbu, pku = self.rot(0, 8)
                    self.mm(pbu[:, 0:128], AhT[:, ch], Gbd[:, :], True, True, r=["AhT", "Gbd"], w=[pku])
                    self.tt("dve", Usb[q][:, :], pbu[:, 0:128], UV[:, n, :], ALU.add, r=["UV"], w=[pku, uk])
                    pby, pky = self.rot(0, 8)
                    self.mm(pby[:, 0:128], Rs[:, ch], Gbd[:, :], True, False, r=["Rs", "Gbd"], w=[pky])
                    self.mm(pby[:, 0:64], ArbT[0][:, n, :], Usb[q][:, 0:64], False, False, r=["ArbT0", uk], w=[pky])
                    self.mm(pby[:, 64:128], ArbT[1][:, n, :], Usb[q][:, 64:128], False, True, r=["ArbT1", uk], w=[pky])
                    self.mm(pby[:, 128:130], rkb[:, ch], hsel[:, 0:2], True, True, r=["rkb", "hsel"], w=[pky])
                    pbg, pkg = self.rot(0, 8)
                    self.mm(pbg[:, 0:64], tokB0[:, n, :], Usb[q][:, 0:64], True, True, r=["tokB0", uk], w=[pkg])
                    self.mm(pbg[:, 64:128], tokB1[:, n, :], Usb[q][:, 64:128], True, True, r=["tokB1", uk], w=[pkg])
                    self.stt("dve", Gbd[:, :], pbg[:, 0:128], gam[:, n:n + 1], Tg[q][:, :], ALU.mult, ALU.add, r=["gam", "Tg%d" % q], w=[pkg, "Gbd"])
                    self.tt("dve", ycst[:, n, :], pby[:, 0:128], YV[:, n, :], ALU.add, r=["YV"], w=[pky, ("ycst", n)])
                    self.cp("act", bon[:, n, :], pby[:, 128:130], w=[pky, ("bon", n)])
                if rstage >= 4:
                    yk_all = [("ycst", n) for n in range(16)]
                    y4 = ycst[:].rearrange("p n (h c) -> p (n h) c", c=64)
                    yf = ycst[:].rearrange("p n c -> p (n c)")
                    sq4 = ysqA[:].rearrange("p n (h c) -> p (n h) c", c=64)
                    bc32 = lambda t: t.unsqueeze(2).to_broadcast([128, 32, 64])
                    self.red(smA[:, 0:32], y4, ALU.add, r=yk_all, w=["smA"])
                    self.ts("dve", smA[:, 0:32], smA[:, 0:32], -1.0 / 64, None, ALU.mult, w=["smA"])
                    self.tt("dve", y4, y4, bc32(smA[:, 0:32]), ALU.add, r=["smA"], w=yk_all)
                    self.tt("pool", ysqA[:], ycst[:], ycst[:], ALU.mult, r=yk_all, w=["ysqA"])
                    self.red(smA[:, 32:64], sq4, ALU.add, r=["ysqA"], w=["smA"])
                    self.ts("dve", smA[:, 32:64], smA[:, 32:64], 1.0 / 64, 64e-5, ALU.mult, ALU.add, w=["smA"])
                    self.act(smA[:, 32:64], smA[:, 32:64], AF.Sqrt, w=["smA"])
                    self.recip(smA[:, 32:64], smA[:, 32:64], w=["smA"])
                    self.tt("dve", y4, y4, bc32(smA[:, 32:64]), ALU.mult, r=["smA"], w=yk_all)
                    self.tt("pool", ycst[:], ycst[:], lnw[:, ps_].unsqueeze(1).to_broadcast([128, 16, 128]), ALU.mult, r=["lnw"], w=yk_all)
                    self.tt("pool", ycst[:], ycst[:], lnb[:, ps_].unsqueeze(1).to_broadcast([128, 16, 128]), ALU.add, r=["lnb"], w=yk_all)
                    self.tt("dve", sq4, tokV[:].rearrange("p n (h c) -> p (n h) c", c=64),
                            bc32(bon[:].rearrange("p n h -> p (n h)")), ALU.mult, r=["tokV"] + [("bon", n) for n in range(16)], w=["ysqA"])
                    self.tt("pool", ycst[:], ycst[:], ysqA[:], ALU.add, r=["ysqA"], w=yk_all)
                    for n in range(16):
                        self.dma("pool", d["Y"][n * 128:(n + 1) * 128, 768 + p * 128:768 + (p + 1) * 128], ycst[:, n, :],
                                 r=[("ycst", n)], w=[("Y", "c", p, n)])
                self.P.fence()
                cS.close()

    def phase_out(self, st, l, xin, xkey, xo, xokey):
        c, d = self.c, self.d
        L = lambda k: d["%s%d" % (k, l)]
        identf = c["identf"]
        final = (l == DEPTH - 1)
        pwb = self.sbt(st, "pwb", [128, 16, 1024], BF16)
        pst = [self.sbt(st, "pst", [128, 2, 1024]) for _ in range(2)]
        Yt = [self.sbt(st, "Yt", [128, 1024]) for _ in range(2)]
        Zt = [self.sbt(st, "Zt", [128, 1024]) for _ in range(2)]
        Gt = [self.sbt(st, "Gt", [128, 3072]) for _ in range(2)]
        xt = [self.sbt(st, "xt", [128, 1024]) for _ in range(2)]
        yzT = [self.sbt(st, "yzT", [128, 8, 128], BF16) for _ in range(2)]
        mixed = [self.sbt(st, "mixed", [128, 1024]) for _ in range(2)]
        mixb = [self.sbt(st, "mixb", [128, 1024], BF16) for _ in range(2)]
        yzb = [self.sbt(st, "yzb", [128, 1024], BF16) for _ in range(2)]
        mxT = [self.sbt(st, "mxT", [128, 8, 128], BF16) for _ in range(2)]
        tmpa = [self.sbt(st, "tmpa", [128, 512]) for _ in range(2)]
        xn = [self.sbt(st, "xn", [128, 1024]) for _ in range(2)]
        jk = self.sbt(st, "jkb", [128, 1024], BF16)
        ss = [self.sbt(st, "ss", [128, 1]) for _ in range(2)]
        mhalf = self.sbt(st, "mhalf", [128, 1])
        self.memset("pool", mhalf[:], -0.5, w=["mhalf"])
        self._ta = 0

        def stage_A(i):
            b = i % 2
            rs = slice(i * 128, (i + 1) * 128)
            ky, kz = "Yt%d" % b, "Zt%d" % b
            self.dma("sp", Yt[b][:], d["Y"][rs, :], r=[("Y", i, 0), ("Y", i, 1)] + [("Y", "c", pp, i) for pp in range(2)], w=[ky])
            self.dma("sp", Zt[b][:], d["Z"][rs, :], r=[("Z", i, 0), ("Z", i, 1)], w=[kz])
            kyb = "yzb%d" % b
            self.tt("pool", yzb[b][:], Yt[b][:], Zt[b][:], ALU.mult, r=[kz, ky], w=[kyb])
            pb, pk = self.rot(0, 8)
            pbb = pb[:].bitcast(BF16)
            for ch in range(8):
                self.tr(pbb[:, ch * 128:(ch + 1) * 128], yzb[b][:, ch * 128:(ch + 1) * 128], c["identb"][:], r=[kyb, "identb"], w=[pk])
            self.cp("act", yzT[b][:, :, :], pbb.rearrange("p (c t) -> p c t", c=8), w=[pk, "yzT%d" % b])

        def stage_B(i):
            b = i % 2
            rs = slice(i * 128, (i + 1) * 128)
            kg = "Gt%d" % b
            self.dma("sp", Gt[b][:], d["G"][rs, :], r=[("G", i, m) for m in range(6)], w=[kg])
            for half in range(2):
                cs = slice(half * 512, (half + 1) * 512)
                for bi, (k0, k1) in enumerate(((0, 4), (4, 6), (6, 8))):
                    pb, pk = self.rot(0, 8)
                    for kc in range(k0, k1):
                        self.mm(pb[:, :], yzT[b][:, kc, :], pwb[:, kc, cs], kc == k0, kc == k1 - 1, r=["yzT%d" % b, "pwb"], w=[pk])
                    gsl = Gt[b][:, bi * 1024 + half * 512:bi * 1024 + (half + 1) * 512]
                    if bi == 0:
                        self.tt("dve", mixed[b][:, cs], pb[:, :], gsl, ALU.mult, r=[kg], w=[pk, "mixed%d" % b])
                    else:
                        t_ = self._ta % 2
                        self._ta += 1
                        self.tt("dve", tmpa[t_][:], pb[:, :], gsl, ALU.mult, r=[kg], w=[pk, "tmpa%d" % t_])
                        if bi == 1:
                            self.tt("pool", mixed[b][:, cs], mixed[b][:, cs], tmpa[t_][:], ALU.add, r=["tmpa%d" % t_], w=["mixed%d" % b])
                        else:
                            self.tt("pool", mixb[b][:, cs], mixed[b][:, cs], tmpa[t_][:], ALU.add, r=["tmpa%d" % t_, "mixed%d" % b], w=["mixb%d" % b])

        def stage_CD(i):
            b = i % 2
            rs = slice(i * 128, (i + 1) * 128)
            kx = "xt%d" % b
            self.dma("sp", xt[b][:], xin[rs, :], r=[(xkey, i)], w=[kx])
            pb, pk = self.rot(0, 8)
            pbb = pb[:].bitcast(BF16)
            for ch in range(8):
                self.tr(pbb[:, ch * 128:(ch + 1) * 128], mixb[b][:, ch * 128:(ch + 1) * 128], c["identb"][:], r=["mixb%d" % b, "identb"], w=[pk])
            self.cp("act", mxT[b][:, :, :], pbb.rearrange("p (c t) -> p c t", c=8), w=[pk, "mxT%d" % b])
            for half in range(2):
                cs = slice(half * 512, (half + 1) * 512)
                pb, pk = self.rot(0, 8)
                for kc in range(8):
                    self.mm(pb[:, :], mxT[b][:, kc, :], pwb[:, 8 + kc, cs], kc == 0, kc == 7, r=["mxT%d" % b, "pwb"], w=[pk])
                self.tt("dve", xn[b][:, cs], pb[:, :], xt[b][:, cs], ALU.add, r=[kx], w=[pk, "xn%d" % b])
            if final:
                kss = "ss%d" % b
                self.memset("pool", ss[b][:], 0.0, w=[kss])
                self.act(jk[:], xn[b][:], AF.Square, accum=ss[b][:], r=["xn%d" % b], w=["jkb", kss])
                self.ts("dve", ss[b][:], ss[b][:], 1.0 / D, 1e-6, ALU.mult, ALU.add, w=[kss])
                self.tt("pool", ss[b][:], ss[b][:], mhalf[:], ALU.pow, r=["mhalf"], w=[kss])
                self.stt("dve", xn[b][:], xn[b][:], ss[b][:, 0:1], c["fg"][:], ALU.mult, ALU.mult, r=[kss, "fg"], w=["xn%d" % b])
            self.dma("pool", xo[rs, :], xn[b][:], r=["xn%d" % b], w=[(xokey, i)])

        stage_A(0)
        stage_A(1)
        for q in range(8):
            b = q % 2
            self.dma("sp", pst[b][:], L("pw")[:, 2 * q:2 * q + 2, :], w=["pst%d" % b])
            self.cp("dve", pwb[:, 2 * q, :], pst[b][:, 0, :], r=["pst%d" % b], w=["pwb"])
            self.cp("act" if q % 2 else "pool", pwb[:, 2 * q + 1, :], pst[b][:, 1, :], r=["pst%d" % b], w=["pwb"])
        stage_B(0)
        for i in range(NT):
            if i + 2 < NT:
                stage_A(i + 2)
            if i + 1 < NT:
                stage_B(i + 1)
            stage_CD(i)


FUSED = True
_CACHE = {}


def _get_nc(layers, debug=False):
    key = (tuple(layers), debug)
    if key not in _CACHE:
        kb = KB(layers, debug)
        _CACHE[key] = kb.build()
    return _CACHE[key]


def _run(layers, inp, xs, vfs, debug=False):
    nc = _get_nc(layers, debug)
    perm = _perm()
    consts = _consts()
    base = dict(consts)
    base["fg"] = np.ascontiguousarray(np.broadcast_to(np.asarray(inp["final_g"], np.float32).reshape(1, D), (128, D)))
    for l in layers:
        for k, v in prep_layer(inp, l, perm).items():
            base["%s%d" % (k, l)] = v
    maps = []
    for b in range(8):
        m = dict(base)
        m["x"] = np.ascontiguousarray(xs[b], dtype=np.float32)
        if vfs is not None:
            m["vfirst"] = np.ascontiguousarray(vfs[b], dtype=np.float32)
        maps.append(m)
    res = run_bass_kernel_spmd(nc, maps, core_ids=list(range(8)))
    return res.results


def kernel(**inputs):
    inp = {k: np.asarray(v) for k, v in inputs.items()}
    x = np.asarray(inp["x"], np.float32)
    if FUSED:
        r = _run([0, 1], inp, [x[b] for b in range(8)], None)
        return np.stack([r[b]["xout"] for b in range(8)], 0).astype(np.float32)
    r0 = _run([0], inp, [x[b] for b in range(8)], None)
    r1 = _run([1], inp, [r0[b]["xout"] for b in range(8)], [r0[b]["vfirst"] for b in range(8)])
    return np.stack([r1[b]["xout"] for b in range(8)], 0).astype(np.float32)
```

```python
from contextlib import ExitStack
import numpy as np
import ml_dtypes
import concourse.bass as bass
import concourse.mybir as mybir
from concourse.bass_utils import run_bass_kernel_spmd

F32 = mybir.dt.float32
BF16 = mybir.dt.bfloat16
AF = mybir.ActivationFunctionType
ALU = mybir.AluOpType
AX = mybir.AxisListType

ENGINES = ("sp", "act", "dve", "pool", "pe")
INORDER_ENGINES = ("pe", "sp")
NDMASEM = 12


class _Op:
    __slots__ = ("eng", "fn", "deps", "dma", "sig", "sem", "val", "prev", "idx")


class Prog:
    def __init__(self, nc):
        self.nc = nc
        self.ops = []
        self.lastw = {}
        self.readers = {}

    def add(self, eng, fn, r=(), w=(), dma=False):
        o = _Op()
        o.eng, o.fn, o.dma, o.sig = eng, fn, dma, False
        o.idx = len(self.ops)
        deps = set()
        for k in r:
            if k in self.lastw:
                deps.add(self.lastw[k])
        for k in w:
            if k in self.lastw:
                deps.add(self.lastw[k])
            deps.update(self.readers.get(k, ()))
        o.deps = deps
        for k in r:
            lst = self.readers.setdefault(k, [])
            if not dma:
                lst[:] = [i for i in lst if self.ops[i].dma or self.ops[i].eng != eng]
            lst.append(o.idx)
        for k in w:
            self.lastw[k] = o.idx
            self.readers[k] = []
        self.ops.append(o)
        return o

    def barrier_keys(self):
        return list(self.lastw.keys())

    def _skip(self, d, o):
        if d.dma:
            return False
        if d.eng == "sp":
            return True
        if d.eng == o.eng and o.eng in INORDER_ENGINES:
            return True
        return False

    def emit(self, stack):
        nc = self.nc
        ops = self.ops
        for o in ops:
            for di in o.deps:
                d = ops[di]
                if not self._skip(d, o):
                    d.sig = True
        engsem = {e: stack.enter_context(nc.semaphore("s_" + e)) for e in ENGINES if e != "sp"}
        dmasem = {e: [stack.enter_context(nc.semaphore("d_%s%d" % (e, i))) for i in range(NDMASEM)]
                  for e in ("sp", "pool", "act")}
        cnt = {e: 0 for e in ENGINES}
        dcnt = {e: 0 for e in ENGINES}
        per = {e: [] for e in ENGINES}
        for o in ops:
            per[o.eng].append(o)
            if o.dma:
                n = dcnt[o.eng]
                dcnt[o.eng] += 1
                o.sem = dmasem[o.eng][n % NDMASEM]
                o.val = 16 * (n // NDMASEM + 1)
                o.prev = 16 * (n // NDMASEM)
            elif o.sig:
                cnt[o.eng] += 1
                o.sem = engsem[o.eng]
                o.val = cnt[o.eng]
        self.stats = dict(cnt=cnt, dcnt=dcnt, n={e: len(per[e]) for e in ENGINES})

        def run(e, eng):
            waited = {}
            nw = 0
            for o in per[eng]:
                for di in sorted(o.deps):
                    d = ops[di]
                    if self._skip(d, o):
                        continue
                    key = id(d.sem)
                    if waited.get(key, 0) >= d.val:
                        continue
                    e.wait_ge(d.sem, d.val)
                    nw += 1
                    waited[key] = d.val
                if o.dma and o.prev > 0 and waited.get(id(o.sem), 0) < o.prev:
                    e.wait_ge(o.sem, o.prev)
                    waited[id(o.sem)] = o.prev
                ins = o.fn(e)
                if o.dma:
                    ins.then_inc(o.sem, 16)
                elif o.sig:
                    ins.then_inc(o.sem, 1)
            self.stats.setdefault("waits", {})[eng] = nw

        with nc.Block() as block:
            @block.sync
            def _(e):
                run(e, "sp")

            @block.scalar
            def _(e):
                run(e, "act")

            @block.vector
            def _(e):
                run(e, "dve")

            @block.gpsimd
            def _(e):
                run(e, "pool")

            @block.tensor
            def _(e):
                run(e, "pe")

    def fence(self):
        start = getattr(self, "_fpos", 0)
        deps = set(o.idx for o in self.ops[start:] if o.dma)
        last = {}
        for o in self.ops:
            last[o.eng] = o.idx
        deps.update(last.values())
        for e in ENGINES:
            o = self.add(e, lambda eng: eng.nop())
            o.deps = set(deps)
        self._fpos = len(self.ops)


T = 2048
D = 1024
NT = 16
N_IN = 6808
DEPTH = 2
BIG = 30000.0
NCMP = 127
SEG = dict(a_q=(0, 512), a_kv_cmp=(512, 256), a_kv_slc=(768, 256), a_kv_win=(1024, 256), a_gate=(1280, 24),
           a_z=(1304, 512), b_q=(1816, 256), b_kv=(2072, 256), b_z=(2328, 256), c_shift=(2584, 896),
           c_z=(3480, 256), merge=(3736, 3072))
NFEAT = 18 * 128
TOK0 = NFEAT


def _perm():
    def seg(name, a, b):
        o = SEG[name][0]
        return list(range(o + a, o + b))
    p = []
    for hh in range(4):
        p += seg("a_q", hh * 64, hh * 64 + 64) + seg("a_q", (4 + hh) * 64, (4 + hh) * 64 + 64)
    p += seg("a_kv_cmp", 0, 128)
    p += seg("a_kv_cmp", 128, 256)
    p += seg("a_kv_slc", 0, 128)
    p += seg("a_kv_win", 0, 128)
    p += seg("b_q", 0, 64) + seg("b_q", 128, 192)
    p += seg("b_q", 64, 128) + seg("b_q", 192, 256)
    p += seg("b_kv", 0, 128)
    p += seg("c_shift", 0, 896)
    assert len(p) == NFEAT
    p += seg("a_kv_slc", 128, 256) + seg("a_kv_win", 128, 256) + seg("b_kv", 128, 256) + seg("a_gate", 0, 24)
    p += seg("a_z", 0, 512) + seg("b_z", 0, 256) + seg("c_z", 0, 256)
    p += seg("merge", 0, 3072)
    assert len(p) == N_IN and len(set(p)) == N_IN
    return np.array(p)


def _consts():
    c = {}
    c["identf"] = np.eye(128, dtype=np.float32)
    s = np.arange(128)[:, None]
    t = np.arange(128)[None, :]
    c["triA"] = np.where(s <= t, 0.0, -BIG).astype(np.float32)
    c["triB"] = np.where(s > t, 0.0, -BIG).astype(np.float32)
    E = np.zeros((32, 16, 128), np.float32)
    for j in range(16):
        for p in range(128):
            E[2 * j + p // 64, j, p] = BIG
    E2 = np.zeros((128, 2048), np.float32)
    E2[0:32] = E.reshape(32, 2048)
    E2[64:96] = E.reshape(32, 2048)
    c["Eall"] = E2
    n = np.arange(128)[:, None]
    tt = np.arange(T)[None, :]
    c["penc"] = np.where(16 * n + 31 <= tt, 0.0, -BIG).astype(np.float32)
    Mi = np.zeros((128, 16, 32), np.float32)
    Bc = np.zeros((128, 16, 32), np.float32)
    for i in range(16):
        for p in range(128):
            cur = 2 * i + p // 64
            for j in range(32):
                if j == 0:
                    Bc[p, i, j] = 10.0
                elif j == cur:
                    Bc[p, i, j] = 20.0
                elif j == cur - 1:
                    Bc[p, i, j] = 30.0
                elif j > cur:
                    Bc[p, i, j] = -1.0 - j
                else:
                    Mi[p, i, j] = 1.0
            if cur == 0:
                Bc[p, i, 0] = 20.0
            if cur == 1:
                Bc[p, i, 0] = 30.0
    c["Mi"] = Mi.reshape(128, 512)
    c["Bc"] = Bc.reshape(128, 512)
    ci = np.arange(128)[:, None] * 16
    sj = np.arange(32)[None, :] * 64
    c["ovl"] = ((ci < sj + 64) & (ci + 32 > sj)).astype(np.float32)
    r = np.arange(128)[:, None]
    q = np.arange(128)[None, :]
    c["mSL"] = (q < r).astype(np.float32)
    c["mSU"] = (r < q).astype(np.float32)
    c["mIU"] = (r <= q).astype(np.float32)
    c["bones"] = ((r // 64) == (q // 64)).astype(np.float32)
    c["hsel"] = ((np.arange(128)[:, None] // 64) == np.arange(2)[None, :]).astype(np.float32)
    return c


CONST_SHAPES = dict(identf=[128, 128], triA=[128, 128], triB=[128, 128], Eall=[128, 2048], penc=[128, 2048],
                    Mi=[128, 512], Bc=[128, 512], ovl=[128, 32], mSL=[128, 128], mSU=[128, 128], mIU=[128, 128],
                    bones=[128, 128], hsel=[128, 2])


def layer_input_shapes(l):
    s = dict(win=[D, N_IN], ng=[128, 8], ngb=[128, D], bmerge=[128, 3072], w1k=[128, 32, 128], w1v=[128, 32, 128],
             pek=[128, 32], pev=[128, 32], w2k=[128, 64], w2v=[128, 64], sinks=[128, 4], rv=[128, 20],
             wa=[128, 256], lnw=[128, 256], lnb=[128, 256], pw=[128, 16, 1024])
    if l > 0:
        s["v1"] = [128, 2, 32]
        s["v2"] = [32, 256]
    return s


def prep_layer(inp, l, perm):
    f = lambda a: np.ascontiguousarray(a, dtype=np.float32)
    o = {}
    o["win"] = f(inp["w_in"][l][:, perm])
    o["ng"] = f(inp["norm_g"][l].reshape(8, 128).T)
    o["ngb"] = f(np.broadcast_to(inp["norm_g"][l].reshape(1, D), (128, D)))
    o["bmerge"] = f(np.broadcast_to(inp["b_merge"][l].reshape(1, 3072), (128, 3072)))
    for nm, src in (("w1k", "cmp_w1_k"), ("w1v", "cmp_w1_v")):
        w = inp[src][l].reshape(32, 64, 128).transpose(1, 0, 2)
        o[nm] = f(np.concatenate([w, w], 0))
    for nm, src in (("pek", "cmp_pe_k"), ("pev", "cmp_pe_v")):
        p = inp[src][l].T
        o[nm] = f(np.concatenate([p, p], 0))
    o["w2k"] = f(inp["cmp_w2_k"][l])
    o["w2v"] = f(inp["cmp_w2_v"][l])
    o["sinks"] = f(np.broadcast_to(inp["swa_sinks"][l].reshape(1, 4), (128, 4)))
    rv = np.zeros((128, 20), np.float32)
    rv[:, 0:7] = inp["rwkv_mu"][l].reshape(7, 128).T
    for k, nm in enumerate(("rwkv_w0", "rwkv_a0", "rwkv_k_k", "rwkv_k_a")):
        rv[:, 7 + 2 * k:9 + 2 * k] = inp[nm][l].reshape(2, 128).T
    rv[:, 15:17] = inp["rwkv_r_k"][l].reshape(2, 128).T
    if l > 0:
        rv[:, 17:19] = inp["rwkv_v0"][l - 1].reshape(2, 128).T
    o["rv"] = rv
    o["wa"] = f(np.concatenate([inp["rwkv_w2"][l], inp["rwkv_a2"][l]], 0))
    o["lnw"] = f(np.broadcast_to(inp["rwkv_ln_w"][l].reshape(1, 256), (128, 256)))
    o["lnb"] = f(np.broadcast_to(inp["rwkv_ln_b"][l].reshape(1, 256), (128, 256)))
    pw = np.concatenate([inp["proj_a"][l].reshape(4, 128, 1024), inp["proj_b"][l].reshape(2, 128, 1024),
                         inp["proj_c"][l].reshape(2, 128, 1024), inp["w_out"][l].reshape(8, 128, 1024)], 0)
    o["pw"] = f(pw.transpose(1, 0, 2))
    if l > 0:
        o["v1"] = f(inp["rwkv_v1"][l - 1].reshape(2, 128, 32).transpose(1, 0, 2))
        o["v2"] = f(inp["rwkv_v2"][l - 1])
    return o


class KB:
    def __init__(self, layers, debug=False, upto="out"):
        self.layers = list(layers)
        self.debug = debug
        self.upto = upto
        nc = bass.Bass("TRN2", target_bir_lowering=False)
        nc.allow_low_precision("bf16 matmul operands, fp32 accumulation")
        self.nc = nc
        self.P = Prog(nc)
        self.rotc = {}

    def dma(self, eng, out, in_, r=(), w=()):
        return self.P.add(eng, lambda e: e.dma_start(out=out, in_=in_), r=r, w=w, dma=True)

    def mm(self, out, lhsT, rhs, start, stop, r=(), w=()):
        return self.P.add("pe", lambda e: e.matmul(out, lhsT=lhsT, rhs=rhs, start=start, stop=stop,
                                                   skip_group_check=True), r=r, w=w)

    def tr(self, out, in_, ident, r=(), w=()):
        return self.P.add("pe", lambda e: e.transpose(out=out, in_=in_, identity=ident), r=r, w=w)

    def act(self, out, in_, func, r=(), w=(), bias=None, scale=None, accum=None):
        kw = {}
        if bias is not None:
            kw["bias"] = bias
        if scale is not None:
            kw["scale"] = scale
        if accum is not None:
            kw["accum_out"] = accum
        return self.P.add("act", lambda e: e.activation(out=out, in_=in_, func=func, **kw), r=r, w=w)

    def tt(self, eng, out, in0, in1, op, r=(), w=()):
        return self.P.add(eng, lambda e: e.tensor_tensor(out=out, in0=in0, in1=in1, op=op), r=r, w=w)

    def ts(self, eng, out, in0, s1, s2, op0, op1=None, r=(), w=()):
        if op1 is None:
            return self.P.add(eng, lambda e: e.tensor_scalar(out=out, in0=in0, scalar1=s1, scalar2=None, op0=op0), r=r, w=w)
        return self.P.add(eng, lambda e: e.tensor_scalar(out=out, in0=in0, scalar1=s1, scalar2=s2, op0=op0, op1=op1), r=r, w=w)

    def stt(self, eng, out, in0, scalar, in1, op0, op1, r=(), w=()):
        return self.P.add(eng, lambda e: e.scalar_tensor_tensor(out=out, in0=in0, scalar=scalar, in1=in1, op0=op0, op1=op1), r=r, w=w)

    def cp(self, eng, out, in_, r=(), w=()):
        if eng == "act":
            return self.P.add("act", lambda e: e.copy(out=out, in_=in_), r=r, w=w)
        return self.P.add(eng, lambda e: e.tensor_copy(out=out, in_=in_), r=r, w=w)

    def memset(self, eng, ap, val, w=()):
        return self.P.add(eng, lambda e: e.memset(ap, val), w=w)

    def recip(self, out, in_, r=(), w=()):
        return self.P.add("dve", lambda e: e.reciprocal(out=out, in_=in_), r=r, w=w)

    def red(self, out, in_, op, r=(), w=()):
        return self.P.add("dve", lambda e: e.tensor_reduce(out=out, in_=in_, axis=AX.X, op=op), r=r, w=w)

    def rot(self, lo, hi):
        k = (lo, hi)
        c = self.rotc.get(k, 0)
        self.rotc[k] = c + 1
        b = lo + c % (hi - lo)
        return self.ps[b], "ps%d" % b

    def sbt(self, st, name, shape, dt=F32):
        self._nm = getattr(self, "_nm", 0) + 1
        return st.enter_context(self.nc.sbuf_tensor("%s_%d" % (name, self._nm), shape, dt))

    def build(self):
        nc = self.nc
        layers = self.layers
        dk = "ExternalOutput" if self.debug else "Internal"
        self.d = {}
        self.d["x"] = nc.dram_tensor("x", [T, D], F32, kind="ExternalInput").ap()
        for k, shp in CONST_SHAPES.items():
            self.d[k] = nc.dram_tensor(k, shp, F32, kind="ExternalInput").ap()
        self.d["fg"] = nc.dram_tensor("fg", [128, D], F32, kind="ExternalInput").ap()
        for l in layers:
            for k, shp in layer_input_shapes(l).items():
                self.d["%s%d" % (k, l)] = nc.dram_tensor("%s%d" % (k, l), shp, F32, kind="ExternalInput").ap()
        self.d["Z"] = nc.dram_tensor("scrZ", [T, 1024], F32, kind=dk).ap()
        self.d["G"] = nc.dram_tensor("scrG", [T, 3072], F32, kind=dk).ap()
        self.d["Y"] = nc.dram_tensor("scrY", [T, 1024], F32, kind=dk).ap()
        self.d["F"] = nc.dram_tensor("scrF", [7, 128, T], F32, kind=dk).ap()
        if 0 in layers and 1 in layers:
            self.d["vf"] = nc.dram_tensor("vfirst", [2, 128, T], F32, kind=dk).ap()
        elif 0 in layers:
            self.d["vf"] = nc.dram_tensor("vfirst", [2, 128, T], F32, kind="ExternalOutput").ap()
        else:
            self.d["vf"] = nc.dram_tensor("vfirst", [2, 128, T], F32, kind="ExternalInput").ap()
        self.d["xout"] = nc.dram_tensor("xout", [T, D], F32, kind="ExternalOutput").ap()
        if len(layers) > 1:
            self.d["xmid"] = nc.dram_tensor("xmid", [T, D], F32, kind=dk).ap()

        with ExitStack() as st:
            self.ps = [st.enter_context(nc.psum_tensor("psb%d" % b, [128, 512], F32)) for b in range(8)]
            self.load_consts(st)
            xin = self.d["x"]
            xkey = "x"
            for li, l in enumerate(layers):
                last = (li == len(layers) - 1)
                xo = self.d["xout"] if last else self.d["xmid"]
                xokey = "xout" if last else "xmid"
                self.layer(l, xin, xkey, xo, xokey)
                xin, xkey = xo, xokey
            self.P.fence()
            self.P.emit(st)
        return nc

    def load_consts(self, st):
        d = self.d
        c = self.c = {}
        f32c = ["identf", "Mi", "Bc", "mSL", "mSU", "mIU", "bones", "hsel"]
        for k in f32c:
            c[k] = self.sbt(st, k, CONST_SHAPES[k])
            self.dma("sp", c[k][:], d[k], w=[k])
        c["fg"] = self.sbt(st, "fg", [128, D])
        self.dma("sp", c["fg"][:], d["fg"], w=["fg"])
        bl = ["identf", "triA", "triB", "Eall", "penc", "ovl"]
        for k in bl:
            nm = "identb" if k == "identf" else k
            c[nm] = self.sbt(st, nm, CONST_SHAPES[k], BF16)
        with ExitStack() as s2:
            for k in bl:
                nm = "identb" if k == "identf" else k
                tmp = self.sbt(s2, k + "_f", CONST_SHAPES[k])
                self.dma("sp", tmp[:], d[k], w=[k + "_f"])
                self.cp("pool", c[nm][:], tmp[:], r=[k + "_f"], w=[nm])
            self.P.fence()

    def layer(self, l, xin, xkey, xo, xokey):
        with ExitStack() as sA:
            A = self.alloc_attn(sA)
            with ExitStack() as s1:
                self.phase01(s1, l, xin, xkey, A)
                self.P.fence()
            if self.upto in ("p0", "p1"):
                return
            with ExitStack() as s2:
                self.phase_attn(s2, l, A)
                self.P.fence()
        if self.upto in ("cmp", "attn"):
            return
        with ExitStack() as s3:
            self.phase_rwkv(s3, l)
            self.P.fence()
        if self.upto == "rwkv":
            return
        with ExitStack() as s4:
            self.phase_out(s4, l, xin, xkey, xo, xokey)
            self.P.fence()

    def alloc_attn(self, st):
        A = {}
        A["qTA"] = self.sbt(st, "qTA", [128, 4, T], BF16)
        A["qTB"] = self.sbt(st, "qTB", [128, 2, T], BF16)
        for k in ("kTc", "vTc", "kTs", "kTw", "kTb"):
            A[k] = self.sbt(st, k, [128, T], BF16)
        for k in ("Vs", "Vw", "Vb"):
            A[k] = self.sbt(st, k, [128, 16, 2, 68], BF16)
        A["gate"] = self.sbt(st, "gate", [128, 16, 24])
        return A

    def phase01(self, st, l, xin, xkey, A):
        c, d = self.c, self.d
        L = lambda k: d["%s%d" % (k, l)]
        xnT = self.sbt(st, "xnT", [128, 8, T], BF16)
        ng = self.sbt(st, "ng", [128, 8])
        self.dma("sp", ng[:], L("ng"), w=["ng"])
        wst = [self.sbt(st, "wst", [128, 8, 512]) for _ in range(2)]
        wbf = [self.sbt(st, "wbf", [128, 8, 512], BF16) for _ in range(2)]
        bm = self.sbt(st, "bm", [128, 3072])
        win = L("win").rearrange("(k p) n -> p k n", p=128)
        self._blk = 0

        def load_block(c0, ncol):
            b = self._blk % 2
            self._blk += 1
            for kc in range(8):
                self.dma("sp", wst[b][:, kc, 0:ncol], win[:, kc, c0:c0 + ncol], w=["wst%d" % b])
            self.cp("dve", wbf[b][:, 0:4, 0:ncol], wst[b][:, 0:4, 0:ncol], r=["wst%d" % b], w=["wbf%d" % b])
            self.cp("act", wbf[b][:, 4:6, 0:ncol], wst[b][:, 4:6, 0:ncol], r=["wst%d" % b], w=["wbf%d" % b])
            self.cp("pool", wbf[b][:, 6:8, 0:ncol], wst[b][:, 6:8, 0:ncol], r=["wst%d" % b], w=["wbf%d" % b])
            return wbf[b], "wbf%d" % b

        pre = {}
        s0 = ExitStack()
        xt = [self.sbt(s0, "xt", [128, D]) for _ in range(2)]
        xs = [self.sbt(s0, "xs", [128, D], BF16) for _ in range(2)]
        ngb = self.sbt(s0, "ngb", [128, D])
        self.dma("sp", ngb[:], L("ngb"), w=["ngb"])
        junk = self.sbt(s0, "junk", [128, D], BF16)
        ss = [self.sbt(s0, "ss", [128, 1]) for _ in range(2)]
        for k in ("Vs", "Vw", "Vb"):
            self.memset("pool", A[k][:], 0.0, w=[k])
            self.memset("pool", A[k][:, :, :, 64:65], 1.0, w=[k])
        for i in range(NT):
            b = i % 2
            kx, ks, kss = "xt%d" % b, "xs%d" % b, "ss%d" % b
            self.dma("sp", xt[b][:], xin[i * 128:(i + 1) * 128, :], r=[xkey], w=[kx])
            self.memset("pool", ss[b][:], 0.0, w=[kss])
            self.act(junk[:], xt[b][:], AF.Square, r=[kx], w=["junk", kss], accum=ss[b][:])
            self.ts("dve", ss[b][:], ss[b][:], 1.0 / D, 1e-6, ALU.mult, ALU.add, w=[kss])
            self.act(ss[b][:], ss[b][:], AF.Sqrt, w=[kss])
            self.recip(ss[b][:], ss[b][:], w=[kss])
            self.stt("dve", xs[b][:], xt[b][:], ss[b][:, 0:1], ngb[:], ALU.mult, ALU.mult, r=[kx, kss, "ngb"], w=[ks])
            pb, pk = self.rot(0, 8)
            pbb = pb[:].bitcast(BF16)
            for ch in range(8):
                self.tr(pbb[:, ch * 128:(ch + 1) * 128], xs[b][:, ch * 128:(ch + 1) * 128], c["identb"][:],
                        r=[ks, "identb"], w=[pk])
            self.cp("act" if i % 2 else "dve", xnT[:, :, i * 128:(i + 1) * 128], pbb.rearrange("p (c t) -> p c t", c=8),
                    w=[pk, "xnT"])
            if i == 5:
                pre[0] = load_block(0, 512)
                self.dma("sp", bm[:], L("bmerge"), w=["bm"])
            if i == 11:
                pre[1] = load_block(512, 512)
        self.P.fence()
        s0.close()
        if self.upto == "p0":
            return
        fst = [self.sbt(st, "fst", [128, T]) for _ in range(1)]
        zst = [self.sbt(st, "zst", [128, 512]) for _ in range(3)]

        fdest = []
        for hh in range(4):
            fdest.append((A["qTA"], hh, "qTA"))
        for k in ("kTc", "vTc", "kTs", "kTw"):
            fdest.append((A[k], None, k))
        for hh in range(2):
            fdest.append((A["qTB"], hh, "qTB"))
        fdest.append((A["kTb"], None, "kTb"))
        for cc in range(7):
            fdest.append((None, cc, "F"))
        self._ev = 0
        self._zi = 0

        def comp_feat(s0, nsub):
            def f(wb, wk):
                for s in range(s0, s0 + nsub):
                    dst, idx, dkey = fdest[s]
                    fb = 0
                    for tc in range(4):
                        pb, pk = self.rot(0, 8)
                        for kc in range(8):
                            self.mm(pb[:, :], wb[:, kc, (s - s0) * 128:(s - s0 + 1) * 128], xnT[:, kc, tc * 512:(tc + 1) * 512],
                                    kc == 0, kc == 7, r=[wk, "xnT"], w=[pk])
                        if dst is None:
                            o_ap, okey = fst[fb][:, tc * 512:(tc + 1) * 512], "fst%d" % fb
                        elif idx is None:
                            o_ap, okey = dst[:, tc * 512:(tc + 1) * 512], dkey
                        else:
                            o_ap, okey = dst[:, idx, tc * 512:(tc + 1) * 512], dkey
                        self.cp("act" if self._ev % 2 == 0 else "dve", o_ap, pb[:, :], w=[pk, okey])
                        self._ev += 1
                    if dst is None:
                        self.dma("pool", d["F"][idx], fst[fb][:], r=["fst%d" % fb], w=[("F", idx)])
            return f

        def comp_tok0(wb, wk):
            for i in range(NT):
                pb, pk = self.rot(0, 8)
                for kc in range(8):
                    self.mm(pb[:, 0:408], xnT[:, kc, i * 128:(i + 1) * 128], wb[:, kc, 0:408], kc == 0, kc == 7,
                            r=[wk, "xnT"], w=[pk])
                for vi, k in enumerate(("Vs", "Vw", "Vb")):
                    self.cp("dve" if vi == 1 else "act", A[k][:, i, :, 0:64],
                            pb[:, vi * 128:(vi + 1) * 128].rearrange("p (g e) -> p g e", g=2), w=[pk, k])
                self.act(A["gate"][:, i, :], pb[:, 384:408], AF.Sigmoid, w=[pk, "gate"])

        def comp_zm(blk):
            def f(wb, wk):
                for i in range(NT):
                    pb, pk = self.rot(0, 8)
                    for kc in range(8):
                        self.mm(pb[:, :], xnT[:, kc, i * 128:(i + 1) * 128], wb[:, kc, :], kc == 0, kc == 7,
                                r=[wk, "xnT"], w=[pk])
                    zb = self._zi % 3
                    self._zi += 1
                    zk = "zst%d" % zb
                    if blk < 2:
                        self.act(zst[zb][:], pb[:, :], AF.Silu, w=[pk, zk])
                        self.dma("act", d["Z"][i * 128:(i + 1) * 128, blk * 512:(blk + 1) * 512], zst[zb][:], r=[zk], w=[("Z", i, blk)])
                    else:
                        mb = blk - 2
                        self.tt("dve", zst[zb][:], pb[:, :], bm[:, mb * 512:(mb + 1) * 512], ALU.add, r=["bm"], w=[pk, zk])
                        self.act(zst[zb][:], zst[zb][:], AF.Sigmoid, w=[zk])
                        self.dma("act", d["G"][i * 128:(i + 1) * 128, mb * 512:(mb + 1) * 512], zst[zb][:], r=[zk], w=[("G", i, mb)])
            return f

        blocks = []
        for s0 in range(0, 18, 4):
            nsub = min(4, 18 - s0)
            blocks.append((s0 * 128, nsub * 128, comp_feat(s0, nsub)))
        blocks.append((TOK0, 408, comp_tok0))
        for blk in range(8):
            blocks.append((TOK0 + 408 + blk * 512, 512, comp_zm(blk)))
        assert blocks[0][:2] == (0, 512) and blocks[1][:2] == (512, 512)
        cur = pre[0]
        for bi, (c0, ncol, fn) in enumerate(blocks):
            if bi == 0:
                nxt = pre[1]
            else:
                nxt = load_block(blocks[bi + 1][0], blocks[bi + 1][1]) if bi + 1 < len(blocks) else None
            fn(cur[0], cur[1])
            cur = nxt

    def phase_attn(self, st, l, A):
        c, d = self.c, self.d
        L = lambda k: d["%s%d" % (k, l)]
        ident, identb = c["identf"], c["identb"]
        kcT = self.sbt(st, "kcT", [128, 128], BF16)
        vcaug = self.sbt(st, "vcaug", [128, 2, 100], BF16)
        self.memset("pool", kcT[:], 0.0, w=["kcT"])
        self.memset("pool", vcaug[:], 0.0, w=["vcaug"])
        self.memset("pool", vcaug[:, :, 64:65], 1.0, w=["vcaug"])
        for g in range(2):
            self.cp("pool", vcaug[:, g, 65:97], c["ovl"][:], r=["ovl"], w=["vcaug"])
        with ExitStack() as sc:
            w1f = [self.sbt(sc, "w1f", [128, 32, 128]) for _ in range(2)]
            w1b = [self.sbt(sc, "w1b", [128, 32, 128], BF16) for _ in range(2)]
            pef = self.sbt(sc, "pef", [128, 2, 32])
            peb = self.sbt(sc, "peb", [128, 2, 32], BF16)
            w2f = self.sbt(sc, "w2f", [128, 2, 64])
            w2kd = self.sbt(sc, "w2kd", [128, 128], BF16)
            w2vb = self.sbt(sc, "w2vb", [128, 64], BF16)
            cb = self.sbt(sc, "cb", [128, 4])
            hsb = [self.sbt(sc, "hsb", [128, 128], BF16) for _ in range(2)]
            for wi, nm in enumerate(("w1k", "w1v")):
                self.dma("sp", w1f[wi][:], L(nm), w=["w1f%d" % wi])
                self.cp("dve", w1b[wi][:, 0:16, :], w1f[wi][:, 0:16, :], r=["w1f%d" % wi], w=["w1b%d" % wi])
                self.cp("act" if wi else "pool", w1b[wi][:, 16:32, :], w1f[wi][:, 16:32, :], r=["w1f%d" % wi], w=["w1b%d" % wi])
            self.dma("sp", pef[:, 0, :], L("pek"), w=["pef"])
            self.dma("sp", pef[:, 1, :], L("pev"), w=["pef"])
            self.cp("dve", peb[:], pef[:], r=["pef"], w=["peb"])
            self.dma("sp", w2f[:, 0, :], L("w2k"), w=["w2f"])
            self.dma("sp", w2f[:, 1, :], L("w2v"), w=["w2f"])
            self.cp("dve", w2kd[:, 0:64], w2f[:, 0, :], r=["w2f"], w=["w2kd"])
            self.cp("dve", w2kd[:, 64:128], w2f[:, 0, :], r=["w2f"], w=["w2kd"])
            self.cp("dve", w2vb[:], w2f[:, 1, :], r=["w2f"], w=["w2vb"])
            cnt = 0
            for wi, (src, skey) in enumerate(((A["kTc"], "kTc"), (A["vTc"], "vTc"))):
                for g in range(2):
                    rows = slice(g * 64, (g + 1) * 64)
                    sv = src[rows, :].rearrange("p (n s) -> p n s", s=16)
                    pb, pk = self.rot(0, 8)
                    for pos in range(32):
                        self.mm(pb[:, 0:1], w1b[wi][rows, pos, :], peb[rows, wi, pos:pos + 1], pos == 0, pos == 31,
                                r=["w1b%d" % wi, "peb"], w=[pk])
                    self.cp("dve", cb[:, cnt:cnt + 1], pb[:, 0:1], w=[pk, "cb"])
                    pb2, pk2 = self.rot(0, 8)
                    for pos in range(32):
                        rhs = sv[:, 0:127, pos] if pos < 16 else sv[:, 1:128, pos - 16]
                        self.mm(pb2[:, 0:127], w1b[wi][rows, pos, :], rhs, pos == 0, pos == 31,
                                r=["w1b%d" % wi, skey], w=[pk2])
                    hb = hsb[cnt % 2]
                    hk = "hsb%d" % (cnt % 2)
                    self.act(hb[:, 0:127], pb2[:, 0:127], AF.Silu, bias=cb[:, cnt:cnt + 1], r=["cb"], w=[pk2, hk])
                    pb3, pk3 = self.rot(0, 8)
                    if wi == 0:
                        self.mm(pb3[:, 0:127], w2kd[:, :], hb[:, 0:127], True, True, r=["w2kd", hk], w=[pk3])
                        self.cp("dve", kcT[rows, 0:127], pb3[rows, 0:127], w=[pk3, "kcT"])
                    else:
                        self.mm(pb3[0:127, 0:64], hb[:, 0:127], w2vb[:, :], True, True, r=["w2vb", hk], w=[pk3])
                        self.cp("dve", vcaug[0:127, g, 0:64], pb3[0:127, 0:64], w=[pk3, "vcaug"])
                    cnt += 1
            self.P.fence()
        if self.upto == "cmp":
            return
        NPT = 4
        PT = [self.sbt(st, "PT", [128, 17, 512], BF16) for _ in range(NPT)]
        PTc = [self.sbt(st, "PTc", [128, 512], BF16) for _ in range(2)]
        ya = [self.sbt(st, "ya", [128, 512]) for _ in range(2)]
        yb = [self.sbt(st, "yb", [128, 256]) for _ in range(2)]
        selT2 = self.sbt(st, "selT2", [128, T], BF16)
        self.memset("pool", selT2[:], 0.0, w=["selT2"])
        esink = self.sbt(st, "esink", [128, 4])
        self.dma("sp", esink[:], L("sinks"), w=["esink"])
        self.act(esink[:], esink[:], AF.Exp, w=["esink"])
        NR = 4
        rd = [self.sbt(st, "rd", [128, 4]) for _ in range(NR)]
        coef = [self.sbt(st, "coef", [128, 4]) for _ in range(NR)]
        tmpo = [self.sbt(st, "tmpo", [128, 4, 64]) for _ in range(NR)]
        tmpi = [self.sbt(st, "tmpi", [128, 4, 32]) for _ in range(2)]
        imp = [self.sbt(st, "imp", [128, 32]) for _ in range(2)]
        score = [self.sbt(st, "score", [128, 32]) for _ in range(2)]
        scw = [self.sbt(st, "scw", [128, 32]) for _ in range(2)]
        m8 = [self.sbt(st, "m8", [128, 16]) for _ in range(2)]
        selm = [self.sbt(st, "selm", [128, 96]) for _ in range(2)]
        for t_ in range(2):
            self.memset("pool", selm[t_][:], 0.0, w=["selm%d" % t_])
        self._ri = 0
        self._pt = 0
        self._ptA = 0
        gate = A["gate"]

        def epilogue(ob, ok, nh, wdt, i, g, br, dst, dkey, first):
            k = self._ri % NR
            self._ri += 1
            rk_, ck_, tk_ = "rd%d" % k, "coef%d" % k, "tmpo%d" % k
            O3 = ob[:, 0:nh * wdt].rearrange("p (h c) -> p h c", c=wdt)
            if br == "swa":
                self.tt("dve", rd[k][:, 0:nh], O3[:, :, 64], esink[:, g * 2:(g + 1) * 2], ALU.add, r=["esink"], w=[ok, rk_])
            else:
                self.ts("dve", rd[k][:, 0:nh], O3[:, :, 64], 1e-30, None, ALU.max, w=[ok, rk_])
            self.recip(rd[k][:, 0:nh], rd[k][:, 0:nh], w=[rk_])
            if br == "swa":
                cf = rd[k]
                cfk = rk_
            else:
                gc = {"cmp": 0, "slc": 8, "win": 16}[br] + g * 4
                self.tt("dve", coef[k][:, 0:nh], rd[k][:, 0:nh], gate[:, i, gc:gc + 4], ALU.mult, r=[rk_, "gate"], w=[ck_])
                cf = coef[k]
                cfk = ck_
            dst3 = dst.rearrange("p (h c) -> p h c", c=64)
            if first:
                self.tt("dve", dst3, O3[:, :, 0:64], cf[:, 0:nh].unsqueeze(2).to_broadcast([128, nh, 64]), ALU.mult,
                        r=[cfk], w=[ok, dkey])
            else:
                self.tt("dve", tmpo[k][:, 0:nh, :], O3[:, :, 0:64], cf[:, 0:nh].unsqueeze(2).to_broadcast([128, nh, 64]),
                        ALU.mult, r=[cfk], w=[ok, tk_])
                self.tt("pool", dst3, dst3, tmpo[k][:, 0:nh, :], ALU.add, r=[tk_], w=[dkey])
            return k

        def cmp_tile(i, g, yat, yak):
            M = 128
            rows = slice(g * 64, (g + 1) * 64)
            b = self._pt % 2
            self._pt += 1
            sb_, sk = self.rot(0, 6)
            self.mm(sb_[0:M, :], kcT[rows, 0:M], A["qTA"][rows, :, i * 128:(i + 1) * 128], True, False, r=["kcT", "qTA"], w=[sk])
            self.mm(sb_[0:M, :], identb[0:M, 0:M], c["penc"][0:M, i * 128:(i + 1) * 128].unsqueeze(1).to_broadcast([M, 4, 128]),
                    False, True, r=["identb", "penc"], w=[sk])
            pk_ = "PTc%d" % b
            self.act(PTc[b][0:M, :], sb_[0:M, :], AF.Exp, scale=0.125, w=[sk, pk_])
            cst = getattr(self, "cstage", 9)
            if cst < 2:
                return
            ob, ok = self.rot(6, 8)
            for hh in range(4):
                self.mm(ob[:, hh * 100:hh * 100 + 100], PTc[b][0:M, hh * 128:(hh + 1) * 128], vcaug[0:M, g, 0:100], True, True,
                        r=[pk_, "vcaug"], w=[ok])
            if cst < 3:
                return
            k = epilogue(ob, ok, 4, 100, i, g, "cmp", yat[:, g * 256:(g + 1) * 256], yak, True)
            if cst < 4:
                return
            O3 = ob[:, 0:400].rearrange("p (h c) -> p h c", c=100)
            tb = g
            self.tt("dve", tmpi[tb][:], O3[:, :, 65:97], rd[k][:, 0:4].unsqueeze(2).to_broadcast([128, 4, 32]), ALU.mult,
                    r=["rd%d" % k], w=[ok, "tmpi%d" % tb])
            self.red(imp[tb][:], tmpi[tb][:].rearrange("p h j -> p j h"), ALU.add, r=["tmpi%d" % tb], w=["imp%d" % tb])
            if cst < 5:
                return
            sc_, sk_ = score[tb], "score%d" % tb
            self.tt("dve", sc_[:], imp[tb][:], c["Mi"][:, i * 32:(i + 1) * 32], ALU.mult, r=["imp%d" % tb, "Mi"], w=[sk_])
            self.tt("dve", sc_[:], sc_[:], c["Bc"][:, i * 32:(i + 1) * 32], ALU.add, r=["Bc"], w=[sk_])
            mk, wk_, lk = "m8%d" % tb, "scw%d" % tb, "selm%d" % (i % 2)
            sm_ = selm[i % 2]
            self.P.add("dve", lambda e: e.max(out=m8[tb][:, 0:8], in_=sc_[:]), r=[sk_], w=[mk])
            self.P.add("dve", lambda e: e.match_replace(out=scw[tb][:], in_to_replace=m8[tb][:, 0:8], in_values=sc_[:],
                                                        imm_value=-1e30), r=[sk_, mk], w=[wk_])
            self.P.add("dve", lambda e: e.max(out=m8[tb][:, 8:16], in_=scw[tb][:]), r=[wk_], w=[mk])
            self.ts("dve", sm_[:, g * 64:g * 64 + 32], sc_[:], m8[tb][:, 15:16], -1.0, ALU.is_ge, ALU.add, r=[sk_, mk], w=[lk])
            if cst < 6 or g == 0:
                return
            xb, xk = self.rot(7, 8)
            self.tr(xb[0:96, 0:128], sm_[:, :], ident[:, :], r=[lk, "identf"], w=[xk])
            self.cp("act", selT2[0:96, i * 128:(i + 1) * 128], xb[0:96, 0:128], w=[xk, "selT2"])

        def attn_A2(i, br, dsts):
            if br == "slc":
                kT, kk_, V, vk, q, qk, nh, js = A["kTs"], "kTs", A["Vs"], "Vs", A["qTA"], "qTA", 4, list(range(0, i + 1))
            elif br == "win":
                kT, kk_, V, vk, q, qk, nh, js = A["kTw"], "kTw", A["Vw"], "Vw", A["qTA"], "qTA", 4, list(range(max(0, i - 4), i + 1))
            else:
                kT, kk_, V, vk, q, qk, nh, js = A["kTb"], "kTb", A["Vb"], "Vb", A["qTB"], "qTB", 2, list(range(max(0, i - 1), i + 1))
            N = nh * 128
            qs = slice(i * 128, (i + 1) * 128)
            pts = []
            for g in range(2):
                b = self._ptA % NPT
                self._ptA += 1
                pts.append((PT[b], "PT%d" % b))
            for idx, j in enumerate(js):
                banks = [self.rot(0, 6) for _ in range(2)]
                pens = [[], []]
                for g in range(2):
                    if br == "slc" and j < i and i >= 8:
                        er = slice(g * 64, g * 64 + 32)
                        pens[g].append((c["Eall"][er, j * 128:(j + 1) * 128],
                                        selT2[er, qs].unsqueeze(1).to_broadcast([32, nh, 128]), ["Eall", "selT2"]))
                    if j == i:
                        pens[g].append((identb[:, :], c["triA"][:, :].unsqueeze(1).to_broadcast([128, nh, 128]), ["identb", "triA"]))
                    if (br == "win" and j == i - 4) or (br == "swa" and j == i - 1):
                        pens[g].append((identb[:, :], c["triB"][:, :].unsqueeze(1).to_broadcast([128, nh, 128]), ["identb", "triB"]))
                for g in range(2):
                    rows = slice(g * 64, (g + 1) * 64)
                    sb_, sk = banks[g]
                    self.mm(sb_[:, 0:N], kT[rows, j * 128:(j + 1) * 128], q[rows, 0:nh, qs], True, len(pens[g]) == 0,
                            r=[kk_, qk], w=[sk])
                for g in range(2):
                    sb_, sk = banks[g]
                    for pi, (l_, r_, keys) in enumerate(pens[g]):
                        self.mm(sb_[:, 0:N], l_, r_, False, pi == len(pens[g]) - 1, r=keys, w=[sk])
                for g in range(2):
                    sb_, sk = banks[g]
                    self.act(pts[g][0][:, idx, 0:N], sb_[:, 0:N], AF.Exp, scale=0.125, w=[sk, pts[g][1]])
            return [dict(i=i, g=g, br=br, dst=dsts[g][0], dkey=dsts[g][1], pt=pts[g][0], ptk=pts[g][1], V=V, vk=vk, nh=nh, js=js, after=None)
                    for g in range(2)]

        def attn_B(S_):
            i, g, br, nh, js, pt, ptk, V, vk = S_["i"], S_["g"], S_["br"], S_["nh"], S_["js"], S_["pt"], S_["ptk"], S_["V"], S_["vk"]
            ob, ok = self.rot(6, 8)
            for hh in range(nh):
                for idx, j in enumerate(js):
                    self.mm(ob[:, hh * 68:hh * 68 + 68], pt[:, idx, hh * 128:(hh + 1) * 128], V[:, j, g, 0:68],
                            idx == 0, idx == len(js) - 1, r=[ptk, vk], w=[ok])
            epilogue(ob, ok, nh, 68, i, g, br, S_["dst"], S_["dkey"], br == "swa")
            if S_["after"] is not None:
                S_["after"]()

        def mk_after(i, b, yak, ybk):
            def f():
                self.dma("pool", d["Y"][i * 128:(i + 1) * 128, 0:512], ya[b][:], r=[yak], w=[("Y", i, 0)])
                self.dma("pool", d["Y"][i * 128:(i + 1) * 128, 512:768], yb[b][:], r=[ybk], w=[("Y", i, 1)])
            return f

        pending = []
        for i in range(NT):
            b = i % 2
            yak, ybk = "ya%d" % b, "yb%d" % b
            for g in range(2):
                cmp_tile(i, g, ya[b], yak)
            for br in ("slc", "win", "swa"):
                if br == "swa":
                    dsts = [(yb[b][:, g * 128:(g + 1) * 128], ybk) for g in range(2)]
                else:
                    dsts = [(ya[b][:, g * 256:(g + 1) * 256], yak) for g in range(2)]
                sts = attn_A2(i, br, dsts)
                if br == "swa":
                    sts[1]["after"] = mk_after(i, b, yak, ybk)
                while pending:
                    attn_B(pending.pop(0))
                pending.extend(sts)
        while pending:
            attn_B(pending.pop(0))

    def phase_rwkv(self, st, l):
        c, d = self.c, self.d
        L = lambda k: d["%s%d" % (k, l)]
        F = d["F"]
        identf, bones, hsel = c["identf"], c["bones"], c["hsel"]
        rv = self.sbt(st, "rv", [128, 20])
        self.dma("sp", rv[:], L("rv"), w=["rv"])
        omka = self.sbt(st, "omka", [128, 2])
        self.ts("dve", omka[:], rv[:, 13:15], -1.0, 1.0, ALU.mult, ALU.add, r=["rv"], w=["omka"])
        lnw = self.sbt(st, "lnw", [128, 256])
        lnb = self.sbt(st, "lnb", [128, 256])
        self.dma("sp", lnw[:], L("lnw"), w=["lnw"])
        self.dma("sp", lnb[:], L("lnb"), w=["lnb"])
        wab = self.sbt(st, "wab", [128, 256], BF16)
        thad = self.sbt(st, "thad", [128, T], BF16)
        if l > 0:
            v1b = self.sbt(st, "v1b", [128, 2, 32], BF16)
            v2b = self.sbt(st, "v2b", [32, 256], BF16)
            t1 = self.sbt(st, "t1", [32, T], BF16)

        def load_shift(dst, dkey, cidx, f, fp, fk, fpk):
            self.dma("sp", f[:, :], F[cidx], r=[("F", cidx)], w=[fk])
            self.memset("pool", fp[:, 0:1], 0.0, w=[fpk])
            self.dma("sp", fp[:, 1:T], F[cidx][:, 0:T - 1], r=[("F", cidx)], w=[fpk])
            self.tt("pool", fp[:, :], fp[:, :], f[:, :], ALU.subtract, r=[fk], w=[fpk])
            self.stt("dve", dst, fp[:, :], rv[:, cidx:cidx + 1], f[:, :], ALU.mult, ALU.add, r=[fpk, fk, "rv"], w=[dkey])

        with ExitStack() as sp:
            f = self.sbt(sp, "f", [128, T])
            fp = self.sbt(sp, "fp", [128, T])
            wdx = self.sbt(sp, "wdx", [128, T])
            wf = self.sbt(sp, "wf", [128, 256])
            self.dma("sp", wf[:], L("wa"), w=["wf"])
            self.cp("dve", wab[:], wf[:], r=["wf"], w=["wab"])
            load_shift(wdx[:, :], "wdx", 6, f, fp, "f", "fp")
            self.act(thad[0:64, :], wdx[0:64, :], AF.Tanh, r=["wdx"], w=["thad"])
            self.cp("dve", thad[64:128, :], wdx[64:128, :], r=["wdx"], w=["thad"])
            if l > 0:
                v1f = self.sbt(sp, "v1f", [128, 2, 32])
                v2f = self.sbt(sp, "v2f", [32, 256])
                vxb = self.sbt(sp, "vxb", [128, T], BF16)
                self.dma("sp", v1f[:], L("v1"), w=["v1f"])
                self.dma("sp", v2f[:], L("v2"), w=["v2f"])
                self.cp("dve", v1b[:], v1f[:], r=["v1f"], w=["v1b"])
                self.cp("dve", v2b[:], v2f[:], r=["v2f"], w=["v2b"])
                banks = [self.rot(0, 8) for _ in range(4)]
                for p in range(2):
                    load_shift(wdx[:, :], "wdx", 4 + p, f, fp, "f", "fp")
                    self.cp("dve", vxb[:], wdx[:], r=["wdx"], w=["vxb"])
                    for tc in range(4):
                        pb, pk = banks[tc]
                        self.mm(pb[0:32, :], v1b[:, p, :], vxb[:, tc * 512:(tc + 1) * 512], p == 0, p == 1, r=["v1b", "vxb"], w=[pk])
                for tc in range(4):
                    pb, pk = banks[tc]
                    self.cp("act", t1[0:32, tc * 512:(tc + 1) * 512], pb[0:32, :], w=[pk, "t1"])
            self.P.fence()

        v2d = lambda t: t[:].rearrange("p n t -> p (n t)")
        for p in range(2):
            ps_ = slice(p * 128, (p + 1) * 128)
            with ExitStack() as sP:
                Rs = self.sbt(sP, "Rs", [128, T])
                AhT = self.sbt(sP, "AhT", [128, T])
                rkb = self.sbt(sP, "rkb", [128, T])
                ArbT = [self.sbt(sP, "ArbT", [128, 16, 128]) for _ in range(2)]
                UV = self.sbt(sP, "UV", [128, 16, 128])
                YV = self.sbt(sP, "YV", [128, 16, 128])
                KVbd = self.sbt(sP, "KVbd", [128, 16, 128])
                tokB0 = self.sbt(sP, "tokB0", [128, 16, 128])
                tokB1 = self.sbt(sP, "tokB1", [128, 16, 128])
                tokV = self.sbt(sP, "tokV", [128, 16, 128])
                ycst = self.sbt(sP, "ycst", [128, 16, 128])
                Gbd = self.sbt(sP, "Gbd", [128, 128])
                gam = self.sbt(sP, "gam", [128, 16])
                sM = ExitStack()
                As = self.sbt(sM, "As", [128, T])
                Ks = self.sbt(sM, "Ks", [128, T])
                Bs = self.sbt(sM, "Bs", [128, T])
                tokA = self.sbt(sM, "tokA", [128, 16, 128])
                tokK = self.sbt(sM, "tokK", [128, 16, 128])
                pkk = lambda n: ("Pk", n // 4)
                Vt, kVt = AhT[:, :], "AhT"
                lw, klw = v2d(ArbT[0]), "ArbT0"
                aT, kaT = v2d(ArbT[1]), "ArbT1"
                kk, kkk = v2d(UV), "UV"
                e1, ke1 = v2d(YV), "YV"
                e2, ke2 = v2d(KVbd), "KVbd"
                e3, ke3 = v2d(ycst), "ycst"
                cl = [(v2d(tokB0), "tokB0"), (v2d(tokV), "tokV")]
                HF = 1024
                H = lambda ap, hf: ap[:, hf * HF:(hf + 1) * HF]
                K2 = lambda k, hf: (k, hf)
                KB2 = lambda k: [(k, 0), (k, 1)]
                cl0, kcl0 = cl[0]
                cl1, kcl1 = cl[1]

                def load_shift2(dst, dkey, cidx, f, fk, fp, fpk):
                    self.dma("sp", f[:, 0:HF], F[cidx][:, 0:HF], r=[("F", cidx)], w=[K2(fk, 0)])
                    self.dma("sp", f[:, HF:T], F[cidx][:, HF:T], r=[("F", cidx)], w=[K2(fk, 1)])
                    self.ts("pool", fp[:, 0:1], f[:, 0:1], -1.0, None, ALU.mult, r=[K2(fk, 0)], w=[K2(fpk, 0)])
                    self.tt("pool", fp[:, 1:HF], f[:, 0:HF - 1], f[:, 1:HF], ALU.subtract, r=[K2(fk, 0)], w=[K2(fpk, 0)])
                    self.tt("pool", fp[:, HF:T], f[:, HF - 1:T - 1], f[:, HF:T], ALU.subtract, r=KB2(fk), w=[K2(fpk, 1)])
                    for hf in range(2):
                        self.stt("dve", H(dst, hf), H(fp, hf), rv[:, cidx:cidx + 1], H(f, hf), ALU.mult, ALU.add,
                                 r=[K2(fpk, hf), K2(fk, hf), "rv"], w=[K2(dkey, hf)])

                load_shift2(Rs[:, :], "Rs", p, e2, ke2, e3, ke3)
                load_shift2(Ks[:, :], "Ks", 2 + p, kk, kkk, e1, ke1)
                load_shift2(Vt, kVt, 4 + p, cl0, kcl0, cl1, kcl1)
                Rsa, Ksa, Bsa, Asa, rkba = Rs[:, :], Ks[:, :], Bs[:, :], As[:, :], rkb[:, :]
                steps = []

                def st_sig(hf):
                    for tc in (2 * hf, 2 * hf + 1):
                        ts_ = slice(tc * 512, (tc + 1) * 512)
                        pb, pk = self.rot(0, 8)
                        self.mm(pb[:, :], wab[0:64, ps_], thad[0:64, ts_], True, True, r=["wab", "thad"], w=[pk])
                        self.act(lw[:, ts_], pb[:, :], AF.Sigmoid, bias=rv[:, 7 + p:8 + p], r=["rv"], w=[pk, K2(klw, hf)])
                        pb, pk = self.rot(0, 8)
                        self.mm(pb[:, :], wab[64:128, ps_], thad[64:128, ts_], True, True, r=["wab", "thad"], w=[pk])
                        self.act(aT[:, ts_], pb[:, :], AF.Sigmoid, bias=rv[:, 9 + p:10 + p], r=["rv"], w=[pk, K2(kaT, hf)])
                        if l > 0:
                            pb, pk = self.rot(0, 8)
                            self.mm(pb[:, :], v2b[0:32, ps_], t1[0:32, ts_], True, True, r=["v2b", "t1"], w=[pk])
                            self.act(e1[:, ts_], pb[:, :], AF.Sigmoid, bias=rv[:, 17 + p:18 + p], r=["rv"], w=[pk, K2(ke1, hf)])
                steps.append(st_sig)
                if l > 0:
                    self.dma("sp", e2, d["vf"][p], r=[("vf", p)], w=KB2(ke2))
                    steps.append(lambda hf: self.tt("pool", H(e2, hf), H(e2, hf), H(Vt, hf), ALU.subtract, r=[K2(kVt, hf)], w=[K2(ke2, hf)]))
                    steps.append(lambda hf: self.tt("dve", H(e2, hf), H(e2, hf), H(e1, hf), ALU.mult, r=[K2(ke1, hf)], w=[K2(ke2, hf)]))
                    steps.append(lambda hf: self.tt("pool", H(Vt, hf), H(Vt, hf), H(e2, hf), ALU.add, r=[K2(ke2, hf)], w=[K2(kVt, hf)]))
                else:
                    self.dma("pool", d["vf"][p], Vt, r=KB2(kVt), w=[("vf", p)])
                steps.append(lambda hf: self.act(H(kk, hf), H(Ksa, hf), AF.Copy, scale=rv[:, 11 + p:12 + p], r=[K2("Ks", hf), "rv"], w=[K2(kkk, hf)]))
                steps.append(lambda hf: self.act(H(e1, hf), H(kk, hf), AF.Square, r=[K2(kkk, hf)], w=[K2(ke1, hf)]))

                def st_norm(hf):
                    for tc in (2 * hf, 2 * hf + 1):
                        ts_ = slice(tc * 512, (tc + 1) * 512)
                        pb, pk = self.rot(0, 8)
                        self.mm(pb[:, :], bones[:, :], e1[:, ts_], True, True, r=["bones", K2(ke1, hf)], w=[pk])
                        self.act(e2[:, ts_], pb[:, :], AF.Sqrt, w=[pk, K2(ke2, hf)])
                steps.append(st_norm)
                steps.append(lambda hf: self.ts("dve", H(e2, hf), H(e2, hf), 1e-12, None, ALU.max, w=[K2(ke2, hf)]))
                steps.append(lambda hf: self.recip(H(e2, hf), H(e2, hf), w=[K2(ke2, hf)]))
                steps.append(lambda hf: self.tt("dve", H(kk, hf), H(kk, hf), H(e2, hf), ALU.mult, r=[K2(ke2, hf)], w=[K2(kkk, hf)]))
                steps.append(lambda hf: self.act(H(e1, hf), H(aT, hf), AF.Identity, bias=omka[:, p:p + 1], scale=rv[:, 13 + p:14 + p],
                                                 r=[K2(kaT, hf), "rv", "omka"], w=[K2(ke1, hf)]))
                steps.append(lambda hf: self.tt("pool", H(Ksa, hf), H(Ksa, hf), H(e1, hf), ALU.mult, r=[K2(ke1, hf)], w=[K2("Ks", hf)]))
                steps.append(lambda hf: self.stt("dve", H(rkba, hf), H(Rsa, hf), rv[:, 15 + p:16 + p], H(Ksa, hf), ALU.mult, ALU.mult,
                                                 r=[K2("Rs", hf), K2("Ks", hf), "rv"], w=[K2("rkb", hf)]))
                steps.append(lambda hf: self.tt("pool", H(Bsa, hf), H(kk, hf), H(aT, hf), ALU.mult, r=[K2(kkk, hf), K2(kaT, hf)], w=[K2("Bs", hf)]))
                chain_src = [(lw, klw)]
                for si, sh in enumerate((1, 2, 4, 8, 16, 32, 64)):
                    dst, dkey = cl[si % 2]
                    src, skey = chain_src[-1]

                    def st_scan(hf, src=src, skey=skey, dst=dst, dkey=dkey, sh=sh):
                        s3 = H(src, hf).rearrange("p (n t) -> p n t", t=128)
                        d3 = H(dst, hf).rearrange("p (n t) -> p n t", t=128)
                        self.cp("act", d3[:, :, 0:sh], s3[:, :, 0:sh], r=[K2(skey, hf)], w=[K2(dkey, hf)])
                        self.tt("dve", d3[:, :, sh:128], s3[:, :, sh:128], s3[:, :, 0:128 - sh], ALU.add, r=[K2(skey, hf)], w=[K2(dkey, hf)])
                    steps.append(st_scan)
                    chain_src.append((dst, dkey))
                csrc, cskey = chain_src[-1]
                CW = -float(np.exp(-0.5))
                steps.append(lambda hf: self.act(H(e1, hf), H(csrc, hf), AF.Exp, scale=CW, r=[K2(cskey, hf)], w=[K2(ke1, hf)]))
                steps.append(lambda hf: self.act(H(e2, hf), H(csrc, hf), AF.Exp, scale=-CW, r=[K2(cskey, hf)], w=[K2(ke2, hf)]))
                steps.append(lambda hf: self.tt("pool", H(e3, hf), H(csrc, hf), H(lw, hf), ALU.subtract, r=[K2(cskey, hf), K2(klw, hf)], w=[K2(ke3, hf)]))
                steps.append(lambda hf: self.act(H(e3, hf), H(e3, hf), AF.Exp, scale=CW, w=[K2(ke3, hf)]))
                steps.append(lambda hf: self.cp("act", gam[:, hf * 8:(hf + 1) * 8], H(e1, hf).rearrange("p (n t) -> p n t", t=128)[:, :, 127],
                                                r=[K2(ke1, hf)], w=[K2("gam", hf)]))
                steps.append(lambda hf: self.tt("dve", H(Rsa, hf), H(Rsa, hf), H(e1, hf), ALU.mult, r=[K2(ke1, hf)], w=[K2("Rs", hf)]))
                steps.append(lambda hf: self.tt("pool", H(Ksa, hf), H(Ksa, hf), H(e2, hf), ALU.mult, r=[K2(ke2, hf)], w=[K2("Ks", hf)]))
                steps.append(lambda hf: self.tt("pool", H(Bsa, hf), H(Bsa, hf), H(e2, hf), ALU.mult, r=[K2(ke2, hf)], w=[K2("Bs", hf)]))
                steps.append(lambda hf: self.stt("dve", H(Asa, hf), H(kk, hf), -1.0, H(e3, hf), ALU.mult, ALU.mult, r=[K2(kkk, hf), K2(ke3, hf)], w=[K2("As", hf)]))
                for stp in steps:
                    for hf in range(2):
                        stp(hf)
                self.memset("pool", tokB0[:], 0.0, w=KB2("tokB0"))
                self.memset("pool", tokB1[:], 0.0, w=["tokB1"])
                for n in range(16):
                    ch = slice(n * 128, (n + 1) * 128)
                    hf = n // 8
                    pb, pk = self.rot(0, 8)
                    self.tr(pb[:, 0:128], Ks[:, ch], identf[:, :], r=[K2("Ks", hf), "identf"], w=[pk])
                    self.tr(pb[:, 128:256], As[:, ch], identf[:, :], r=[K2("As", hf), "identf"], w=[pk])
                    self.tr(pb[:, 256:384], Bs[:, ch], identf[:, :], r=[K2("Bs", hf), "identf"], w=[pk])
                    self.tr(pb[:, 384:512], Vt[:, ch], identf[:, :], r=[K2(kVt, hf), "identf"], w=[pk])
                    self.cp("act", tokK[:, n, :], pb[:, 0:128], w=[pk, "tokK"])
                    self.cp("act", tokA[:, n, :], pb[:, 128:256], w=[pk, "tokA"])
                    self.cp("dve", tokB0[:, n, 0:64], pb[:, 256:320], w=[pk, K2("tokB0", hf)])
                    self.cp("dve", tokB1[:, n, 64:128], pb[:, 320:384], w=[pk, "tokB1"])
                    self.cp("dve", tokV[:, n, :], pb[:, 384:512], w=[pk, K2("tokV", hf)])
                self.P.fence()
                rstage = getattr(self, "rstage", 9)
                if rstage < 3:
                    sM.close()
                    continue
                self.memset("pool", KVbd[:], 0.0, w=["KVbd"])
                self.memset("pool", Gbd[:], 0.0, w=["Gbd"])
                for n0 in range(0, 16, 4):
                    pb, pk = self.rot(0, 8)
                    for q in range(4):
                        self.mm(pb[:, q * 128:(q + 1) * 128], tokK[:, n0 + q, :], tokV[:, n0 + q, :], True, True, r=["tokK", "tokV"], w=[pk])
                    p3 = pb[:].rearrange("p (q t) -> p q t", q=4)
                    for hd in range(2):
                        hs = slice(hd * 64, (hd + 1) * 64)
                        self.cp("act" if hd else "dve", KVbd[hs, n0:n0 + 4, hs], p3[hs, :, hs], w=[pk, "KVbd"])
                sC = ExitStack()
                XT = [ycst, tokK]
                Pk = [self.sbt(sC, "Pk", [128, 16, 128], BF16) for _ in range(2)]
                Nk = [self.sbt(sC, "Nk", [128, 16, 128], BF16) for _ in range(2)]
                XTb = [self.sbt(sC, "XTb", [128, 16, 128], BF16) for _ in range(2)]
                Lk = [self.sbt(sC, "Lk", [128, 4, 2, 128]) for _ in range(2)]
                WV = [self.sbt(sC, "WV", [128, 4, 64]) for _ in range(2)]
                pkk = lambda hd, n: ("Pk", hd, n // 4)
                nkk = lambda hd, n: ("Nk", hd, n // 4)
                xtk = lambda hd, n: ("XT", hd, n // 4)
                xbk = lambda hd, n: ("XTb", hd, n // 4)
                bc4 = lambda m: c[m][:, :].unsqueeze(1).to_broadcast([128, 4, 128])
                q4 = lambda pb: pb[:].rearrange("p (q t) -> p q t", q=4)
                HS = [slice(0, 64), slice(64, 128)]
                ev = 0
                for n0 in range(0, 16, 4):
                    bk = [[self.rot(0, 8) for _ in range(2)] for _ in range(3)]
                    for q in range(4):
                        ch = slice((n0 + q) * 128, (n0 + q + 1) * 128)
                        qs = slice(q * 128, (q + 1) * 128)
                        for which, (lh, rh, lkey, rkey) in enumerate(((As, Bs, "As", "Bs"), (Bs, As, "Bs", "As"), (Bs, Rs, "Bs", "Rs"))):
                            for hd in range(2):
                                hs = HS[hd]
                                self.mm(bk[which][hd][0][:, qs], lh[hs, ch], rh[hs, ch], True, True, r=[lkey, rkey], w=[bk[which][hd][1]])
                    for hd in range(2):
                        xk = [xtk(hd, n0)] + (["tokK"] if hd == 1 else [])
                        self.tt("dve", Pk[hd][:, n0:n0 + 4, :], q4(bk[0][hd][0]), bc4("mSL"), ALU.mult, r=["mSL"], w=[bk[0][hd][1], pkk(hd, n0)])
                        self.tt("dve", XT[hd][:, n0:n0 + 4, :], q4(bk[1][hd][0]), bc4("mSU"), ALU.mult, r=["mSU"], w=[bk[1][hd][1]] + xk)
                        self.tt("dve", ArbT[hd][:, n0:n0 + 4, :], q4(bk[2][hd][0]), bc4("mIU"), ALU.mult, r=["mIU"], w=[bk[2][hd][1], "ArbT%d" % hd])
                        self.cp("act", Nk[hd][:, n0:n0 + 4, :], XT[hd][:, n0:n0 + 4, :], r=[xtk(hd, n0)], w=[nkk(hd, n0)])
                        self.tt("pool", XT[hd][:, n0:n0 + 4, :], XT[hd][:, n0:n0 + 4, :], identf[:, :].unsqueeze(1).to_broadcast([128, 4, 128]),
                                ALU.add, r=["identf"], w=[xtk(hd, n0)])
                        self.cp("act", XTb[hd][:, n0:n0 + 4, :], XT[hd][:, n0:n0 + 4, :], r=[xtk(hd, n0)], w=[xbk(hd, n0)])
                G4 = list(range(0, 16, 4))
                for hd in range(2):
                    for k in range(1, 7):
                        pbanks, nbanks, xbanks = {}, {}, {}
                        for n0 in G4:
                            pbanks[n0] = self.rot(0, 8)
                            for q in range(4):
                                n = n0 + q
                                self.mm(pbanks[n0][0][:, q * 128:(q + 1) * 128], Nk[hd][:, n, :], Pk[hd][:, n, :], True, True,
                                        r=[nkk(hd, n0), pkk(hd, n0)], w=[pbanks[n0][1]])
                        if k <= 5:
                            for n0 in G4:
                                nbanks[n0] = self.rot(0, 8)
                                for q in range(4):
                                    n = n0 + q
                                    self.mm(nbanks[n0][0][:, q * 128:(q + 1) * 128], Pk[hd][:, n, :], Nk[hd][:, n, :], True, True,
                                            r=[nkk(hd, n0), pkk(hd, n0)], w=[nbanks[n0][1]])
                        for n0 in G4:
                            self.cp("act", Pk[hd][:, n0:n0 + 4, :], q4(pbanks[n0][0]), w=[pbanks[n0][1], pkk(hd, n0)])
                        if k <= 5:
                            for n0 in G4:
                                self.cp("act" if (n0 // 4) % 2 else "dve", Nk[hd][:, n0:n0 + 4, :], q4(nbanks[n0][0]), w=[nbanks[n0][1], nkk(hd, n0)])
                        for n0 in G4:
                            xbanks[n0] = self.rot(0, 8)
                            for q in range(4):
                                n = n0 + q
                                self.mm(xbanks[n0][0][:, q * 128:(q + 1) * 128], Pk[hd][:, n, :], XTb[hd][:, n, :], True, True,
                                        r=[pkk(hd, n0), xbk(hd, n0)], w=[xbanks[n0][1]])
                        for n0 in G4:
                            self.tt("dve", XT[hd][:, n0:n0 + 4, :], q4(xbanks[n0][0]), XT[hd][:, n0:n0 + 4, :], ALU.add, w=[xbanks[n0][1], xtk(hd, n0)])
                        if k < 6:
                            for n0 in G4:
                                self.cp("act" if (n0 // 4) % 2 else "pool", XTb[hd][:, n0:n0 + 4, :], XT[hd][:, n0:n0 + 4, :], r=[xtk(hd, n0)], w=[xbk(hd, n0)])
                for n0 in range(0, 16, 4):
                    bL = [self.rot(0, 8) for _ in range(2)]
                    bA = [self.rot(0, 8) for _ in range(2)]
                    for q in range(4):
                        ch = slice((n0 + q) * 128, (n0 + q + 1) * 128)
                        qs = slice(q * 128, (q + 1) * 128)
                        for hd in range(2):
                            self.mm(bL[hd][0][:, qs], Ks[HS[hd], ch], As[HS[hd], ch], True, True, r=["Ks", "As"], w=[bL[hd][1]])
                        for hd in range(2):
                            self.mm(bA[hd][0][:, qs], Ks[HS[hd], ch], Rs[HS[hd], ch], True, True, r=["Ks", "Rs"], w=[bA[hd][1]])
                    for hd in range(2):
                        self.tt("dve", Lk[hd][:, :, 0, :], q4(bL[hd][0]), bc4("mSU"), ALU.mult, r=["mSU"], w=[bL[hd][1], "Lk%d" % hd])
                        self.tt("dve", Lk[hd][:, :, 1, :], q4(bA[hd][0]), bc4("mIU"), ALU.mult, r=["mIU"], w=[bA[hd][1], "Lk%d" % hd])
                    for hd in range(2):
                        hs = HS[hd]
                        lk_, wk_ = "Lk%d" % hd, "WV%d" % hd
                        bw, kw = self.rot(0, 8)
                        bh, kh = self.rot(0, 8)
                        for q in range(4):
                            n = n0 + q
                            self.mm(bw[:, q * 64:(q + 1) * 64], Lk[hd][:, q, 0, :], tokV[:, n, hs], True, True, r=[lk_, "tokV"], w=[kw])
                            self.mm(bw[:, 256 + q * 64:256 + (q + 1) * 64], Lk[hd][:, q, 1, :], tokV[:, n, hs], True, True, r=[lk_, "tokV"], w=[kw])
                            self.mm(bh[:, q * 128:(q + 1) * 128], tokA[:, n, :], XT[hd][:, n, :], True, True, r=["tokA", xtk(hd, n0)], w=[kh])
                        self.cp("act", WV[hd][:, :, :], bw[:, 0:256].rearrange("p (q v) -> p q v", q=4), w=[kw, wk_])
                        self.cp("act", YV[:, n0:n0 + 4, hs], bw[:, 256:512].rearrange("p (q v) -> p q v", q=4), w=[kw, "YV"])
                        self.cp("dve", AhT[hs, n0 * 128:(n0 + 4) * 128], bh[hs, :], w=[kh, "AhT"])
                        bu, ku = self.rot(0, 8)
                        for q in range(4):
                            n = n0 + q
                            self.mm(bu[:, q * 64:(q + 1) * 64], XT[hd][:, n, :], WV[hd][:, q, :], True, True, r=[xtk(hd, n0), wk_], w=[ku])
                        self.cp("act", UV[:, n0:n0 + 4, hs], bu[:, 0:256].rearrange("p (q v) -> p q v", q=4), w=[ku, "UV"])
                self.tt("pool", KVbd[:], KVbd[:], gam[:, :].unsqueeze(2).to_broadcast([128, 16, 128]), ALU.mult, r=["gam"], w=["KVbd"])
                self.P.fence()
                sC.close()
                sM.close()
                cS = ExitStack()
                Usb = [self.sbt(cS, "Usb", [128, 128]) for _ in range(2)]
                Tg = [self.sbt(cS, "Tg", [128, 128]) for _ in range(2)]
                ysqA = self.sbt(cS, "ysqA", [128, 16, 128])
                smA = self.sbt(cS, "smA", [128, 64])
                bon = self.sbt(cS, "bon", [128, 16, 2])
                for n in range(16 if rstage >= 4 else 0):
                    ch = slice(n * 128, (n + 1) * 128)
                    q = n % 2
                    uk = "Usb%d" % q
                    self.stt("dve", Tg[q][:, :], Gbd[:, :], gam[:, n:n + 1], KVbd[:, n, :], ALU.mult, ALU.add, r=["Gbd", "KVbd", "gam"], w=["Tg%d" % q])
                    pbu, pku = self.rot(0, 8)
                    self.mm(pbu[:, 0:128], AhT[:, ch], Gbd[:, :], True, True, r=["AhT", "Gbd"], w=[pku])
                    self.tt("dve", Usb[q][:, :], pbu[:, 0:128], UV[:, n, :], ALU.add, r=["UV"], w=[pku, uk])
                    pby, pky = self.rot(0, 8)
                    self.mm(pby[:, 0:128], Rs[:, ch], Gbd[:, :], True, False, r=["Rs", "Gbd"], w=[pky])
                    self.mm(pby[:, 0:64], ArbT[0][:, n, :], Usb[q][:, 0:64], False, False, r=["ArbT0", uk], w=[pky])
                    self.mm(pby[:, 64:128], ArbT[1][:, n, :], Usb[q][:, 64:128], False, True, r=["ArbT1", uk], w=[pky])
                    self.mm(pby[:, 128:130], rkb[:, ch], hsel[:, 0:2], True, True, r=["rkb", "hsel"], w=[pky])
                    pbg, pkg = self.rot(0, 8)
                    self.mm(pbg[:, 0:64], tokB0[:, n, :], Usb[q][:, 0:64], True, True, r=["tokB0", uk], w=[pkg])
                    self.mm(pbg[:, 64:128], tokB1[:, n, :], Usb[q][:, 64:128], True, True, r=["tokB1", uk], w=[pkg])
                    self.stt("dve", Gbd[:, :], pbg[:, 0:128], gam[:, n:n + 1], Tg[q][:, :], ALU.mult, ALU.add, r=["gam", "Tg%d" % q], w=[pkg, "Gbd"])
                    self.tt("dve", ycst[:, n, :], pby[:, 0:128], YV[:, n, :], ALU.add, r=["YV"], w=[pky, ("ycst", n)])
                    self.cp("act", bon[:, n, :], pby[:, 128:130], w=[pky, ("bon", n)])
                if rstage >= 4:
                    yk_all = [("ycst", n) for n in range(16)]
                    y4 = ycst[:].rearrange("p n (h c) -> p (n h) c", c=64)
                    yf = ycst[:].rearrange("p n c -> p (n c)")
                    sq4 = ysqA[:].rearrange("p n (h c) -> p (n h) c", c=64)
                    bc32 = lambda t: t.unsqueeze(2).to_broadcast([128, 32, 64])
                    self.red(smA[:, 0:32], y4, ALU.add, r=yk_all, w=["smA"])
                    self.ts("dve", smA[:, 0:32], smA[:, 0:32], -1.0 / 64, None, ALU.mult, w=["smA"])
                    self.tt("dve", y4, y4, bc32(smA[:, 0:32]), ALU.add, r=["smA"], w=yk_all)
                    self.tt("pool", ysqA[:], ycst[:], ycst[:], ALU.mult, r=yk_all, w=["ysqA"])
                    self.red(smA[:, 32:64], sq4, ALU.add, r=["ysqA"], w=["smA"])
                    self.ts("dve", smA[:, 32:64], smA[:, 32:64], 1.0 / 64, 64e-5, ALU.mult, ALU.add, w=["smA"])
                    self.act(smA[:, 32:64], smA[:, 32:64], AF.Sqrt, w=["smA"])
                    self.recip(smA[:, 32:64], smA[:, 32:64], w=["smA"])
                    self.tt("dve", y4, y4, bc32(smA[:, 32:64]), ALU.mult, r=["smA"], w=yk_all)
                    self.tt("pool", ycst[:], ycst[:], lnw[:, ps_].unsqueeze(1).to_broadcast([128, 16, 128]), ALU.mult, r=["lnw"], w=yk_all)
                    self.tt("pool", ycst[:], ycst[:], lnb[:, ps_].unsqueeze(1).to_broadcast([128, 16, 128]), ALU.add, r=["lnb"], w=yk_all)
                    self.tt("dve", sq4, tokV[:].rearrange("p n (h c) -> p (n h) c", c=64),
                            bc32(bon[:].rearrange("p n h -> p (n h)")), ALU.mult, r=["tokV"] + [("bon", n) for n in range(16)], w=["ysqA"])
                    self.tt("pool", ycst[:], ycst[:], ysqA[:], ALU.add, r=["ysqA"], w=yk_all)
                    for n in range(16):
                        self.dma("act", d["Y"][n * 128:(n + 1) * 128, 768 + p * 128:768 + (p + 1) * 128], ycst[:, n, :],
                                 r=[("ycst", n)], w=[("Y", "c", p, n)])
                self.P.fence()
                cS.close()

    def phase_out(self, st, l, xin, xkey, xo, xokey):
        c, d = self.c, self.d
        L = lambda k: d["%s%d" % (k, l)]
        identf = c["identf"]
        final = (l == DEPTH - 1)
        pwb = self.sbt(st, "pwb", [128, 16, 1024], BF16)
        pst = [self.sbt(st, "pst", [128, 2, 1024]) for _ in range(2)]
        Yt = [self.sbt(st, "Yt", [128, 1024]) for _ in range(2)]
        Zt = [self.sbt(st, "Zt", [128, 1024]) for _ in range(2)]
        Gt = [self.sbt(st, "Gt", [128, 3072]) for _ in range(2)]
        xt = [self.sbt(st, "xt", [128, 1024]) for _ in range(2)]
        yzT = [self.sbt(st, "yzT", [128, 8, 128], BF16) for _ in range(2)]
        mixed = [self.sbt(st, "mixed", [128, 1024]) for _ in range(2)]
        mixb = [self.sbt(st, "mixb", [128, 1024], BF16) for _ in range(2)]
        yzb = [self.sbt(st, "yzb", [128, 1024], BF16) for _ in range(2)]
        mxT = [self.sbt(st, "mxT", [128, 8, 128], BF16) for _ in range(2)]
        tmpa = [self.sbt(st, "tmpa", [128, 512]) for _ in range(2)]
        xn = [self.sbt(st, "xn", [128, 1024]) for _ in range(2)]
        jk = self.sbt(st, "jkb", [128, 1024], BF16)
        ss = [self.sbt(st, "ss", [128, 1]) for _ in range(2)]
        self._ta = 0

        def stage_A(i):
            b = i % 2
            rs = slice(i * 128, (i + 1) * 128)
            ky, kz = "Yt%d" % b, "Zt%d" % b
            self.dma("sp", Yt[b][:], d["Y"][rs, :], r=[("Y", i, 0), ("Y", i, 1)] + [("Y", "c", pp, i) for pp in range(2)], w=[ky])
            self.dma("sp", Zt[b][:], d["Z"][rs, :], r=[("Z", i, 0), ("Z", i, 1)], w=[kz])
            kyb = "yzb%d" % b
            self.tt("pool", yzb[b][:], Yt[b][:], Zt[b][:], ALU.mult, r=[kz, ky], w=[kyb])
            pb, pk = self.rot(0, 8)
            pbb = pb[:].bitcast(BF16)
            for ch in range(8):
                self.tr(pbb[:, ch * 128:(ch + 1) * 128], yzb[b][:, ch * 128:(ch + 1) * 128], c["identb"][:], r=[kyb, "identb"], w=[pk])
            self.cp("act", yzT[b][:, :, :], pbb.rearrange("p (c t) -> p c t", c=8), w=[pk, "yzT%d" % b])

        def stage_B(i):
            b = i % 2
            rs = slice(i * 128, (i + 1) * 128)
            kg = "Gt%d" % b
            self.dma("sp", Gt[b][:], d["G"][rs, :], r=[("G", i, m) for m in range(6)], w=[kg])
            for half in range(2):
                cs = slice(half * 512, (half + 1) * 512)
                for bi, (k0, k1) in enumerate(((0, 4), (4, 6), (6, 8))):
                    pb, pk = self.rot(0, 8)
                    for kc in range(k0, k1):
                        self.mm(pb[:, :], yzT[b][:, kc, :], pwb[:, kc, cs], kc == k0, kc == k1 - 1, r=["yzT%d" % b, "pwb"], w=[pk])
                    gsl = Gt[b][:, bi * 1024 + half * 512:bi * 1024 + (half + 1) * 512]
                    if bi == 0:
                        self.tt("dve", mixed[b][:, cs], pb[:, :], gsl, ALU.mult, r=[kg], w=[pk, "mixed%d" % b])
                    else:
                        t_ = self._ta % 2
                        self._ta += 1
                        self.tt("dve", tmpa[t_][:], pb[:, :], gsl, ALU.mult, r=[kg], w=[pk, "tmpa%d" % t_])
                        if bi == 1:
                            self.tt("pool", mixed[b][:, cs], mixed[b][:, cs], tmpa[t_][:], ALU.add, r=["tmpa%d" % t_], w=["mixed%d" % b])
                        else:
                            self.tt("pool", mixb[b][:, cs], mixed[b][:, cs], tmpa[t_][:], ALU.add, r=["tmpa%d" % t_, "mixed%d" % b], w=["mixb%d" % b])

        def stage_CD(i):
            b = i % 2
            rs = slice(i * 128, (i + 1) * 128)
            kx = "xt%d" % b
            self.dma("sp", xt[b][:], xin[rs, :], r=[(xkey, i)], w=[kx])
            pb, pk = self.rot(0, 8)
            pbb = pb[:].bitcast(BF16)
            for ch in range(8):
                self.tr(pbb[:, ch * 128:(ch + 1) * 128], mixb[b][:, ch * 128:(ch + 1) * 128], c["identb"][:], r=["mixb%d" % b, "identb"], w=[pk])
            self.cp("act", mxT[b][:, :, :], pbb.rearrange("p (c t) -> p c t", c=8), w=[pk, "mxT%d" % b])
            for half in range(2):
                cs = slice(half * 512, (half + 1) * 512)
                pb, pk = self.rot(0, 8)
                for kc in range(8):
                    self.mm(pb[:, :], mxT[b][:, kc, :], pwb[:, 8 + kc, cs], kc == 0, kc == 7, r=["mxT%d" % b, "pwb"], w=[pk])
                self.tt("dve", xn[b][:, cs], pb[:, :], xt[b][:, cs], ALU.add, r=[kx], w=[pk, "xn%d" % b])
            if final:
                kss = "ss%d" % b
                self.memset("pool", ss[b][:], 0.0, w=[kss])
                self.act(jk[:], xn[b][:], AF.Square, accum=ss[b][:], r=["xn%d" % b], w=["jkb", kss])
                self.ts("dve", ss[b][:], ss[b][:], 1.0 / D, 1e-6, ALU.mult, ALU.add, w=[kss])
                self.act(ss[b][:], ss[b][:], AF.Sqrt, w=[kss])
                self.recip(ss[b][:], ss[b][:], w=[kss])
                self.stt("dve", xn[b][:], xn[b][:], ss[b][:, 0:1], c["fg"][:], ALU.mult, ALU.mult, r=[kss, "fg"], w=["xn%d" % b])
            self.dma("act", xo[rs, :], xn[b][:], r=["xn%d" % b], w=[(xokey, i)])

        stage_A(0)
        stage_A(1)
        for q in range(8):
            b = q % 2
            self.dma("sp", pst[b][:], L("pw")[:, 2 * q:2 * q + 2, :], w=["pst%d" % b])
            self.cp("dve", pwb[:, 2 * q, :], pst[b][:, 0, :], r=["pst%d" % b], w=["pwb"])
            self.cp("act" if q % 2 else "pool", pwb[:, 2 * q + 1, :], pst[b][:, 1, :], r=["pst%d" % b], w=["pwb"])
        stage_B(0)
        for i in range(NT):
            if i + 2 < NT:
                stage_A(i + 2)
            if i + 1 < NT:
                stage_B(i + 1)
            stage_CD(i)


FUSED = True
_CACHE = {}


def _get_nc(layers, debug=False):
    key = (tuple(layers), debug)
    if key not in _CACHE:
        kb = KB(layers, debug)
        _CACHE[key] = kb.build()
    return _CACHE[key]


def _run(layers, inp, xs, vfs, debug=False):
    nc = _get_nc(layers, debug)
    perm = _perm()
    consts = _consts()
    base = dict(consts)
    base["fg"] = np.ascontiguousarray(np.broadcast_to(np.asarray(inp["final_g"], np.float32).reshape(1, D), (128, D)))
    for l in layers:
        for k, v in prep_layer(inp, l, perm).items():
            base["%s%d" % (k, l)] = v
    maps = []
    for b in range(8):
        m = dict(base)
        m["x"] = np.ascontiguousarray(xs[b], dtype=np.float32)
        if vfs is not None:
            m["vfirst"] = np.ascontiguousarray(vfs[b], dtype=np.float32)
        maps.append(m)
    res = run_bass_kernel_spmd(nc, maps, core_ids=list(range(8)))
    return res.results


def kernel(**inputs):
    inp = {k: np.asarray(v) for k, v in inputs.items()}
    x = np.asarray(inp["x"], np.float32)
    if FUSED:
        r = _run([0, 1], inp, [x[b] for b in range(8)], None)
        return np.stack([r[b]["xout"] for b in range(8)], 0).astype(np.float32)
    r0 = _run([0], inp, [x[b] for b in range(8)], None)
    r1 = _run([1], inp, [r0[b]["xout"] for b in range(8)], [r0[b]["vfirst"] for b in range(8)])
    return np.stack([r1[b]["xout"] for b in range(8)], 0).astype(np.float32)
```

```python
from contextlib import ExitStack
import numpy as np
import ml_dtypes
import concourse.bass as bass
import concourse.mybir as mybir
from concourse.bass_utils import run_bass_kernel_spmd

F32 = mybir.dt.float32
BF16 = mybir.dt.bfloat16
AF = mybir.ActivationFunctionType
ALU = mybir.AluOpType
AX = mybir.AxisListType

ENGINES = ("sp", "act", "dve", "pool", "pe")
INORDER_ENGINES = ("pe", "sp")
NDMASEM = 12


class _Op:
    __slots__ = ("eng", "fn", "deps", "dma", "sig", "sem", "val", "prev", "idx")


class Prog:
    def __init__(self, nc):
        self.nc = nc
        self.ops = []
        self.lastw = {}
        self.readers = {}

    def add(self, eng, fn, r=(), w=(), dma=False):
        o = _Op()
        o.eng, o.fn, o.dma, o.sig = eng, fn, dma, False
        o.idx = len(self.ops)
        deps = set()
        for k in r:
            if k in self.lastw:
                deps.add(self.lastw[k])
        for k in w:
            if k in self.lastw:
                deps.add(self.lastw[k])
            deps.update(self.readers.get(k, ()))
        o.deps = deps
        for k in r:
            lst = self.readers.setdefault(k, [])
            if not dma:
                lst[:] = [i for i in lst if self.ops[i].dma or self.ops[i].eng != eng]
            lst.append(o.idx)
        for k in w:
            self.lastw[k] = o.idx
            self.readers[k] = []
        self.ops.append(o)
        return o

    def barrier_keys(self):
        return list(self.lastw.keys())

    def _skip(self, d, o):
        if d.dma:
            return False
        if d.eng == "sp":
            return True
        if d.eng == o.eng and o.eng in INORDER_ENGINES:
            return True
        return False

    def emit(self, stack):
        nc = self.nc
        ops = self.ops
        for o in ops:
            for di in o.deps:
                d = ops[di]
                if not self._skip(d, o):
                    d.sig = True
        engsem = {e: stack.enter_context(nc.semaphore("s_" + e)) for e in ENGINES if e != "sp"}
        dmasem = {e: [stack.enter_context(nc.semaphore("d_%s%d" % (e, i))) for i in range(NDMASEM)]
                  for e in ("sp", "pool", "act")}
        cnt = {e: 0 for e in ENGINES}
        dcnt = {e: 0 for e in ENGINES}
        per = {e: [] for e in ENGINES}
        for o in ops:
            per[o.eng].append(o)
            if o.dma:
                n = dcnt[o.eng]
                dcnt[o.eng] += 1
                o.sem = dmasem[o.eng][n % NDMASEM]
                o.val = 16 * (n // NDMASEM + 1)
                o.prev = 16 * (n // NDMASEM)
            elif o.sig:
                cnt[o.eng] += 1
                o.sem = engsem[o.eng]
                o.val = cnt[o.eng]
        self.stats = dict(cnt=cnt, dcnt=dcnt, n={e: len(per[e]) for e in ENGINES})

        def run(e, eng):
            waited = {}
            nw = 0
            for o in per[eng]:
                for di in sorted(o.deps):
                    d = ops[di]
                    if self._skip(d, o):
                        continue
                    key = id(d.sem)
                    if waited.get(key, 0) >= d.val:
                        continue
                    e.wait_ge(d.sem, d.val)
                    nw += 1
                    waited[key] = d.val
                if o.dma and o.prev > 0 and waited.get(id(o.sem), 0) < o.prev:
                    e.wait_ge(o.sem, o.prev)
                    waited[id(o.sem)] = o.prev
                ins = o.fn(e)
                if o.dma:
                    ins.then_inc(o.sem, 16)
                elif o.sig:
                    ins.then_inc(o.sem, 1)
            self.stats.setdefault("waits", {})[eng] = nw

        with nc.Block() as block:
            @block.sync
            def _(e):
                run(e, "sp")

            @block.scalar
            def _(e):
                run(e, "act")

            @block.vector
            def _(e):
                run(e, "dve")

            @block.gpsimd
            def _(e):
                run(e, "pool")

            @block.tensor
            def _(e):
                run(e, "pe")

    def fence(self):
        start = getattr(self, "_fpos", 0)
        deps = set(o.idx for o in self.ops[start:] if o.dma)
        last = {}
        for o in self.ops:
            last[o.eng] = o.idx
        deps.update(last.values())
        for e in ENGINES:
            o = self.add(e, lambda eng: eng.nop())
            o.deps = set(deps)
        self._fpos = len(self.ops)


T = 2048
D = 1024
NT = 16
N_IN = 6808
DEPTH = 2
BIG = 30000.0
NCMP = 127
SEG = dict(a_q=(0, 512), a_kv_cmp=(512, 256), a_kv_slc=(768, 256), a_kv_win=(1024, 256), a_gate=(1280, 24),
           a_z=(1304, 512), b_q=(1816, 256), b_kv=(2072, 256), b_z=(2328, 256), c_shift=(2584, 896),
           c_z=(3480, 256), merge=(3736, 3072))
NFEAT = 18 * 128
TOK0 = NFEAT


def _perm():
    def seg(name, a, b):
        o = SEG[name][0]
        return list(range(o + a, o + b))
    p = []
    for hh in range(4):
        p += seg("a_q", hh * 64, hh * 64 + 64) + seg("a_q", (4 + hh) * 64, (4 + hh) * 64 + 64)
    p += seg("a_kv_cmp", 0, 128)
    p += seg("a_kv_cmp", 128, 256)
    p += seg("a_kv_slc", 0, 128)
    p += seg("a_kv_win", 0, 128)
    p += seg("b_q", 0, 64) + seg("b_q", 128, 192)
    p += seg("b_q", 64, 128) + seg("b_q", 192, 256)
    p += seg("b_kv", 0, 128)
    p += seg("c_shift", 0, 896)
    assert len(p) == NFEAT
    p += seg("a_kv_slc", 128, 256) + seg("a_kv_win", 128, 256) + seg("b_kv", 128, 256) + seg("a_gate", 0, 24)
    p += seg("a_z", 0, 512) + seg("b_z", 0, 256) + seg("c_z", 0, 256)
    p += seg("merge", 0, 3072)
    assert len(p) == N_IN and len(set(p)) == N_IN
    return np.array(p)


def _consts():
    c = {}
    c["identf"] = np.eye(128, dtype=np.float32)
    s = np.arange(128)[:, None]
    t = np.arange(128)[None, :]
    c["triA"] = np.where(s <= t, 0.0, -BIG).astype(np.float32)
    c["triB"] = np.where(s > t, 0.0, -BIG).astype(np.float32)
    E = np.zeros((32, 16, 128), np.float32)
    for j in range(16):
        for p in range(128):
            E[2 * j + p // 64, j, p] = BIG
    E2 = np.zeros((128, 2048), np.float32)
    E2[0:32] = E.reshape(32, 2048)
    E2[64:96] = E.reshape(32, 2048)
    c["Eall"] = E2
    n = np.arange(128)[:, None]
    tt = np.arange(T)[None, :]
    c["penc"] = np.where(16 * n + 31 <= tt, 0.0, -BIG).astype(np.float32)
    Mi = np.zeros((128, 16, 32), np.float32)
    Bc = np.zeros((128, 16, 32), np.float32)
    for i in range(16):
        for p in range(128):
            cur = 2 * i + p // 64
            for j in range(32):
                if j == 0:
                    Bc[p, i, j] = 10.0
                elif j == cur:
                    Bc[p, i, j] = 20.0
                elif j == cur - 1:
                    Bc[p, i, j] = 30.0
                elif j > cur:
                    Bc[p, i, j] = -1.0 - j
                else:
                    Mi[p, i, j] = 1.0
            if cur == 0:
                Bc[p, i, 0] = 20.0
            if cur == 1:
                Bc[p, i, 0] = 30.0
    c["Mi"] = Mi.reshape(128, 512)
    c["Bc"] = Bc.reshape(128, 512)
    ci = np.arange(128)[:, None] * 16
    sj = np.arange(32)[None, :] * 64
    c["ovl"] = ((ci < sj + 64) & (ci + 32 > sj)).astype(np.float32)
    r = np.arange(128)[:, None]
    q = np.arange(128)[None, :]
    c["mSL"] = (q < r).astype(np.float32)
    c["mSU"] = (r < q).astype(np.float32)
    c["mIU"] = (r <= q).astype(np.float32)
    c["bones"] = ((r // 64) == (q // 64)).astype(np.float32)
    c["hsel"] = ((np.arange(128)[:, None] // 64) == np.arange(2)[None, :]).astype(np.float32)
    return c


CONST_SHAPES = dict(identf=[128, 128], triA=[128, 128], triB=[128, 128], Eall=[128, 2048], penc=[128, 2048],
                    Mi=[128, 512], Bc=[128, 512], ovl=[128, 32], mSL=[128, 128], mSU=[128, 128], mIU=[128, 128],
                    bones=[128, 128], hsel=[128, 2])


def layer_input_shapes(l):
    s = dict(win=[D, N_IN], ng=[128, 8], ngb=[128, D], bmerge=[128, 3072], w1k=[128, 32, 128], w1v=[128, 32, 128],
             pek=[128, 32], pev=[128, 32], w2k=[128, 64], w2v=[128, 64], sinks=[128, 4], rv=[128, 20],
             wa=[128, 256], lnw=[128, 256], lnb=[128, 256], pw=[128, 16, 1024])
    if l > 0:
        s["v1"] = [128, 2, 32]
        s["v2"] = [32, 256]
    return s


def prep_layer(inp, l, perm):
    f = lambda a: np.ascontiguousarray(a, dtype=np.float32)
    o = {}
    o["win"] = f(inp["w_in"][l][:, perm])
    o["ng"] = f(inp["norm_g"][l].reshape(8, 128).T)
    o["ngb"] = f(np.broadcast_to(inp["norm_g"][l].reshape(1, D), (128, D)))
    o["bmerge"] = f(np.broadcast_to(inp["b_merge"][l].reshape(1, 3072), (128, 3072)))
    for nm, src in (("w1k", "cmp_w1_k"), ("w1v", "cmp_w1_v")):
        w = inp[src][l].reshape(32, 64, 128).transpose(1, 0, 2)
        o[nm] = f(np.concatenate([w, w], 0))
    for nm, src in (("pek", "cmp_pe_k"), ("pev", "cmp_pe_v")):
        p = inp[src][l].T
        o[nm] = f(np.concatenate([p, p], 0))
    o["w2k"] = f(inp["cmp_w2_k"][l])
    o["w2v"] = f(inp["cmp_w2_v"][l])
    o["sinks"] = f(np.broadcast_to(inp["swa_sinks"][l].reshape(1, 4), (128, 4)))
    rv = np.zeros((128, 20), np.float32)
    rv[:, 0:7] = inp["rwkv_mu"][l].reshape(7, 128).T
    for k, nm in enumerate(("rwkv_w0", "rwkv_a0", "rwkv_k_k", "rwkv_k_a")):
        rv[:, 7 + 2 * k:9 + 2 * k] = inp[nm][l].reshape(2, 128).T
    rv[:, 15:17] = inp["rwkv_r_k"][l].reshape(2, 128).T
    if l > 0:
        rv[:, 17:19] = inp["rwkv_v0"][l - 1].reshape(2, 128).T
    o["rv"] = rv
    o["wa"] = f(np.concatenate([inp["rwkv_w2"][l], inp["rwkv_a2"][l]], 0))
    o["lnw"] = f(np.broadcast_to(inp["rwkv_ln_w"][l].reshape(1, 256), (128, 256)))
    o["lnb"] = f(np.broadcast_to(inp["rwkv_ln_b"][l].reshape(1, 256), (128, 256)))
    pw = np.concatenate([inp["proj_a"][l].reshape(4, 128, 1024), inp["proj_b"][l].reshape(2, 128, 1024),
                         inp["proj_c"][l].reshape(2, 128, 1024), inp["w_out"][l].reshape(8, 128, 1024)], 0)
    o["pw"] = f(pw.transpose(1, 0, 2))
    if l > 0:
        o["v1"] = f(inp["rwkv_v1"][l - 1].reshape(2, 128, 32).transpose(1, 0, 2))
        o["v2"] = f(inp["rwkv_v2"][l - 1])
    return o


class KB:
    def __init__(self, layers, debug=False, upto="out"):
        self.layers = list(layers)
        self.debug = debug
        self.upto = upto
        nc = bass.Bass("TRN2", target_bir_lowering=False)
        nc.allow_low_precision("bf16 matmul operands, fp32 accumulation")
        self.nc = nc
        self.P = Prog(nc)
        self.rotc = {}

    def dma(self, eng, out, in_, r=(), w=()):
        return self.P.add(eng, lambda e: e.dma_start(out=out, in_=in_), r=r, w=w, dma=True)

    def mm(self, out, lhsT, rhs, start, stop, r=(), w=()):
        return self.P.add("pe", lambda e: e.matmul(out, lhsT=lhsT, rhs=rhs, start=start, stop=stop,
                                                   skip_group_check=True), r=r, w=w)

    def tr(self, out, in_, ident, r=(), w=()):
        return self.P.add("pe", lambda e: e.transpose(out=out, in_=in_, identity=ident), r=r, w=w)

    def act(self, out, in_, func, r=(), w=(), bias=None, scale=None, accum=None):
        kw = {}
        if bias is not None:
            kw["bias"] = bias
        if scale is not None:
            kw["scale"] = scale
        if accum is not None:
            kw["accum_out"] = accum
        return self.P.add("act", lambda e: e.activation(out=out, in_=in_, func=func, **kw), r=r, w=w)

    def tt(self, eng, out, in0, in1, op, r=(), w=()):
        return self.P.add(eng, lambda e: e.tensor_tensor(out=out, in0=in0, in1=in1, op=op), r=r, w=w)

    def ts(self, eng, out, in0, s1, s2, op0, op1=None, r=(), w=()):
        if op1 is None:
            return self.P.add(eng, lambda e: e.tensor_scalar(out=out, in0=in0, scalar1=s1, scalar2=None, op0=op0), r=r, w=w)
        return self.P.add(eng, lambda e: e.tensor_scalar(out=out, in0=in0, scalar1=s1, scalar2=s2, op0=op0, op1=op1), r=r, w=w)

    def stt(self, eng, out, in0, scalar, in1, op0, op1, r=(), w=()):
        return self.P.add(eng, lambda e: e.scalar_tensor_tensor(out=out, in0=in0, scalar=scalar, in1=in1, op0=op0, op1=op1), r=r, w=w)

    def cp(self, eng, out, in_, r=(), w=()):
        if eng == "act":
            return self.P.add("act", lambda e: e.copy(out=out, in_=in_), r=r, w=w)
        return self.P.add(eng, lambda e: e.tensor_copy(out=out, in_=in_), r=r, w=w)

    def memset(self, eng, ap, val, w=()):
        return self.P.add(eng, lambda e: e.memset(ap, val), w=w)

    def recip(self, out, in_, r=(), w=()):
        return self.P.add("dve", lambda e: e.reciprocal(out=out, in_=in_), r=r, w=w)

    def red(self, out, in_, op, r=(), w=()):
        return self.P.add("dve", lambda e: e.tensor_reduce(out=out, in_=in_, axis=AX.X, op=op), r=r, w=w)

    def rot(self, lo, hi):
        k = (lo, hi)
        c = self.rotc.get(k, 0)
        self.rotc[k] = c + 1
        b = lo + c % (hi - lo)
        return self.ps[b], "ps%d" % b

    def sbt(self, st, name, shape, dt=F32):
        self._nm = getattr(self, "_nm", 0) + 1
        return st.enter_context(self.nc.sbuf_tensor("%s_%d" % (name, self._nm), shape, dt))

    def build(self):
        nc = self.nc
        layers = self.layers
        dk = "ExternalOutput" if self.debug else "Internal"
        self.d = {}
        self.d["x"] = nc.dram_tensor("x", [T, D], F32, kind="ExternalInput").ap()
        for k, shp in CONST_SHAPES.items():
            self.d[k] = nc.dram_tensor(k, shp, F32, kind="ExternalInput").ap()
        self.d["fg"] = nc.dram_tensor("fg", [128, D], F32, kind="ExternalInput").ap()
        for l in layers:
            for k, shp in layer_input_shapes(l).items():
                self.d["%s%d" % (k, l)] = nc.dram_tensor("%s%d" % (k, l), shp, F32, kind="ExternalInput").ap()
        self.d["Z"] = nc.dram_tensor("scrZ", [T, 1024], F32, kind=dk).ap()
        self.d["G"] = nc.dram_tensor("scrG", [T, 3072], F32, kind=dk).ap()
        self.d["Y"] = nc.dram_tensor("scrY", [T, 1024], F32, kind=dk).ap()
        self.d["F"] = nc.dram_tensor("scrF", [7, 128, T], F32, kind=dk).ap()
        if 0 in layers and 1 in layers:
            self.d["vf"] = nc.dram_tensor("vfirst", [2, 128, T], F32, kind=dk).ap()
        elif 0 in layers:
            self.d["vf"] = nc.dram_tensor("vfirst", [2, 128, T], F32, kind="ExternalOutput").ap()
        else:
            self.d["vf"] = nc.dram_tensor("vfirst", [2, 128, T], F32, kind="ExternalInput").ap()
        self.d["xout"] = nc.dram_tensor("xout", [T, D], F32, kind="ExternalOutput").ap()
        if len(layers) > 1:
            self.d["xmid"] = nc.dram_tensor("xmid", [T, D], F32, kind=dk).ap()

        with ExitStack() as st:
            self.ps = [st.enter_context(nc.psum_tensor("psb%d" % b, [128, 512], F32)) for b in range(8)]
            self.load_consts(st)
            xin = self.d["x"]
            xkey = "x"
            for li, l in enumerate(layers):
                last = (li == len(layers) - 1)
                xo = self.d["xout"] if last else self.d["xmid"]
                xokey = "xout" if last else "xmid"
                self.layer(l, xin, xkey, xo, xokey)
                xin, xkey = xo, xokey
            self.P.fence()
            self.P.emit(st)
        return nc

    def load_consts(self, st):
        d = self.d
        c = self.c = {}
        f32c = ["identf", "Mi", "Bc", "mSL", "mSU", "mIU", "bones", "hsel"]
        for k in f32c:
            c[k] = self.sbt(st, k, CONST_SHAPES[k])
            self.dma("sp", c[k][:], d[k], w=[k])
        c["fg"] = self.sbt(st, "fg", [128, D])
        self.dma("sp", c["fg"][:], d["fg"], w=["fg"])
        bl = ["identf", "triA", "triB", "Eall", "penc", "ovl"]
        for k in bl:
            nm = "identb" if k == "identf" else k
            c[nm] = self.sbt(st, nm, CONST_SHAPES[k], BF16)
        with ExitStack() as s2:
            for k in bl:
                nm = "identb" if k == "identf" else k
                tmp = self.sbt(s2, k + "_f", CONST_SHAPES[k])
                self.dma("sp", tmp[:], d[k], w=[k + "_f"])
                self.cp("pool", c[nm][:], tmp[:], r=[k + "_f"], w=[nm])
            self.P.fence()

    def layer(self, l, xin, xkey, xo, xokey):
        with ExitStack() as sA:
            A = self.alloc_attn(sA)
            with ExitStack() as s1:
                self.phase01(s1, l, xin, xkey, A)
                self.P.fence()
            if self.upto in ("p0", "p1"):
                return
            with ExitStack() as s2:
                self.phase_attn(s2, l, A)
                self.P.fence()
        if self.upto in ("cmp", "attn"):
            return
        with ExitStack() as s3:
            self.phase_rwkv(s3, l)
            self.P.fence()
        if self.upto == "rwkv":
            return
        with ExitStack() as s4:
            self.phase_out(s4, l, xin, xkey, xo, xokey)
            self.P.fence()

    def alloc_attn(self, st):
        A = {}
        A["qTA"] = self.sbt(st, "qTA", [128, 4, T], BF16)
        A["qTB"] = self.sbt(st, "qTB", [128, 2, T], BF16)
        for k in ("kTc", "vTc", "kTs", "kTw", "kTb"):
            A[k] = self.sbt(st, k, [128, T], BF16)
        for k in ("Vs", "Vw", "Vb"):
            A[k] = self.sbt(st, k, [128, 16, 2, 68], BF16)
        A["gate"] = self.sbt(st, "gate", [128, 16, 24])
        return A

    def phase01(self, st, l, xin, xkey, A):
        c, d = self.c, self.d
        L = lambda k: d["%s%d" % (k, l)]
        xnT = self.sbt(st, "xnT", [128, 8, T], BF16)
        ng = self.sbt(st, "ng", [128, 8])
        self.dma("sp", ng[:], L("ng"), w=["ng"])
        wst = [self.sbt(st, "wst", [128, 8, 512]) for _ in range(2)]
        wbf = [self.sbt(st, "wbf", [128, 8, 512], BF16) for _ in range(2)]
        bm = self.sbt(st, "bm", [128, 3072])
        win = L("win").rearrange("(k p) n -> p k n", p=128)
        self._blk = 0

        def load_block(c0, ncol):
            b = self._blk % 2
            self._blk += 1
            for kc in range(8):
                self.dma("sp", wst[b][:, kc, 0:ncol], win[:, kc, c0:c0 + ncol], w=["wst%d" % b])
            self.cp("dve", wbf[b][:, 0:4, 0:ncol], wst[b][:, 0:4, 0:ncol], r=["wst%d" % b], w=["wbf%d" % b])
            self.cp("act", wbf[b][:, 4:6, 0:ncol], wst[b][:, 4:6, 0:ncol], r=["wst%d" % b], w=["wbf%d" % b])
            self.cp("pool", wbf[b][:, 6:8, 0:ncol], wst[b][:, 6:8, 0:ncol], r=["wst%d" % b], w=["wbf%d" % b])
            return wbf[b], "wbf%d" % b

        pre = {}
        s0 = ExitStack()
        xt = [self.sbt(s0, "xt", [128, D]) for _ in range(2)]
        xs = [self.sbt(s0, "xs", [128, D], BF16) for _ in range(2)]
        ngb = self.sbt(s0, "ngb", [128, D])
        self.dma("sp", ngb[:], L("ngb"), w=["ngb"])
        junk = self.sbt(s0, "junk", [128, D], BF16)
        ss = [self.sbt(s0, "ss", [128, 1]) for _ in range(2)]
        for k in ("Vs", "Vw", "Vb"):
            self.memset("pool", A[k][:], 0.0, w=[k])
            self.memset("pool", A[k][:, :, :, 64:65], 1.0, w=[k])
        for i in range(NT):
            b = i % 2
            kx, ks, kss = "xt%d" % b, "xs%d" % b, "ss%d" % b
            self.dma("sp", xt[b][:], xin[i * 128:(i + 1) * 128, :], r=[xkey], w=[kx])
            self.memset("pool", ss[b][:], 0.0, w=[kss])
            self.act(junk[:], xt[b][:], AF.Square, r=[kx], w=["junk", kss], accum=ss[b][:])
            self.ts("dve", ss[b][:], ss[b][:], 1.0 / D, 1e-6, ALU.mult, ALU.add, w=[kss])
            self.act(ss[b][:], ss[b][:], AF.Sqrt, w=[kss])
            self.recip(ss[b][:], ss[b][:], w=[kss])
            self.stt("dve", xs[b][:], xt[b][:], ss[b][:, 0:1], ngb[:], ALU.mult, ALU.mult, r=[kx, kss, "ngb"], w=[ks])
            pb, pk = self.rot(0, 8)
            pbb = pb[:].bitcast(BF16)
            for ch in range(8):
                self.tr(pbb[:, ch * 128:(ch + 1) * 128], xs[b][:, ch * 128:(ch + 1) * 128], c["identb"][:],
                        r=[ks, "identb"], w=[pk])
            self.cp("act" if i % 2 else "dve", xnT[:, :, i * 128:(i + 1) * 128], pbb.rearrange("p (c t) -> p c t", c=8),
                    w=[pk, "xnT"])
            if i == 5:
                pre[0] = load_block(0, 512)
                self.dma("sp", bm[:], L("bmerge"), w=["bm"])
            if i == 11:
                pre[1] = load_block(512, 512)
        self.P.fence()
        s0.close()
        if self.upto == "p0":
            return
        fst = [self.sbt(st, "fst", [128, T]) for _ in range(1)]
        zst = [self.sbt(st, "zst", [128, 512]) for _ in range(3)]

        fdest = []
        for hh in range(4):
            fdest.append((A["qTA"], hh, "qTA"))
        for k in ("kTc", "vTc", "kTs", "kTw"):
            fdest.append((A[k], None, k))
        for hh in range(2):
            fdest.append((A["qTB"], hh, "qTB"))
        fdest.append((A["kTb"], None, "kTb"))
        for cc in range(7):
            fdest.append((None, cc, "F"))
        self._ev = 0
        self._zi = 0

        def comp_feat(s0, nsub):
            def f(wb, wk):
                for s in range(s0, s0 + nsub):
                    dst, idx, dkey = fdest[s]
                    fb = 0
                    for tc in range(4):
                        pb, pk = self.rot(0, 8)
                        for kc in range(8):
                            self.mm(pb[:, :], wb[:, kc, (s - s0) * 128:(s - s0 + 1) * 128], xnT[:, kc, tc * 512:(tc + 1) * 512],
                                    kc == 0, kc == 7, r=[wk, "xnT"], w=[pk])
                        if dst is None:
                            o_ap, okey = fst[fb][:, tc * 512:(tc + 1) * 512], "fst%d" % fb
                        elif idx is None:
                            o_ap, okey = dst[:, tc * 512:(tc + 1) * 512], dkey
                        else:
                            o_ap, okey = dst[:, idx, tc * 512:(tc + 1) * 512], dkey
                        self.cp("act" if self._ev % 2 == 0 else "dve", o_ap, pb[:, :], w=[pk, okey])
                        self._ev += 1
                    if dst is None:
                        self.dma("act", d["F"][idx], fst[fb][:], r=["fst%d" % fb], w=[("F", idx)])
            return f

        def comp_tok0(wb, wk):
            for i in range(NT):
                pb, pk = self.rot(0, 8)
                for kc in range(8):
                    self.mm(pb[:, 0:408], xnT[:, kc, i * 128:(i + 1) * 128], wb[:, kc, 0:408], kc == 0, kc == 7,
                            r=[wk, "xnT"], w=[pk])
                for vi, k in enumerate(("Vs", "Vw", "Vb")):
                    self.cp("dve" if vi == 1 else "act", A[k][:, i, :, 0:64],
                            pb[:, vi * 128:(vi + 1) * 128].rearrange("p (g e) -> p g e", g=2), w=[pk, k])
                self.act(A["gate"][:, i, :], pb[:, 384:408], AF.Sigmoid, w=[pk, "gate"])

        def comp_zm(blk):
            def f(wb, wk):
                for i in range(NT):
                    pb, pk = self.rot(0, 8)
                    for kc in range(8):
                        self.mm(pb[:, :], xnT[:, kc, i * 128:(i + 1) * 128], wb[:, kc, :], kc == 0, kc == 7,
                                r=[wk, "xnT"], w=[pk])
                    zb = self._zi % 3
                    self._zi += 1
                    zk = "zst%d" % zb
                    if blk < 2:
                        self.act(zst[zb][:], pb[:, :], AF.Silu, w=[pk, zk])
                        self.dma("act", d["Z"][i * 128:(i + 1) * 128, blk * 512:(blk + 1) * 512], zst[zb][:], r=[zk], w=[("Z", i, blk)])
                    else:
                        mb = blk - 2
                        self.tt("dve", zst[zb][:], pb[:, :], bm[:, mb * 512:(mb + 1) * 512], ALU.add, r=["bm"], w=[pk, zk])
                        self.act(zst[zb][:], zst[zb][:], AF.Sigmoid, w=[zk])
                        self.dma("act", d["G"][i * 128:(i + 1) * 128, mb * 512:(mb + 1) * 512], zst[zb][:], r=[zk], w=[("G", i, mb)])
            return f

        blocks = []
        for s0 in range(0, 18, 4):
            nsub = min(4, 18 - s0)
            blocks.append((s0 * 128, nsub * 128, comp_feat(s0, nsub)))
        blocks.append((TOK0, 408, comp_tok0))
        for blk in range(8):
            blocks.append((TOK0 + 408 + blk * 512, 512, comp_zm(blk)))
        assert blocks[0][:2] == (0, 512) and blocks[1][:2] == (512, 512)
        cur = pre[0]
        for bi, (c0, ncol, fn) in enumerate(blocks):
            if bi == 0:
                nxt = pre[1]
            else:
                nxt = load_block(blocks[bi + 1][0], blocks[bi + 1][1]) if bi + 1 < len(blocks) else None
            fn(cur[0], cur[1])
            cur = nxt

    def phase_attn(self, st, l, A):
        c, d = self.c, self.d
        L = lambda k: d["%s%d" % (k, l)]
        ident, identb = c["identf"], c["identb"]
        kcT = self.sbt(st, "kcT", [128, 128], BF16)
        vcaug = self.sbt(st, "vcaug", [128, 2, 100], BF16)
        self.memset("pool", kcT[:], 0.0, w=["kcT"])
        self.memset("pool", vcaug[:], 0.0, w=["vcaug"])
        self.memset("pool", vcaug[:, :, 64:65], 1.0, w=["vcaug"])
        for g in range(2):
            self.cp("pool", vcaug[:, g, 65:97], c["ovl"][:], r=["ovl"], w=["vcaug"])
        with ExitStack() as sc:
            w1f = [self.sbt(sc, "w1f", [128, 32, 128]) for _ in range(2)]
            w1b = [self.sbt(sc, "w1b", [128, 32, 128], BF16) for _ in range(2)]
            pef = self.sbt(sc, "pef", [128, 2, 32])
            peb = self.sbt(sc, "peb", [128, 2, 32], BF16)
            w2f = self.sbt(sc, "w2f", [128, 2, 64])
            w2kd = self.sbt(sc, "w2kd", [128, 128], BF16)
            w2vb = self.sbt(sc, "w2vb", [128, 64], BF16)
            cb = self.sbt(sc, "cb", [128, 4])
            hsb = [self.sbt(sc, "hsb", [128, 128], BF16) for _ in range(2)]
            for wi, nm in enumerate(("w1k", "w1v")):
                self.dma("sp", w1f[wi][:], L(nm), w=["w1f%d" % wi])
                self.cp("dve", w1b[wi][:, 0:16, :], w1f[wi][:, 0:16, :], r=["w1f%d" % wi], w=["w1b%d" % wi])
                self.cp("act" if wi else "pool", w1b[wi][:, 16:32, :], w1f[wi][:, 16:32, :], r=["w1f%d" % wi], w=["w1b%d" % wi])
            self.dma("sp", pef[:, 0, :], L("pek"), w=["pef"])
            self.dma("sp", pef[:, 1, :], L("pev"), w=["pef"])
            self.cp("dve", peb[:], pef[:], r=["pef"], w=["peb"])
            self.dma("sp", w2f[:, 0, :], L("w2k"), w=["w2f"])
            self.dma("sp", w2f[:, 1, :], L("w2v"), w=["w2f"])
            self.cp("dve", w2kd[:, 0:64], w2f[:, 0, :], r=["w2f"], w=["w2kd"])
            self.cp("dve", w2kd[:, 64:128], w2f[:, 0, :], r=["w2f"], w=["w2kd"])
            self.cp("dve", w2vb[:], w2f[:, 1, :], r=["w2f"], w=["w2vb"])
            cnt = 0
            for wi, (src, skey) in enumerate(((A["kTc"], "kTc"), (A["vTc"], "vTc"))):
                for g in range(2):
                    rows = slice(g * 64, (g + 1) * 64)
                    sv = src[rows, :].rearrange("p (n s) -> p n s", s=16)
                    pb, pk = self.rot(0, 8)
                    for pos in range(32):
                        self.mm(pb[:, 0:1], w1b[wi][rows, pos, :], peb[rows, wi, pos:pos + 1], pos == 0, pos == 31,
                                r=["w1b%d" % wi, "peb"], w=[pk])
                    self.cp("dve", cb[:, cnt:cnt + 1], pb[:, 0:1], w=[pk, "cb"])
                    pb2, pk2 = self.rot(0, 8)
                    for pos in range(32):
                        rhs = sv[:, 0:127, pos] if pos < 16 else sv[:, 1:128, pos - 16]
                        self.mm(pb2[:, 0:127], w1b[wi][rows, pos, :], rhs, pos == 0, pos == 31,
                                r=["w1b%d" % wi, skey], w=[pk2])
                    hb = hsb[cnt % 2]
                    hk = "hsb%d" % (cnt % 2)
                    self.act(hb[:, 0:127], pb2[:, 0:127], AF.Silu, bias=cb[:, cnt:cnt + 1], r=["cb"], w=[pk2, hk])
                    pb3, pk3 = self.rot(0, 8)
                    if wi == 0:
                        self.mm(pb3[:, 0:127], w2kd[:, :], hb[:, 0:127], True, True, r=["w2kd", hk], w=[pk3])
                        self.cp("dve", kcT[rows, 0:127], pb3[rows, 0:127], w=[pk3, "kcT"])
                    else:
                        self.mm(pb3[0:127, 0:64], hb[:, 0:127], w2vb[:, :], True, True, r=["w2vb", hk], w=[pk3])
                        self.cp("dve", vcaug[0:127, g, 0:64], pb3[0:127, 0:64], w=[pk3, "vcaug"])
                    cnt += 1
            self.P.fence()
        if self.upto == "cmp":
            return
        NPT = 4
        PT = [self.sbt(st, "PT", [128, 17, 512], BF16) for _ in range(NPT)]
        PTc = [self.sbt(st, "PTc", [128, 512], BF16) for _ in range(2)]
        ya = [self.sbt(st, "ya", [128, 512]) for _ in range(2)]
        yb = [self.sbt(st, "yb", [128, 256]) for _ in range(2)]
        selT2 = self.sbt(st, "selT2", [128, T], BF16)
        self.memset("pool", selT2[:], 0.0, w=["selT2"])
        esink = self.sbt(st, "esink", [128, 4])
        self.dma("sp", esink[:], L("sinks"), w=["esink"])
        self.act(esink[:], esink[:], AF.Exp, w=["esink"])
        NR = 4
        rd = [self.sbt(st, "rd", [128, 4]) for _ in range(NR)]
        coef = [self.sbt(st, "coef", [128, 4]) for _ in range(NR)]
        tmpo = [self.sbt(st, "tmpo", [128, 4, 64]) for _ in range(NR)]
        tmpi = [self.sbt(st, "tmpi", [128, 4, 32]) for _ in range(2)]
        imp = [self.sbt(st, "imp", [128, 32]) for _ in range(2)]
        score = [self.sbt(st, "score", [128, 32]) for _ in range(2)]
        scw = [self.sbt(st, "scw", [128, 32]) for _ in range(2)]
        m8 = [self.sbt(st, "m8", [128, 16]) for _ in range(2)]
        selm = [self.sbt(st, "selm", [128, 96]) for _ in range(2)]
        for t_ in range(2):
            self.memset("pool", selm[t_][:], 0.0, w=["selm%d" % t_])
        self._ri = 0
        self._pt = 0
        self._ptA = 0
        gate = A["gate"]

        def epilogue(ob, ok, nh, wdt, i, g, br, dst, dkey, first):
            k = self._ri % NR
            self._ri += 1
            rk_, ck_, tk_ = "rd%d" % k, "coef%d" % k, "tmpo%d" % k
            O3 = ob[:, 0:nh * wdt].rearrange("p (h c) -> p h c", c=wdt)
            if br == "swa":
                self.tt("dve", rd[k][:, 0:nh], O3[:, :, 64], esink[:, g * 2:(g + 1) * 2], ALU.add, r=["esink"], w=[ok, rk_])
            else:
                self.ts("dve", rd[k][:, 0:nh], O3[:, :, 64], 1e-30, None, ALU.max, w=[ok, rk_])
            self.recip(rd[k][:, 0:nh], rd[k][:, 0:nh], w=[rk_])
            if br == "swa":
                cf = rd[k]
                cfk = rk_
            else:
                gc = {"cmp": 0, "slc": 8, "win": 16}[br] + g * 4
                self.tt("dve", coef[k][:, 0:nh], rd[k][:, 0:nh], gate[:, i, gc:gc + 4], ALU.mult, r=[rk_, "gate"], w=[ck_])
                cf = coef[k]
                cfk = ck_
            dst3 = dst.rearrange("p (h c) -> p h c", c=64)
            if first:
                self.tt("dve", dst3, O3[:, :, 0:64], cf[:, 0:nh].unsqueeze(2).to_broadcast([128, nh, 64]), ALU.mult,
                        r=[cfk], w=[ok, dkey])
            else:
                self.tt("dve", tmpo[k][:, 0:nh, :], O3[:, :, 0:64], cf[:, 0:nh].unsqueeze(2).to_broadcast([128, nh, 64]),
                        ALU.mult, r=[cfk], w=[ok, tk_])
                self.tt("pool", dst3, dst3, tmpo[k][:, 0:nh, :], ALU.add, r=[tk_], w=[dkey])
            return k

        def cmp_tile(i, g, yat, yak):
            M = 128
            rows = slice(g * 64, (g + 1) * 64)
            b = self._pt % 2
            self._pt += 1
            sb_, sk = self.rot(0, 6)
            self.mm(sb_[0:M, :], kcT[rows, 0:M], A["qTA"][rows, :, i * 128:(i + 1) * 128], True, False, r=["kcT", "qTA"], w=[sk])
            self.mm(sb_[0:M, :], identb[0:M, 0:M], c["penc"][0:M, i * 128:(i + 1) * 128].unsqueeze(1).to_broadcast([M, 4, 128]),
                    False, True, r=["identb", "penc"], w=[sk])
            pk_ = "PTc%d" % b
            self.act(PTc[b][0:M, :], sb_[0:M, :], AF.Exp, scale=0.125, w=[sk, pk_])
            cst = getattr(self, "cstage", 9)
            if cst < 2:
                return
            ob, ok = self.rot(6, 8)
            for hh in range(4):
                self.mm(ob[:, hh * 100:hh * 100 + 100], PTc[b][0:M, hh * 128:(hh + 1) * 128], vcaug[0:M, g, 0:100], True, True,
                        r=[pk_, "vcaug"], w=[ok])
            if cst < 3:
                return
            k = epilogue(ob, ok, 4, 100, i, g, "cmp", yat[:, g * 256:(g + 1) * 256], yak, True)
            if cst < 4:
                return
            O3 = ob[:, 0:400].rearrange("p (h c) -> p h c", c=100)
            tb = g
            self.tt("dve", tmpi[tb][:], O3[:, :, 65:97], rd[k][:, 0:4].unsqueeze(2).to_broadcast([128, 4, 32]), ALU.mult,
                    r=["rd%d" % k], w=[ok, "tmpi%d" % tb])
            self.red(imp[tb][:], tmpi[tb][:].rearrange("p h j -> p j h"), ALU.add, r=["tmpi%d" % tb], w=["imp%d" % tb])
            if cst < 5:
                return
            sc_, sk_ = score[tb], "score%d" % tb
            self.tt("dve", sc_[:], imp[tb][:], c["Mi"][:, i * 32:(i + 1) * 32], ALU.mult, r=["imp%d" % tb, "Mi"], w=[sk_])
            self.tt("dve", sc_[:], sc_[:], c["Bc"][:, i * 32:(i + 1) * 32], ALU.add, r=["Bc"], w=[sk_])
            mk, wk_, lk = "m8%d" % tb, "scw%d" % tb, "selm%d" % (i % 2)
            sm_ = selm[i % 2]
            self.P.add("dve", lambda e: e.max(out=m8[tb][:, 0:8], in_=sc_[:]), r=[sk_], w=[mk])
            self.P.add("dve", lambda e: e.match_replace(out=scw[tb][:], in_to_replace=m8[tb][:, 0:8], in_values=sc_[:],
                                                        imm_value=-1e30), r=[sk_, mk], w=[wk_])
            self.P.add("dve", lambda e: e.max(out=m8[tb][:, 8:16], in_=scw[tb][:]), r=[wk_], w=[mk])
            self.ts("dve", sm_[:, g * 64:g * 64 + 32], sc_[:], m8[tb][:, 15:16], -1.0, ALU.is_ge, ALU.add, r=[sk_, mk], w=[lk])
            if cst < 6 or g == 0:
                return
            xb, xk = self.rot(7, 8)
            self.tr(xb[0:96, 0:128], sm_[:, :], ident[:, :], r=[lk, "identf"], w=[xk])
            self.cp("act", selT2[0:96, i * 128:(i + 1) * 128], xb[0:96, 0:128], w=[xk, "selT2"])

        def attn_A2(i, br, dsts):
            if br == "slc":
                kT, kk_, V, vk, q, qk, nh, js = A["kTs"], "kTs", A["Vs"], "Vs", A["qTA"], "qTA", 4, list(range(0, i + 1))
            elif br == "win":
                kT, kk_, V, vk, q, qk, nh, js = A["kTw"], "kTw", A["Vw"], "Vw", A["qTA"], "qTA", 4, list(range(max(0, i - 4), i + 1))
            else:
                kT, kk_, V, vk, q, qk, nh, js = A["kTb"], "kTb", A["Vb"], "Vb", A["qTB"], "qTB", 2, list(range(max(0, i - 1), i + 1))
            N = nh * 128
            qs = slice(i * 128, (i + 1) * 128)
            pts = []
            for g in range(2):
                b = self._ptA % NPT
                self._ptA += 1
                pts.append((PT[b], "PT%d" % b))
            for idx, j in enumerate(js):
                banks = [self.rot(0, 6) for _ in range(2)]
                pens = [[], []]
                for g in range(2):
                    if br == "slc" and j < i and i >= 8:
                        er = slice(g * 64, g * 64 + 32)
                        pens[g].append((c["Eall"][er, j * 128:(j + 1) * 128],
                                        selT2[er, qs].unsqueeze(1).to_broadcast([32, nh, 128]), ["Eall", "selT2"]))
                    if j == i:
                        pens[g].append((identb[:, :], c["triA"][:, :].unsqueeze(1).to_broadcast([128, nh, 128]), ["identb", "triA"]))
                    if (br == "win" and j == i - 4) or (br == "swa" and j == i - 1):
                        pens[g].append((identb[:, :], c["triB"][:, :].unsqueeze(1).to_broadcast([128, nh, 128]), ["identb", "triB"]))
                for g in range(2):
                    rows = slice(g * 64, (g + 1) * 64)
                    sb_, sk = banks[g]
                    self.mm(sb_[:, 0:N], kT[rows, j * 128:(j + 1) * 128], q[rows, 0:nh, qs], True, len(pens[g]) == 0,
                            r=[kk_, qk], w=[sk])
                for g in range(2):
                    sb_, sk = banks[g]
                    for pi, (l_, r_, keys) in enumerate(pens[g]):
                        self.mm(sb_[:, 0:N], l_, r_, False, pi == len(pens[g]) - 1, r=keys, w=[sk])
                for g in range(2):
                    sb_, sk = banks[g]
                    self.act(pts[g][0][:, idx, 0:N], sb_[:, 0:N], AF.Exp, scale=0.125, w=[sk, pts[g][1]])
            return [dict(i=i, g=g, br=br, dst=dsts[g][0], dkey=dsts[g][1], pt=pts[g][0], ptk=pts[g][1], V=V, vk=vk, nh=nh, js=js, after=None)
                    for g in range(2)]

        def attn_B(S_):
            i, g, br, nh, js, pt, ptk, V, vk = S_["i"], S_["g"], S_["br"], S_["nh"], S_["js"], S_["pt"], S_["ptk"], S_["V"], S_["vk"]
            ob, ok = self.rot(6, 8)
            for hh in range(nh):
                for idx, j in enumerate(js):
                    self.mm(ob[:, hh * 68:hh * 68 + 68], pt[:, idx, hh * 128:(hh + 1) * 128], V[:, j, g, 0:68],
                            idx == 0, idx == len(js) - 1, r=[ptk, vk], w=[ok])
            epilogue(ob, ok, nh, 68, i, g, br, S_["dst"], S_["dkey"], br == "swa")
            if S_["after"] is not None:
                S_["after"]()

        def mk_after(i, b, yak, ybk):
            def f():
                self.dma("pool", d["Y"][i * 128:(i + 1) * 128, 0:512], ya[b][:], r=[yak], w=[("Y", i, 0)])
                self.dma("pool", d["Y"][i * 128:(i + 1) * 128, 512:768], yb[b][:], r=[ybk], w=[("Y", i, 1)])
            return f

        pending = []
        for i in range(NT):
            b = i % 2
            yak, ybk = "ya%d" % b, "yb%d" % b
            for g in range(2):
                cmp_tile(i, g, ya[b], yak)
            for br in ("slc", "win", "swa"):
                if br == "swa":
                    dsts = [(yb[b][:, g * 128:(g + 1) * 128], ybk) for g in range(2)]
                else:
                    dsts = [(ya[b][:, g * 256:(g + 1) * 256], yak) for g in range(2)]
                sts = attn_A2(i, br, dsts)
                if br == "swa":
                    sts[1]["after"] = mk_after(i, b, yak, ybk)
                while pending:
                    attn_B(pending.pop(0))
                pending.extend(sts)
        while pending:
            attn_B(pending.pop(0))

    def phase_rwkv(self, st, l):
        c, d = self.c, self.d
        L = lambda k: d["%s%d" % (k, l)]
        F = d["F"]
        identf, bones, hsel = c["identf"], c["bones"], c["hsel"]
        rv = self.sbt(st, "rv", [128, 20])
        self.dma("sp", rv[:], L("rv"), w=["rv"])
        omka = self.sbt(st, "omka", [128, 2])
        self.ts("dve", omka[:], rv[:, 13:15], -1.0, 1.0, ALU.mult, ALU.add, r=["rv"], w=["omka"])
        lnw = self.sbt(st, "lnw", [128, 256])
        lnb = self.sbt(st, "lnb", [128, 256])
        self.dma("sp", lnw[:], L("lnw"), w=["lnw"])
        self.dma("sp", lnb[:], L("lnb"), w=["lnb"])
        wab = self.sbt(st, "wab", [128, 256], BF16)
        thad = self.sbt(st, "thad", [128, T], BF16)
        if l > 0:
            v1b = self.sbt(st, "v1b", [128, 2, 32], BF16)
            v2b = self.sbt(st, "v2b", [32, 256], BF16)
            t1 = self.sbt(st, "t1", [32, T], BF16)

        def load_shift(dst, dkey, cidx, f, fp, fk, fpk):
            self.dma("sp", f[:, :], F[cidx], r=[("F", cidx)], w=[fk])
            self.memset("pool", fp[:, 0:1], 0.0, w=[fpk])
            self.dma("sp", fp[:, 1:T], F[cidx][:, 0:T - 1], r=[("F", cidx)], w=[fpk])
            self.tt("pool", fp[:, :], fp[:, :], f[:, :], ALU.subtract, r=[fk], w=[fpk])
            self.stt("dve", dst, fp[:, :], rv[:, cidx:cidx + 1], f[:, :], ALU.mult, ALU.add, r=[fpk, fk, "rv"], w=[dkey])

        with ExitStack() as sp:
            f = self.sbt(sp, "f", [128, T])
            fp = self.sbt(sp, "fp", [128, T])
            wdx = self.sbt(sp, "wdx", [128, T])
            wf = self.sbt(sp, "wf", [128, 256])
            self.dma("sp", wf[:], L("wa"), w=["wf"])
            self.cp("dve", wab[:], wf[:], r=["wf"], w=["wab"])
            load_shift(wdx[:, :], "wdx", 6, f, fp, "f", "fp")
            self.act(thad[0:64, :], wdx[0:64, :], AF.Tanh, r=["wdx"], w=["thad"])
            self.cp("dve", thad[64:128, :], wdx[64:128, :], r=["wdx"], w=["thad"])
            if l > 0:
                v1f = self.sbt(sp, "v1f", [128, 2, 32])
                v2f = self.sbt(sp, "v2f", [32, 256])
                vxb = self.sbt(sp, "vxb", [128, T], BF16)
                self.dma("sp", v1f[:], L("v1"), w=["v1f"])
                self.dma("sp", v2f[:], L("v2"), w=["v2f"])
                self.cp("dve", v1b[:], v1f[:], r=["v1f"], w=["v1b"])
                self.cp("dve", v2b[:], v2f[:], r=["v2f"], w=["v2b"])
                banks = [self.rot(0, 8) for _ in range(4)]
                for p in range(2):
                    load_shift(wdx[:, :], "wdx", 4 + p, f, fp, "f", "fp")
                    self.cp("dve", vxb[:], wdx[:], r=["wdx"], w=["vxb"])
                    for tc in range(4):
                        pb, pk = banks[tc]
                        self.mm(pb[0:32, :], v1b[:, p, :], vxb[:, tc * 512:(tc + 1) * 512], p == 0, p == 1, r=["v1b", "vxb"], w=[pk])
                for tc in range(4):
                    pb, pk = banks[tc]
                    self.cp("act", t1[0:32, tc * 512:(tc + 1) * 512], pb[0:32, :], w=[pk, "t1"])
            self.P.fence()

        v2d = lambda t: t[:].rearrange("p n t -> p (n t)")
        for p in range(2):
            ps_ = slice(p * 128, (p + 1) * 128)
            with ExitStack() as sP:
                Rs = self.sbt(sP, "Rs", [128, T])
                AhT = self.sbt(sP, "AhT", [128, T])
                rkb = self.sbt(sP, "rkb", [128, T])
                ArbT = [self.sbt(sP, "ArbT", [128, 16, 128]) for _ in range(2)]
                UV = self.sbt(sP, "UV", [128, 16, 128])
                YV = self.sbt(sP, "YV", [128, 16, 128])
                KVbd = self.sbt(sP, "KVbd", [128, 16, 128])
                tokB0 = self.sbt(sP, "tokB0", [128, 16, 128])
                tokB1 = self.sbt(sP, "tokB1", [128, 16, 128])
                tokV = self.sbt(sP, "tokV", [128, 16, 128])
                ycst = self.sbt(sP, "ycst", [128, 16, 128])
                Gbd = self.sbt(sP, "Gbd", [128, 128])
                gam = self.sbt(sP, "gam", [128, 16])
                sM = ExitStack()
                As = self.sbt(sM, "As", [128, T])
                Ks = self.sbt(sM, "Ks", [128, T])
                Bs = self.sbt(sM, "Bs", [128, T])
                tokA = self.sbt(sM, "tokA", [128, 16, 128])
                tokK = self.sbt(sM, "tokK", [128, 16, 128])
                pkk = lambda n: ("Pk", n // 4)
                Vt, kVt = AhT[:, :], "AhT"
                lw, klw = v2d(ArbT[0]), "ArbT0"
                aT, kaT = v2d(ArbT[1]), "ArbT1"
                kk, kkk = v2d(UV), "UV"
                e1, ke1 = v2d(YV), "YV"
                e2, ke2 = v2d(KVbd), "KVbd"
                e3, ke3 = v2d(ycst), "ycst"
                cl = [(v2d(tokB0), "tokB0"), (v2d(tokV), "tokV")]
                HF = 1024
                H = lambda ap, hf: ap[:, hf * HF:(hf + 1) * HF]
                K2 = lambda k, hf: (k, hf)
                KB2 = lambda k: [(k, 0), (k, 1)]
                cl0, kcl0 = cl[0]
                cl1, kcl1 = cl[1]

                def load_shift2(dst, dkey, cidx, f, fk, fp, fpk):
                    self.dma("sp", f[:, 0:HF], F[cidx][:, 0:HF], r=[("F", cidx)], w=[K2(fk, 0)])
                    self.dma("sp", f[:, HF:T], F[cidx][:, HF:T], r=[("F", cidx)], w=[K2(fk, 1)])
                    self.ts("pool", fp[:, 0:1], f[:, 0:1], -1.0, None, ALU.mult, r=[K2(fk, 0)], w=[K2(fpk, 0)])
                    self.tt("pool", fp[:, 1:HF], f[:, 0:HF - 1], f[:, 1:HF], ALU.subtract, r=[K2(fk, 0)], w=[K2(fpk, 0)])
                    self.tt("pool", fp[:, HF:T], f[:, HF - 1:T - 1], f[:, HF:T], ALU.subtract, r=KB2(fk), w=[K2(fpk, 1)])
                    for hf in range(2):
                        self.stt("dve", H(dst, hf), H(fp, hf), rv[:, cidx:cidx + 1], H(f, hf), ALU.mult, ALU.add,
                                 r=[K2(fpk, hf), K2(fk, hf), "rv"], w=[K2(dkey, hf)])

                load_shift2(Rs[:, :], "Rs", p, e2, ke2, e3, ke3)
                load_shift2(Ks[:, :], "Ks", 2 + p, kk, kkk, e1, ke1)
                load_shift2(Vt, kVt, 4 + p, cl0, kcl0, cl1, kcl1)
                Rsa, Ksa, Bsa, Asa, rkba = Rs[:, :], Ks[:, :], Bs[:, :], As[:, :], rkb[:, :]
                steps = []

                def st_sig(hf):
                    for tc in (2 * hf, 2 * hf + 1):
                        ts_ = slice(tc * 512, (tc + 1) * 512)
                        pb, pk = self.rot(0, 8)
                        self.mm(pb[:, :], wab[0:64, ps_], thad[0:64, ts_], True, True, r=["wab", "thad"], w=[pk])
                        self.act(lw[:, ts_], pb[:, :], AF.Sigmoid, bias=rv[:, 7 + p:8 + p], r=["rv"], w=[pk, K2(klw, hf)])
                        pb, pk = self.rot(0, 8)
                        self.mm(pb[:, :], wab[64:128, ps_], thad[64:128, ts_], True, True, r=["wab", "thad"], w=[pk])
                        self.act(aT[:, ts_], pb[:, :], AF.Sigmoid, bias=rv[:, 9 + p:10 + p], r=["rv"], w=[pk, K2(kaT, hf)])
                        if l > 0:
                            pb, pk = self.rot(0, 8)
                            self.mm(pb[:, :], v2b[0:32, ps_], t1[0:32, ts_], True, True, r=["v2b", "t1"], w=[pk])
                            self.act(e1[:, ts_], pb[:, :], AF.Sigmoid, bias=rv[:, 17 + p:18 + p], r=["rv"], w=[pk, K2(ke1, hf)])
                steps.append(st_sig)
                if l > 0:
                    self.dma("sp", e2, d["vf"][p], r=[("vf", p)], w=KB2(ke2))
                    steps.append(lambda hf: self.tt("pool", H(e2, hf), H(e2, hf), H(Vt, hf), ALU.subtract, r=[K2(kVt, hf)], w=[K2(ke2, hf)]))
                    steps.append(lambda hf: self.tt("dve", H(e2, hf), H(e2, hf), H(e1, hf), ALU.mult, r=[K2(ke1, hf)], w=[K2(ke2, hf)]))
                    steps.append(lambda hf: self.tt("pool", H(Vt, hf), H(Vt, hf), H(e2, hf), ALU.add, r=[K2(ke2, hf)], w=[K2(kVt, hf)]))
                else:
                    self.dma("act", d["vf"][p], Vt, r=KB2(kVt), w=[("vf", p)])
                steps.append(lambda hf: self.act(H(kk, hf), H(Ksa, hf), AF.Copy, scale=rv[:, 11 + p:12 + p], r=[K2("Ks", hf), "rv"], w=[K2(kkk, hf)]))
                steps.append(lambda hf: self.act(H(e1, hf), H(kk, hf), AF.Square, r=[K2(kkk, hf)], w=[K2(ke1, hf)]))

                def st_norm(hf):
                    for tc in (2 * hf, 2 * hf + 1):
                        ts_ = slice(tc * 512, (tc + 1) * 512)
                        pb, pk = self.rot(0, 8)
                        self.mm(pb[:, :], bones[:, :], e1[:, ts_], True, True, r=["bones", K2(ke1, hf)], w=[pk])
                        self.act(e2[:, ts_], pb[:, :], AF.Sqrt, w=[pk, K2(ke2, hf)])
                steps.append(st_norm)
                steps.append(lambda hf: self.ts("dve", H(e2, hf), H(e2, hf), 1e-12, None, ALU.max, w=[K2(ke2, hf)]))
                steps.append(lambda hf: self.recip(H(e2, hf), H(e2, hf), w=[K2(ke2, hf)]))
                steps.append(lambda hf: self.tt("dve", H(kk, hf), H(kk, hf), H(e2, hf), ALU.mult, r=[K2(ke2, hf)], w=[K2(kkk, hf)]))
                steps.append(lambda hf: self.act(H(e1, hf), H(aT, hf), AF.Identity, bias=omka[:, p:p + 1], scale=rv[:, 13 + p:14 + p],
                                                 r=[K2(kaT, hf), "rv", "omka"], w=[K2(ke1, hf)]))
                steps.append(lambda hf: self.tt("pool", H(Ksa, hf), H(Ksa, hf), H(e1, hf), ALU.mult, r=[K2(ke1, hf)], w=[K2("Ks", hf)]))
                steps.append(lambda hf: self.stt("dve", H(rkba, hf), H(Rsa, hf), rv[:, 15 + p:16 + p], H(Ksa, hf), ALU.mult, ALU.mult,
                                                 r=[K2("Rs", hf), K2("Ks", hf), "rv"], w=[K2("rkb", hf)]))
                steps.append(lambda hf: self.tt("pool", H(Bsa, hf), H(kk, hf), H(aT, hf), ALU.mult, r=[K2(kkk, hf), K2(kaT, hf)], w=[K2("Bs", hf)]))
                chain_src = [(lw, klw)]
                for si, sh in enumerate((1, 2, 4, 8, 16, 32, 64)):
                    dst, dkey = cl[si % 2]
                    src, skey = chain_src[-1]

                    def st_scan(hf, src=src, skey=skey, dst=dst, dkey=dkey, sh=sh):
                        s3 = H(src, hf).rearrange("p (n t) -> p n t", t=128)
                        d3 = H(dst, hf).rearrange("p (n t) -> p n t", t=128)
                        self.cp("act", d3[:, :, 0:sh], s3[:, :, 0:sh], r=[K2(skey, hf)], w=[K2(dkey, hf)])
                        self.tt("dve", d3[:, :, sh:128], s3[:, :, sh:128], s3[:, :, 0:128 - sh], ALU.add, r=[K2(skey, hf)], w=[K2(dkey, hf)])
                    steps.append(st_scan)
                    chain_src.append((dst, dkey))
                csrc, cskey = chain_src[-1]
                CW = -float(np.exp(-0.5))
                steps.append(lambda hf: self.act(H(e1, hf), H(csrc, hf), AF.Exp, scale=CW, r=[K2(cskey, hf)], w=[K2(ke1, hf)]))
                steps.append(lambda hf: self.act(H(e2, hf), H(csrc, hf), AF.Exp, scale=-CW, r=[K2(cskey, hf)], w=[K2(ke2, hf)]))
                steps.append(lambda hf: self.tt("pool", H(e3, hf), H(csrc, hf), H(lw, hf), ALU.subtract, r=[K2(cskey, hf), K2(klw, hf)], w=[K2(ke3, hf)]))
                steps.append(lambda hf: self.act(H(e3, hf), H(e3, hf), AF.Exp, scale=CW, w=[K2(ke3, hf)]))
                steps.append(lambda hf: self.cp("act", gam[:, hf * 8:(hf + 1) * 8], H(e1, hf).rearrange("p (n t) -> p n t", t=128)[:, :, 127],
                                                r=[K2(ke1, hf)], w=[K2("gam", hf)]))
                steps.append(lambda hf: self.tt("dve", H(Rsa, hf), H(Rsa, hf), H(e1, hf), ALU.mult, r=[K2(ke1, hf)], w=[K2("Rs", hf)]))
                steps.append(lambda hf: self.tt("pool", H(Ksa, hf), H(Ksa, hf), H(e2, hf), ALU.mult, r=[K2(ke2, hf)], w=[K2("Ks", hf)]))
                steps.append(lambda hf: self.tt("pool", H(Bsa, hf), H(Bsa, hf), H(e2, hf), ALU.mult, r=[K2(ke2, hf)], w=[K2("Bs", hf)]))
                steps.append(lambda hf: self.stt("dve", H(Asa, hf), H(kk, hf), -1.0, H(e3, hf), ALU.mult, ALU.mult, r=[K2(kkk, hf), K2(ke3, hf)], w=[K2("As", hf)]))
                for stp in steps:
                    for hf in range(2):
                        stp(hf)
                self.memset("pool", tokB0[:], 0.0, w=KB2("tokB0"))
                self.memset("pool", tokB1[:], 0.0, w=["tokB1"])
                for n in range(16):
                    ch = slice(n * 128, (n + 1) * 128)
                    hf = n // 8
                    pb, pk = self.rot(0, 8)
                    self.tr(pb[:, 0:128], Ks[:, ch], identf[:, :], r=[K2("Ks", hf), "identf"], w=[pk])
                    self.tr(pb[:, 128:256], As[:, ch], identf[:, :], r=[K2("As", hf), "identf"], w=[pk])
                    self.tr(pb[:, 256:384], Bs[:, ch], identf[:, :], r=[K2("Bs", hf), "identf"], w=[pk])
                    self.tr(pb[:, 384:512], Vt[:, ch], identf[:, :], r=[K2(kVt, hf), "identf"], w=[pk])
                    self.cp("act", tokK[:, n, :], pb[:, 0:128], w=[pk, "tokK"])
                    self.cp("act", tokA[:, n, :], pb[:, 128:256], w=[pk, "tokA"])
                    self.cp("dve", tokB0[:, n, 0:64], pb[:, 256:320], w=[pk, K2("tokB0", hf)])
                    self.cp("dve", tokB1[:, n, 64:128], pb[:, 320:384], w=[pk, "tokB1"])
                    self.cp("dve", tokV[:, n, :], pb[:, 384:512], w=[pk, K2("tokV", hf)])
                self.P.fence()
                rstage = getattr(self, "rstage", 9)
                if rstage < 3:
                    sM.close()
                    continue
                self.memset("pool", KVbd[:], 0.0, w=["KVbd"])
                self.memset("pool", Gbd[:], 0.0, w=["Gbd"])
                for n0 in range(0, 16, 4):
                    pb, pk = self.rot(0, 8)
                    for q in range(4):
                        self.mm(pb[:, q * 128:(q + 1) * 128], tokK[:, n0 + q, :], tokV[:, n0 + q, :], True, True, r=["tokK", "tokV"], w=[pk])
                    p3 = pb[:].rearrange("p (q t) -> p q t", q=4)
                    for hd in range(2):
                        hs = slice(hd * 64, (hd + 1) * 64)
                        self.cp("act" if hd else "dve", KVbd[hs, n0:n0 + 4, hs], p3[hs, :, hs], w=[pk, "KVbd"])
                sC = ExitStack()
                XT = [ycst, tokK]
                Pk = [self.sbt(sC, "Pk", [128, 16, 128], BF16) for _ in range(2)]
                Nk = [self.sbt(sC, "Nk", [128, 16, 128], BF16) for _ in range(2)]
                XTb = [self.sbt(sC, "XTb", [128, 16, 128], BF16) for _ in range(2)]
                Lk = [self.sbt(sC, "Lk", [128, 4, 2, 128]) for _ in range(2)]
                WV = [self.sbt(sC, "WV", [128, 4, 64]) for _ in range(2)]
                pkk = lambda hd, n: ("Pk", hd, n // 4)
                nkk = lambda hd, n: ("Nk", hd, n // 4)
                xtk = lambda hd, n: ("XT", hd, n // 4)
                xbk = lambda hd, n: ("XTb", hd, n // 4)
                bc4 = lambda m: c[m][:, :].unsqueeze(1).to_broadcast([128, 4, 128])
                q4 = lambda pb: pb[:].rearrange("p (q t) -> p q t", q=4)
                HS = [slice(0, 64), slice(64, 128)]
                ev = 0
                for n0 in range(0, 16, 4):
                    bk = [[self.rot(0, 8) for _ in range(2)] for _ in range(3)]
                    for q in range(4):
                        ch = slice((n0 + q) * 128, (n0 + q + 1) * 128)
                        qs = slice(q * 128, (q + 1) * 128)
                        for which, (lh, rh, lkey, rkey) in enumerate(((As, Bs, "As", "Bs"), (Bs, As, "Bs", "As"), (Bs, Rs, "Bs", "Rs"))):
                            for hd in range(2):
                                hs = HS[hd]
                                self.mm(bk[which][hd][0][:, qs], lh[hs, ch], rh[hs, ch], True, True, r=[lkey, rkey], w=[bk[which][hd][1]])
                    for hd in range(2):
                        xk = [xtk(hd, n0)] + (["tokK"] if hd == 1 else [])
                        self.tt("dve", Pk[hd][:, n0:n0 + 4, :], q4(bk[0][hd][0]), bc4("mSL"), ALU.mult, r=["mSL"], w=[bk[0][hd][1], pkk(hd, n0)])
                        self.tt("dve", XT[hd][:, n0:n0 + 4, :], q4(bk[1][hd][0]), bc4("mSU"), ALU.mult, r=["mSU"], w=[bk[1][hd][1]] + xk)
                        self.tt("dve", ArbT[hd][:, n0:n0 + 4, :], q4(bk[2][hd][0]), bc4("mIU"), ALU.mult, r=["mIU"], w=[bk[2][hd][1], "ArbT%d" % hd])
                        self.cp("act", Nk[hd][:, n0:n0 + 4, :], XT[hd][:, n0:n0 + 4, :], r=[xtk(hd, n0)], w=[nkk(hd, n0)])
                        self.tt("pool", XT[hd][:, n0:n0 + 4, :], XT[hd][:, n0:n0 + 4, :], identf[:, :].unsqueeze(1).to_broadcast([128, 4, 128]),
                                ALU.add, r=["identf"], w=[xtk(hd, n0)])
                        self.cp("act", XTb[hd][:, n0:n0 + 4, :], XT[hd][:, n0:n0 + 4, :], r=[xtk(hd, n0)], w=[xbk(hd, n0)])
                G4 = list(range(0, 16, 4))
                for hd in range(2):
                    for k in range(1, 7):
                        pbanks, nbanks, xbanks = {}, {}, {}
                        for n0 in G4:
                            pbanks[n0] = self.rot(0, 8)
                            for q in range(4):
                                n = n0 + q
                                self.mm(pbanks[n0][0][:, q * 128:(q + 1) * 128], Nk[hd][:, n, :], Pk[hd][:, n, :], True, True,
                                        r=[nkk(hd, n0), pkk(hd, n0)], w=[pbanks[n0][1]])
                        if k <= 5:
                            for n0 in G4:
                                nbanks[n0] = self.rot(0, 8)
                                for q in range(4):
                                    n = n0 + q
                                    self.mm(nbanks[n0][0][:, q * 128:(q + 1) * 128], Pk[hd][:, n, :], Nk[hd][:, n, :], True, True,
                                            r=[nkk(hd, n0), pkk(hd, n0)], w=[nbanks[n0][1]])
                        for n0 in G4:
                            self.cp("act", Pk[hd][:, n0:n0 + 4, :], q4(pbanks[n0][0]), w=[pbanks[n0][1], pkk(hd, n0)])
                        if k <= 5:
                            for n0 in G4:
                                self.cp("act" if (n0 // 4) % 2 else "dve", Nk[hd][:, n0:n0 + 4, :], q4(nbanks[n0][0]), w=[nbanks[n0][1], nkk(hd, n0)])
                        for n0 in G4:
                            xbanks[n0] = self.rot(0, 8)
                            for q in range(4):
                                n = n0 + q
                                self.mm(xbanks[n0][0][:, q * 128:(q + 1) * 128], Pk[hd][:, n, :], XTb[hd][:, n, :], True, True,
                                        r=[pkk(hd, n0), xbk(hd, n0)], w=[xbanks[n0][1]])
                        for n0 in G4:
                            self.tt("dve", XT[hd][:, n0:n0 + 4, :], q4(xbanks[n0][0]), XT[hd][:, n0:n0 + 4, :], ALU.add, w=[xbanks[n0][1], xtk(hd, n0)])
                        if k < 6:
                            for n0 in G4:
                                self.cp("act" if (n0 // 4) % 2 else "pool", XTb[hd][:, n0:n0 + 4, :], XT[hd][:, n0:n0 + 4, :], r=[xtk(hd, n0)], w=[xbk(hd, n0)])
                for n0 in range(0, 16, 4):
                    bL = [self.rot(0, 8) for _ in range(2)]
                    bA = [self.rot(0, 8) for _ in range(2)]
                    for q in range(4):
                        ch = slice((n0 + q) * 128, (n0 + q + 1) * 128)
                        qs = slice(q * 128, (q + 1) * 128)
                        for hd in range(2):
                            self.mm(bL[hd][0][:, qs], Ks[HS[hd], ch], As[HS[hd], ch], True, True, r=["Ks", "As"], w=[bL[hd][1]])
                        for hd in range(2):
                            self.mm(bA[hd][0][:, qs], Ks[HS[hd], ch], Rs[HS[hd], ch], True, True, r=["Ks", "Rs"], w=[bA[hd][1]])
                    for hd in range(2):
                        self.tt("dve", Lk[hd][:, :, 0, :], q4(bL[hd][0]), bc4("mSU"), ALU.mult, r=["mSU"], w=[bL[hd][1], "Lk%d" % hd])
                        self.tt("dve", Lk[hd][:, :, 1, :], q4(bA[hd][0]), bc4("mIU"), ALU.mult, r=["mIU"], w=[bA[hd][1], "Lk%d" % hd])
                    for hd in range(2):
                        hs = HS[hd]
                        lk_, wk_ = "Lk%d" % hd, "WV%d" % hd
                        bw, kw = self.rot(0, 8)
                        bh, kh = self.rot(0, 8)
                        for q in range(4):
                            n = n0 + q
                            self.mm(bw[:, q * 64:(q + 1) * 64], Lk[hd][:, q, 0, :], tokV[:, n, hs], True, True, r=[lk_, "tokV"], w=[kw])
                            self.mm(bw[:, 256 + q * 64:256 + (q + 1) * 64], Lk[hd][:, q, 1, :], tokV[:, n, hs], True, True, r=[lk_, "tokV"], w=[kw])
                            self.mm(bh[:, q * 128:(q + 1) * 128], tokA[:, n, :], XT[hd][:, n, :], True, True, r=["tokA", xtk(hd, n0)], w=[kh])
                        self.cp("act", WV[hd][:, :, :], bw[:, 0:256].rearrange("p (q v) -> p q v", q=4), w=[kw, wk_])
                        self.cp("act", YV[:, n0:n0 + 4, hs], bw[:, 256:512].rearrange("p (q v) -> p q v", q=4), w=[kw, "YV"])
                        self.cp("dve", AhT[hs, n0 * 128:(n0 + 4) * 128], bh[hs, :], w=[kh, "AhT"])
                        bu, ku = self.rot(0, 8)
                        for q in range(4):
                            n = n0 + q
                            self.mm(bu[:, q * 64:(q + 1) * 64], XT[hd][:, n, :], WV[hd][:, q, :], True, True, r=[xtk(hd, n0), wk_], w=[ku])
                        self.cp("act", UV[:, n0:n0 + 4, hs], bu[:, 0:256].rearrange("p (q v) -> p q v", q=4), w=[ku, "UV"])
                self.tt("pool", KVbd[:], KVbd[:], gam[:, :].unsqueeze(2).to_broadcast([128, 16, 128]), ALU.mult, r=["gam"], w=["KVbd"])
                self.P.fence()
                sC.close()
                sM.close()
                cS = ExitStack()
                Usb = [self.sbt(cS, "Usb", [128, 128]) for _ in range(2)]
                Tg = [self.sbt(cS, "Tg", [128, 128]) for _ in range(2)]
                ysqA = self.sbt(cS, "ysqA", [128, 16, 128])
                smA = self.sbt(cS, "smA", [128, 64])
                bon = self.sbt(cS, "bon", [128, 16, 2])
                for n in range(16 if rstage >= 4 else 0):
                    ch = slice(n * 128, (n + 1) * 128)
                    q = n % 2
                    uk = "Usb%d" % q
                    self.stt("dve", Tg[q][:, :], Gbd[:, :], gam[:, n:n + 1], KVbd[:, n, :], ALU.mult, ALU.add, r=["Gbd", "KVbd", "gam"], w=["Tg%d" % q])
                    pbu, pku = self.rot(0, 8)
                    self.mm(pbu[:, 0:128], AhT[:, ch], Gbd[:, :], True, True, r=["AhT", "Gbd"], w=[pku])
                    self.tt("dve", Usb[q][:, :], pbu[:, 0:128], UV[:, n, :], ALU.add, r=["UV"], w=[pku, uk])
                    pby, pky = self.rot(0, 8)
                    self.mm(pby[:, 0:128], Rs[:, ch], Gbd[:, :], True, False, r=["Rs", "Gbd"], w=[pky])
                    self.mm(pby[:, 0:64], ArbT[0][:, n, :], Usb[q][:, 0:64], False, False, r=["ArbT0", uk], w=[pky])
                    self.mm(pby[:, 64:128], ArbT[1][:, n, :], Usb[q][:, 64:128], False, True, r=["ArbT1", uk], w=[pky])
                    self.mm(pby[:, 128:130], rkb[:, ch], hsel[:, 0:2], True, True, r=["rkb", "hsel"], w=[pky])
                    pbg, pkg = self.rot(0, 8)
                    self.mm(pbg[:, 0:64], tokB0[:, n, :], Usb[q][:, 0:64], True, True, r=["tokB0", uk], w=[pkg])
                    self.mm(pbg[:, 64:128], tokB1[:, n, :], Usb[q][:, 64:128], True, True, r=["tokB1", uk], w=[pkg])
                    self.stt("dve", Gbd[:, :], pbg[:, 0:128], gam[:, n:n + 1], Tg[q][:, :], ALU.mult, ALU.add, r=["gam", "Tg%d" % q], w=[pkg, "Gbd"])
                    self.tt("dve", ycst[:, n, :], pby[:, 0:128], YV[:, n, :], ALU.add, r=["YV"], w=[pky, ("ycst", n)])
                    self.cp("act", bon[:, n, :], pby[:, 128:130], w=[pky, ("bon", n)])
                if rstage >= 4:
                    yk_all = [("ycst", n) for n in range(16)]
                    y4 = ycst[:].rearrange("p n (h c) -> p (n h) c", c=64)
                    yf = ycst[:].rearrange("p n c -> p (n c)")
                    sq4 = ysqA[:].rearrange("p n (h c) -> p (n h) c", c=64)
                    bc32 = lambda t: t.unsqueeze(2).to_broadcast([128, 32, 64])
                    self.red(smA[:, 0:32], y4, ALU.add, r=yk_all, w=["smA"])
                    self.ts("dve", smA[:, 0:32], smA[:, 0:32], -1.0 / 64, None, ALU.mult, w=["smA"])
                    self.tt("dve", y4, y4, bc32(smA[:, 0:32]), ALU.add, r=["smA"], w=yk_all)
                    self.tt("pool", ysqA[:], ycst[:], ycst[:], ALU.mult, r=yk_all, w=["ysqA"])
                    self.red(smA[:, 32:64], sq4, ALU.add, r=["ysqA"], w=["smA"])
                    self.ts("dve", smA[:, 32:64], smA[:, 32:64], 1.0 / 64, 64e-5, ALU.mult, ALU.add, w=["smA"])
                    self.act(smA[:, 32:64], smA[:, 32:64], AF.Sqrt, w=["smA"])
                    self.recip(smA[:, 32:64], smA[:, 32:64], w=["smA"])
                    self.tt("dve", y4, y4, bc32(smA[:, 32:64]), ALU.mult, r=["smA"], w=yk_all)
                    self.tt("pool", ycst[:], ycst[:], lnw[:, ps_].unsqueeze(1).to_broadcast([128, 16, 128]), ALU.mult, r=["lnw"], w=yk_all)
                    self.tt("pool", ycst[:], ycst[:], lnb[:, ps_].unsqueeze(1).to_broadcast([128, 16, 128]), ALU.add, r=["lnb"], w=yk_all)
                    self.tt("dve", sq4, tokV[:].rearrange("p n (h c) -> p (n h) c", c=64),
                            bc32(bon[:].rearrange("p n h -> p (n h)")), ALU.mult, r=["tokV"] + [("bon", n) for n in range(16)], w=["ysqA"])
                    self.tt("pool", ycst[:], ycst[:], ysqA[:], ALU.add, r=["ysqA"], w=yk_all)
                    for n in range(16):
                        self.dma("act", d["Y"][n * 128:(n + 1) * 128, 768 + p * 128:768 + (p + 1) * 128], ycst[:, n, :],
                                 r=[("ycst", n)], w=[("Y", "c", p, n)])
                self.P.fence()
                cS.close()

    def phase_out(self, st, l, xin, xkey, xo, xokey):
        c, d = self.c, self.d
        L = lambda k: d["%s%d" % (k, l)]
        identf = c["identf"]
        final = (l == DEPTH - 1)
        pwb = self.sbt(st, "pwb", [128, 16, 1024], BF16)
        pst = [self.sbt(st, "pst", [128, 2, 1024]) for _ in range(2)]
        Yt = [self.sbt(st, "Yt", [128, 1024]) for _ in range(2)]
        Zt = [self.sbt(st, "Zt", [128, 1024]) for _ in range(2)]
        Gt = [self.sbt(st, "Gt", [128, 3072]) for _ in range(2)]
        xt = [self.sbt(st, "xt", [128, 1024]) for _ in range(2)]
        yzT = [self.sbt(st, "yzT", [128, 8, 128], BF16) for _ in range(2)]
        mixed = [self.sbt(st, "mixed", [128, 1024]) for _ in range(2)]
        mixb = [self.sbt(st, "mixb", [128, 1024], BF16) for _ in range(2)]
        yzb = [self.sbt(st, "yzb", [128, 1024], BF16) for _ in range(2)]
        mxT = [self.sbt(st, "mxT", [128, 8, 128], BF16) for _ in range(2)]
        tmpa = [self.sbt(st, "tmpa", [128, 512]) for _ in range(2)]
        xn = [self.sbt(st, "xn", [128, 1024]) for _ in range(2)]
        jk = self.sbt(st, "jkb", [128, 1024], BF16)
        ss = [self.sbt(st, "ss", [128, 1]) for _ in range(2)]
        self._ta = 0

        def stage_A(i):
            b = i % 2
            rs = slice(i * 128, (i + 1) * 128)
            ky, kz = "Yt%d" % b, "Zt%d" % b
            self.dma("sp", Yt[b][:], d["Y"][rs, :], r=[("Y", i, 0), ("Y", i, 1)] + [("Y", "c", pp, i) for pp in range(2)], w=[ky])
            self.dma("sp", Zt[b][:], d["Z"][rs, :], r=[("Z", i, 0), ("Z", i, 1)], w=[kz])
            kyb = "yzb%d" % b
            self.tt("pool", yzb[b][:], Yt[b][:], Zt[b][:], ALU.mult, r=[kz, ky], w=[kyb])
            pb, pk = self.rot(0, 8)
            pbb = pb[:].bitcast(BF16)
            for ch in range(8):
                self.tr(pbb[:, ch * 128:(ch + 1) * 128], yzb[b][:, ch * 128:(ch + 1) * 128], c["identb"][:], r=[kyb, "identb"], w=[pk])
            self.cp("act", yzT[b][:, :, :], pbb.rearrange("p (c t) -> p c t", c=8), w=[pk, "yzT%d" % b])

        def stage_B(i):
            b = i % 2
            rs = slice(i * 128, (i + 1) * 128)
            kg = "Gt%d" % b
            self.dma("sp", Gt[b][:], d["G"][rs, :], r=[("G", i, m) for m in range(6)], w=[kg])
            for half in range(2):
                cs = slice(half * 512, (half + 1) * 512)
                for bi, (k0, k1) in enumerate(((0, 4), (4, 6), (6, 8))):
                    pb, pk = self.rot(0, 8)
                    for kc in range(k0, k1):
                        self.mm(pb[:, :], yzT[b][:, kc, :], pwb[:, kc, cs], kc == k0, kc == k1 - 1, r=["yzT%d" % b, "pwb"], w=[pk])
                    gsl = Gt[b][:, bi * 1024 + half * 512:bi * 1024 + (half + 1) * 512]
                    if bi == 0:
                        self.tt("dve", mixed[b][:, cs], pb[:, :], gsl, ALU.mult, r=[kg], w=[pk, "mixed%d" % b])
                    else:
                        t_ = self._ta % 2
                        self._ta += 1
                        self.tt("dve", tmpa[t_][:], pb[:, :], gsl, ALU.mult, r=[kg], w=[pk, "tmpa%d" % t_])
                        if bi == 1:
                            self.tt("pool", mixed[b][:, cs], mixed[b][:, cs], tmpa[t_][:], ALU.add, r=["tmpa%d" % t_], w=["mixed%d" % b])
                        else:
                            self.tt("pool", mixb[b][:, cs], mixed[b][:, cs], tmpa[t_][:], ALU.add, r=["tmpa%d" % t_, "mixed%d" % b], w=["mixb%d" % b])

        def stage_CD(i):
            b = i % 2
            rs = slice(i * 128, (i + 1) * 128)
            kx = "xt%d" % b
            self.dma("sp", xt[b][:], xin[rs, :], r=[(xkey, i)], w=[kx])
            pb, pk = self.rot(0, 8)
            pbb = pb[:].bitcast(BF16)
            for ch in range(8):
                self.tr(pbb[:, ch * 128:(ch + 1) * 128], mixb[b][:, ch * 128:(ch + 1) * 128], c["identb"][:], r=["mixb%d" % b, "identb"], w=[pk])
            self.cp("act", mxT[b][:, :, :], pbb.rearrange("p (c t) -> p c t", c=8), w=[pk, "mxT%d" % b])
            for half in range(2):
                cs = slice(half * 512, (half + 1) * 512)
                pb, pk = self.rot(0, 8)
                for kc in range(8):
                    self.mm(pb[:, :], mxT[b][:, kc, :], pwb[:, 8 + kc, cs], kc == 0, kc == 7, r=["mxT%d" % b, "pwb"], w=[pk])
                self.tt("dve", xn[b][:, cs], pb[:, :], xt[b][:, cs], ALU.add, r=[kx], w=[pk, "xn%d" % b])
            if final:
                kss = "ss%d" % b
                self.memset("pool", ss[b][:], 0.0, w=[kss])
                self.act(jk[:], xn[b][:], AF.Square, accum=ss[b][:], r=["xn%d" % b], w=["jkb", kss])
                self.ts("dve", ss[b][:], ss[b][:], 1.0 / D, 1e-6, ALU.mult, ALU.add, w=[kss])
                self.act(ss[b][:], ss[b][:], AF.Sqrt, w=[kss])
                self.recip(ss[b][:], ss[b][:], w=[kss])
                self.stt("dve", xn[b][:], xn[b][:], ss[b][:, 0:1], c["fg"][:], ALU.mult, ALU.mult, r=[kss, "fg"], w=["xn%d" % b])
            self.dma("act", xo[rs, :], xn[b][:], r=["xn%d" % b], w=[(xokey, i)])

        stage_A(0)
        stage_A(1)
        for q in range(8):
            b = q % 2
            self.dma("sp", pst[b][:], L("pw")[:, 2 * q:2 * q + 2, :], w=["pst%d" % b])
            self.cp("dve", pwb[:, 2 * q, :], pst[b][:, 0, :], r=["pst%d" % b], w=["pwb"])
            self.cp("act" if q % 2 else "pool", pwb[:, 2 * q + 1, :], pst[b][:, 1, :], r=["pst%d" % b], w=["pwb"])
        stage_B(0)
        for i in range(NT):
            if i + 2 < NT:
                stage_A(i + 2)
            if i + 1 < NT:
                stage_B(i + 1)
            stage_CD(i)


FUSED = True
_CACHE = {}


def _get_nc(layers, debug=False):
    key = (tuple(layers), debug)
    if key not in _CACHE:
        kb = KB(layers, debug)
        _CACHE[key] = kb.build()
    return _CACHE[key]


def _run(layers, inp, xs, vfs, debug=False):
    nc = _get_nc(layers, debug)
    perm = _perm()
    consts = _consts()
    base = dict(consts)
    base["fg"] = np.ascontiguousarray(np.broadcast_to(np.asarray(inp["final_g"], np.float32).reshape(1, D), (128, D)))
    for l in layers:
        for k, v in prep_layer(inp, l, perm).items():
            base["%s%d" % (k, l)] = v
    maps = []
    for b in range(8):
        m = dict(base)
        m["x"] = np.ascontiguousarray(xs[b], dtype=np.float32)
        if vfs is not None:
            m["vfirst"] = np.ascontiguousarray(vfs[b], dtype=np.float32)
        maps.append(m)
    res = run_bass_kernel_spmd(nc, maps, core_ids=list(range(8)))
    return res.results


def kernel(**inputs):
    inp = {k: np.asarray(v) for k, v in inputs.items()}
    x = np.asarray(inp["x"], np.float32)
    if FUSED:
        r = _run([0, 1], inp, [x[b] for b in range(8)], None)
        return np.stack([r[b]["xout"] for b in range(8)], 0).astype(np.float32)
    r0 = _run([0], inp, [x[b] for b in range(8)], None)
    r1 = _run([1], inp, [r0[b]["xout"] for b in range(8)], [r0[b]["vfirst"] for b in range(8)])
    return np.stack([r1[b]["xout"] for b in range(8)], 0).astype(np.float32)
```
